# Optimizing a Trainium2 kernel written in Bass

```python
import math
import jax, jax.numpy as jnp
from jax import lax
import numpy as np

D_MODEL = 1024
BATCH = 8
SEQ = 2048
DEPTH = 2

CTX_LEN = 256
GRID_W = 64
EPS = 1e-6
HY_WIDTH = 256
HY_ORDER = 2
HY_EMB = 33
HY_BANDS = (HY_EMB - 1) // 2
HY_FILTER_HIDDEN = 64
HY_MIN_DECAY = math.log(1e-2) / 1.5
HY_MAX_DECAY = math.log(1e-2) / 0.3
FN_GROUPS = 4
FN_GROUP_DIM = 64
FN_WIDTH = FN_GROUPS * FN_GROUP_DIM
DA_HEADS = 4
DA_QK_DIM = 64
DA_V_DIM = 2 * DA_QK_DIM
DA_WIDTH = DA_HEADS * DA_V_DIM
ROPE_BASE = 10000.0
Q_BLOCK = 128
N_BRANCH = 3
P_HY = 3 * HY_WIDTH
P_FN = FN_WIDTH
P_Q = DA_HEADS * 2 * DA_QK_DIM
P_K = DA_HEADS * 2 * DA_QK_DIM
P_V = DA_WIDTH
P_GATE = N_BRANCH * D_MODEL
P_IN = P_HY + P_FN + P_Q + P_K + P_V + P_GATE
SPLIT_IDX = (P_HY, P_HY + P_FN, P_HY + P_FN + P_Q, P_HY + P_FN + P_Q + P_K, P_HY + P_FN + P_Q + P_K + P_V)
PEER_HEADS = 8
PEER_NKEYS = 128
PEER_EXPERTS = PEER_NKEYS * PEER_NKEYS
PEER_DQ = 256
PEER_TOPK = 16
PEER_CHUNK = 128

kernel_name = 'hybrid_hyena_fnet_diffattn_peer_dit'


def rms_norm(x, g):
    x32 = x.astype(jnp.float32)
    y = x32 * lax.rsqrt(jnp.mean(x32 * x32, axis=-1, keepdims=True) + EPS)
    return (y * g.astype(jnp.float32)).astype(x.dtype)


def modulate(x, g, shift, scale):
    return rms_norm(x, g) * (1 + scale) + shift


def short_conv(z, w, b):
    zp = jnp.pad(z, ((0, 0), (1, 1), (0, 0)))
    return zp[:, :-2] * w[0] + zp[:, 1:-1] * w[1] + zp[:, 2:] * w[2] + b


def hyena_kernels(L, w1, b1, freq, w2, b2, w3):
    f32 = lambda a: a.astype(jnp.float32)
    pos = jnp.arange(L, dtype=jnp.float32)
    t = pos / max(L - 1, 1)
    w = 2.0 * math.pi * pos / L
    f = jnp.linspace(1e-4, HY_BANDS - 1, HY_BANDS, dtype=jnp.float32)
    feats = jnp.concatenate([t[:, None], jnp.cos(w[:, None] * f), -jnp.sin(w[:, None] * f)], axis=-1)
    fr = f32(freq)
    h = jnp.sin(fr * (feats @ f32(w1) + f32(b1)))
    h = jnp.sin(fr * (h @ f32(w2) + f32(b2)))
    h = (h @ f32(w3)).reshape(L, 2, HY_ORDER, HY_WIDTH)
    deltas = jnp.abs(jnp.linspace(HY_MIN_DECAY, HY_MAX_DECAY, HY_WIDTH, dtype=jnp.float32))
    h = h * jnp.exp(-t[:, None, None, None] * deltas)
    k = jnp.concatenate([h[:, 0], jnp.zeros((1, HY_ORDER, HY_WIDTH), jnp.float32), h[:0:-1, 1]], axis=0)
    k = k / jnp.sum(jnp.abs(k), axis=0, keepdims=True)
    return jnp.fft.rfft(k, axis=0)


def hyena_mix(u, kf, bias):
    L = u.shape[1]
    v, x1, x2 = jnp.split(u.astype(jnp.float32), 3, axis=-1)
    z = v
    for o, gate in enumerate((x1, x2)):
        zf = jnp.fft.rfft(z, n=2 * L, axis=1)
        conv = jnp.fft.irfft(zf * kf[:, o], n=2 * L, axis=1)[:, :L]
        z = gate * (conv + bias[o].astype(jnp.float32) * z)
    return z.astype(u.dtype)


def fourier_mix(z):
    B, L, _ = z.shape
    zg = z.astype(jnp.float32).reshape(B, L, FN_GROUPS, FN_GROUP_DIM)
    y = jnp.fft.fftn(zg, axes=(1, 3), norm='ortho').real
    return y.reshape(B, L, FN_WIDTH).astype(z.dtype)


def axial_rope(L):
    rows = L // GRID_W
    row = jnp.repeat(jnp.arange(rows), GRID_W).astype(jnp.float32)
    col = jnp.tile(jnp.arange(GRID_W), rows).astype(jnp.float32)
    half = DA_QK_DIM // 2
    inv = ROPE_BASE ** (-jnp.arange(0, half, 2, dtype=jnp.float32) / half)
    ang = jnp.stack([row[:, None] * inv, col[:, None] * inv], axis=1)
    return jnp.cos(ang), jnp.sin(ang)


def apply_rope(x, cos, sin):
    shp = x.shape
    xr = x.astype(jnp.float32).reshape(shp[:-1] + (2, 2, DA_QK_DIM // 4))
    x1, x2 = xr[..., 0, :], xr[..., 1, :]
    cc = cos[None, :, None, None]
    ss = sin[None, :, None, None]
    out = jnp.stack([x1 * cc - x2 * ss, x2 * cc + x1 * ss], axis=-2)
    return out.reshape(shp).astype(x.dtype)


def split_in(h):
    B, L, _ = h.shape
    hy, fn, q, k, v, gates = jnp.split(h, SPLIT_IDX, axis=-1)
    q = q.reshape(B, L, DA_HEADS, 2, DA_QK_DIM)
    k = k.reshape(B, L, DA_HEADS, 2, DA_QK_DIM)
    v = v.reshape(B, L, DA_HEADS, DA_V_DIM)
    return hy, fn, q, k, v, gates


def heads_first(a):
    return jnp.moveaxis(a, 2, 1)


def diff_attend(q, k, v, lam_val):
    s = jnp.einsum('bhqcd,bhkcd->bhcqk', q, k).astype(jnp.float32) * (DA_QK_DIM ** -0.5)
    p = jax.nn.softmax(s, axis=-1)
    a = p[:, :, 0] - lam_val * p[:, :, 1]
    return jnp.einsum('bhqk,bhkd->bhqd', a.astype(v.dtype), v)


def diff_out(o, g_sub, lam_init):
    B, H, L, _ = o.shape
    o = rms_norm(o, g_sub) * (1.0 - lam_init)
    return jnp.moveaxis(o, 1, 2).reshape(B, L, DA_WIDTH)


def merge(hy_o, fn_o, at_o, gates, w_hy, w_fn, w_at, w_out):
    g = jax.nn.sigmoid(gates.astype(jnp.float32)).astype(gates.dtype)
    g1, g2, g3 = jnp.split(g, N_BRANCH, axis=-1)
    y = g1 * (hy_o @ w_hy) + g2 * (fn_o @ w_fn) + g3 * (at_o @ w_at)
    return y @ w_out


def peer(t, wq, keys, u, v):
    T = t.shape[0]
    q = (t @ wq).reshape(T, PEER_HEADS, 2, PEER_DQ // 2)
    s = jnp.einsum('thpd,hpnd->thpn', q, keys).astype(jnp.float32)
    s1, i1 = lax.top_k(s[:, :, 0], PEER_TOPK)
    s2, i2 = lax.top_k(s[:, :, 1], PEER_TOPK)
    cand = (s1[..., :, None] + s2[..., None, :]).reshape(T, PEER_HEADS, PEER_TOPK * PEER_TOPK)
    sc, ci = lax.top_k(cand, PEER_TOPK)
    e1 = jnp.take_along_axis(i1, ci // PEER_TOPK, axis=-1)
    e2 = jnp.take_along_axis(i2, ci % PEER_TOPK, axis=-1)
    idx = (e1 * PEER_NKEYS + e2).reshape(T, PEER_HEADS * PEER_TOPK)
    g = jax.nn.softmax(sc, axis=-1).reshape(T, PEER_HEADS * PEER_TOPK).astype(t.dtype)
    nchunk = T // PEER_CHUNK

    def chunk(args):
        tc, ic, gc = args
        hid = jnp.einsum('td,tkd->tk', tc, u[ic])
        return jnp.einsum('tk,tkd->td', gc * jax.nn.gelu(hid, approximate=False), v[ic])

    y = lax.map(chunk, (t.reshape(nchunk, PEER_CHUNK, -1), idx.reshape(nchunk, PEER_CHUNK, -1),
                        g.reshape(nchunk, PEER_CHUNK, -1)))
    return y.reshape(T, -1)


def setup_inputs(seed: int = 0) -> dict:
    key = jax.random.key(seed)
    ks = jax.random.split(key, 32)
    D = D_MODEL
    L = DEPTH

    def nrm(i, shape, scale):
        return jax.random.normal(ks[i], shape, jnp.float32) * scale

    return {
        'x': nrm(0, (BATCH, SEQ, D), 1.0),
        'c': nrm(1, (BATCH, D), 1.0),
        'ctx': nrm(2, (BATCH, CTX_LEN, D), 1.0),
        'c_ctx': nrm(3, (D,), 1.0),
        'w_ada': nrm(4, (L, D, 6 * D), 0.5 * D ** -0.5),
        'b_ada': nrm(5, (L, 6 * D), 0.02),
        'g_mix': 1.0 + nrm(6, (L, D), 0.02),
        'g_ffn': 1.0 + nrm(7, (L, D), 0.02),
        'w_in': nrm(8, (L, D, P_IN), D ** -0.5),
        'hy_conv_w': nrm(9, (L, 3, P_HY), 0.5),
        'hy_conv_b': nrm(10, (L, P_HY), 0.02),
        'hy_w1': nrm(11, (L, HY_EMB, HY_FILTER_HIDDEN), HY_EMB ** -0.5),
        'hy_b1': nrm(12, (L, HY_FILTER_HIDDEN), 0.02),
        'hy_freq': 1.0 + nrm(13, (L, HY_FILTER_HIDDEN), 0.1),
        'hy_w2': nrm(14, (L, HY_FILTER_HIDDEN, HY_FILTER_HIDDEN), HY_FILTER_HIDDEN ** -0.5),
        'hy_b2': nrm(15, (L, HY_FILTER_HIDDEN), 0.02),
        'hy_w3': nrm(16, (L, HY_FILTER_HIDDEN, 2 * HY_ORDER * HY_WIDTH), HY_FILTER_HIDDEN ** -0.5),
        'hy_bias': nrm(17, (L, HY_ORDER, HY_WIDTH), 0.5),
        'g_q': 1.0 + nrm(18, (L, 2, DA_QK_DIM), 0.02),
        'g_k': 1.0 + nrm(19, (L, 2, DA_QK_DIM), 0.02),
        'lam': nrm(20, (L, 4, DA_QK_DIM), 0.1),
        'g_sub': 1.0 + nrm(21, (L, DA_V_DIM), 0.02),
        'w_hy': nrm(22, (L, HY_WIDTH, D), HY_WIDTH ** -0.5),
        'w_fn': nrm(23, (L, FN_WIDTH, D), FN_WIDTH ** -0.5),
        'w_at': nrm(24, (L, DA_WIDTH, D), DA_WIDTH ** -0.5),
        'w_out': nrm(25, (L, D, D), D ** -0.5),
        'peer_wq': nrm(26, (L, D, PEER_HEADS * PEER_DQ), D ** -0.5),
        'peer_keys': nrm(27, (L, PEER_HEADS, 2, PEER_NKEYS, PEER_DQ // 2), (PEER_DQ // 2) ** -0.5),
        'peer_u': nrm(28, (L, PEER_EXPERTS, D), D ** -0.5),
        'peer_v': nrm(29, (L, PEER_EXPERTS, D), 0.25),
    }


def reference(x, c, ctx, c_ctx, w_ada, b_ada, g_mix, g_ffn, w_in, hy_conv_w, hy_conv_b,
              hy_w1, hy_b1, hy_freq, hy_w2, hy_b2, hy_w3, hy_bias, g_q, g_k, lam, g_sub,
              w_hy, w_fn, w_at, w_out, peer_wq, peer_keys, peer_u, peer_v):
    B, S, D = x.shape
    C = ctx.shape[1]
    cos, sin = axial_rope(S)
    xl, xc = x, ctx
    for l in range(DEPTH):
        last = l == DEPTH - 1
        mod_l = jnp.split((jax.nn.silu(c) @ w_ada[l] + b_ada[l])[:, None, :], 6, axis=-1)
        mod_c = jnp.split((jax.nn.silu(c_ctx) @ w_ada[l] + b_ada[l])[None, None, :], 6, axis=-1)
        lam_init = 0.8 - 0.6 * math.exp(-0.3 * l)
        lf = lam[l].astype(jnp.float32)
        lam_val = jnp.exp(jnp.sum(lf[0] * lf[1])) - jnp.exp(jnp.sum(lf[2] * lf[3])) + lam_init
        hy_args = (hy_w1[l], hy_b1[l], hy_freq[l], hy_w2[l], hy_b2[l], hy_w3[l])

        hy_l, fn_l, q_l, k_l, v_l, gate_l = split_in(modulate(xl, g_mix[l], mod_l[0], mod_l[1]) @ w_in[l])
        hy_c, fn_c, q_c, k_c, v_c, gate_c = split_in(modulate(xc, g_mix[l], mod_c[0], mod_c[1]) @ w_in[l])
        q_l = apply_rope(rms_norm(q_l, g_q[l]), cos, sin)
        k_l = apply_rope(rms_norm(k_l, g_k[l]), cos, sin)
        k_c = heads_first(rms_norm(k_c, g_k[l]))
        v_c = heads_first(v_c)
        k_all = jnp.concatenate([heads_first(k_l), k_c], axis=2)
        v_all = jnp.concatenate([heads_first(v_l), v_c], axis=2)
        qb = heads_first(q_l).reshape(B, DA_HEADS, S // Q_BLOCK, Q_BLOCK, 2, DA_QK_DIM)
        o_l = lax.map(lambda qq: diff_attend(qq, k_all, v_all, lam_val), jnp.moveaxis(qb, 2, 0))
        o_l = jnp.moveaxis(o_l, 0, 2).reshape(B, DA_HEADS, S, DA_V_DIM)
        att_l = diff_out(o_l, g_sub[l], lam_init)
        hyo_l = hyena_mix(short_conv(hy_l, hy_conv_w[l], hy_conv_b[l]), hyena_kernels(S, *hy_args), hy_bias[l])
        mix_l = merge(hyo_l, fourier_mix(fn_l), att_l, gate_l, w_hy[l], w_fn[l], w_at[l], w_out[l])
        if not last:
            q_c = heads_first(rms_norm(q_c, g_q[l]))
            att_c = diff_out(diff_attend(q_c, k_c, v_c, lam_val), g_sub[l], lam_init)
            hyo_c = hyena_mix(short_conv(hy_c, hy_conv_w[l], hy_conv_b[l]), hyena_kernels(C, *hy_args), hy_bias[l])
            mix_c = merge(hyo_c, fourier_mix(fn_c), att_c, gate_c, w_hy[l], w_fn[l], w_at[l], w_out[l])
            xc = xc + mod_c[2] * mix_c
        xl = xl + mod_l[2] * mix_l

        n_l = modulate(xl, g_ffn[l], mod_l[3], mod_l[4]).reshape(B * S, D)
        if last:
            y_l = peer(n_l, peer_wq[l], peer_keys[l], peer_u[l], peer_v[l])
        else:
            n_c = modulate(xc, g_ffn[l], mod_c[3], mod_c[4]).reshape(B * C, D)
            y = peer(jnp.concatenate([n_c, n_l], axis=0), peer_wq[l], peer_keys[l], peer_u[l], peer_v[l])
            y_l = y[B * C:]
            xc = xc + mod_c[5] * y[:B * C].reshape(B, C, D)
        xl = xl + mod_l[5] * y_l.reshape(B, S, D)
    return xl
```

```python
import numpy as np, math
from contextlib import ExitStack
import concourse.bass as bass
import concourse.mybir as mybir
from concourse.bass_utils import run_bass_kernel_spmd

F32 = mybir.dt.float32
BF16 = mybir.dt.bfloat16
ALU = mybir.AluOpType
AF = mybir.ActivationFunctionType
AX = mybir.AxisListType

NSLOT = 40
ENGS = ("pe", "act", "dve", "pool", "sp")


class Buf:
    __slots__ = ("w", "rs", "rd")

    def __init__(self):
        self.w = None
        self.rs = {}
        self.rd = []


class Op:
    __slots__ = ("eng", "fn", "deps", "sig", "cnt", "slot", "dma")

    def __init__(self, eng, fn, dma=False):
        self.eng = eng
        self.fn = fn
        self.deps = ()
        self.sig = False
        self.cnt = 0
        self.slot = -1
        self.dma = dma


class Prog:
    def __init__(self, nc):
        self.nc = nc
        self.streams = {e: [] for e in ENGS}
        self.dmas = []
        self.live_dmas = []
        self.last_real = {e: None for e in ENGS}

    def op(self, eng, fn, reads=(), writes=(), dma=False):
        o = Op(eng, fn, dma)
        deps = set()
        for b in reads:
            if b.w is not None:
                deps.add(b.w)
        for b in writes:
            if b.w is not None:
                deps.add(b.w)
            deps.update(b.rs.values())
            deps.update(b.rd)
        if eng == "pe" and not dma:
            deps = {d for d in deps if d.dma or d.eng != "pe"}
        o.deps = deps
        for b in writes:
            b.w = o
            b.rs = {}
            b.rd = []
        for b in reads:
            if dma:
                b.rd.append(o)
            else:
                b.rs[eng] = o
        self.streams[eng].append(o)
        if not dma:
            self.last_real[eng] = o
        if dma:
            self.dmas.append(o)
            self.live_dmas.append(o)
        return o

    def dma(self, out, in_, reads=(), writes=(), eng="sp"):
        return self.op(eng, lambda e: e.dma_start(out=out, in_=in_), reads, writes, dma=True)

    def barrier(self):
        last = dict(self.last_real)
        live = list(self.live_dmas)
        self.live_dmas = []
        for e in ENGS:
            o = Op(e, None)
            o.deps = {last[x] for x in ENGS if x != e and last[x] is not None}
            o.deps.update(live)
            self.streams[e].append(o)

    def emit(self, final_dmas):
        nc = self.nc
        for e in ENGS:
            for o in self.streams[e]:
                for d in o.deps:
                    d.sig = True
        with ExitStack() as es:
            sems = {e: es.enter_context(nc.semaphore("s_" + e)) for e in ENGS}
            dsem = [es.enter_context(nc.semaphore("d%d" % i)) for i in range(NSLOT)]
            for e in ENGS:
                c = 0
                for o in self.streams[e]:
                    if o.dma:
                        continue
                    if o.sig:
                        c += 1
                        o.cnt = c
            slot_cnt = [0] * NSLOT
            slot_prev = [None] * NSLOT
            for i, o in enumerate(self.dmas):
                s = i % NSLOT
                o.slot = s
                slot_cnt[s] += 16
                o.cnt = slot_cnt[s]
                if slot_prev[s] is not None:
                    o.deps = set(o.deps)
                    o.deps.add(slot_prev[s])
                slot_prev[s] = o
            block = es.enter_context(nc.Block())

            def run(ename, eng):
                waited = {}
                for o in self.streams[ename]:
                    for d in o.deps:
                        sem = dsem[d.slot] if d.dma else sems[d.eng]
                        if waited.get(sem.name, 0) >= d.cnt:
                            continue
                        eng.wait_ge(sem, d.cnt)
                        waited[sem.name] = d.cnt
                    if o.fn is None:
                        continue
                    ins = o.fn(eng)
                    if o.dma:
                        ins.then_inc(dsem[o.slot], 16)
                    elif o.sig:
                        ins.then_inc(sems[ename], 1)
                if ename == "sp":
                    for d in final_dmas:
                        eng.wait_ge(dsem[d.slot], d.cnt)

            @block.sync
            def _(e):
                run("sp", e)

            @block.tensor
            def _(e):
                run("pe", e)

            @block.scalar
            def _(e):
                run("act", e)

            @block.vector
            def _(e):
                run("dve", e)

            @block.gpsimd
            def _(e):
                run("pool", e)

D = 1024
S = 2048
C = 256
T = S + C
EPS = 1e-6
PI = math.pi
NE = 16384


def tblocks(lo, hi, step=512):
    return [(t, min(step, hi - t)) for t in range(lo, hi, step)]


def build(depth=2, stages=None, dbg=()):
    nc = bass.Bass("TRN2", target_bir_lowering=False)
    P = Prog(nc)

    def din(name, shape, dt=F32):
        return nc.dram_tensor(name, list(shape), dt, kind="ExternalInput").ap()

    def dscr(name, shape, dt=F32):
        if name in dbg:
            return nc.dram_tensor(name, list(shape), dt, kind="ExternalOutput").ap()
        return nc.dram_tensor(name, list(shape), dt).ap()

    xT_d = din("xT", [128, 8, T])
    cc_d = din("cc", [128, 8, 2])
    w_ada_d = din("w_ada", [2, D, 6 * D])
    bada_d = din("bada", [2, 128, 6, 8])
    gmf_d = din("gmf", [2, 128, 2, 8])
    w_in_d = din("w_in", [2, D, 5632])
    w_gate_d = din("w_gate", [2, 8, 128, 8, 3, 128])
    hcw_d = din("hcw", [2, 128, 6, 4])
    hy_w1_d = din("hy_w1", [2, 33, 64])
    hy_fb_d = din("hy_fb", [2, 64, 3])
    hy_w2_d = din("hy_w2", [2, 64, 64])
    hy_w3_d = din("hy_w3", [2, 64, 1024])
    hy_bias_d = din("hy_bias", [2, 128, 2, 2])
    gqk_d = din("gqk", [2, 128, 2])
    lam_d = din("lam", [2, 1, 256])
    gsub_d = din("gsub", [2, 128, 1])
    w_br_d = din("w_br", [2, D, D])
    w_out_d = din("w_out", [2, D, D])
    wq_d = din("peer_wq", [2, D, 2048])
    keysT_d = din("keysT", [2, 128, 16, 128])
    uT_d = din("uT", [2, D, NE])
    v_d_in = din("peer_v", [2, NE, D])
    cst_d = din("cst", [128, 6, 128])
    rope_d = din("rope", [2, 128, S])
    feats_d = {L: din("feats%d" % L, [33, L]) for L in (S, C)}
    dec_d = {L: din("dec%d" % L, [L, 256]) for L in (S, C)}
    tF_d = {L: din("tF%d" % L, [2, L // 128, 128, L // 128, 128]) for L in (S, C)}
    RL = {}
    for L in (S, C):
        TB = min(512, L)
        G = min(4, L // 128)
        RL[L] = (TB, G, L // TB, (L // 128) // G)
    tI_d = {L: din("tI%d" % L, [2, RL[L][2], RL[L][3], 128, RL[L][1], RL[L][0]]) for L in (S, C)}
    tN_d = {L: din("tN%d" % L, [2, RL[L][2], RL[L][3], 128, RL[L][1], RL[L][0]]) for L in (S, C)}
    yT_d = nc.dram_tensor("yT", [128, 8, S], F32, kind="ExternalOutput").ap()
    hy_s = dscr("hy_s", [6, 128, T])
    fn_s = dscr("fn_s", [2, 128, T])
    q_s = dscr("q_s", [4, 128, T], BF16)
    k_s = dscr("k_s", [4, 128, T], BF16)
    v_s = dscr("v_s", [128, 18, 512], BF16)
    kf_s = {L: dscr("kf_s%d" % L, [2, L // 128, 128, 512]) for L in (S, C)}
    hyo_s = dscr("hyo_s", [2, 128, T], BF16)
    fno_s = dscr("fno_s", [2, 128, T], BF16)
    ato_s = dscr("ato_s", [4, 128, T], BF16)
    nT_s = dscr("nT_s", [128, 8, T], BF16)
    uTb_s = dscr("uTb_s", [128, 8, NE], BF16)
    vb_s = dscr("vb_s", [128, 128, D], BF16)
    wT_s = dscr("wT_s", [2, 128, 128, 128], BF16)
    dbg_out = {}

    es = ExitStack()
    xT = es.enter_context(nc.sbuf_tensor("xT_sb", [128, 8, T], F32))
    cst = es.enter_context(nc.sbuf_tensor("cst_sb", [128, 6, 128], F32))
    cstb = es.enter_context(nc.sbuf_tensor("cstb_sb", [128, 2, 128], BF16))
    sm = es.enter_context(nc.sbuf_tensor("small_sb", [128, 512], F32))
    ARENA = 33280
    AR = es.enter_context(nc.sbuf_tensor("arena", [128, ARENA], F32))
    PS = [es.enter_context(nc.psum_tensor("ps%d" % i, [128, 512], F32)) for i in range(8)]
    PB = [Buf() for _ in range(8)]
    ident32 = cst[:, 0, :]
    ones32 = cst[:, 1, :]
    bd64 = cst[:, 2, :]
    rot = cst[:, 3, :]
    bdc = cst[:, 4, :]
    bds = cst[:, 5, :]
    identb = cstb[:, 0, :]
    onesb = cstb[:, 1, :]
    b_cst = Buf()
    b_x = [[Buf() for _ in range(5)] for _ in range(8)]
    b_sm = Buf()
    sc = sm[:, 0:16].rearrange("p (k j) -> p k j", k=8)
    modv = sm[:, 16:112].rearrange("p (g m j) -> p g m j", g=6, m=8)
    A_mix = sm[:, 112:128].rearrange("p (k j) -> p k j", k=8)
    A_ffn = sm[:, 128:144].rearrange("p (k j) -> p k j", k=8)
    gmf = sm[:, 144:160].rearrange("p (a k) -> p a k", a=2)
    tmp16 = sm[:, 160:176].rearrange("p (k j) -> p k j", k=8)
    gqk = sm[:, 176:178]
    gsub_s = sm[:, 178:179]
    neg_lam = sm[:, 179:180]
    lamw = sm[:, 180:182]
    hcw = sm[:, 184:208].rearrange("p (c t) -> p c t", c=6)
    hbias = sm[:, 208:212].rearrange("p (o c) -> p o c", o=2)
    fb = sm[:, 212:217]
    bada = sm[:, 224:272].rearrange("p (g m) -> p g m", g=6)
    lamt = sm[:, 272:400]
    epsc = sm[:, 183:184]

    class Arena:
        def __init__(self):
            self.off = 0

        def reset(self):
            P.barrier()
            self.off = 0

        def f32(self, n):
            a = AR[:, self.off:self.off + n]
            self.off += n
            assert self.off <= ARENA, self.off
            return a

        def bf(self, n):
            w = (n + 1) // 2
            return self.f32(w).bitcast(BF16)[:, 0:n]

    ar = Arena()

    def mm(out, lhsT, rhs, start, stop, rd, wr):
        P.op("pe", lambda e: e.matmul(out, lhsT=lhsT, rhs=rhs, start=start, stop=stop), rd, wr)

    def tr(out, in_, idn, rd, wr):
        P.op("pe", lambda e: e.transpose(out=out, in_=in_, identity=idn), rd, wr)

    def act(out, in_, func, rd, wr, bias=0.0, scale=1.0):
        P.op("act", lambda e: e.activation(out=out, in_=in_, func=func, bias=bias, scale=scale), rd, wr)

    def tt(eng, out, in0, in1, op, rd, wr):
        P.op(eng, lambda e: e.tensor_tensor(out=out, in0=in0, in1=in1, op=op), rd, wr)

    def ts(eng, out, in0, s1, s2, op0, op1, rd, wr):
        if s2 is None:
            P.op(eng, lambda e: e.tensor_scalar(out=out, in0=in0, scalar1=s1, scalar2=None, op0=op0), rd, wr)
        else:
            P.op(eng, lambda e: e.tensor_scalar(out=out, in0=in0, scalar1=s1, scalar2=s2, op0=op0, op1=op1), rd, wr)

    def stt(out, in0, scalar, in1, op0, op1, rd, wr):
        P.op("dve", lambda e: e.scalar_tensor_tensor(out=out, in0=in0, scalar=scalar, in1=in1, op0=op0, op1=op1), rd, wr)

    def cp(eng, out, in_, rd, wr):
        if eng == "act":
            act(out, in_, AF.Copy, rd, wr)
        else:
            P.op(eng, lambda e: e.tensor_copy(out=out, in_=in_), rd, wr)

    def recip(out, in_, rd, wr):
        P.op("dve", lambda e: e.reciprocal(out=out, in_=in_), rd, wr)

    def rsqrt_mean(out, in_, n, rd, wr):
        act(out, in_, AF.Sqrt, list(rd) + [b_sm], wr, bias=epsc, scale=1.0 / n)
        recip(out, out, wr, wr)

    P.dma(cst[:, :, :], cst_d, writes=[b_cst])
    for k in range(8):
        P.dma(xT[:, k, :], xT_d[:, k, :], writes=b_x[k])
    P.dma(sc, cc_d, writes=[b_sm])
    cp("dve", cstb[:, 0, :], ident32, [b_cst], [b_cst])
    cp("dve", cstb[:, 1, :], ones32, [b_cst], [b_cst])
    act(sc, sc, AF.Silu, [b_sm], [b_sm])
    P.op("dve", lambda e: e.memset(epsc, EPS), [], [b_sm])
    ar.reset()

    def want(name):
        return stages is None or name in stages

    def layer(l):
        last = l == depth - 1
        lam_init = 0.8 - 0.6 * math.exp(-0.3 * l)
        streams = [(0, S)] if last else [(0, S), (S, C)]
        tb_all = tblocks(0, T)
        tb_mix = tblocks(0, S) if last else tb_all

        ar.reset()
        P.dma(bada, bada_d[l], writes=[b_sm])
        P.dma(gmf, gmf_d[l], writes=[b_sm])
        P.dma(gqk, gqk_d[l], writes=[b_sm])
        P.dma(gsub_s, gsub_d[l], writes=[b_sm])
        P.dma(hcw, hcw_d[l], writes=[b_sm])
        P.dma(hbias, hy_bias_d[l], writes=[b_sm])
        P.dma(fb[0:64, 0:3], hy_fb_d[l], writes=[b_sm])
        P.dma(lamt, lam_d[l][:, 0:128].partition_broadcast(128), writes=[b_sm])
        lam2 = ar.f32(128)
        b_l2 = Buf()
        P.dma(lam2, lam_d[l][:, 128:256].partition_broadcast(128), writes=[b_l2])
        wts = [(ar.f32(8 * 1024), Buf()) for _ in range(2)]
        for g in range(6):
            wt, bw = wts[g % 2]
            wt3 = wt.rearrange("p (k m) -> p k m", k=8)
            P.dma(wt3, w_ada_d[l][:, g * 1024:(g + 1) * 1024].rearrange("(k p) m -> p k m", p=128), writes=[bw])
            for m in range(8):
                for k in range(8):
                    mm(PS[0][:, m * 2:m * 2 + 2], wt3[:, k, m * 128:(m + 1) * 128], sc[:, k, :], k == 0, k == 7, [bw, b_sm], [PB[0]])
            tt("dve", modv[:, g], PS[0][:, 0:16].rearrange("p (m j) -> p m j", m=8),
               bada[:, g, :].unsqueeze(2).to_broadcast([128, 8, 2]), ALU.add, [PB[0], b_sm], [b_sm])
        for (Aap, gi, si) in ((A_mix, 0, 1), (A_ffn, 1, 4)):
            ts("dve", tmp16, modv[:, si], 1.0, None, ALU.add, None, [b_sm], [b_sm])
            tt("dve", Aap, tmp16, gmf[:, gi, :].unsqueeze(2).to_broadcast([128, 8, 2]), ALU.mult, [b_sm], [b_sm])
        tt("dve", lamt[:, 0:64], lamt[:, 0:64], lamt[:, 64:128], ALU.mult, [b_sm], [b_sm])
        tt("dve", lam2[:, 0:64], lam2[:, 0:64], lam2[:, 64:128], ALU.mult, [b_l2], [b_l2])
        P.op("dve", lambda e: e.reduce_sum(out=lamw[:, 0:1], in_=lamt[:, 0:64], axis=AX.X), [b_sm], [b_sm])
        P.op("dve", lambda e: e.reduce_sum(out=lamw[:, 1:2], in_=lam2[:, 0:64], axis=AX.X), [b_l2, b_sm], [b_sm])
        act(lamw, lamw, AF.Exp, [b_sm], [b_sm])
        tt("dve", neg_lam, lamw[:, 1:2], lamw[:, 0:1], ALU.subtract, [b_sm], [b_sm])
        ts("dve", neg_lam, neg_lam, -lam_init, None, ALU.add, None, [b_sm], [b_sm])
        ts("dve", gsub_s, gsub_s, 1.0 - lam_init, None, ALU.mult, None, [b_sm], [b_sm])
        tt("dve", fb[0:64, 3:4], fb[0:64, 0:1], fb[0:64, 1:2], ALU.mult, [b_sm], [b_sm])
        tt("dve", fb[0:64, 4:5], fb[0:64, 2:3], fb[0:64, 1:2], ALU.mult, [b_sm], [b_sm])

        def mod_tmps(w):
            return dict(sq=[(ar.f32(w), Buf()) for _ in range(3)], rs=[(ar.f32(w), Buf()) for _ in range(2)],
                        tm=[(ar.f32(w), Buf()) for _ in range(3)], c=[0, 0, 0])

        def modulate(Aap, gB, blocks, emit_cb, mt):
            for (t0, Tn) in blocks:
                j = 0 if t0 < S else 1
                xb = t0 // 512
                for k in range(8):
                    sq, b_sq = mt["sq"][mt["c"][0] % 3]
                    mt["c"][0] += 1
                    act(sq[:, :Tn], xT[:, k, t0:t0 + Tn], AF.Square, [b_x[k][xb]], [b_sq])
                    mm(PS[7][:, :Tn], ones32, sq[:, :Tn], k == 0, k == 7, [b_sq, b_cst], [PB[7]])
                r, b_r = mt["rs"][mt["c"][1] % 2]
                mt["c"][1] += 1
                rsqrt_mean(r[:, :Tn], PS[7][:, :Tn], D, [PB[7]], [b_r])
                for k in range(8):
                    tm, b_t = mt["tm"][mt["c"][2] % 3]
                    mt["c"][2] += 1
                    tt("dve", tm[:, :Tn], xT[:, k, t0:t0 + Tn], r[:, :Tn], ALU.mult, [b_x[k][xb], b_r], [b_t])
                    emit_cb(k, t0, Tn, tm, b_t, Aap[:, k, j:j + 1], modv[:, gB, k, j:j + 1])

        def hyena_filters(L):
            ar.reset()
            nj = L // 128
            w1 = ar.f32(64)
            w2 = ar.f32(64)
            w3 = ar.f32(1024)
            b_w = Buf()
            P.dma(w1[0:33, :], hy_w1_d[l], writes=[b_w])
            P.dma(w2[0:64, :], hy_w2_d[l], writes=[b_w])
            P.dma(w3[0:64, :], hy_w3_d[l], writes=[b_w])
            h2T = ar.f32(L)
            b_h2 = Buf()
            off_mlp = ar.off
            ft = [(ar.f32(512), Buf()) for _ in range(2)]
            aa = [(ar.f32(512), Buf()) for _ in range(2)]
            h1 = [(ar.f32(512), Buf()) for _ in range(2)]
            aa2 = [(ar.f32(512), Buf()) for _ in range(2)]
            sx = [(ar.f32(512), Buf()) for _ in range(3)]

            def sin_act(out, a, Tn, b_in, b_out):
                (s4, b_s4), (c4, b_c4), (q, b_q) = sx
                act(s4[0:64, :Tn], a, AF.Sin, [b_in], [b_s4], scale=0.25)
                act(c4[0:64, :Tn], a, AF.Abs, [b_in], [b_c4])
                act(c4[0:64, :Tn], c4[0:64, :Tn], AF.Sin, [b_c4, b_sm], [b_c4], bias=halfpi[0:64, :], scale=-0.25)
                tt("dve", q[0:64, :Tn], s4[0:64, :Tn], s4[0:64, :Tn], ALU.mult, [b_s4], [b_q])
                ts("dve", q[0:64, :Tn], q[0:64, :Tn], -2.0, 1.0, ALU.mult, ALU.add, [b_q], [b_q])
                tt("dve", c4[0:64, :Tn], s4[0:64, :Tn], c4[0:64, :Tn], ALU.mult, [b_s4, b_c4], [b_c4])
                stt(out, c4[0:64, :Tn], 4.0, q[0:64, :Tn], ALU.mult, ALU.mult, [b_c4, b_q], [b_out])

            for bi, (t0, Tn) in enumerate(tblocks(0, L)):
                f, b_f = ft[bi % 2]
                P.dma(f[0:33, :Tn], feats_d[L][:, t0:t0 + Tn], writes=[b_f])
                mm(PS[0][0:64, :Tn], w1[0:33, :], f[0:33, :Tn], True, True, [b_w, b_f], [PB[0]])
                a, b_a = aa[bi % 2]
                ts("dve", a[0:64, :Tn], PS[0][0:64, :Tn], fb[0:64, 1:2], fb[0:64, 3:4], ALU.mult, ALU.add, [PB[0], b_sm], [b_a])
                hh, b_h = h1[bi % 2]
                sin_act(hh[0:64, :Tn], a[0:64, :Tn], Tn, b_a, b_h)
                mm(PS[1][0:64, :Tn], w2[0:64, :], hh[0:64, :Tn], True, True, [b_w, b_h], [PB[1]])
                a2_, b_a2 = aa2[bi % 2]
                ts("dve", a2_[0:64, :Tn], PS[1][0:64, :Tn], fb[0:64, 1:2], fb[0:64, 4:5], ALU.mult, ALU.add, [PB[1], b_sm], [b_a2])
                sin_act(h2T[0:64, t0:t0 + Tn], a2_[0:64, :Tn], Tn, b_a2, b_h2)
            dec = ar.f32(nj * 256).rearrange("p (j c) -> p j c", j=nj)
            b_dec = Buf()
            P.dma(dec, dec_d[L].rearrange("(j p) c -> p j c", p=128), writes=[b_dec])
            hd = ar.f32(nj * 1024).rearrange("p (j c) -> p j c", j=nj)
            b_hd = [Buf() for _ in range(nj)]
            off_hd = ar.off
            for pc in range(nj):
                for half in range(2):
                    pb = half
                    mm(PS[pb][:, :], h2T[0:64, pc * 128:(pc + 1) * 128], w3[0:64, half * 512:(half + 1) * 512], True, True, [b_h2, b_w], [PB[pb]])
                    tt("dve", hd[:, pc, half * 512:(half + 1) * 512].rearrange("p (o c) -> p o c", o=2),
                       PS[pb][:, :].rearrange("p (o c) -> p o c", o=2),
                       dec[:, pc, :].unsqueeze(1).to_broadcast([128, 2, 256]), ALU.mult, [PB[pb], b_dec], [b_hd[pc]])
            P.op("dve", lambda e: e.memset(hd[0:1, 0, 512:1024], 0.0), [], [b_hd[0]])
            ab = [(ar.f32(1024), Buf()) for _ in range(2)]
            for pc in range(nj):
                a, b_a = ab[pc % 2]
                act(a, hd[:, pc, :], AF.Abs, [b_hd[pc]], [b_a])
                for half in range(2):
                    mm(PS[2 + half][:, :], ones32, a[:, half * 512:(half + 1) * 512], pc == 0, pc == nj - 1, [b_a, b_cst], [PB[2 + half]])
            rn = ar.f32(512)
            b_rn = Buf()
            cp("dve", rn, PS[2][:, :], [PB[2]], [b_rn])
            tt("dve", rn, rn, PS[3][:, :], ALU.add, [b_rn, PB[3]], [b_rn])
            recip(rn, rn, [b_rn], [b_rn])
            tmpe = [(ar.f32(512), Buf()) for _ in range(2)]
            for pc in range(nj):
                te, b_te = tmpe[pc % 2]
                hf = hd[:, pc, 0:512]
                hb = hd[:, pc, 512:1024]
                tt("dve", te, hf, hb, ALU.add, [b_hd[pc]], [b_te])
                tt("pool", hb, hf, hb, ALU.subtract, [b_hd[pc]], [b_hd[pc]])
                tt("dve", hf, te, rn, ALU.mult, [b_te, b_rn], [b_hd[pc]])
                tt("pool", hb, hb, rn, ALU.mult, [b_hd[pc], b_rn], [b_hd[pc]])
            P.barrier()
            ar.off = off_mlp
            tabA = ar.f32(2 * nj * 128)
            tabB = ar.f32(2 * nj * 128)
            ar.off = off_hd
            sts = [(ar.f32(1024).rearrange("p (c n) -> p c n", c=2), Buf()) for _ in range(2)]
            tabs = [(t_.rearrange("p (c j f) -> p c j f", c=2, j=nj), Buf()) for t_ in (tabA, tabB)]
            for fc in range(nj):
                tb_, b_tb = tabs[fc % 2]
                for c in range(2):
                    P.dma(tb_[:, c], tF_d[L][c, fc], writes=[b_tb])
                for c in range(2):
                    for jc in range(nj):
                        mm(PS[4 + c][:, :], tb_[:, c, jc, :], hd[:, jc, c * 512:(c + 1) * 512], jc == 0, jc == nj - 1, [b_tb, b_hd[jc]], [PB[4 + c]])
                st, b_st = sts[fc % 2]
                cp("act", st[:, 0, :], PS[4][:, :], [PB[4]], [b_st])
                cp("dve", st[:, 1, :], PS[5][:, :], [PB[5]], [b_st])
                for c in range(2):
                    P.dma(kf_s[L][c, fc], st[:, c, :], reads=[b_st], writes=[b_kf[L]])

        b_kf = {S: Buf(), C: Buf()}
        halfpi = sm[:, 182:183]
        P.op("dve", lambda e: e.memset(halfpi, PI / 2), [], [b_sm])
        if want("hyf"):
            for (t0, L) in streams:
                hyena_filters(L)

        ar.reset()
        nT = ar.bf(8 * T).rearrange("p (k t) -> p k t", k=8)
        b_n = [[Buf() for _ in range(5)] for _ in range(8)]
        b_nTs = Buf()

        def emit_mix(k, t0, Tn, tm, b_t, Asc, Bsc):
            act(nT[:, k, t0:t0 + Tn], tm[:, :Tn], AF.Identity, [b_t, b_sm], [b_n[k][t0 // 512]], bias=Bsc, scale=Asc)

        mark = ar.off
        if want("proj") or want("merge"):
            modulate(A_mix, 0, tb_all, emit_mix, mod_tmps(512))
            for k in range(8):
                P.dma(nT_s[:, k, :], nT[:, k, :], reads=b_n[k], writes=[b_nTs])
        ar.off = mark
        P.barrier()

        b_hy = Buf()
        b_fn = Buf()
        b_q = Buf()
        b_k = Buf()
        b_v = Buf()
        if want("proj"):
            w32 = [(ar.f32(8 * 512).rearrange("p (k m) -> p k m", k=8), Buf()) for _ in range(1)]
            wbf = [(ar.bf(8 * 512).rearrange("p (k m) -> p k m", k=8), Buf()) for _ in range(2)]
            stg = [(ar.f32(512), Buf()) for _ in range(2)]
            sqt = [(ar.f32(512), Buf()) for _ in range(2)]
            rt = [(ar.f32(512), Buf()) for _ in range(2)]
            xnt = [(ar.f32(512), Buf()) for _ in range(2)]
            t1t = [(ar.f32(512), Buf()) for _ in range(2)]
            t2t = [(ar.f32(512), Buf()) for _ in range(2)]
            obt = [(ar.bf(512), Buf()) for _ in range(2)]
            rope_sb = ar.f32(2 * S).rearrange("p (c t) -> p c t", c=2)
            b_rope = Buf()
            for c in range(2):
                P.dma(rope_sb[:, c, :], rope_d[c], writes=[b_rope])
            cnt = [0]

            def qk_cb(ps, pb, which, h, t0, Tn):
                i = cnt[0] % 2
                cnt[0] += 1
                sq_, b_sq_ = sqt[i]
                act(sq_[:, :Tn], ps[:, :Tn], AF.Square, [pb], [b_sq_])
                mm(PS[6][:, :Tn], bd64, sq_[:, :Tn], True, True, [b_sq_, b_cst], [PB[6]])
                r_, b_r_ = rt[i]
                rsqrt_mean(r_[:, :Tn], PS[6][:, :Tn], 64, [PB[6]], [b_r_])
                xn, b_xn = xnt[i]
                stt(xn[:, :Tn], ps[:, :Tn], gqk[:, which:which + 1], r_[:, :Tn], ALU.mult, ALU.mult, [pb, b_r_, b_sm], [b_xn])
                ob, b_ob = obt[i]
                if t0 < S:
                    mm(PS[5][:, :Tn], rot, xn[:, :Tn], True, True, [b_xn, b_cst], [PB[5]])
                    t1, b_t1 = t1t[i]
                    t2, b_t2 = t2t[i]
                    tt("pool", t1[:, :Tn], xn[:, :Tn], rope_sb[:, 0, t0:t0 + Tn], ALU.mult, [b_xn, b_rope], [b_t1])
                    tt("dve", t2[:, :Tn], PS[5][:, :Tn], rope_sb[:, 1, t0:t0 + Tn], ALU.mult, [PB[5], b_rope], [b_t2])
                    tt("pool", ob[:, :Tn], t1[:, :Tn], t2[:, :Tn], ALU.add, [b_t1, b_t2], [b_ob])
                else:
                    cp("pool", ob[:, :Tn], xn[:, :Tn], [b_xn], [b_ob])
                dst = (q_s if which == 0 else k_s)[h][:, t0:t0 + Tn]
                P.dma(dst, ob[:, :Tn], reads=[b_ob], writes=[b_q if which == 0 else b_k])

            pcnt = [0]
            for g in range(5):
                w3_, b_w3 = w32[0]
                P.dma(w3_, w_in_d[l][:, g * 512:(g + 1) * 512].rearrange("(k p) m -> p k m", p=128), writes=[b_w3])
                wb_, b_wb = wbf[g % 2]
                cp("act", wb_[:, 0:4, :], w3_[:, 0:4, :], [b_w3], [b_wb])
                cp("pool", wb_[:, 4:8, :], w3_[:, 4:8, :], [b_w3], [b_wb])
                if g < 4:
                    for mi in range(4):
                        for (t0, Tn) in tb_all:
                            if g == 2 and t0 >= S and last:
                                continue
                            pi = pcnt[0] % 4
                            pcnt[0] += 1
                            for k in range(8):
                                mm(PS[pi][:, :Tn], wb_[:, k, mi * 128:(mi + 1) * 128], nT[:, k, t0:t0 + Tn], k == 0, k == 7,
                                   [b_wb, b_n[k][t0 // 512]], [PB[pi]])
                            if g < 2:
                                st, b_st = stg[pcnt[0] % 2]
                                cp("act" if pcnt[0] % 2 else "dve", st[:, :Tn], PS[pi][:, :Tn], [PB[pi]], [b_st])
                                ch = g * 4 + mi
                                if ch < 6:
                                    P.dma(hy_s[ch][:, t0:t0 + Tn], st[:, :Tn], reads=[b_st], writes=[b_hy])
                                else:
                                    P.dma(fn_s[ch - 6][:, t0:t0 + Tn], st[:, :Tn], reads=[b_st], writes=[b_fn])
                            else:
                                qk_cb(PS[pi], PB[pi], g - 2, mi, t0, Tn)
                else:
                    for i in range(18):
                        pi = pcnt[0] % 4
                        pcnt[0] += 1
                        for k in range(8):
                            mm(PS[pi][:, :], nT[:, k, i * 128:(i + 1) * 128], wb_[:, k, :], k == 0, k == 7, [b_wb, b_n[k][i // 4]], [PB[pi]])
                        ob, b_ob = obt[i % 2]
                        cp("act" if i % 2 else "dve", ob, PS[pi][:, :], [PB[pi]], [b_ob])
                        P.dma(v_s[:, i, :], ob, reads=[b_ob], writes=[b_v])

        b_ato = Buf()
        if want("attn"):
            ar.reset()
            kT = ar.bf(4 * T).rearrange("p (h t) -> p h t", h=4)
            qT = ar.bf(4 * T).rearrange("p (h t) -> p h t", h=4)
            vv = ar.bf(18 * 512).rearrange("p (j c) -> p j c", j=18)
            b_kT = Buf()
            b_qT = Buf()
            b_vv = Buf()
            for h in range(4):
                P.dma(kT[:, h, :], k_s[h], reads=[b_k], writes=[b_kT])
                P.dma(qT[:, h, :], q_s[h], reads=[b_q], writes=[b_qT])
            P.dma(vv, v_s, reads=[b_v], writes=[b_vv])
            Et = [(ar.bf(512), Buf()) for _ in range(3)]
            r0 = ar.f32(512)
            t0_ = ar.f32(512)
            t1_ = ar.f32(512)
            sq_ = ar.f32(512)
            rr_ = ar.f32(512)
            b_r0, b_t0, b_t1, b_sq2, b_rr = [Buf() for _ in range(5)]
            aob = [(ar.bf(512), Buf()) for _ in range(2)]
            ec = 0
            oc = 0
            qblocks = tblocks(0, S) if last else tb_all
            for h in range(4):
                for (q0, Tn) in qblocks:
                    keys = list(range(18)) if q0 < S else [16, 17]
                    for c in range(2):
                        for idx, j in enumerate(keys):
                            sp = ec % 2
                            mm(PS[sp][:, :Tn], kT[64 * c:64 * c + 64, h, j * 128:(j + 1) * 128], qT[64 * c:64 * c + 64, h, q0:q0 + Tn],
                               True, True, [b_kT, b_qT], [PB[sp]])
                            E, b_E = Et[ec % 3]
                            ec += 1
                            act(E[:, :Tn], PS[sp][:, :Tn], AF.Exp, [PB[sp]], [b_E], scale=0.125)
                            mm(PS[2 + 2 * c][:, :Tn], vv[:, j, h * 128:(h + 1) * 128], E[:, :Tn], idx == 0, idx == len(keys) - 1, [b_vv, b_E], [PB[2 + 2 * c]])
                            mm(PS[3 + 2 * c][:, :Tn], onesb, E[:, :Tn], idx == 0, idx == len(keys) - 1, [b_cst, b_E], [PB[3 + 2 * c]])
                    recip(r0[:, :Tn], PS[3][:, :Tn], [PB[3]], [b_r0])
                    tt("dve", t0_[:, :Tn], PS[2][:, :Tn], r0[:, :Tn], ALU.mult, [PB[2], b_r0], [b_t0])
                    recip(r0[:, :Tn], PS[5][:, :Tn], [PB[5]], [b_r0])
                    tt("dve", t1_[:, :Tn], PS[4][:, :Tn], r0[:, :Tn], ALU.mult, [PB[4], b_r0], [b_t1])
                    stt(t0_[:, :Tn], t1_[:, :Tn], neg_lam, t0_[:, :Tn], ALU.mult, ALU.add, [b_t1, b_t0, b_sm], [b_t0])
                    act(sq_[:, :Tn], t0_[:, :Tn], AF.Square, [b_t0], [b_sq2])
                    mm(PS[6][:, :Tn], ones32, sq_[:, :Tn], True, True, [b_sq2, b_cst], [PB[6]])
                    rsqrt_mean(rr_[:, :Tn], PS[6][:, :Tn], 128, [PB[6]], [b_rr])
                    ao, b_ao = aob[oc % 2]
                    oc += 1
                    stt(ao[:, :Tn], t0_[:, :Tn], gsub_s, rr_[:, :Tn], ALU.mult, ALU.mult, [b_t0, b_rr, b_sm], [b_ao])
                    P.dma(ato_s[h][:, q0:q0 + Tn], ao[:, :Tn], reads=[b_ao], writes=[b_ato])

        b_hyo = Buf()

        def hyena_main(t0, L):
            TB, G, nTB, nG = RL[L]
            nj = L // 128
            for cc in range(2):
                ar.reset()
                raw = ar.f32(L)
                b_raw = Buf()
                us = [ar.f32(L) for _ in range(3)]
                b_us = [Buf() for _ in range(3)]
                for jx in range(3):
                    ch = jx * 2 + cc
                    P.dma(raw, hy_s[ch][:, t0:t0 + L], reads=[b_hy], writes=[b_raw])
                    u = us[jx]
                    ts("dve", u, raw, hcw[:, ch, 1:2], hcw[:, ch, 3:4], ALU.mult, ALU.add, [b_raw, b_sm], [b_us[jx]])
                    stt(u[:, 1:L], raw[:, 0:L - 1], hcw[:, ch, 0:1], u[:, 1:L], ALU.mult, ALU.add, [b_raw, b_us[jx], b_sm], [b_us[jx]])
                    stt(u[:, 0:L - 1], raw[:, 1:L], hcw[:, ch, 2:3], u[:, 0:L - 1], ALU.mult, ALU.add, [b_raw, b_us[jx], b_sm], [b_us[jx]])
                zbuf = raw
                b_zb = b_raw
                ztm = ar.f32(L).rearrange("p (j c) -> p j c", j=nj)
                b_ztm = Buf()
                Z = ar.f32(2 * L).rearrange("p (c j n) -> p c j n", c=2, j=nj)
                b_Z = Buf()
                Kf = ar.f32(2 * L).rearrange("p (c j n) -> p c j n", c=2, j=nj)
                b_Kf = Buf()
                X = ar.f32(2 * L).rearrange("p (c j n) -> p c j n", c=2, j=nj)
                b_X = Buf()
                tmpx = ztm
                b_tx = b_ztm
                tabs = [(ar.f32(2048), Buf()) for _ in range(4)]
                tcnt = 0
                z, b_z = us[0], b_us[0]
                for o in range(2):
                    for scn in range(nj):
                        tr(PS[0][:, (scn % 4) * 128:(scn % 4 + 1) * 128], z[:, scn * 128:(scn + 1) * 128], ident32, [b_z, b_cst], [PB[0]])
                        if scn % 4 == 3 or scn == nj - 1:
                            n4 = scn % 4 + 1
                            cp("act", ztm[:, scn - n4 + 1:scn + 1, :], PS[0][:, 0:n4 * 128].rearrange("p (j c) -> p j c", j=n4), [PB[0]], [b_ztm])
                    for c in range(2):
                        P.dma(Kf[:, c], kf_s[L][c][:, :, o * 256 + cc * 128:o * 256 + cc * 128 + 128].rearrange("j p n -> p j n"), reads=[b_kf[L]], writes=[b_Kf])
                    for fc in range(nj):
                        tbc, b_tbc = tabs[tcnt % 4]
                        tbs, b_tbs = tabs[(tcnt + 1) % 4]
                        tcnt += 2
                        tbc3 = tbc[:, 0:nj * 128].rearrange("p (j f) -> p j f", j=nj)
                        tbs3 = tbs[:, 0:nj * 128].rearrange("p (j f) -> p j f", j=nj)
                        P.dma(tbc3, tF_d[L][0, fc], writes=[b_tbc])
                        P.dma(tbs3, tF_d[L][1, fc], writes=[b_tbs])
                        pz = 1 + fc % 2
                        for sc_ in range(nj):
                            mm(PS[pz][:, 0:128], tbc3[:, sc_, :], ztm[:, sc_, :], sc_ == 0, sc_ == nj - 1, [b_tbc, b_ztm], [PB[pz]])
                        for sc_ in range(nj):
                            mm(PS[pz][:, 128:256], tbs3[:, sc_, :], ztm[:, sc_, :], sc_ == 0, sc_ == nj - 1, [b_tbs, b_ztm], [PB[pz]])
                        cp("act", Z[:, :, fc, :], PS[pz][:, 0:256].rearrange("p (c n) -> p c n", c=2), [PB[pz]], [b_Z])
                    Zr, Zi, Kr, Ki = Z[:, 0], Z[:, 1], Kf[:, 0], Kf[:, 1]
                    tt("dve", X[:, 0], Zr, Kr, ALU.mult, [b_Z, b_Kf], [b_X])
                    tt("pool", tmpx, Zi, Ki, ALU.mult, [b_Z, b_Kf], [b_tx])
                    tt("dve", X[:, 0], X[:, 0], tmpx, ALU.subtract, [b_X, b_tx], [b_X])
                    tt("pool", X[:, 1], Zr, Ki, ALU.mult, [b_Z, b_Kf], [b_X])
                    tt("dve", tmpx, Zi, Kr, ALU.mult, [b_Z, b_Kf, b_X], [b_tx])
                    tt("dve", X[:, 1], X[:, 1], tmpx, ALU.add, [b_X, b_tx], [b_X])
                    gate, b_g = us[1 + o], b_us[1 + o]
                    znew, b_zn = (zbuf, b_zb) if o == 0 else (us[0], b_us[0])
                    for tb in range(nTB):
                        py = 3 + tb % 2
                        for g in range(nG):
                            tbc, b_tbc = tabs[tcnt % 4]
                            tbs, b_tbs = tabs[(tcnt + 1) % 4]
                            tcnt += 2
                            tc3 = tbc[:, 0:G * TB].rearrange("p (g t) -> p g t", g=G)
                            ts3 = tbs[:, 0:G * TB].rearrange("p (g t) -> p g t", g=G)
                            P.dma(tc3, tI_d[L][0, tb, g], writes=[b_tbc])
                            P.dma(ts3, tI_d[L][1, tb, g], writes=[b_tbs])
                            for fi in range(G):
                                fc = g * G + fi
                                mm(PS[py][:, :TB], X[:, 0, fc, :], tc3[:, fi, :], fc == 0, False, [b_X, b_tbc], [PB[py]])
                                mm(PS[py][:, :TB], X[:, 1, fc, :], ts3[:, fi, :], False, fc == nj - 1, [b_X, b_tbs], [PB[py]])
                        sl = slice(tb * TB, (tb + 1) * TB)
                        stt(tmpx.rearrange("p j n -> p (j n)")[:, sl], z[:, sl], hbias[:, o, cc:cc + 1], PS[py][:, :TB], ALU.mult, ALU.add, [b_z, PB[py], b_sm, b_X], [b_tx])
                        tt("dve", znew[:, sl], tmpx.rearrange("p j n -> p (j n)")[:, sl], gate[:, sl], ALU.mult, [b_tx, b_g], [b_zn])
                    z, b_z = znew, b_zn
                ob = ar.bf(L)
                b_ob = Buf()
                cp("act", ob, z, [b_z], [b_ob])
                P.dma(hyo_s[cc][:, t0:t0 + L], ob, reads=[b_ob], writes=[b_hyo])

        if want("hyena"):
            for (t0, L) in streams:
                hyena_main(t0, L)

        b_fno = Buf()

        def fnet(t0, L):
            TB, G, nTB, nG = RL[L]
            nj = L // 128
            ar.reset()
            fz = [(ar.f32(L), Buf()) for _ in range(2)]
            zc = ar.f32(nj * 256).rearrange("p (j c) -> p j c", j=nj)
            zs = ar.f32(nj * 256).rearrange("p (j c) -> p j c", j=nj)
            b_zc = Buf()
            for cc in range(2):
                f, b_f = fz[cc]
                P.dma(f, fn_s[cc][:, t0:t0 + L], reads=[b_fn], writes=[b_f])
                for scn in range(nj):
                    pa = scn % 2
                    mm(PS[pa][:, 0:128], f[:, scn * 128:(scn + 1) * 128], bdc, True, True, [b_f, b_cst], [PB[pa]])
                    mm(PS[pa][:, 128:256], f[:, scn * 128:(scn + 1) * 128], bds, True, True, [b_f, b_cst], [PB[pa]])
                    cp("act", zc[:, scn, cc * 128:(cc + 1) * 128], PS[pa][:, 0:128], [PB[pa]], [b_zc])
                    cp("dve", zs[:, scn, cc * 128:(cc + 1) * 128], PS[pa][:, 128:256], [PB[pa]], [b_zc])
            tabs = [(ar.f32(2048), Buf()) for _ in range(4)]
            obs = [(ar.bf(512), Buf()) for _ in range(2)]
            tcnt = 0
            oc = 0
            for tb in range(nTB):
                for g in range(nG):
                    tbc, b_tbc = tabs[tcnt % 4]
                    tbs, b_tbs = tabs[(tcnt + 1) % 4]
                    tcnt += 2
                    tc3 = tbc[:, 0:G * TB].rearrange("p (g t) -> p g t", g=G)
                    ts3 = tbs[:, 0:G * TB].rearrange("p (g t) -> p g t", g=G)
                    P.dma(tc3, tN_d[L][0, tb, g], writes=[b_tbc])
                    P.dma(ts3, tN_d[L][1, tb, g], writes=[b_tbs])
                    for cc in range(2):
                        py = 2 + cc + 2 * (tb % 2)
                        for fi in range(G):
                            sc_ = g * G + fi
                            mm(PS[py][:, :TB], zc[:, sc_, cc * 128:(cc + 1) * 128], tc3[:, fi, :], sc_ == 0, False, [b_zc, b_tbc], [PB[py]])
                            mm(PS[py][:, :TB], zs[:, sc_, cc * 128:(cc + 1) * 128], ts3[:, fi, :], False, sc_ == nj - 1, [b_zc, b_tbs], [PB[py]])
                for cc in range(2):
                    py = 2 + cc + 2 * (tb % 2)
                    ob, b_ob = obs[oc % 2]
                    oc += 1
                    cp("act" if cc else "dve", ob[:, :TB], PS[py][:, :TB], [PB[py]], [b_ob])
                    P.dma(fno_s[cc][:, t0 + tb * TB:t0 + (tb + 1) * TB], ob[:, :TB], reads=[b_ob], writes=[b_fno])

        if want("fnet"):
            for (t0, L) in streams:
                fnet(t0, L)

        if want("merge"):
            ar.reset()
            wbr = ar.bf(8 * D).rearrange("p (k m) -> p k m", k=8)
            wo = ar.bf(8 * D).rearrange("p (k m) -> p k m", k=8)
            b_wbr = Buf()
            b_wo = Buf()
            w32 = ar.f32(8 * 512).rearrange("p (k m) -> p k m", k=8)
            b_w32 = Buf()
            for (src, dst, bd) in ((w_br_d, wbr, b_wbr), (w_out_d, wo, b_wo)):
                for half in range(2):
                    P.dma(w32, src[l][:, half * 512:(half + 1) * 512].rearrange("(k p) m -> p k m", p=128), writes=[b_w32])
                    cp("act", dst[:, 0:4, half * 512:(half + 1) * 512], w32[:, 0:4, :], [b_w32], [bd])
                    cp("dve", dst[:, 4:8, half * 512:(half + 1) * 512], w32[:, 4:8, :], [b_w32], [bd])
            ar.off -= 8 * 512
            P.barrier()
            nb = [(ar.bf(8 * 512).rearrange("p (k t) -> p k t", k=8), Buf()) for _ in range(2)]
            sb_ = [(ar.bf(8 * 512).rearrange("p (k t) -> p k t", k=8), Buf()) for _ in range(2)]
            g32 = [(ar.f32(8 * 384).rearrange("p (k j c) -> p k j c", k=8, j=3), Buf()) for _ in range(2)]
            gbf = [(ar.bf(8 * 384).rearrange("p (k j c) -> p k j c", k=8, j=3), Buf()) for _ in range(2)]
            sig = [(ar.f32(512), Buf()) for _ in range(3)]
            yacc = [(ar.f32(512), Buf()) for _ in range(2)]
            ytmp = [(ar.f32(512), Buf()) for _ in range(2)]
            yT = ar.bf(8 * 512).rearrange("p (k t) -> p k t", k=8)
            b_yT = [Buf() for _ in range(8)]
            KR = ((0, 2), (2, 4), (4, 8))
            gc = 0
            for bi, (t0, Tn) in enumerate(tb_mix):
                nbk, b_nb = nb[bi % 2]
                sbk, b_sb = sb_[bi % 2]
                P.dma(nbk[:, :, :Tn], nT_s[:, :, t0:t0 + Tn], reads=[b_nTs], writes=[b_nb])
                for cc in range(2):
                    P.dma(sbk[:, cc, :Tn], hyo_s[cc][:, t0:t0 + Tn], reads=[b_hyo], writes=[b_sb])
                    P.dma(sbk[:, 2 + cc, :Tn], fno_s[cc][:, t0:t0 + Tn], reads=[b_fno], writes=[b_sb])
                for h in range(4):
                    P.dma(sbk[:, 4 + h, :Tn], ato_s[h][:, t0:t0 + Tn], reads=[b_ato], writes=[b_sb])
                for m in range(8):
                    gw, b_gw = g32[gc % 2]
                    gb, b_gb = gbf[gc % 2]
                    gc += 1
                    P.dma(gw, w_gate_d[l, m], writes=[b_gw])
                    cp("pool", gb, gw, [b_gw], [b_gb])
                    ya, b_ya = yacc[m % 2]
                    yt, b_yt = ytmp[m % 2]
                    for j in range(3):
                        pg = j
                        pbr = 3 + j
                        for k in range(8):
                            mm(PS[pg][:, :Tn], gb[:, k, j, :], nbk[:, k, :Tn], k == 0, k == 7, [b_gb, b_nb], [PB[pg]])
                        k0, k1 = KR[j]
                        for k in range(k0, k1):
                            mm(PS[pbr][:, :Tn], wbr[:, k, m * 128:(m + 1) * 128], sbk[:, k, :Tn], k == k0, k == k1 - 1, [b_wbr, b_sb], [PB[pbr]])
                        sg, b_sg = sig[(m * 3 + j) % 3]
                        act(sg[:, :Tn], PS[pg][:, :Tn], AF.Sigmoid, [PB[pg]], [b_sg])
                        if j == 0:
                            tt("dve", ya[:, :Tn], sg[:, :Tn], PS[pbr][:, :Tn], ALU.mult, [b_sg, PB[pbr]], [b_ya])
                        else:
                            tt("dve", yt[:, :Tn], sg[:, :Tn], PS[pbr][:, :Tn], ALU.mult, [b_sg, PB[pbr]], [b_yt])
                            if j == 1:
                                tt("pool", ya[:, :Tn], ya[:, :Tn], yt[:, :Tn], ALU.add, [b_ya, b_yt], [b_ya])
                            else:
                                tt("pool", yT[:, m, :Tn], ya[:, :Tn], yt[:, :Tn], ALU.add, [b_ya, b_yt], [b_yT[m]])
                jj = 0 if t0 < S else 1
                for mo in range(8):
                    po = 6 + mo % 2
                    for k in range(8):
                        mm(PS[po][:, :Tn], wo[:, k, mo * 128:(mo + 1) * 128], yT[:, k, :Tn], k == 0, k == 7, [b_wo, b_yT[k]], [PB[po]])
                    xb = b_x[mo][t0 // 512]
                    stt(xT[:, mo, t0:t0 + Tn], PS[po][:, :Tn], modv[:, 2, mo, jj:jj + 1], xT[:, mo, t0:t0 + Tn], ALU.mult, ALU.add, [PB[po], xb, b_sm], [xb])

        if want("peer"):
            peer(l, last, modulate, mod_tmps, A_ffn)

    def peer(l, last, modulate, mod_tmps, A_ffn):
        ar.reset()
        b_ub = Buf()
        b_vb = Buf()
        ld = [(ar.f32(4096), Buf()) for _ in range(3)]
        cv = [(ar.bf(4096), Buf()) for _ in range(3)]
        jobs = []
        for k in range(8):
            for eb in range(4):
                jobs.append(("u", k, eb))
        for e1g in range(32):
            jobs.append(("v", e1g, 0))

        def j_load(ci):
            kind, i0_, i1_ = jobs[ci]
            a, b_a = ld[ci % 3]
            if kind == "u":
                P.dma(a, uT_d[l][i0_ * 128:(i0_ + 1) * 128, i1_ * 4096:(i1_ + 1) * 4096], writes=[b_a])
            else:
                P.dma(a.rearrange("p (e d) -> p e d", e=4), v_d_in[l][i0_ * 512:(i0_ + 1) * 512, :].rearrange("(e p) d -> p e d", p=128), writes=[b_a])

        j_load(0)
        j_load(1)
        for ci in range(len(jobs)):
            if ci + 2 < len(jobs):
                j_load(ci + 2)
            kind, i0_, i1_ = jobs[ci]
            a, b_a = ld[ci % 3]
            o, b_o = cv[ci % 3]
            cp(("act", "dve", "pool")[ci % 3], o, a, [b_a], [b_o])
            if kind == "u":
                P.dma(uTb_s[:, i0_, i1_ * 4096:(i1_ + 1) * 4096], o, reads=[b_o], writes=[b_ub])
            else:
                P.dma(vb_s[:, i0_ * 4:(i0_ + 1) * 4, :], o.rearrange("p (e d) -> p e d", e=4), reads=[b_o], writes=[b_vb])
        ar.reset()
        keysT = ar.f32(2048).rearrange("p (h n) -> p h n", h=16)
        b_ky = Buf()
        P.dma(keysT, keysT_d[l], writes=[b_ky])
        nbf = ar.bf(8 * 256).rearrange("p (k t) -> p k t", k=8)
        b_nbf = [Buf() for _ in range(8)]
        s1k = [ar.f32(1024).rearrange("p (h n) -> p h n", h=8) for _ in range(2)]
        a2k = [ar.f32(1024).rearrange("p (h n) -> p h n", h=8) for _ in range(2)]
        a1t = [ar.f32(128).rearrange("p (h a) -> p h a", h=8) for _ in range(2)]
        top1k = [ar.f32(128).rearrange("p (h a) -> p h a", h=8) for _ in range(2)]
        theta = [ar.f32(8) for _ in range(2)]
        b_keep = [Buf() for _ in range(2)]
        zer = ar.bf(512)
        b_zer = Buf()
        P.op("pool", lambda e: e.memset(zer, 0.0), [], [b_zer])
        base = ar.off
        blocks = tblocks(0, S if last else T, 256)
        b_wT = [Buf(), Buf()]
        for (t0, Tn) in blocks:
            jj = 0 if t0 < S else 1
            P.barrier()
            ar.off = base
            mt = mod_tmps(256)
            n32 = ar.f32(8 * 256).rearrange("p (k t) -> p k t", k=8)
            b_n32 = [Buf() for _ in range(8)]
            wq = [(ar.f32(8 * 128).rearrange("p (k m) -> p k m", k=8), Buf()) for _ in range(2)]
            qT = ar.f32(16 * 256).rearrange("p (h t) -> p h t", h=16)
            b_qT = Buf()
            s_sb = ar.f32(2048).rearrange("p (h n) -> p h n", h=16)
            b_s = Buf()
            tmpm = ar.f32(256)
            b_tm = Buf()
            tmpm2 = ar.f32(256)
            b_tm2 = Buf()
            top = ar.f32(256).rearrange("p (h a) -> p h a", h=16)
            b_top = Buf()
            cand = ar.f32(2048).rearrange("p (h a b) -> p h a b", h=8, a=16)
            b_cand = Buf()
            ctop = ar.f32(192).rearrange("p (h a) -> p h a", h=8)
            b_ct = Buf()
            misc = ar.f32(16)
            b_mi = Buf()
            ex16 = ar.f32(128).rearrange("p (h a) -> p h a", h=8)

            def emit_ffn(k, t0_, Tn_, tm, b_t, Asc, Bsc):
                act(n32[:, k, :], tm[:, :Tn_], AF.Identity, [b_t, b_sm], [b_n32[k]], bias=Bsc, scale=Asc)
                cp("pool", nbf[:, k, :], n32[:, k, :], [b_n32[k]], [b_nbf[k]])

            modulate(A_ffn, 3, [(t0, Tn)], emit_ffn, mt)
            for hp in range(16):
                w, b_w = wq[hp % 2]
                P.dma(w, wq_d[l][:, hp * 128:(hp + 1) * 128].rearrange("(k p) m -> p k m", p=128), writes=[b_w])
                pq = hp % 2
                for k in range(8):
                    mm(PS[pq][:, :Tn], w[:, k, :], n32[:, k, :], k == 0, k == 7, [b_w, b_n32[k]], [PB[pq]])
                cp("act" if hp % 2 else "dve", qT[:, hp, :], PS[pq][:, :Tn], [PB[pq]], [b_qT])
            for ti in range(2):
                tsl = slice(ti * 128, (ti + 1) * 128)
                bk = b_keep[ti]
                for hp in range(16):
                    pbk = hp // 4
                    mm(PS[pbk][:, (hp % 4) * 128:(hp % 4 + 1) * 128], qT[:, hp, tsl], keysT[:, hp, :], True, True, [b_qT, b_ky], [PB[pbk]])
                for q4 in range(4):
                    cp("act" if q4 % 2 else "dve", s_sb[:, q4 * 4:(q4 + 1) * 4, :], PS[q4][:, :].rearrange("p (h n) -> p h n", h=4), [PB[q4]], [b_s])
                for hp in range(16):
                    P.op("dve", (lambda hp: lambda e: e.max(out=top[:, hp, 0:8], in_=s_sb[:, hp, :]))(hp), [b_s], [b_top])
                    P.op("dve", (lambda hp: lambda e: e.match_replace(out=tmpm[:, 0:128], in_to_replace=top[:, hp, 0:8], in_values=s_sb[:, hp, :], imm_value=-1e30))(hp), [b_s, b_top], [b_tm])
                    P.op("dve", (lambda hp: lambda e: e.max(out=top[:, hp, 8:16], in_=tmpm[:, 0:128]))(hp), [b_tm], [b_top])
                top4 = top.rearrange("p (h c) a -> p h c a", c=2)
                s4v = s_sb.rearrange("p (h c) n -> p h c n", c=2)
                tt("dve", cand, top4[:, :, 0, :].unsqueeze(3).to_broadcast([128, 8, 16, 16]),
                   top4[:, :, 1, :].unsqueeze(2).to_broadcast([128, 8, 16, 16]), ALU.add, [b_top], [b_cand])
                for h in range(8):
                    ch = cand[:, h].rearrange("p a b -> p (a b)")
                    P.op("dve", (lambda h, ch: lambda e: e.max(out=ctop[:, h, 0:8], in_=ch))(h, ch), [b_cand], [b_ct])
                    P.op("dve", (lambda h, ch: lambda e: e.match_replace(out=tmpm, in_to_replace=ctop[:, h, 0:8], in_values=ch, imm_value=-1e30))(h, ch), [b_cand, b_ct], [b_tm])
                    P.op("dve", (lambda h: lambda e: e.max(out=ctop[:, h, 8:16], in_=tmpm))(h), [b_tm], [b_ct])
                    P.op("dve", (lambda h: lambda e: e.match_replace(out=tmpm2, in_to_replace=ctop[:, h, 8:16], in_values=tmpm, imm_value=-1e30))(h), [b_tm, b_ct], [b_tm2])
                    P.op("dve", (lambda h: lambda e: e.max(out=ctop[:, h, 16:24], in_=tmpm2))(h), [b_tm2], [b_ct])
                tt("dve", ex16, ctop[:, :, 0:16], ctop[:, :, 0:1].to_broadcast([128, 8, 16]), ALU.subtract, [b_ct], [b_mi])
                act(ex16, ex16, AF.Exp, [b_mi], [b_mi])
                P.op("dve", lambda e: e.reduce_sum(out=misc[:, 8:16], in_=ex16, axis=AX.X), [b_mi], [b_mi])
                recip(misc[:, 8:16], misc[:, 8:16], [b_mi], [b_mi])
                m8 = misc[:, 0:8].unsqueeze(2)
                tt("dve", m8, ctop[:, :, 15:16], ctop[:, :, 16:17], ALU.add, [b_ct, b_mi], [b_mi])
                stt(m8, m8, 0.5, ctop[:, :, 0:1], ALU.mult, ALU.subtract, [b_mi, b_ct], [b_mi])
                act(misc[:, 0:8], misc[:, 0:8], AF.Exp, [b_mi], [b_mi])
                tt("dve", theta[ti], misc[:, 0:8], misc[:, 8:16], ALU.mult, [b_mi], [bk])
                cp("pool", s1k[ti], s4v[:, :, 0, :], [b_s], [bk])
                cp("pool", top1k[ti], top4[:, :, 0, :], [b_top], [bk])
                tt("dve", a2k[ti], s4v[:, :, 1, :], top4[:, :, 1, 0:1].to_broadcast([128, 8, 128]), ALU.subtract, [b_s, b_top], [bk])
                act(a2k[ti], a2k[ti], AF.Exp, [bk], [bk])
                tt("dve", a1t[ti], top4[:, :, 0, :], top4[:, :, 0, 0:1].to_broadcast([128, 8, 16]), ALU.subtract, [b_top], [bk])
                act(a1t[ti], a1t[ti], AF.Exp, [bk], [bk])
                tt("dve", a1t[ti], a1t[ti], misc[:, 8:16].unsqueeze(2).to_broadcast([128, 8, 16]), ALU.mult, [bk, b_mi], [bk])
            P.barrier()
            ar.off = base
            pmt = ar.f32(2048).rearrange("p (h a e) -> p h a e", h=8, a=16)
            b_pm = Buf()
            csl = [(ar.bf(2048).rearrange("p (h a e) -> p h a e", h=8, a=16), Buf()) for _ in range(2)]
            CT = ar.bf(128 * 128).rearrange("p (t e) -> p t e", t=128)
            b_CT = Buf()
            osl = ar.bf(4096).rearrange("p (h a e) -> p h a e", h=8, a=16)
            b_osl = Buf()
            OTs = [(ar.bf(4096).rearrange("p (t e) -> p t e", t=128), Buf()) for _ in range(2)]
            WTs = [(ar.bf(4096).rearrange("p (e t) -> p e t", e=32), Buf()) for _ in range(2)]
            PSb = [PS[i][:, :].bitcast(BF16) for i in range(8)]
            evc = 0
            for ti in range(2):
                bk = b_keep[ti]
                for es in range(8):
                    tt("pool", pmt, a1t[ti].unsqueeze(3).to_broadcast([128, 8, 16, 16]),
                       a2k[ti][:, :, es * 16:(es + 1) * 16].unsqueeze(2).to_broadcast([128, 8, 16, 16]), ALU.mult, [bk], [b_pm])
                    cs, b_cs = csl[es % 2]
                    for h in range(8):
                        pmh = pmt[:, h].rearrange("p a e -> p (a e)")
                        stt(cs[:, h].rearrange("p a e -> p (a e)"), pmh, theta[ti][:, h:h + 1], pmh, ALU.is_ge, ALU.mult, [b_pm, bk], [b_cs])
                    csf = cs.rearrange("p h a e -> p (h a) e")
                    for half in range(2):
                        pb = 2 + (es * 2 + half) % 2
                        for e in range(8):
                            tr(PSb[pb][:, e * 128:(e + 1) * 128], csf[:, :, half * 8 + e], identb, [b_cs, b_cst], [PB[pb]])
                        e0 = es * 16 + half * 8
                        cp("act", CT[:, :, e0:e0 + 8].rearrange("p t e -> p e t"), PSb[pb][:, :].rearrange("p (e t) -> p e t", e=8), [PB[pb]], [b_CT])
                for r in range(4):
                    tt("dve", osl, s1k[ti][:, :, r * 32:(r + 1) * 32].unsqueeze(2).to_broadcast([128, 8, 16, 32]),
                       top1k[ti].unsqueeze(3).to_broadcast([128, 8, 16, 32]), ALU.is_equal, [bk], [b_osl])
                    osf = osl.rearrange("p h a e -> p (h a) e")
                    ot, b_ot = OTs[r % 2]
                    for q in range(4):
                        pb = 4 + q % 2
                        for e in range(8):
                            tr(PSb[pb][:, e * 128:(e + 1) * 128], osf[:, :, q * 8 + e], identb, [b_osl, b_cst], [PB[pb]])
                        cp("act", ot[:, :, q * 8:(q + 1) * 8].rearrange("p t e -> p e t"), PSb[pb][:, :].rearrange("p (e t) -> p e t", e=8), [PB[pb]], [b_ot])
                    wt, b_wt = WTs[r % 2]
                    for tg in range(8):
                        pb = 6 + tg % 2
                        for tk in range(16):
                            t_ = tg * 16 + tk
                            mm(PS[pb][:, tk * 32:(tk + 1) * 32], CT[:, t_, :], ot[:, t_, :], True, True, [b_CT, b_ot], [PB[pb]])
                        cp("dve" if evc % 2 else "act", wt[:, :, tg * 16:(tg + 1) * 16], PS[pb][:, :].rearrange("p (t e) -> p e t", t=16), [PB[pb]], [b_wt])
                        evc += 1
                    P.dma(wT_s[ti][:, r * 32:(r + 1) * 32, :], wt, reads=[b_wt], writes=[b_wT[ti]])
            P.barrier()
            ar.off = base
            ub = [(ar.bf(8 * 512).rearrange("p (k e) -> p k e", k=8), Buf()) for _ in range(2)]
            vbf = [(ar.bf(4 * 1024).rearrange("p (e d) -> p e d", e=4), Buf()) for _ in range(2)]
            Wt = [(ar.bf(4 * 256).rearrange("p (e t) -> p e t", e=4), Buf()) for _ in range(2)]
            gel = [(ar.f32(512), Buf()) for _ in range(2)]
            GT = [(ar.bf(512).rearrange("p (e t) -> p e t", e=2), Buf()) for _ in range(2)]
            for pb in range(4, 8):
                mm(PS[pb][:, :], zer[:, 0:128], zer[:, 0:512], True, False, [b_zer], [PB[pb]])
            for g in range(32):
                u_, b_u = ub[g % 2]
                v_, b_vv = vbf[g % 2]
                w_, b_w_ = Wt[g % 2]
                P.dma(u_, uTb_s[:, :, g * 512:(g + 1) * 512], reads=[b_ub], writes=[b_u])
                P.dma(v_, vb_s[:, g * 4:(g + 1) * 4, :], reads=[b_vb], writes=[b_vv])
                for ti in range(2):
                    P.dma(w_[:, :, ti * 128:(ti + 1) * 128], wT_s[ti][:, g * 4:(g + 1) * 4, :], reads=[b_wT[ti]], writes=[b_w_])
                for sub in range(2):
                    it = g * 2 + sub
                    ph = it % 2
                    for i in range(2):
                        chn = sub * 2 + i
                        for k in range(8):
                            mm(PS[ph][:, i * 256:(i + 1) * 256], u_[:, k, chn * 128:(chn + 1) * 128], nbf[:, k, :], k == 0, k == 7, [b_u, b_nbf[k]], [PB[ph]])
                    ge, b_ge = gel[it % 2]
                    gt, b_gt = GT[it % 2]
                    act(ge, PS[ph][:, :], AF.Gelu, [PB[ph]], [b_ge])
                    tt("dve", gt.rearrange("p e t -> p (e t)"), ge, w_[:, sub * 2:(sub + 1) * 2, :].rearrange("p e t -> p (e t)"), ALU.mult, [b_ge, b_w_], [b_gt])
                    for dk in range(8):
                        pbo = 4 + dk // 2
                        for i in range(2):
                            mm(PS[pbo][:, (dk % 2) * 256:(dk % 2 + 1) * 256], v_[:, sub * 2 + i, dk * 128:(dk + 1) * 128], gt[:, i, :], False, False, [b_vv, b_gt], [PB[pbo]])
            for dk in range(8):
                pbo = 4 + dk // 2
                xb = b_x[dk][t0 // 512]
                stt(xT[:, dk, t0:t0 + 256], PS[pbo][:, (dk % 2) * 256:(dk % 2 + 1) * 256], modv[:, 5, dk, jj:jj + 1], xT[:, dk, t0:t0 + 256], ALU.mult, ALU.add, [PB[pbo], xb, b_sm], [xb])

    for l in range(depth):
        layer(l)

    P.barrier()
    fin = []
    for k in range(8):
        fin.append(P.dma(yT_d[:, k, :], xT[:, k, 0:S], reads=b_x[k]))
    for name, (src, shape) in dbg_out.items():
        pass
    P.emit(list(P.dmas[-8:]))
    es.close()
    return nc

_CONST = {}


def _consts():
    if _CONST:
        return _CONST
    f64 = np.float64
    c = np.zeros((6, 128, 128), f64)
    c[0] = np.eye(128)
    c[1] = 1.0
    c[2, :64, :64] = 1.0
    c[2, 64:, 64:] = 1.0
    for base in range(0, 128, 32):
        for d in range(16):
            c[3, base + d + 16, base + d] = -1.0
            c[3, base + d, base + d + 16] = 1.0
    ci = np.arange(64)
    ang = 2 * np.pi * np.outer(ci, ci) / 64.0
    for b in range(2):
        c[4, b * 64:(b + 1) * 64, b * 64:(b + 1) * 64] = np.cos(ang)
        c[5, b * 64:(b + 1) * 64, b * 64:(b + 1) * 64] = np.sin(ang)
    _CONST["cst"] = np.ascontiguousarray(c.transpose(1, 0, 2)).astype(np.float32)
    t = np.arange(S)
    row = (t // 64).astype(f64)
    col = (t % 64).astype(f64)
    inv = 10000.0 ** (-np.arange(0, 32, 2, dtype=f64) / 32.0)
    d = np.arange(128) % 64
    pos = np.where((d // 32)[:, None] == 0, row[None, :], col[None, :])
    a = pos * inv[d % 16][:, None]
    _CONST["rope"] = np.stack([np.cos(a), np.sin(a)]).astype(np.float32)
    for L in (S, C):
        p = np.arange(L, dtype=f64)
        tt_ = p / max(L - 1, 1)
        w = 2.0 * np.pi * p / L
        fr = np.linspace(1e-4, 15, 16)
        feats = np.concatenate([tt_[:, None], np.cos(w[:, None] * fr), -np.sin(w[:, None] * fr)], axis=-1)
        _CONST["feats%d" % L] = np.ascontiguousarray(feats.T).astype(np.float32)
        deltas = np.abs(np.linspace(math.log(1e-2) / 1.5, math.log(1e-2) / 0.3, 256))
        _CONST["dec%d" % L] = np.exp(-tt_[:, None] * deltas[None, :]).astype(np.float32)
        nj = L // 128
        s_ = np.arange(L)
        kk = np.outer(s_, 2 * s_ + 1) % (4 * L)
        angF = np.pi * kk / (2.0 * L)
        TcF = np.cos(angF)
        TsF = -np.sin(angF)
        tF = np.stack([TcF, TsF]).reshape(2, nj, 128, nj, 128).transpose(0, 3, 2, 1, 4)
        _CONST["tF%d" % L] = np.ascontiguousarray(tF).astype(np.float32)
        TB = min(512, L)
        G = min(4, nj)
        nTB = L // TB
        nG = nj // G
        TcI = TcF.T / L
        TsI = TsF.T / L
        def rl(M):
            return M.reshape(nG, G, 128, nTB, TB).transpose(3, 0, 2, 1, 4)
        _CONST["tI%d" % L] = np.ascontiguousarray(np.stack([rl(TcI), rl(TsI)])).astype(np.float32)
        k2 = np.outer(s_, s_) % L
        ang2 = 2 * np.pi * k2 / L
        sc_ = 1.0 / math.sqrt(64.0 * L)
        _CONST["tN%d" % L] = np.ascontiguousarray(np.stack([rl(np.cos(ang2) * sc_), rl(-np.sin(ang2) * sc_)])).astype(np.float32)
    return _CONST


def _prep(inp):
    f = lambda a: np.ascontiguousarray(np.asarray(a, dtype=np.float32))
    w = {}
    w["w_ada"] = f(inp["w_ada"])
    w["bada"] = f(np.asarray(inp["b_ada"]).reshape(2, 6, 8, 128).transpose(0, 3, 1, 2))
    w["gmf"] = f(np.stack([np.asarray(inp["g_mix"]).reshape(2, 8, 128), np.asarray(inp["g_ffn"]).reshape(2, 8, 128)], axis=1).transpose(0, 3, 1, 2))
    win = np.asarray(inp["w_in"])
    w["w_in"] = f(win)
    w["w_gate"] = f(win[:, :, 2560:].reshape(2, 8, 128, 3, 8, 128).transpose(0, 4, 2, 1, 3, 5))
    hcw = np.concatenate([np.asarray(inp["hy_conv_w"]), np.asarray(inp["hy_conv_b"])[:, None, :]], axis=1)
    w["hcw"] = f(hcw.reshape(2, 4, 6, 128).transpose(0, 3, 2, 1))
    w["hy_w1"] = f(inp["hy_w1"])
    w["hy_fb"] = f(np.stack([inp["hy_b1"], inp["hy_freq"], inp["hy_b2"]], axis=-1))
    w["hy_w2"] = f(inp["hy_w2"])
    w["hy_w3"] = f(inp["hy_w3"])
    w["hy_bias"] = f(np.asarray(inp["hy_bias"]).reshape(2, 2, 2, 128).transpose(0, 3, 1, 2))
    w["gqk"] = f(np.stack([np.asarray(inp["g_q"]).reshape(2, 128), np.asarray(inp["g_k"]).reshape(2, 128)], axis=-1))
    w["lam"] = f(np.asarray(inp["lam"]).reshape(2, 1, 256))
    w["gsub"] = f(np.asarray(inp["g_sub"]).reshape(2, 128, 1))
    w["w_br"] = f(np.concatenate([inp["w_hy"], inp["w_fn"], inp["w_at"]], axis=1))
    w["w_out"] = f(inp["w_out"])
    w["peer_wq"] = f(inp["peer_wq"])
    w["keysT"] = f(np.asarray(inp["peer_keys"]).reshape(2, 16, 128, 128).transpose(0, 3, 1, 2))
    w["uT"] = f(np.asarray(inp["peer_u"]).transpose(0, 2, 1))
    w["peer_v"] = f(inp["peer_v"])
    w.update(_consts())
    return w


def _core_inputs(inp, b):
    X = np.concatenate([np.asarray(inp["x"][b]), np.asarray(inp["ctx"][b])], axis=0)
    xT = np.ascontiguousarray(X.T.reshape(8, 128, T).transpose(1, 0, 2)).astype(np.float32)
    cc = np.stack([np.asarray(inp["c"][b]), np.asarray(inp["c_ctx"])], axis=-1)
    cc = np.ascontiguousarray(cc.reshape(8, 128, 2).transpose(1, 0, 2)).astype(np.float32)
    return {"xT": xT, "cc": cc}


_NC = {}


def kernel(**inp):
    w = _prep(inp)
    if "nc" not in _NC:
        _NC["nc"] = build()
    nc = _NC["nc"]
    in_maps = []
    for b in range(8):
        m = dict(w)
        m.update(_core_inputs(inp, b))
        in_maps.append(m)
    res = run_bass_kernel_spmd(nc, in_maps, core_ids=list(range(8)))
    out = np.empty((8, S, D), np.float32)
    for b in range(8):
        yT = np.asarray(res.results[b]["yT"])
        out[b] = yT.transpose(2, 1, 0).reshape(S, D)
    return out
```

```python
import numpy as np, math
from contextlib import ExitStack
import concourse.bass as bass
import concourse.mybir as mybir
from concourse.bass_utils import run_bass_kernel_spmd

F32 = mybir.dt.float32
BF16 = mybir.dt.bfloat16
ALU = mybir.AluOpType
AF = mybir.ActivationFunctionType
AX = mybir.AxisListType

NSLOT = 40
ENGS = ("pe", "act", "dve", "pool", "sp")


class Buf:
    __slots__ = ("w", "rs", "rd")

    def __init__(self):
        self.w = None
        self.rs = {}
        self.rd = []


class Op:
    __slots__ = ("eng", "fn", "deps", "sig", "cnt", "slot", "dma")

    def __init__(self, eng, fn, dma=False):
        self.eng = eng
        self.fn = fn
        self.deps = ()
        self.sig = False
        self.cnt = 0
        self.slot = -1
        self.dma = dma


class Prog:
    def __init__(self, nc):
        self.nc = nc
        self.streams = {e: [] for e in ENGS}
        self.dmas = []
        self.live_dmas = []
        self.last_real = {e: None for e in ENGS}

    def op(self, eng, fn, reads=(), writes=(), dma=False):
        o = Op(eng, fn, dma)
        deps = set()
        for b in reads:
            if b.w is not None:
                deps.add(b.w)
        for b in writes:
            if b.w is not None:
                deps.add(b.w)
            deps.update(b.rs.values())
            deps.update(b.rd)
        if eng == "pe" and not dma:
            deps = {d for d in deps if d.dma or d.eng != "pe"}
        o.deps = deps
        for b in writes:
            b.w = o
            b.rs = {}
            b.rd = []
        for b in reads:
            if dma:
                b.rd.append(o)
            else:
                b.rs[eng] = o
        self.streams[eng].append(o)
        if not dma:
            self.last_real[eng] = o
        if dma:
            self.dmas.append(o)
            self.live_dmas.append(o)
        return o

    def dma(self, out, in_, reads=(), writes=(), eng="sp"):
        return self.op(eng, lambda e: e.dma_start(out=out, in_=in_), reads, writes, dma=True)

    def barrier(self):
        last = dict(self.last_real)
        live = list(self.live_dmas)
        self.live_dmas = []
        for e in ENGS:
            o = Op(e, None)
            o.deps = {last[x] for x in ENGS if x != e and last[x] is not None}
            o.deps.update(live)
            self.streams[e].append(o)

    def emit(self, final_dmas):
        nc = self.nc
        for e in ENGS:
            for o in self.streams[e]:
                for d in o.deps:
                    d.sig = True
        with ExitStack() as es:
            sems = {e: es.enter_context(nc.semaphore("s_" + e)) for e in ENGS}
            dsem = [es.enter_context(nc.semaphore("d%d" % i)) for i in range(NSLOT)]
            for e in ENGS:
                c = 0
                for o in self.streams[e]:
                    if o.dma:
                        continue
                    if o.sig:
                        c += 1
                        o.cnt = c
            slot_cnt = [0] * NSLOT
            slot_prev = [None] * NSLOT
            for i, o in enumerate(self.dmas):
                s = i % NSLOT
                o.slot = s
                slot_cnt[s] += 16
                o.cnt = slot_cnt[s]
                if slot_prev[s] is not None:
                    o.deps = set(o.deps)
                    o.deps.add(slot_prev[s])
                slot_prev[s] = o
            block = es.enter_context(nc.Block())

            def run(ename, eng):
                waited = {}
                for o in self.streams[ename]:
                    for d in o.deps:
                        sem = dsem[d.slot] if d.dma else sems[d.eng]
                        if waited.get(sem.name, 0) >= d.cnt:
                            continue
                        eng.wait_ge(sem, d.cnt)
                        waited[sem.name] = d.cnt
                    if o.fn is None:
                        continue
                    ins = o.fn(eng)
                    if o.dma:
                        ins.then_inc(dsem[o.slot], 16)
                    elif o.sig:
                        ins.then_inc(sems[ename], 1)
                if ename == "sp":
                    for d in final_dmas:
                        eng.wait_ge(dsem[d.slot], d.cnt)

            @block.sync
            def _(e):
                run("sp", e)

            @block.tensor
            def _(e):
                run("pe", e)

            @block.scalar
            def _(e):
                run("act", e)

            @block.vector
            def _(e):
                run("dve", e)

            @block.gpsimd
            def _(e):
                run("pool", e)

D = 1024
S = 2048
C = 256
T = S + C
EPS = 1e-6
PI = math.pi
NE = 16384


def tblocks(lo, hi, step=512):
    return [(t, min(step, hi - t)) for t in range(lo, hi, step)]


def build(depth=2, stages=None, dbg=()):
    nc = bass.Bass("TRN2", target_bir_lowering=False)
    P = Prog(nc)

    def din(name, shape, dt=F32):
        return nc.dram_tensor(name, list(shape), dt, kind="ExternalInput").ap()

    def dscr(name, shape, dt=F32):
        if name in dbg:
            return nc.dram_tensor(name, list(shape), dt, kind="ExternalOutput").ap()
        return nc.dram_tensor(name, list(shape), dt).ap()

    xT_d = din("xT", [128, 8, T])
    cc_d = din("cc", [128, 8, 2])
    w_ada_d = din("w_ada", [2, D, 6 * D])
    bada_d = din("bada", [2, 128, 6, 8])
    gmf_d = din("gmf", [2, 128, 2, 8])
    w_in_d = din("w_in", [2, D, 5632])
    w_gate_d = din("w_gate", [2, 8, 128, 8, 3, 128])
    hcw_d = din("hcw", [2, 128, 6, 4])
    hy_w1_d = din("hy_w1", [2, 33, 64])
    hy_fb_d = din("hy_fb", [2, 64, 3])
    hy_w2_d = din("hy_w2", [2, 64, 64])
    hy_w3_d = din("hy_w3", [2, 64, 1024])
    hy_bias_d = din("hy_bias", [2, 128, 2, 2])
    gqk_d = din("gqk", [2, 128, 2])
    lam_d = din("lam", [2, 1, 256])
    gsub_d = din("gsub", [2, 128, 1])
    w_br_d = din("w_br", [2, D, D])
    w_out_d = din("w_out", [2, D, D])
    wq_d = din("peer_wq", [2, D, 2048])
    keysT_d = din("keysT", [2, 128, 16, 128])
    uT_d = din("uT", [2, D, NE])
    v_d_in = din("peer_v", [2, NE, D])
    cst_d = din("cst", [128, 6, 128])
    rope_d = din("rope", [2, 128, S])
    feats_d = {L: din("feats%d" % L, [33, L]) for L in (S, C)}
    dec_d = {L: din("dec%d" % L, [L, 256]) for L in (S, C)}
    tF_d = {L: din("tF%d" % L, [2, L // 128, 128, L // 128, 128]) for L in (S, C)}
    RL = {}
    for L in (S, C):
        TB = min(512, L)
        G = min(4, L // 128)
        RL[L] = (TB, G, L // TB, (L // 128) // G)
    tI_d = {L: din("tI%d" % L, [2, RL[L][2], RL[L][3], 128, RL[L][1], RL[L][0]]) for L in (S, C)}
    tN_d = {L: din("tN%d" % L, [2, RL[L][2], RL[L][3], 128, RL[L][1], RL[L][0]]) for L in (S, C)}
    yT_d = nc.dram_tensor("yT", [128, 8, S], F32, kind="ExternalOutput").ap()
    hy_s = dscr("hy_s", [6, 128, T])
    fn_s = dscr("fn_s", [2, 128, T])
    q_s = dscr("q_s", [4, 128, T], BF16)
    k_s = dscr("k_s", [4, 128, T], BF16)
    v_s = dscr("v_s", [128, 18, 512], BF16)
    kf_s = {L: dscr("kf_s%d" % L, [2, L // 128, 128, 512]) for L in (S, C)}
    hyo_s = dscr("hyo_s", [2, 128, T], BF16)
    fno_s = dscr("fno_s", [2, 128, T], BF16)
    ato_s = dscr("ato_s", [4, 128, T], BF16)
    nT_s = dscr("nT_s", [128, 8, T], BF16)
    uTb_s = dscr("uTb_s", [128, 8, NE], BF16)
    vb_s = dscr("vb_s", [128, 128, D], BF16)
    wT_s = dscr("wT_s", [2, 128, 128, 128], BF16)
    dbg_out = {}

    es = ExitStack()
    xT = es.enter_context(nc.sbuf_tensor("xT_sb", [128, 8, T], F32))
    cst = es.enter_context(nc.sbuf_tensor("cst_sb", [128, 6, 128], F32))
    cstb = es.enter_context(nc.sbuf_tensor("cstb_sb", [128, 2, 128], BF16))
    sm = es.enter_context(nc.sbuf_tensor("small_sb", [128, 512], F32))
    ARENA = 33280
    AR = es.enter_context(nc.sbuf_tensor("arena", [128, ARENA], F32))
    PS = [es.enter_context(nc.psum_tensor("ps%d" % i, [128, 512], F32)) for i in range(8)]
    PB = [Buf() for _ in range(8)]
    ident32 = cst[:, 0, :]
    ones32 = cst[:, 1, :]
    bd64 = cst[:, 2, :]
    rot = cst[:, 3, :]
    bdc = cst[:, 4, :]
    bds = cst[:, 5, :]
    identb = cstb[:, 0, :]
    onesb = cstb[:, 1, :]
    b_cst = Buf()
    b_x = [[Buf() for _ in range(5)] for _ in range(8)]
    b_sm = Buf()
    sc = sm[:, 0:16].rearrange("p (k j) -> p k j", k=8)
    modv = sm[:, 16:112].rearrange("p (g m j) -> p g m j", g=6, m=8)
    A_mix = sm[:, 112:128].rearrange("p (k j) -> p k j", k=8)
    A_ffn = sm[:, 128:144].rearrange("p (k j) -> p k j", k=8)
    gmf = sm[:, 144:160].rearrange("p (a k) -> p a k", a=2)
    tmp16 = sm[:, 160:176].rearrange("p (k j) -> p k j", k=8)
    gqk = sm[:, 176:178]
    gsub_s = sm[:, 178:179]
    neg_lam = sm[:, 179:180]
    lamw = sm[:, 180:182]
    hcw = sm[:, 184:208].rearrange("p (c t) -> p c t", c=6)
    hbias = sm[:, 208:212].rearrange("p (o c) -> p o c", o=2)
    fb = sm[:, 212:217]
    bada = sm[:, 224:272].rearrange("p (g m) -> p g m", g=6)
    lamt = sm[:, 272:400]
    epsc = sm[:, 183:184]

    class Arena:
        def __init__(self):
            self.off = 0

        def reset(self):
            P.barrier()
            self.off = 0

        def f32(self, n):
            a = AR[:, self.off:self.off + n]
            self.off += n
            assert self.off <= ARENA, self.off
            return a

        def bf(self, n):
            w = (n + 1) // 2
            return self.f32(w).bitcast(BF16)[:, 0:n]

    ar = Arena()

    def mm(out, lhsT, rhs, start, stop, rd, wr):
        P.op("pe", lambda e: e.matmul(out, lhsT=lhsT, rhs=rhs, start=start, stop=stop), rd, wr)

    def tr(out, in_, idn, rd, wr):
        P.op("pe", lambda e: e.transpose(out=out, in_=in_, identity=idn), rd, wr)

    def act(out, in_, func, rd, wr, bias=0.0, scale=1.0):
        P.op("act", lambda e: e.activation(out=out, in_=in_, func=func, bias=bias, scale=scale), rd, wr)

    def tt(eng, out, in0, in1, op, rd, wr):
        P.op(eng, lambda e: e.tensor_tensor(out=out, in0=in0, in1=in1, op=op), rd, wr)

    def ts(eng, out, in0, s1, s2, op0, op1, rd, wr):
        if s2 is None:
            P.op(eng, lambda e: e.tensor_scalar(out=out, in0=in0, scalar1=s1, scalar2=None, op0=op0), rd, wr)
        else:
            P.op(eng, lambda e: e.tensor_scalar(out=out, in0=in0, scalar1=s1, scalar2=s2, op0=op0, op1=op1), rd, wr)

    def stt(out, in0, scalar, in1, op0, op1, rd, wr):
        P.op("dve", lambda e: e.scalar_tensor_tensor(out=out, in0=in0, scalar=scalar, in1=in1, op0=op0, op1=op1), rd, wr)

    def cp(eng, out, in_, rd, wr):
        if eng == "act":
            act(out, in_, AF.Copy, rd, wr)
        else:
            P.op(eng, lambda e: e.tensor_copy(out=out, in_=in_), rd, wr)

    def recip(out, in_, rd, wr):
        P.op("dve", lambda e: e.reciprocal(out=out, in_=in_), rd, wr)

    def rsqrt_mean(out, in_, n, rd, wr):
        act(out, in_, AF.Sqrt, list(rd) + [b_sm], wr, bias=epsc, scale=1.0 / n)
        recip(out, out, wr, wr)

    P.dma(cst[:, :, :], cst_d, writes=[b_cst])
    for k in range(8):
        P.dma(xT[:, k, :], xT_d[:, k, :], writes=b_x[k])
    P.dma(sc, cc_d, writes=[b_sm])
    cp("dve", cstb[:, 0, :], ident32, [b_cst], [b_cst])
    cp("dve", cstb[:, 1, :], ones32, [b_cst], [b_cst])
    act(sc, sc, AF.Silu, [b_sm], [b_sm])
    P.op("dve", lambda e: e.memset(epsc, EPS), [], [b_sm])
    ar.reset()

    def want(name):
        return stages is None or name in stages

    def layer(l):
        last = l == depth - 1
        lam_init = 0.8 - 0.6 * math.exp(-0.3 * l)
        streams = [(0, S)] if last else [(0, S), (S, C)]
        tb_all = tblocks(0, T)
        tb_mix = tblocks(0, S) if last else tb_all

        ar.reset()
        P.dma(bada, bada_d[l], writes=[b_sm])
        P.dma(gmf, gmf_d[l], writes=[b_sm])
        P.dma(gqk, gqk_d[l], writes=[b_sm])
        P.dma(gsub_s, gsub_d[l], writes=[b_sm])
        P.dma(hcw, hcw_d[l], writes=[b_sm])
        P.dma(hbias, hy_bias_d[l], writes=[b_sm])
        P.dma(fb[0:64, 0:3], hy_fb_d[l], writes=[b_sm])
        P.dma(lamt, lam_d[l][:, 0:128].partition_broadcast(128), writes=[b_sm])
        lam2 = ar.f32(128)
        b_l2 = Buf()
        P.dma(lam2, lam_d[l][:, 128:256].partition_broadcast(128), writes=[b_l2])
        wts = [(ar.f32(8 * 1024), Buf()) for _ in range(2)]
        for g in range(6):
            wt, bw = wts[g % 2]
            wt3 = wt.rearrange("p (k m) -> p k m", k=8)
            P.dma(wt3, w_ada_d[l][:, g * 1024:(g + 1) * 1024].rearrange("(k p) m -> p k m", p=128), writes=[bw])
            for m in range(8):
                for k in range(8):
                    mm(PS[0][:, m * 2:m * 2 + 2], wt3[:, k, m * 128:(m + 1) * 128], sc[:, k, :], k == 0, k == 7, [bw, b_sm], [PB[0]])
            tt("dve", modv[:, g], PS[0][:, 0:16].rearrange("p (m j) -> p m j", m=8),
               bada[:, g, :].unsqueeze(2).to_broadcast([128, 8, 2]), ALU.add, [PB[0], b_sm], [b_sm])
        for (Aap, gi, si) in ((A_mix, 0, 1), (A_ffn, 1, 4)):
            ts("dve", tmp16, modv[:, si], 1.0, None, ALU.add, None, [b_sm], [b_sm])
            tt("dve", Aap, tmp16, gmf[:, gi, :].unsqueeze(2).to_broadcast([128, 8, 2]), ALU.mult, [b_sm], [b_sm])
        tt("dve", lamt[:, 0:64], lamt[:, 0:64], lamt[:, 64:128], ALU.mult, [b_sm], [b_sm])
        tt("dve", lam2[:, 0:64], lam2[:, 0:64], lam2[:, 64:128], ALU.mult, [b_l2], [b_l2])
        P.op("dve", lambda e: e.reduce_sum(out=lamw[:, 0:1], in_=lamt[:, 0:64], axis=AX.X), [b_sm], [b_sm])
        P.op("dve", lambda e: e.reduce_sum(out=lamw[:, 1:2], in_=lam2[:, 0:64], axis=AX.X), [b_l2, b_sm], [b_sm])
        act(lamw, lamw, AF.Exp, [b_sm], [b_sm])
        tt("dve", neg_lam, lamw[:, 1:2], lamw[:, 0:1], ALU.subtract, [b_sm], [b_sm])
        ts("dve", neg_lam, neg_lam, -lam_init, None, ALU.add, None, [b_sm], [b_sm])
        ts("dve", gsub_s, gsub_s, 1.0 - lam_init, None, ALU.mult, None, [b_sm], [b_sm])
        tt("dve", fb[0:64, 3:4], fb[0:64, 0:1], fb[0:64, 1:2], ALU.mult, [b_sm], [b_sm])
        tt("dve", fb[0:64, 4:5], fb[0:64, 2:3], fb[0:64, 1:2], ALU.mult, [b_sm], [b_sm])

        def mod_tmps(w):
            return dict(sq=[(ar.f32(w), Buf()) for _ in range(3)], rs=[(ar.f32(w), Buf()) for _ in range(2)],
                        tm=[(ar.f32(w), Buf()) for _ in range(3)], c=[0, 0, 0])

        def modulate(Aap, gB, blocks, emit_cb, mt):
            for (t0, Tn) in blocks:
                j = 0 if t0 < S else 1
                xb = t0 // 512
                for k in range(8):
                    sq, b_sq = mt["sq"][mt["c"][0] % 3]
                    mt["c"][0] += 1
                    act(sq[:, :Tn], xT[:, k, t0:t0 + Tn], AF.Square, [b_x[k][xb]], [b_sq])
                    mm(PS[7][:, :Tn], ones32, sq[:, :Tn], k == 0, k == 7, [b_sq, b_cst], [PB[7]])
                r, b_r = mt["rs"][mt["c"][1] % 2]
                mt["c"][1] += 1
                rsqrt_mean(r[:, :Tn], PS[7][:, :Tn], D, [PB[7]], [b_r])
                for k in range(8):
                    tm, b_t = mt["tm"][mt["c"][2] % 3]
                    mt["c"][2] += 1
                    tt("dve", tm[:, :Tn], xT[:, k, t0:t0 + Tn], r[:, :Tn], ALU.mult, [b_x[k][xb], b_r], [b_t])
                    emit_cb(k, t0, Tn, tm, b_t, Aap[:, k, j:j + 1], modv[:, gB, k, j:j + 1])

        def hyena_filters(L):
            ar.reset()
            nj = L // 128
            w1 = ar.f32(64)
            w2 = ar.f32(64)
            w3 = ar.f32(1024)
            b_w = Buf()
            P.dma(w1[0:33, :], hy_w1_d[l], writes=[b_w])
            P.dma(w2[0:64, :], hy_w2_d[l], writes=[b_w])
            P.dma(w3[0:64, :], hy_w3_d[l], writes=[b_w])
            h2T = ar.f32(L)
            b_h2 = Buf()
            off_mlp = ar.off
            ft = [(ar.f32(512), Buf()) for _ in range(2)]
            aa = [(ar.f32(512), Buf()) for _ in range(2)]
            h1 = [(ar.f32(512), Buf()) for _ in range(2)]
            aa2 = [(ar.f32(512), Buf()) for _ in range(2)]
            sx = [(ar.f32(512), Buf()) for _ in range(3)]

            def sin_act(out, a, Tn, b_in, b_out):
                (s4, b_s4), (c4, b_c4), (q, b_q) = sx
                act(s4[0:64, :Tn], a, AF.Sin, [b_in], [b_s4], scale=0.25)
                act(c4[0:64, :Tn], a, AF.Abs, [b_in], [b_c4])
                act(c4[0:64, :Tn], c4[0:64, :Tn], AF.Sin, [b_c4, b_sm], [b_c4], bias=halfpi[0:64, :], scale=-0.25)
                tt("dve", q[0:64, :Tn], s4[0:64, :Tn], s4[0:64, :Tn], ALU.mult, [b_s4], [b_q])
                ts("dve", q[0:64, :Tn], q[0:64, :Tn], -2.0, 1.0, ALU.mult, ALU.add, [b_q], [b_q])
                tt("dve", c4[0:64, :Tn], s4[0:64, :Tn], c4[0:64, :Tn], ALU.mult, [b_s4, b_c4], [b_c4])
                stt(out, c4[0:64, :Tn], 4.0, q[0:64, :Tn], ALU.mult, ALU.mult, [b_c4, b_q], [b_out])

            for bi, (t0, Tn) in enumerate(tblocks(0, L)):
                f, b_f = ft[bi % 2]
                P.dma(f[0:33, :Tn], feats_d[L][:, t0:t0 + Tn], writes=[b_f])
                mm(PS[0][0:64, :Tn], w1[0:33, :], f[0:33, :Tn], True, True, [b_w, b_f], [PB[0]])
                a, b_a = aa[bi % 2]
                ts("dve", a[0:64, :Tn], PS[0][0:64, :Tn], fb[0:64, 1:2], fb[0:64, 3:4], ALU.mult, ALU.add, [PB[0], b_sm], [b_a])
                hh, b_h = h1[bi % 2]
                sin_act(hh[0:64, :Tn], a[0:64, :Tn], Tn, b_a, b_h)
                mm(PS[1][0:64, :Tn], w2[0:64, :], hh[0:64, :Tn], True, True, [b_w, b_h], [PB[1]])
                a2_, b_a2 = aa2[bi % 2]
                ts("dve", a2_[0:64, :Tn], PS[1][0:64, :Tn], fb[0:64, 1:2], fb[0:64, 4:5], ALU.mult, ALU.add, [PB[1], b_sm], [b_a2])
                sin_act(h2T[0:64, t0:t0 + Tn], a2_[0:64, :Tn], Tn, b_a2, b_h2)
            dec = ar.f32(nj * 256).rearrange("p (j c) -> p j c", j=nj)
            b_dec = Buf()
            P.dma(dec, dec_d[L].rearrange("(j p) c -> p j c", p=128), writes=[b_dec])
            hd = ar.f32(nj * 1024).rearrange("p (j c) -> p j c", j=nj)
            b_hd = [Buf() for _ in range(nj)]
            off_hd = ar.off
            for pc in range(nj):
                for half in range(2):
                    pb = half
                    mm(PS[pb][:, :], h2T[0:64, pc * 128:(pc + 1) * 128], w3[0:64, half * 512:(half + 1) * 512], True, True, [b_h2, b_w], [PB[pb]])
                    tt("dve", hd[:, pc, half * 512:(half + 1) * 512].rearrange("p (o c) -> p o c", o=2),
                       PS[pb][:, :].rearrange("p (o c) -> p o c", o=2),
                       dec[:, pc, :].unsqueeze(1).to_broadcast([128, 2, 256]), ALU.mult, [PB[pb], b_dec], [b_hd[pc]])
            P.op("dve", lambda e: e.memset(hd[0:1, 0, 512:1024], 0.0), [], [b_hd[0]])
            ab = [(ar.f32(1024), Buf()) for _ in range(2)]
            for pc in range(nj):
                a, b_a = ab[pc % 2]
                act(a, hd[:, pc, :], AF.Abs, [b_hd[pc]], [b_a])
                for half in range(2):
                    mm(PS[2 + half][:, :], ones32, a[:, half * 512:(half + 1) * 512], pc == 0, pc == nj - 1, [b_a, b_cst], [PB[2 + half]])
            rn = ar.f32(512)
            b_rn = Buf()
            cp("dve", rn, PS[2][:, :], [PB[2]], [b_rn])
            tt("dve", rn, rn, PS[3][:, :], ALU.add, [b_rn, PB[3]], [b_rn])
            recip(rn, rn, [b_rn], [b_rn])
            tmpe = [(ar.f32(512), Buf()) for _ in range(2)]
            for pc in range(nj):
                te, b_te = tmpe[pc % 2]
                hf = hd[:, pc, 0:512]
                hb = hd[:, pc, 512:1024]
                tt("dve", te, hf, hb, ALU.add, [b_hd[pc]], [b_te])
                tt("pool", hb, hf, hb, ALU.subtract, [b_hd[pc]], [b_hd[pc]])
                tt("dve", hf, te, rn, ALU.mult, [b_te, b_rn], [b_hd[pc]])
                tt("pool", hb, hb, rn, ALU.mult, [b_hd[pc], b_rn], [b_hd[pc]])
            P.barrier()
            ar.off = off_mlp
            tabA = ar.f32(2 * nj * 128)
            tabB = ar.f32(2 * nj * 128)
            ar.off = off_hd
            sts = [(ar.f32(1024).rearrange("p (c n) -> p c n", c=2), Buf()) for _ in range(2)]
            tabs = [(t_.rearrange("p (c j f) -> p c j f", c=2, j=nj), Buf()) for t_ in (tabA, tabB)]
            for fc in range(nj):
                tb_, b_tb = tabs[fc % 2]
                for c in range(2):
                    P.dma(tb_[:, c], tF_d[L][c, fc], writes=[b_tb])
                for c in range(2):
                    for jc in range(nj):
                        mm(PS[4 + c][:, :], tb_[:, c, jc, :], hd[:, jc, c * 512:(c + 1) * 512], jc == 0, jc == nj - 1, [b_tb, b_hd[jc]], [PB[4 + c]])
                st, b_st = sts[fc % 2]
                cp("act", st[:, 0, :], PS[4][:, :], [PB[4]], [b_st])
                cp("dve", st[:, 1, :], PS[5][:, :], [PB[5]], [b_st])
                for c in range(2):
                    P.dma(kf_s[L][c, fc], st[:, c, :], reads=[b_st], writes=[b_kf[L]])

        b_kf = {S: Buf(), C: Buf()}
        halfpi = sm[:, 182:183]
        P.op("dve", lambda e: e.memset(halfpi, PI / 2), [], [b_sm])
        if want("hyf"):
            for (t0, L) in streams:
                hyena_filters(L)

        ar.reset()
        nT = ar.bf(8 * T).rearrange("p (k t) -> p k t", k=8)
        b_n = [[Buf() for _ in range(5)] for _ in range(8)]
        b_nTs = Buf()

        def emit_mix(k, t0, Tn, tm, b_t, Asc, Bsc):
            act(nT[:, k, t0:t0 + Tn], tm[:, :Tn], AF.Identity, [b_t, b_sm], [b_n[k][t0 // 512]], bias=Bsc, scale=Asc)

        mark = ar.off
        if want("proj") or want("merge"):
            modulate(A_mix, 0, tb_all, emit_mix, mod_tmps(512))
            for k in range(8):
                P.dma(nT_s[:, k, :], nT[:, k, :], reads=b_n[k], writes=[b_nTs])
        ar.off = mark
        P.barrier()

        b_hy = Buf()
        b_fn = Buf()
        b_q = Buf()
        b_k = Buf()
        b_v = Buf()
        if want("proj"):
            w32 = [(ar.f32(8 * 512).rearrange("p (k m) -> p k m", k=8), Buf()) for _ in range(1)]
            wbf = [(ar.bf(8 * 512).rearrange("p (k m) -> p k m", k=8), Buf()) for _ in range(2)]
            stg = [(ar.f32(512), Buf()) for _ in range(2)]
            sqt = [(ar.f32(512), Buf()) for _ in range(2)]
            rt = [(ar.f32(512), Buf()) for _ in range(2)]
            xnt = [(ar.f32(512), Buf()) for _ in range(2)]
            t1t = [(ar.f32(512), Buf()) for _ in range(2)]
            t2t = [(ar.f32(512), Buf()) for _ in range(2)]
            obt = [(ar.bf(512), Buf()) for _ in range(2)]
            rope_sb = ar.f32(2 * S).rearrange("p (c t) -> p c t", c=2)
            b_rope = Buf()
            for c in range(2):
                P.dma(rope_sb[:, c, :], rope_d[c], writes=[b_rope])
            cnt = [0]

            def qk_cb(ps, pb, which, h, t0, Tn):
                i = cnt[0] % 2
                cnt[0] += 1
                sq_, b_sq_ = sqt[i]
                act(sq_[:, :Tn], ps[:, :Tn], AF.Square, [pb], [b_sq_])
                mm(PS[6][:, :Tn], bd64, sq_[:, :Tn], True, True, [b_sq_, b_cst], [PB[6]])
                r_, b_r_ = rt[i]
                rsqrt_mean(r_[:, :Tn], PS[6][:, :Tn], 64, [PB[6]], [b_r_])
                xn, b_xn = xnt[i]
                stt(xn[:, :Tn], ps[:, :Tn], gqk[:, which:which + 1], r_[:, :Tn], ALU.mult, ALU.mult, [pb, b_r_, b_sm], [b_xn])
                ob, b_ob = obt[i]
                if t0 < S:
                    mm(PS[5][:, :Tn], rot, xn[:, :Tn], True, True, [b_xn, b_cst], [PB[5]])
                    t1, b_t1 = t1t[i]
                    t2, b_t2 = t2t[i]
                    tt("pool", t1[:, :Tn], xn[:, :Tn], rope_sb[:, 0, t0:t0 + Tn], ALU.mult, [b_xn, b_rope], [b_t1])
                    tt("dve", t2[:, :Tn], PS[5][:, :Tn], rope_sb[:, 1, t0:t0 + Tn], ALU.mult, [PB[5], b_rope], [b_t2])
                    tt("pool", ob[:, :Tn], t1[:, :Tn], t2[:, :Tn], ALU.add, [b_t1, b_t2], [b_ob])
                else:
                    cp("pool", ob[:, :Tn], xn[:, :Tn], [b_xn], [b_ob])
                dst = (q_s if which == 0 else k_s)[h][:, t0:t0 + Tn]
                P.dma(dst, ob[:, :Tn], reads=[b_ob], writes=[b_q if which == 0 else b_k])

            pcnt = [0]
            for g in range(5):
                w3_, b_w3 = w32[0]
                P.dma(w3_, w_in_d[l][:, g * 512:(g + 1) * 512].rearrange("(k p) m -> p k m", p=128), writes=[b_w3])
                wb_, b_wb = wbf[g % 2]
                cp("act", wb_[:, 0:4, :], w3_[:, 0:4, :], [b_w3], [b_wb])
                cp("pool", wb_[:, 4:8, :], w3_[:, 4:8, :], [b_w3], [b_wb])
                if g < 4:
                    for mi in range(4):
                        for (t0, Tn) in tb_all:
                            if g == 2 and t0 >= S and last:
                                continue
                            pi = pcnt[0] % 4
                            pcnt[0] += 1
                            for k in range(8):
                                mm(PS[pi][:, :Tn], wb_[:, k, mi * 128:(mi + 1) * 128], nT[:, k, t0:t0 + Tn], k == 0, k == 7,
                                   [b_wb, b_n[k][t0 // 512]], [PB[pi]])
                            if g < 2:
                                st, b_st = stg[pcnt[0] % 2]
                                cp("act" if pcnt[0] % 2 else "dve", st[:, :Tn], PS[pi][:, :Tn], [PB[pi]], [b_st])
                                ch = g * 4 + mi
                                if ch < 6:
                                    P.dma(hy_s[ch][:, t0:t0 + Tn], st[:, :Tn], reads=[b_st], writes=[b_hy])
                                else:
                                    P.dma(fn_s[ch - 6][:, t0:t0 + Tn], st[:, :Tn], reads=[b_st], writes=[b_fn])
                            else:
                                qk_cb(PS[pi], PB[pi], g - 2, mi, t0, Tn)
                else:
                    for i in range(18):
                        pi = pcnt[0] % 4
                        pcnt[0] += 1
                        for k in range(8):
                            mm(PS[pi][:, :], nT[:, k, i * 128:(i + 1) * 128], wb_[:, k, :], k == 0, k == 7, [b_wb, b_n[k][i // 4]], [PB[pi]])
                        ob, b_ob = obt[i % 2]
                        cp("act" if i % 2 else "dve", ob, PS[pi][:, :], [PB[pi]], [b_ob])
                        P.dma(v_s[:, i, :], ob, reads=[b_ob], writes=[b_v])

        b_ato = Buf()
        if want("attn"):
            ar.reset()
            kT = ar.bf(4 * T).rearrange("p (h t) -> p h t", h=4)
            qT = ar.bf(4 * T).rearrange("p (h t) -> p h t", h=4)
            vv = ar.bf(18 * 512).rearrange("p (j c) -> p j c", j=18)
            b_kT = Buf()
            b_qT = Buf()
            b_vv = Buf()
            for h in range(4):
                P.dma(kT[:, h, :], k_s[h], reads=[b_k], writes=[b_kT])
                P.dma(qT[:, h, :], q_s[h], reads=[b_q], writes=[b_qT])
            P.dma(vv, v_s, reads=[b_v], writes=[b_vv])
            Et = [(ar.bf(512), Buf()) for _ in range(3)]
            r0 = ar.f32(512)
            t0_ = ar.f32(512)
            t1_ = ar.f32(512)
            sq_ = ar.f32(512)
            rr_ = ar.f32(512)
            b_r0, b_t0, b_t1, b_sq2, b_rr = [Buf() for _ in range(5)]
            aob = [(ar.bf(512), Buf()) for _ in range(2)]
            ec = 0
            oc = 0
            qblocks = tblocks(0, S) if last else tb_all
            for h in range(4):
                for (q0, Tn) in qblocks:
                    keys = list(range(18)) if q0 < S else [16, 17]
                    for c in range(2):
                        for idx, j in enumerate(keys):
                            sp = ec % 2
                            mm(PS[sp][:, :Tn], kT[64 * c:64 * c + 64, h, j * 128:(j + 1) * 128], qT[64 * c:64 * c + 64, h, q0:q0 + Tn],
                               True, True, [b_kT, b_qT], [PB[sp]])
                            E, b_E = Et[ec % 3]
                            ec += 1
                            act(E[:, :Tn], PS[sp][:, :Tn], AF.Exp, [PB[sp]], [b_E], scale=0.125)
                            mm(PS[2 + 2 * c][:, :Tn], vv[:, j, h * 128:(h + 1) * 128], E[:, :Tn], idx == 0, idx == len(keys) - 1, [b_vv, b_E], [PB[2 + 2 * c]])
                            mm(PS[3 + 2 * c][:, :Tn], onesb, E[:, :Tn], idx == 0, idx == len(keys) - 1, [b_cst, b_E], [PB[3 + 2 * c]])
                    recip(r0[:, :Tn], PS[3][:, :Tn], [PB[3]], [b_r0])
                    tt("dve", t0_[:, :Tn], PS[2][:, :Tn], r0[:, :Tn], ALU.mult, [PB[2], b_r0], [b_t0])
                    recip(r0[:, :Tn], PS[5][:, :Tn], [PB[5]], [b_r0])
                    tt("dve", t1_[:, :Tn], PS[4][:, :Tn], r0[:, :Tn], ALU.mult, [PB[4], b_r0], [b_t1])
                    stt(t0_[:, :Tn], t1_[:, :Tn], neg_lam, t0_[:, :Tn], ALU.mult, ALU.add, [b_t1, b_t0, b_sm], [b_t0])
                    act(sq_[:, :Tn], t0_[:, :Tn], AF.Square, [b_t0], [b_sq2])
                    mm(PS[6][:, :Tn], ones32, sq_[:, :Tn], True, True, [b_sq2, b_cst], [PB[6]])
                    rsqrt_mean(rr_[:, :Tn], PS[6][:, :Tn], 128, [PB[6]], [b_rr])
                    ao, b_ao = aob[oc % 2]
                    oc += 1
                    stt(ao[:, :Tn], t0_[:, :Tn], gsub_s, rr_[:, :Tn], ALU.mult, ALU.mult, [b_t0, b_rr, b_sm], [b_ao])
                    P.dma(ato_s[h][:, q0:q0 + Tn], ao[:, :Tn], reads=[b_ao], writes=[b_ato])

        b_hyo = Buf()

        def hyena_main(t0, L):
            TB, G, nTB, nG = RL[L]
            nj = L // 128
            for cc in range(2):
                ar.reset()
                raw = ar.f32(L)
                b_raw = Buf()
                us = [ar.f32(L) for _ in range(3)]
                b_us = [Buf() for _ in range(3)]
                for jx in range(3):
                    ch = jx * 2 + cc
                    P.dma(raw, hy_s[ch][:, t0:t0 + L], reads=[b_hy], writes=[b_raw])
                    u = us[jx]
                    ts("dve", u, raw, hcw[:, ch, 1:2], hcw[:, ch, 3:4], ALU.mult, ALU.add, [b_raw, b_sm], [b_us[jx]])
                    stt(u[:, 1:L], raw[:, 0:L - 1], hcw[:, ch, 0:1], u[:, 1:L], ALU.mult, ALU.add, [b_raw, b_us[jx], b_sm], [b_us[jx]])
                    stt(u[:, 0:L - 1], raw[:, 1:L], hcw[:, ch, 2:3], u[:, 0:L - 1], ALU.mult, ALU.add, [b_raw, b_us[jx], b_sm], [b_us[jx]])
                zbuf = raw
                b_zb = b_raw
                ztm = ar.f32(L).rearrange("p (j c) -> p j c", j=nj)
                b_ztm = Buf()
                Z = ar.f32(2 * L).rearrange("p (c j n) -> p c j n", c=2, j=nj)
                b_Z = Buf()
                Kf = ar.f32(2 * L).rearrange("p (c j n) -> p c j n", c=2, j=nj)
                b_Kf = Buf()
                X = ar.f32(2 * L).rearrange("p (c j n) -> p c j n", c=2, j=nj)
                b_X = Buf()
                tmpx = ztm
                b_tx = b_ztm
                tabs = [(ar.f32(2048), Buf()) for _ in range(4)]
                tcnt = 0
                z, b_z = us[0], b_us[0]
                for o in range(2):
                    for scn in range(nj):
                        tr(PS[0][:, (scn % 4) * 128:(scn % 4 + 1) * 128], z[:, scn * 128:(scn + 1) * 128], ident32, [b_z, b_cst], [PB[0]])
                        if scn % 4 == 3 or scn == nj - 1:
                            n4 = scn % 4 + 1
                            cp("act", ztm[:, scn - n4 + 1:scn + 1, :], PS[0][:, 0:n4 * 128].rearrange("p (j c) -> p j c", j=n4), [PB[0]], [b_ztm])
                    for c in range(2):
                        P.dma(Kf[:, c], kf_s[L][c][:, :, o * 256 + cc * 128:o * 256 + cc * 128 + 128].rearrange("j p n -> p j n"), reads=[b_kf[L]], writes=[b_Kf])
                    for fc in range(nj):
                        tbc, b_tbc = tabs[tcnt % 4]
                        tbs, b_tbs = tabs[(tcnt + 1) % 4]
                        tcnt += 2
                        tbc3 = tbc[:, 0:nj * 128].rearrange("p (j f) -> p j f", j=nj)
                        tbs3 = tbs[:, 0:nj * 128].rearrange("p (j f) -> p j f", j=nj)
                        P.dma(tbc3, tF_d[L][0, fc], writes=[b_tbc])
                        P.dma(tbs3, tF_d[L][1, fc], writes=[b_tbs])
                        pz = 1 + fc % 2
                        for sc_ in range(nj):
                            mm(PS[pz][:, 0:128], tbc3[:, sc_, :], ztm[:, sc_, :], sc_ == 0, sc_ == nj - 1, [b_tbc, b_ztm], [PB[pz]])
                        for sc_ in range(nj):
                            mm(PS[pz][:, 128:256], tbs3[:, sc_, :], ztm[:, sc_, :], sc_ == 0, sc_ == nj - 1, [b_tbs, b_ztm], [PB[pz]])
                        cp("act", Z[:, :, fc, :], PS[pz][:, 0:256].rearrange("p (c n) -> p c n", c=2), [PB[pz]], [b_Z])
                    Zr, Zi, Kr, Ki = Z[:, 0], Z[:, 1], Kf[:, 0], Kf[:, 1]
                    tt("dve", X[:, 0], Zr, Kr, ALU.mult, [b_Z, b_Kf], [b_X])
                    tt("pool", tmpx, Zi, Ki, ALU.mult, [b_Z, b_Kf], [b_tx])
                    tt("dve", X[:, 0], X[:, 0], tmpx, ALU.subtract, [b_X, b_tx], [b_X])
                    tt("pool", X[:, 1], Zr, Ki, ALU.mult, [b_Z, b_Kf], [b_X])
                    tt("dve", tmpx, Zi, Kr, ALU.mult, [b_Z, b_Kf, b_X], [b_tx])
                    tt("dve", X[:, 1], X[:, 1], tmpx, ALU.add, [b_X, b_tx], [b_X])
                    gate, b_g = us[1 + o], b_us[1 + o]
                    znew, b_zn = (zbuf, b_zb) if o == 0 else (us[0], b_us[0])
                    for tb in range(nTB):
                        py = 3 + tb % 2
                        for g in range(nG):
                            tbc, b_tbc = tabs[tcnt % 4]
                            tbs, b_tbs = tabs[(tcnt + 1) % 4]
                            tcnt += 2
                            tc3 = tbc[:, 0:G * TB].rearrange("p (g t) -> p g t", g=G)
                            ts3 = tbs[:, 0:G * TB].rearrange("p (g t) -> p g t", g=G)
                            P.dma(tc3, tI_d[L][0, tb, g], writes=[b_tbc])
                            P.dma(ts3, tI_d[L][1, tb, g], writes=[b_tbs])
                            for fi in range(G):
                                fc = g * G + fi
                                mm(PS[py][:, :TB], X[:, 0, fc, :], tc3[:, fi, :], fc == 0, False, [b_X, b_tbc], [PB[py]])
                                mm(PS[py][:, :TB], X[:, 1, fc, :], ts3[:, fi, :], False, fc == nj - 1, [b_X, b_tbs], [PB[py]])
                        sl = slice(tb * TB, (tb + 1) * TB)
                        stt(tmpx.rearrange("p j n -> p (j n)")[:, sl], z[:, sl], hbias[:, o, cc:cc + 1], PS[py][:, :TB], ALU.mult, ALU.add, [b_z, PB[py], b_sm, b_X], [b_tx])
                        tt("dve", znew[:, sl], tmpx.rearrange("p j n -> p (j n)")[:, sl], gate[:, sl], ALU.mult, [b_tx, b_g], [b_zn])
                    z, b_z = znew, b_zn
                ob = ar.bf(L)
                b_ob = Buf()
                cp("act", ob, z, [b_z], [b_ob])
                P.dma(hyo_s[cc][:, t0:t0 + L], ob, reads=[b_ob], writes=[b_hyo])

        if want("hyena"):
            for (t0, L) in streams:
                hyena_main(t0, L)

        b_fno = Buf()

        def fnet(t0, L):
            TB, G, nTB, nG = RL[L]
            nj = L // 128
            ar.reset()
            fz = [(ar.f32(L), Buf()) for _ in range(2)]
            zc = ar.f32(nj * 256).rearrange("p (j c) -> p j c", j=nj)
            zs = ar.f32(nj * 256).rearrange("p (j c) -> p j c", j=nj)
            b_zc = Buf()
            for cc in range(2):
                f, b_f = fz[cc]
                P.dma(f, fn_s[cc][:, t0:t0 + L], reads=[b_fn], writes=[b_f])
                for scn in range(nj):
                    pa = scn % 2
                    mm(PS[pa][:, 0:128], f[:, scn * 128:(scn + 1) * 128], bdc, True, True, [b_f, b_cst], [PB[pa]])
                    mm(PS[pa][:, 128:256], f[:, scn * 128:(scn + 1) * 128], bds, True, True, [b_f, b_cst], [PB[pa]])
                    cp("act", zc[:, scn, cc * 128:(cc + 1) * 128], PS[pa][:, 0:128], [PB[pa]], [b_zc])
                    cp("dve", zs[:, scn, cc * 128:(cc + 1) * 128], PS[pa][:, 128:256], [PB[pa]], [b_zc])
            tabs = [(ar.f32(2048), Buf()) for _ in range(4)]
            obs = [(ar.bf(512), Buf()) for _ in range(2)]
            tcnt = 0
            oc = 0
            for tb in range(nTB):
                for g in range(nG):
                    tbc, b_tbc = tabs[tcnt % 4]
                    tbs, b_tbs = tabs[(tcnt + 1) % 4]
                    tcnt += 2
                    tc3 = tbc[:, 0:G * TB].rearrange("p (g t) -> p g t", g=G)
                    ts3 = tbs[:, 0:G * TB].rearrange("p (g t) -> p g t", g=G)
                    P.dma(tc3, tN_d[L][0, tb, g], writes=[b_tbc])
                    P.dma(ts3, tN_d[L][1, tb, g], writes=[b_tbs])
                    for cc in range(2):
                        py = 2 + cc + 2 * (tb % 2)
                        for fi in range(G):
                            sc_ = g * G + fi
                            mm(PS[py][:, :TB], zc[:, sc_, cc * 128:(cc + 1) * 128], tc3[:, fi, :], sc_ == 0, False, [b_zc, b_tbc], [PB[py]])
                            mm(PS[py][:, :TB], zs[:, sc_, cc * 128:(cc + 1) * 128], ts3[:, fi, :], False, sc_ == nj - 1, [b_zc, b_tbs], [PB[py]])
                for cc in range(2):
                    py = 2 + cc + 2 * (tb % 2)
                    ob, b_ob = obs[oc % 2]
                    oc += 1
                    cp("act" if cc else "dve", ob[:, :TB], PS[py][:, :TB], [PB[py]], [b_ob])
                    P.dma(fno_s[cc][:, t0 + tb * TB:t0 + (tb + 1) * TB], ob[:, :TB], reads=[b_ob], writes=[b_fno])

        if want("fnet"):
            for (t0, L) in streams:
                fnet(t0, L)

        if want("merge"):
            ar.reset()
            wbr = ar.bf(8 * D).rearrange("p (k m) -> p k m", k=8)
            wo = ar.bf(8 * D).rearrange("p (k m) -> p k m", k=8)
            b_wbr = Buf()
            b_wo = Buf()
            w32 = ar.f32(8 * 512).rearrange("p (k m) -> p k m", k=8)
            b_w32 = Buf()
            for (src, dst, bd) in ((w_br_d, wbr, b_wbr), (w_out_d, wo, b_wo)):
                for half in range(2):
                    P.dma(w32, src[l][:, half * 512:(half + 1) * 512].rearrange("(k p) m -> p k m", p=128), writes=[b_w32])
                    cp("act", dst[:, 0:4, half * 512:(half + 1) * 512], w32[:, 0:4, :], [b_w32], [bd])
                    cp("dve", dst[:, 4:8, half * 512:(half + 1) * 512], w32[:, 4:8, :], [b_w32], [bd])
            ar.off -= 8 * 512
            P.barrier()
            nb = [(ar.bf(8 * 512).rearrange("p (k t) -> p k t", k=8), Buf()) for _ in range(2)]
            sb_ = [(ar.bf(8 * 512).rearrange("p (k t) -> p k t", k=8), Buf()) for _ in range(2)]
            g32 = [(ar.f32(8 * 384).rearrange("p (k j c) -> p k j c", k=8, j=3), Buf()) for _ in range(2)]
            gbf = [(ar.bf(8 * 384).rearrange("p (k j c) -> p k j c", k=8, j=3), Buf()) for _ in range(2)]
            sig = [(ar.f32(512), Buf()) for _ in range(3)]
            yacc = [(ar.f32(512), Buf()) for _ in range(2)]
            ytmp = [(ar.f32(512), Buf()) for _ in range(2)]
            yT = ar.bf(8 * 512).rearrange("p (k t) -> p k t", k=8)
            b_yT = [Buf() for _ in range(8)]
            KR = ((0, 2), (2, 4), (4, 8))
            gc = 0
            for bi, (t0, Tn) in enumerate(tb_mix):
                nbk, b_nb = nb[bi % 2]
                sbk, b_sb = sb_[bi % 2]
                P.dma(nbk[:, :, :Tn], nT_s[:, :, t0:t0 + Tn], reads=[b_nTs], writes=[b_nb])
                for cc in range(2):
                    P.dma(sbk[:, cc, :Tn], hyo_s[cc][:, t0:t0 + Tn], reads=[b_hyo], writes=[b_sb])
                    P.dma(sbk[:, 2 + cc, :Tn], fno_s[cc][:, t0:t0 + Tn], reads=[b_fno], writes=[b_sb])
                for h in range(4):
                    P.dma(sbk[:, 4 + h, :Tn], ato_s[h][:, t0:t0 + Tn], reads=[b_ato], writes=[b_sb])
                for m in range(8):
                    gw, b_gw = g32[gc % 2]
                    gb, b_gb = gbf[gc % 2]
                    gc += 1
                    P.dma(gw, w_gate_d[l, m], writes=[b_gw])
                    cp("pool", gb, gw, [b_gw], [b_gb])
                    ya, b_ya = yacc[m % 2]
                    yt, b_yt = ytmp[m % 2]
                    for j in range(3):
                        pg = j
                        pbr = 3 + j
                        for k in range(8):
                            mm(PS[pg][:, :Tn], gb[:, k, j, :], nbk[:, k, :Tn], k == 0, k == 7, [b_gb, b_nb], [PB[pg]])
                        k0, k1 = KR[j]
                        for k in range(k0, k1):
                            mm(PS[pbr][:, :Tn], wbr[:, k, m * 128:(m + 1) * 128], sbk[:, k, :Tn], k == k0, k == k1 - 1, [b_wbr, b_sb], [PB[pbr]])
                        sg, b_sg = sig[(m * 3 + j) % 3]
                        act(sg[:, :Tn], PS[pg][:, :Tn], AF.Sigmoid, [PB[pg]], [b_sg])
                        if j == 0:
                            tt("dve", ya[:, :Tn], sg[:, :Tn], PS[pbr][:, :Tn], ALU.mult, [b_sg, PB[pbr]], [b_ya])
                        else:
                            tt("dve", yt[:, :Tn], sg[:, :Tn], PS[pbr][:, :Tn], ALU.mult, [b_sg, PB[pbr]], [b_yt])
                            if j == 1:
                                tt("pool", ya[:, :Tn], ya[:, :Tn], yt[:, :Tn], ALU.add, [b_ya, b_yt], [b_ya])
                            else:
                                tt("pool", yT[:, m, :Tn], ya[:, :Tn], yt[:, :Tn], ALU.add, [b_ya, b_yt], [b_yT[m]])
                jj = 0 if t0 < S else 1
                for mo in range(8):
                    po = 6 + mo % 2
                    for k in range(8):
                        mm(PS[po][:, :Tn], wo[:, k, mo * 128:(mo + 1) * 128], yT[:, k, :Tn], k == 0, k == 7, [b_wo, b_yT[k]], [PB[po]])
                    xb = b_x[mo][t0 // 512]
                    stt(xT[:, mo, t0:t0 + Tn], PS[po][:, :Tn], modv[:, 2, mo, jj:jj + 1], xT[:, mo, t0:t0 + Tn], ALU.mult, ALU.add, [PB[po], xb, b_sm], [xb])

        if want("peer"):
            peer(l, last, modulate, mod_tmps, A_ffn)

    def peer(l, last, modulate, mod_tmps, A_ffn):
        ar.reset()
        b_ub = Buf()
        b_vb = Buf()
        ld = [(ar.f32(4096), Buf()) for _ in range(3)]
        cv = [(ar.bf(4096), Buf()) for _ in range(3)]
        jobs = []
        for k in range(8):
            for eb in range(4):
                jobs.append(("u", k, eb))
        for e1g in range(32):
            jobs.append(("v", e1g, 0))

        def j_load(ci):
            kind, i0_, i1_ = jobs[ci]
            a, b_a = ld[ci % 3]
            if kind == "u":
                P.dma(a, uT_d[l][i0_ * 128:(i0_ + 1) * 128, i1_ * 4096:(i1_ + 1) * 4096], writes=[b_a])
            else:
                P.dma(a.rearrange("p (e d) -> p e d", e=4), v_d_in[l][i0_ * 512:(i0_ + 1) * 512, :].rearrange("(e p) d -> p e d", p=128), writes=[b_a])

        j_load(0)
        j_load(1)
        for ci in range(len(jobs)):
            if ci + 2 < len(jobs):
                j_load(ci + 2)
            kind, i0_, i1_ = jobs[ci]
            a, b_a = ld[ci % 3]
            o, b_o = cv[ci % 3]
            cp(("act", "dve", "pool")[ci % 3], o, a, [b_a], [b_o])
            if kind == "u":
                P.dma(uTb_s[:, i0_, i1_ * 4096:(i1_ + 1) * 4096], o, reads=[b_o], writes=[b_ub])
            else:
                P.dma(vb_s[:, i0_ * 4:(i0_ + 1) * 4, :], o.rearrange("p (e d) -> p e d", e=4), reads=[b_o], writes=[b_vb])
        ar.reset()
        keysT = ar.f32(2048).rearrange("p (h n) -> p h n", h=16)
        b_ky = Buf()
        P.dma(keysT, keysT_d[l], writes=[b_ky])
        nbf = ar.bf(8 * 256).rearrange("p (k t) -> p k t", k=8)
        b_nbf = [Buf() for _ in range(8)]
        s1k = [ar.f32(1024).rearrange("p (h n) -> p h n", h=8) for _ in range(2)]
        a2k = [ar.f32(1024).rearrange("p (h n) -> p h n", h=8) for _ in range(2)]
        a1t = [ar.f32(128).rearrange("p (h a) -> p h a", h=8) for _ in range(2)]
        top1k = [ar.f32(128).rearrange("p (h a) -> p h a", h=8) for _ in range(2)]
        theta = [ar.f32(8) for _ in range(2)]
        b_keep = [Buf() for _ in range(2)]
        zer = ar.bf(512)
        b_zer = Buf()
        P.op("pool", lambda e: e.memset(zer, 0.0), [], [b_zer])
        base = ar.off
        blocks = tblocks(0, S if last else T, 256)
        b_wT = [Buf(), Buf()]
        for (t0, Tn) in blocks:
            jj = 0 if t0 < S else 1
            P.barrier()
            ar.off = base
            mt = mod_tmps(256)
            n32 = ar.f32(8 * 256).rearrange("p (k t) -> p k t", k=8)
            b_n32 = [Buf() for _ in range(8)]
            wq = [(ar.f32(8 * 128).rearrange("p (k m) -> p k m", k=8), Buf()) for _ in range(2)]
            qT = ar.f32(16 * 256).rearrange("p (h t) -> p h t", h=16)
            b_qT = Buf()
            s_sb = ar.f32(2048).rearrange("p (h n) -> p h n", h=16)
            b_s = Buf()
            tmpm = ar.f32(256)
            b_tm = Buf()
            tmpm2 = ar.f32(256)
            b_tm2 = Buf()
            top = ar.f32(256).rearrange("p (h a) -> p h a", h=16)
            b_top = Buf()
            cand = ar.f32(2048).rearrange("p (h a b) -> p h a b", h=8, a=16)
            b_cand = Buf()
            ctop = ar.f32(192).rearrange("p (h a) -> p h a", h=8)
            b_ct = Buf()
            misc = ar.f32(16)
            b_mi = Buf()
            ex16 = ar.f32(128).rearrange("p (h a) -> p h a", h=8)

            def emit_ffn(k, t0_, Tn_, tm, b_t, Asc, Bsc):
                act(n32[:, k, :], tm[:, :Tn_], AF.Identity, [b_t, b_sm], [b_n32[k]], bias=Bsc, scale=Asc)
                cp("pool", nbf[:, k, :], n32[:, k, :], [b_n32[k]], [b_nbf[k]])

            modulate(A_ffn, 3, [(t0, Tn)], emit_ffn, mt)
            for hp in range(16):
                w, b_w = wq[hp % 2]
                P.dma(w, wq_d[l][:, hp * 128:(hp + 1) * 128].rearrange("(k p) m -> p k m", p=128), writes=[b_w])
                pq = hp % 2
                for k in range(8):
                    mm(PS[pq][:, :Tn], w[:, k, :], n32[:, k, :], k == 0, k == 7, [b_w, b_n32[k]], [PB[pq]])
                cp("act" if hp % 2 else "dve", qT[:, hp, :], PS[pq][:, :Tn], [PB[pq]], [b_qT])
            for ti in range(2):
                tsl = slice(ti * 128, (ti + 1) * 128)
                bk = b_keep[ti]
                for hp in range(16):
                    pbk = hp // 4
                    mm(PS[pbk][:, (hp % 4) * 128:(hp % 4 + 1) * 128], qT[:, hp, tsl], keysT[:, hp, :], True, True, [b_qT, b_ky], [PB[pbk]])
                for q4 in range(4):
                    cp("act" if q4 % 2 else "dve", s_sb[:, q4 * 4:(q4 + 1) * 4, :], PS[q4][:, :].rearrange("p (h n) -> p h n", h=4), [PB[q4]], [b_s])
                for hp in range(16):
                    P.op("dve", (lambda hp: lambda e: e.max(out=top[:, hp, 0:8], in_=s_sb[:, hp, :]))(hp), [b_s], [b_top])
                    P.op("dve", (lambda hp: lambda e: e.match_replace(out=tmpm[:, 0:128], in_to_replace=top[:, hp, 0:8], in_values=s_sb[:, hp, :], imm_value=-1e30))(hp), [b_s, b_top], [b_tm])
                    P.op("dve", (lambda hp: lambda e: e.max(out=top[:, hp, 8:16], in_=tmpm[:, 0:128]))(hp), [b_tm], [b_top])
                top4 = top.rearrange("p (h c) a -> p h c a", c=2)
                s4v = s_sb.rearrange("p (h c) n -> p h c n", c=2)
                tt("dve", cand, top4[:, :, 0, :].unsqueeze(3).to_broadcast([128, 8, 16, 16]),
                   top4[:, :, 1, :].unsqueeze(2).to_broadcast([128, 8, 16, 16]), ALU.add, [b_top], [b_cand])
                for h in range(8):
                    ch = cand[:, h].rearrange("p a b -> p (a b)")
                    P.op("dve", (lambda h, ch: lambda e: e.max(out=ctop[:, h, 0:8], in_=ch))(h, ch), [b_cand], [b_ct])
                    P.op("dve", (lambda h, ch: lambda e: e.match_replace(out=tmpm, in_to_replace=ctop[:, h, 0:8], in_values=ch, imm_value=-1e30))(h, ch), [b_cand, b_ct], [b_tm])
                    P.op("dve", (lambda h: lambda e: e.max(out=ctop[:, h, 8:16], in_=tmpm))(h), [b_tm], [b_ct])
                    P.op("dve", (lambda h: lambda e: e.match_replace(out=tmpm2, in_to_replace=ctop[:, h, 8:16], in_values=tmpm, imm_value=-1e30))(h), [b_tm, b_ct], [b_tm2])
                    P.op("dve", (lambda h: lambda e: e.max(out=ctop[:, h, 16:24], in_=tmpm2))(h), [b_tm2], [b_ct])
                tt("dve", ex16, ctop[:, :, 0:16], ctop[:, :, 0:1].to_broadcast([128, 8, 16]), ALU.subtract, [b_ct], [b_mi])
                act(ex16, ex16, AF.Exp, [b_mi], [b_mi])
                P.op("dve", lambda e: e.reduce_sum(out=misc[:, 8:16], in_=ex16, axis=AX.X), [b_mi], [b_mi])
                recip(misc[:, 8:16], misc[:, 8:16], [b_mi], [b_mi])
                m8 = misc[:, 0:8].unsqueeze(2)
                tt("dve", m8, ctop[:, :, 15:16], ctop[:, :, 16:17], ALU.add, [b_ct, b_mi], [b_mi])
                stt(m8, m8, 0.5, ctop[:, :, 0:1], ALU.mult, ALU.subtract, [b_mi, b_ct], [b_mi])
                act(misc[:, 0:8], misc[:, 0:8], AF.Exp, [b_mi], [b_mi])
                tt("dve", theta[ti], misc[:, 0:8], misc[:, 8:16], ALU.mult, [b_mi], [bk])
                cp("pool", s1k[ti], s4v[:, :, 0, :], [b_s], [bk])
                cp("pool", top1k[ti], top4[:, :, 0, :], [b_top], [bk])
                tt("dve", a2k[ti], s4v[:, :, 1, :], top4[:, :, 1, 0:1].to_broadcast([128, 8, 128]), ALU.subtract, [b_s, b_top], [bk])
                act(a2k[ti], a2k[ti], AF.Exp, [bk], [bk])
                tt("dve", a1t[ti], top4[:, :, 0, :], top4[:, :, 0, 0:1].to_broadcast([128, 8, 16]), ALU.subtract, [b_top], [bk])
                act(a1t[ti], a1t[ti], AF.Exp, [bk], [bk])
                tt("dve", a1t[ti], a1t[ti], misc[:, 8:16].unsqueeze(2).to_broadcast([128, 8, 16]), ALU.mult, [bk, b_mi], [bk])
            P.barrier()
            ar.off = base
            pmt = ar.f32(2048).rearrange("p (h a e) -> p h a e", h=8, a=16)
            b_pm = Buf()
            csl = [(ar.bf(2048).rearrange("p (h a e) -> p h a e", h=8, a=16), Buf()) for _ in range(2)]
            CT = ar.bf(128 * 128).rearrange("p (e t) -> p e t", e=128)
            b_CT = Buf()
            osl = ar.bf(4096).rearrange("p (h a e) -> p h a e", h=8, a=16)
            b_osl = Buf()
            OTs = [(ar.bf(4096).rearrange("p (e t) -> p e t", e=32), Buf()) for _ in range(2)]
            WTs = [(ar.bf(4096).rearrange("p (e t) -> p e t", e=32), Buf()) for _ in range(2)]
            PSb = [PS[i][:, :].bitcast(BF16) for i in range(8)]
            evc = 0
            for ti in range(2):
                bk = b_keep[ti]
                for es in range(8):
                    tt("pool", pmt, a1t[ti].unsqueeze(3).to_broadcast([128, 8, 16, 16]),
                       a2k[ti][:, :, es * 16:(es + 1) * 16].unsqueeze(2).to_broadcast([128, 8, 16, 16]), ALU.mult, [bk], [b_pm])
                    cs, b_cs = csl[es % 2]
                    for h in range(8):
                        pmh = pmt[:, h].rearrange("p a e -> p (a e)")
                        stt(cs[:, h].rearrange("p a e -> p (a e)"), pmh, theta[ti][:, h:h + 1], pmh, ALU.is_ge, ALU.mult, [b_pm, bk], [b_cs])
                    csf = cs.rearrange("p h a e -> p (h a) e")
                    for half in range(2):
                        pb = 2 + (es * 2 + half) % 2
                        for e in range(8):
                            tr(PSb[pb][:, e * 128:(e + 1) * 128], csf[:, :, half * 8 + e], identb, [b_cs, b_cst], [PB[pb]])
                        e0 = es * 16 + half * 8
                        cp("act" if half else "dve", CT[:, e0:e0 + 8, :], PSb[pb][:, :].rearrange("p (e t) -> p e t", e=8), [PB[pb]], [b_CT])
                for r in range(4):
                    tt("dve", osl, s1k[ti][:, :, r * 32:(r + 1) * 32].unsqueeze(2).to_broadcast([128, 8, 16, 32]),
                       top1k[ti].unsqueeze(3).to_broadcast([128, 8, 16, 32]), ALU.is_equal, [bk], [b_osl])
                    osf = osl.rearrange("p h a e -> p (h a) e")
                    ot, b_ot = OTs[r % 2]
                    for q in range(4):
                        pb = 4 + q % 2
                        for e in range(8):
                            tr(PSb[pb][:, e * 128:(e + 1) * 128], osf[:, :, q * 8 + e], identb, [b_osl, b_cst], [PB[pb]])
                        cp("act" if q % 2 else "dve", ot[:, q * 8:(q + 1) * 8, :], PSb[pb][:, :].rearrange("p (e t) -> p e t", e=8), [PB[pb]], [b_ot])
                    wt, b_wt = WTs[r % 2]
                    for tg in range(8):
                        pb = 6 + tg % 2
                        for tk in range(16):
                            t_ = tg * 16 + tk
                            mm(PS[pb][:, tk * 32:(tk + 1) * 32], CT[:, :, t_], ot[:, :, t_], True, True, [b_CT, b_ot], [PB[pb]])
                        cp("dve" if evc % 2 else "act", wt[:, :, tg * 16:(tg + 1) * 16], PS[pb][:, :].rearrange("p (t e) -> p e t", t=16), [PB[pb]], [b_wt])
                        evc += 1
                    P.dma(wT_s[ti][:, r * 32:(r + 1) * 32, :], wt, reads=[b_wt], writes=[b_wT[ti]])
            P.barrier()
            ar.off = base
            ub = [(ar.bf(8 * 512).rearrange("p (k e) -> p k e", k=8), Buf()) for _ in range(2)]
            vbf = [(ar.bf(4 * 1024).rearrange("p (e d) -> p e d", e=4), Buf()) for _ in range(2)]
            Wt = [(ar.bf(4 * 256).rearrange("p (e t) -> p e t", e=4), Buf()) for _ in range(2)]
            gel = [(ar.f32(512), Buf()) for _ in range(2)]
            GT = [(ar.bf(512).rearrange("p (e t) -> p e t", e=2), Buf()) for _ in range(2)]
            for pb in range(4, 8):
                mm(PS[pb][:, :], zer[:, 0:128], zer[:, 0:512], True, False, [b_zer], [PB[pb]])
            for g in range(32):
                u_, b_u = ub[g % 2]
                v_, b_vv = vbf[g % 2]
                w_, b_w_ = Wt[g % 2]
                P.dma(u_, uTb_s[:, :, g * 512:(g + 1) * 512], reads=[b_ub], writes=[b_u])
                P.dma(v_, vb_s[:, g * 4:(g + 1) * 4, :], reads=[b_vb], writes=[b_vv])
                for ti in range(2):
                    P.dma(w_[:, :, ti * 128:(ti + 1) * 128], wT_s[ti][:, g * 4:(g + 1) * 4, :], reads=[b_wT[ti]], writes=[b_w_])
                for sub in range(2):
                    it = g * 2 + sub
                    ph = it % 2
                    for i in range(2):
                        chn = sub * 2 + i
                        for k in range(8):
                            mm(PS[ph][:, i * 256:(i + 1) * 256], u_[:, k, chn * 128:(chn + 1) * 128], nbf[:, k, :], k == 0, k == 7, [b_u, b_nbf[k]], [PB[ph]])
                    ge, b_ge = gel[it % 2]
                    gt, b_gt = GT[it % 2]
                    act(ge, PS[ph][:, :], AF.Gelu, [PB[ph]], [b_ge])
                    tt("dve", gt.rearrange("p e t -> p (e t)"), ge, w_[:, sub * 2:(sub + 1) * 2, :].rearrange("p e t -> p (e t)"), ALU.mult, [b_ge, b_w_], [b_gt])
                    for dk in range(8):
                        pbo = 4 + dk // 2
                        for i in range(2):
                            mm(PS[pbo][:, (dk % 2) * 256:(dk % 2 + 1) * 256], v_[:, sub * 2 + i, dk * 128:(dk + 1) * 128], gt[:, i, :], False, False, [b_vv, b_gt], [PB[pbo]])
            for dk in range(8):
                pbo = 4 + dk // 2
                xb = b_x[dk][t0 // 512]
                stt(xT[:, dk, t0:t0 + 256], PS[pbo][:, (dk % 2) * 256:(dk % 2 + 1) * 256], modv[:, 5, dk, jj:jj + 1], xT[:, dk, t0:t0 + 256], ALU.mult, ALU.add, [PB[pbo], xb, b_sm], [xb])

    for l in range(depth):
        layer(l)

    P.barrier()
    fin = []
    for k in range(8):
        fin.append(P.dma(yT_d[:, k, :], xT[:, k, 0:S], reads=b_x[k]))
    for name, (src, shape) in dbg_out.items():
        pass
    P.emit(list(P.dmas[-8:]))
    es.close()
    return nc

_CONST = {}


def _consts():
    if _CONST:
        return _CONST
    f64 = np.float64
    c = np.zeros((6, 128, 128), f64)
    c[0] = np.eye(128)
    c[1] = 1.0
    c[2, :64, :64] = 1.0
    c[2, 64:, 64:] = 1.0
    for base in range(0, 128, 32):
        for d in range(16):
            c[3, base + d + 16, base + d] = -1.0
            c[3, base + d, base + d + 16] = 1.0
    ci = np.arange(64)
    ang = 2 * np.pi * np.outer(ci, ci) / 64.0
    for b in range(2):
        c[4, b * 64:(b + 1) * 64, b * 64:(b + 1) * 64] = np.cos(ang)
        c[5, b * 64:(b + 1) * 64, b * 64:(b + 1) * 64] = np.sin(ang)
    _CONST["cst"] = np.ascontiguousarray(c.transpose(1, 0, 2)).astype(np.float32)
    t = np.arange(S)
    row = (t // 64).astype(f64)
    col = (t % 64).astype(f64)
    inv = 10000.0 ** (-np.arange(0, 32, 2, dtype=f64) / 32.0)
    d = np.arange(128) % 64
    pos = np.where((d // 32)[:, None] == 0, row[None, :], col[None, :])
    a = pos * inv[d % 16][:, None]
    _CONST["rope"] = np.stack([np.cos(a), np.sin(a)]).astype(np.float32)
    for L in (S, C):
        p = np.arange(L, dtype=f64)
        tt_ = p / max(L - 1, 1)
        w = 2.0 * np.pi * p / L
        fr = np.linspace(1e-4, 15, 16)
        feats = np.concatenate([tt_[:, None], np.cos(w[:, None] * fr), -np.sin(w[:, None] * fr)], axis=-1)
        _CONST["feats%d" % L] = np.ascontiguousarray(feats.T).astype(np.float32)
        deltas = np.abs(np.linspace(math.log(1e-2) / 1.5, math.log(1e-2) / 0.3, 256))
        _CONST["dec%d" % L] = np.exp(-tt_[:, None] * deltas[None, :]).astype(np.float32)
        nj = L // 128
        s_ = np.arange(L)
        kk = np.outer(s_, 2 * s_ + 1) % (4 * L)
        angF = np.pi * kk / (2.0 * L)
        TcF = np.cos(angF)
        TsF = -np.sin(angF)
        tF = np.stack([TcF, TsF]).reshape(2, nj, 128, nj, 128).transpose(0, 3, 2, 1, 4)
        _CONST["tF%d" % L] = np.ascontiguousarray(tF).astype(np.float32)
        TB = min(512, L)
        G = min(4, nj)
        nTB = L // TB
        nG = nj // G
        TcI = TcF.T / L
        TsI = TsF.T / L
        def rl(M):
            return M.reshape(nG, G, 128, nTB, TB).transpose(3, 0, 2, 1, 4)
        _CONST["tI%d" % L] = np.ascontiguousarray(np.stack([rl(TcI), rl(TsI)])).astype(np.float32)
        k2 = np.outer(s_, s_) % L
        ang2 = 2 * np.pi * k2 / L
        sc_ = 1.0 / math.sqrt(64.0 * L)
        _CONST["tN%d" % L] = np.ascontiguousarray(np.stack([rl(np.cos(ang2) * sc_), rl(-np.sin(ang2) * sc_)])).astype(np.float32)
    return _CONST


def _prep(inp):
    f = lambda a: np.ascontiguousarray(np.asarray(a, dtype=np.float32))
    w = {}
    w["w_ada"] = f(inp["w_ada"])
    w["bada"] = f(np.asarray(inp["b_ada"]).reshape(2, 6, 8, 128).transpose(0, 3, 1, 2))
    w["gmf"] = f(np.stack([np.asarray(inp["g_mix"]).reshape(2, 8, 128), np.asarray(inp["g_ffn"]).reshape(2, 8, 128)], axis=1).transpose(0, 3, 1, 2))
    win = np.asarray(inp["w_in"])
    w["w_in"] = f(win)
    w["w_gate"] = f(win[:, :, 2560:].reshape(2, 8, 128, 3, 8, 128).transpose(0, 4, 2, 1, 3, 5))
    hcw = np.concatenate([np.asarray(inp["hy_conv_w"]), np.asarray(inp["hy_conv_b"])[:, None, :]], axis=1)
    w["hcw"] = f(hcw.reshape(2, 4, 6, 128).transpose(0, 3, 2, 1))
    w["hy_w1"] = f(inp["hy_w1"])
    w["hy_fb"] = f(np.stack([inp["hy_b1"], inp["hy_freq"], inp["hy_b2"]], axis=-1))
    w["hy_w2"] = f(inp["hy_w2"])
    w["hy_w3"] = f(inp["hy_w3"])
    w["hy_bias"] = f(np.asarray(inp["hy_bias"]).reshape(2, 2, 2, 128).transpose(0, 3, 1, 2))
    w["gqk"] = f(np.stack([np.asarray(inp["g_q"]).reshape(2, 128), np.asarray(inp["g_k"]).reshape(2, 128)], axis=-1))
    w["lam"] = f(np.asarray(inp["lam"]).reshape(2, 1, 256))
    w["gsub"] = f(np.asarray(inp["g_sub"]).reshape(2, 128, 1))
    w["w_br"] = f(np.concatenate([inp["w_hy"], inp["w_fn"], inp["w_at"]], axis=1))
    w["w_out"] = f(inp["w_out"])
    w["peer_wq"] = f(inp["peer_wq"])
    w["keysT"] = f(np.asarray(inp["peer_keys"]).reshape(2, 16, 128, 128).transpose(0, 3, 1, 2))
    w["uT"] = f(np.asarray(inp["peer_u"]).transpose(0, 2, 1))
    w["peer_v"] = f(inp["peer_v"])
    w.update(_consts())
    return w


def _core_inputs(inp, b):
    X = np.concatenate([np.asarray(inp["x"][b]), np.asarray(inp["ctx"][b])], axis=0)
    xT = np.ascontiguousarray(X.T.reshape(8, 128, T).transpose(1, 0, 2)).astype(np.float32)
    cc = np.stack([np.asarray(inp["c"][b]), np.asarray(inp["c_ctx"])], axis=-1)
    cc = np.ascontiguousarray(cc.reshape(8, 128, 2).transpose(1, 0, 2)).astype(np.float32)
    return {"xT": xT, "cc": cc}


_NC = {}


def kernel(**inp):
    w = _prep(inp)
    if "nc" not in _NC:
        _NC["nc"] = build()
    nc = _NC["nc"]
    in_maps = []
    for b in range(8):
        m = dict(w)
        m.update(_core_inputs(inp, b))
        in_maps.append(m)
    res = run_bass_kernel_spmd(nc, in_maps, core_ids=list(range(8)))
    out = np.empty((8, S, D), np.float32)
    for b in range(8):
        yT = np.asarray(res.results[b]["yT"])
        out[b] = yT.transpose(2, 1, 0).reshape(S, D)
    return out
```

```python
import numpy as np, math
from contextlib import ExitStack
import concourse.bass as bass
import concourse.mybir as mybir
from concourse.bass_utils import run_bass_kernel_spmd

F32 = mybir.dt.float32
BF16 = mybir.dt.bfloat16
ALU = mybir.AluOpType
AF = mybir.ActivationFunctionType
AX = mybir.AxisListType

NSLOT = 40
ENGS = ("pe", "act", "dve", "pool", "sp")


class Buf:
    __slots__ = ("w", "rs", "rd")

    def __init__(self):
        self.w = None
        self.rs = {}
        self.rd = []


class Op:
    __slots__ = ("eng", "fn", "deps", "sig", "cnt", "slot", "dma")

    def __init__(self, eng, fn, dma=False):
        self.eng = eng
        self.fn = fn
        self.deps = ()
        self.sig = False
        self.cnt = 0
        self.slot = -1
        self.dma = dma


class Prog:
    def __init__(self, nc):
        self.nc = nc
        self.streams = {e: [] for e in ENGS}
        self.dmas = []
        self.live_dmas = []
        self.last_real = {e: None for e in ENGS}

    def op(self, eng, fn, reads=(), writes=(), dma=False):
        o = Op(eng, fn, dma)
        deps = set()
        for b in reads:
            if b.w is not None:
                deps.add(b.w)
        for b in writes:
            if b.w is not None:
                deps.add(b.w)
            deps.update(b.rs.values())
            deps.update(b.rd)
        if eng == "pe" and not dma:
            deps = {d for d in deps if d.dma or d.eng != "pe"}
        o.deps = deps
        for b in writes:
            b.w = o
            b.rs = {}
            b.rd = []
        for b in reads:
            if dma:
                b.rd.append(o)
            else:
                b.rs[eng] = o
        self.streams[eng].append(o)
        if not dma:
            self.last_real[eng] = o
        if dma:
            self.dmas.append(o)
            self.live_dmas.append(o)
        return o

    def dma(self, out, in_, reads=(), writes=(), eng="sp"):
        return self.op(eng, lambda e: e.dma_start(out=out, in_=in_), reads, writes, dma=True)

    def barrier(self):
        last = dict(self.last_real)
        live = list(self.live_dmas)
        self.live_dmas = []
        for e in ENGS:
            o = Op(e, None)
            o.deps = {last[x] for x in ENGS if x != e and last[x] is not None}
            o.deps.update(live)
            self.streams[e].append(o)

    def emit(self, final_dmas):
        nc = self.nc
        for e in ENGS:
            for o in self.streams[e]:
                for d in o.deps:
                    d.sig = True
        with ExitStack() as es:
            sems = {e: es.enter_context(nc.semaphore("s_" + e)) for e in ENGS}
            dsem = [es.enter_context(nc.semaphore("d%d" % i)) for i in range(NSLOT)]
            for e in ENGS:
                c = 0
                for o in self.streams[e]:
                    if o.dma:
                        continue
                    if o.sig:
                        c += 1
                        o.cnt = c
            slot_cnt = [0] * NSLOT
            slot_prev = [None] * NSLOT
            for i, o in enumerate(self.dmas):
                s = i % NSLOT
                o.slot = s
                slot_cnt[s] += 16
                o.cnt = slot_cnt[s]
                if slot_prev[s] is not None:
                    o.deps = set(o.deps)
                    o.deps.add(slot_prev[s])
                slot_prev[s] = o
            block = es.enter_context(nc.Block())

            def run(ename, eng):
                waited = {}
                for o in self.streams[ename]:
                    for d in o.deps:
                        sem = dsem[d.slot] if d.dma else sems[d.eng]
                        if waited.get(sem.name, 0) >= d.cnt:
                            continue
                        eng.wait_ge(sem, d.cnt)
                        waited[sem.name] = d.cnt
                    if o.fn is None:
                        continue
                    ins = o.fn(eng)
                    if o.dma:
                        ins.then_inc(dsem[o.slot], 16)
                    elif o.sig:
                        ins.then_inc(sems[ename], 1)
                if ename == "sp":
                    for d in final_dmas:
                        eng.wait_ge(dsem[d.slot], d.cnt)

            @block.sync
            def _(e):
                run("sp", e)

            @block.tensor
            def _(e):
                run("pe", e)

            @block.scalar
            def _(e):
                run("act", e)

            @block.vector
            def _(e):
                run("dve", e)

            @block.gpsimd
            def _(e):
                run("pool", e)

D = 1024
S = 2048
C = 256
T = S + C
EPS = 1e-6
PI = math.pi
NE = 16384


def tblocks(lo, hi, step=512):
    return [(t, min(step, hi - t)) for t in range(lo, hi, step)]


def build(depth=2, stages=None, dbg=()):
    nc = bass.Bass("TRN2", target_bir_lowering=False)
    P = Prog(nc)

    def din(name, shape, dt=F32):
        return nc.dram_tensor(name, list(shape), dt, kind="ExternalInput").ap()

    def dscr(name, shape, dt=F32):
        if name in dbg:
            return nc.dram_tensor(name, list(shape), dt, kind="ExternalOutput").ap()
        return nc.dram_tensor(name, list(shape), dt).ap()

    xT_d = din("xT", [128, 8, T])
    cc_d = din("cc", [128, 8, 2])
    w_ada_d = din("w_ada", [2, D, 6 * D])
    bada_d = din("bada", [2, 128, 6, 8])
    gmf_d = din("gmf", [2, 128, 2, 8])
    w_in_d = din("w_in", [2, D, 5632])
    w_gate_d = din("w_gate", [2, 8, 128, 8, 3, 128])
    hcw_d = din("hcw", [2, 128, 6, 4])
    hy_w1_d = din("hy_w1", [2, 33, 64])
    hy_fb_d = din("hy_fb", [2, 64, 3])
    hy_w2_d = din("hy_w2", [2, 64, 64])
    hy_w3_d = din("hy_w3", [2, 64, 1024])
    hy_bias_d = din("hy_bias", [2, 128, 2, 2])
    gqk_d = din("gqk", [2, 128, 2])
    lam_d = din("lam", [2, 1, 256])
    gsub_d = din("gsub", [2, 128, 1])
    w_br_d = din("w_br", [2, D, D])
    w_out_d = din("w_out", [2, D, D])
    wq_d = din("peer_wq", [2, D, 2048])
    keysT_d = din("keysT", [2, 128, 16, 128])
    uT_d = din("uT", [2, D, NE])
    v_d_in = din("peer_v", [2, NE, D])
    cst_d = din("cst", [128, 6, 128])
    rope_d = din("rope", [2, 128, S])
    feats_d = {L: din("feats%d" % L, [33, L]) for L in (S, C)}
    dec_d = {L: din("dec%d" % L, [L, 256]) for L in (S, C)}
    tF_d = {L: din("tF%d" % L, [2, L // 128, 128, L // 128, 128], BF16) for L in (S, C)}
    RL = {}
    for L in (S, C):
        TB = min(512, L)
        G = min(4, L // 128)
        RL[L] = (TB, G, L // TB, (L // 128) // G)
    tI_d = {L: din("tI%d" % L, [2, RL[L][2], RL[L][3], 128, RL[L][1], RL[L][0]], BF16) for L in (S, C)}
    tN_d = {L: din("tN%d" % L, [2, RL[L][2], RL[L][3], 128, RL[L][1], RL[L][0]], BF16) for L in (S, C)}
    yT_d = nc.dram_tensor("yT", [128, 8, S], F32, kind="ExternalOutput").ap()
    hy_s = dscr("hy_s", [6, 128, T])
    fn_s = dscr("fn_s", [2, 128, T])
    q_s = dscr("q_s", [4, 128, T], BF16)
    k_s = dscr("k_s", [4, 128, T], BF16)
    v_s = dscr("v_s", [128, 18, 512], BF16)
    kf_s = {L: dscr("kf_s%d" % L, [2, L // 128, 128, 512]) for L in (S, C)}
    hyo_s = dscr("hyo_s", [2, 128, T], BF16)
    fno_s = dscr("fno_s", [2, 128, T], BF16)
    ato_s = dscr("ato_s", [4, 128, T], BF16)
    nT_s = dscr("nT_s", [128, 8, T], BF16)
    uTb_s = dscr("uTb_s", [32, 128, 8, 512], BF16)
    vb_s = dscr("vb_s", [128, 128, D], BF16)
    wT_s = dscr("wT_s", [2, 128, 128, 128], BF16)
    dbg_out = {}

    es = ExitStack()
    xT = es.enter_context(nc.sbuf_tensor("xT_sb", [128, 8, T], F32))
    cst = es.enter_context(nc.sbuf_tensor("cst_sb", [128, 6, 128], F32))
    cstb = es.enter_context(nc.sbuf_tensor("cstb_sb", [128, 2, 128], BF16))
    sm = es.enter_context(nc.sbuf_tensor("small_sb", [128, 512], F32))
    ARENA = 33280
    AR = es.enter_context(nc.sbuf_tensor("arena", [128, ARENA], F32))
    PS = [es.enter_context(nc.psum_tensor("ps%d" % i, [128, 512], F32)) for i in range(8)]
    PB = [Buf() for _ in range(8)]
    ident32 = cst[:, 0, :]
    ones32 = cst[:, 1, :]
    bd64 = cst[:, 2, :]
    rot = cst[:, 3, :]
    bdc = cst[:, 4, :]
    bds = cst[:, 5, :]
    identb = cstb[:, 0, :]
    onesb = cstb[:, 1, :]
    b_cst = Buf()
    b_x = [[Buf() for _ in range(5)] for _ in range(8)]
    b_sm = Buf()
    sc = sm[:, 0:16].rearrange("p (k j) -> p k j", k=8)
    modv = sm[:, 16:112].rearrange("p (g m j) -> p g m j", g=6, m=8)
    A_mix = sm[:, 112:128].rearrange("p (k j) -> p k j", k=8)
    A_ffn = sm[:, 128:144].rearrange("p (k j) -> p k j", k=8)
    gmf = sm[:, 144:160].rearrange("p (a k) -> p a k", a=2)
    tmp16 = sm[:, 160:176].rearrange("p (k j) -> p k j", k=8)
    gqk = sm[:, 176:178]
    gsub_s = sm[:, 178:179]
    neg_lam = sm[:, 179:180]
    lamw = sm[:, 180:182]
    hcw = sm[:, 184:208].rearrange("p (c t) -> p c t", c=6)
    hbias = sm[:, 208:212].rearrange("p (o c) -> p o c", o=2)
    fb = sm[:, 212:217]
    bada = sm[:, 224:272].rearrange("p (g m) -> p g m", g=6)
    lamt = sm[:, 272:400]
    epsc = sm[:, 183:184]

    class Arena:
        def __init__(self):
            self.off = 0

        def reset(self):
            P.barrier()
            self.off = 0

        def f32(self, n):
            a = AR[:, self.off:self.off + n]
            self.off += n
            assert self.off <= ARENA, self.off
            return a

        def bf(self, n):
            w = (n + 1) // 2
            return self.f32(w).bitcast(BF16)[:, 0:n]

    ar = Arena()

    def mm(out, lhsT, rhs, start, stop, rd, wr):
        P.op("pe", lambda e: e.matmul(out, lhsT=lhsT, rhs=rhs, start=start, stop=stop), rd, wr)

    def tr(out, in_, idn, rd, wr):
        P.op("pe", lambda e: e.transpose(out=out, in_=in_, identity=idn), rd, wr)

    def act(out, in_, func, rd, wr, bias=0.0, scale=1.0):
        P.op("act", lambda e: e.activation(out=out, in_=in_, func=func, bias=bias, scale=scale), rd, wr)

    def tt(eng, out, in0, in1, op, rd, wr):
        P.op(eng, lambda e: e.tensor_tensor(out=out, in0=in0, in1=in1, op=op), rd, wr)

    def ts(eng, out, in0, s1, s2, op0, op1, rd, wr):
        if s2 is None:
            P.op(eng, lambda e: e.tensor_scalar(out=out, in0=in0, scalar1=s1, scalar2=None, op0=op0), rd, wr)
        else:
            P.op(eng, lambda e: e.tensor_scalar(out=out, in0=in0, scalar1=s1, scalar2=s2, op0=op0, op1=op1), rd, wr)

    def stt(out, in0, scalar, in1, op0, op1, rd, wr):
        P.op("dve", lambda e: e.scalar_tensor_tensor(out=out, in0=in0, scalar=scalar, in1=in1, op0=op0, op1=op1), rd, wr)

    def cp(eng, out, in_, rd, wr):
        if eng == "act":
            act(out, in_, AF.Copy, rd, wr)
        else:
            P.op(eng, lambda e: e.tensor_copy(out=out, in_=in_), rd, wr)

    def recip(out, in_, rd, wr):
        P.op("dve", lambda e: e.reciprocal(out=out, in_=in_), rd, wr)

    def rsqrt_mean(out, in_, n, rd, wr):
        act(out, in_, AF.Sqrt, list(rd) + [b_sm], wr, bias=epsc, scale=1.0 / n)
        recip(out, out, wr, wr)

    P.dma(cst[:, :, :], cst_d, writes=[b_cst])
    for k in range(8):
        P.dma(xT[:, k, :], xT_d[:, k, :], writes=b_x[k])
    P.dma(sc, cc_d, writes=[b_sm])
    cp("dve", cstb[:, 0, :], ident32, [b_cst], [b_cst])
    cp("dve", cstb[:, 1, :], ones32, [b_cst], [b_cst])
    act(sc, sc, AF.Silu, [b_sm], [b_sm])
    P.op("dve", lambda e: e.memset(epsc, EPS), [], [b_sm])
    ar.reset()

    def want(name):
        return stages is None or name in stages

    def layer(l):
        last = l == depth - 1
        lam_init = 0.8 - 0.6 * math.exp(-0.3 * l)
        streams = [(0, S)] if last else [(0, S), (S, C)]
        tb_all = tblocks(0, T)
        tb_mix = tblocks(0, S) if last else tb_all

        ar.reset()
        P.dma(bada, bada_d[l], writes=[b_sm])
        P.dma(gmf, gmf_d[l], writes=[b_sm])
        P.dma(gqk, gqk_d[l], writes=[b_sm])
        P.dma(gsub_s, gsub_d[l], writes=[b_sm])
        P.dma(hcw, hcw_d[l], writes=[b_sm])
        P.dma(hbias, hy_bias_d[l], writes=[b_sm])
        P.dma(fb[0:64, 0:3], hy_fb_d[l], writes=[b_sm])
        P.dma(lamt, lam_d[l][:, 0:128].partition_broadcast(128), writes=[b_sm])
        lam2 = ar.f32(128)
        b_l2 = Buf()
        P.dma(lam2, lam_d[l][:, 128:256].partition_broadcast(128), writes=[b_l2])
        wts = [(ar.f32(8 * 1024), Buf()) for _ in range(2)]
        for g in range(6):
            wt, bw = wts[g % 2]
            wt3 = wt.rearrange("p (k m) -> p k m", k=8)
            P.dma(wt3, w_ada_d[l][:, g * 1024:(g + 1) * 1024].rearrange("(k p) m -> p k m", p=128), writes=[bw])
            for m in range(8):
                for k in range(8):
                    mm(PS[0][:, m * 2:m * 2 + 2], wt3[:, k, m * 128:(m + 1) * 128], sc[:, k, :], k == 0, k == 7, [bw, b_sm], [PB[0]])
            tt("dve", modv[:, g], PS[0][:, 0:16].rearrange("p (m j) -> p m j", m=8),
               bada[:, g, :].unsqueeze(2).to_broadcast([128, 8, 2]), ALU.add, [PB[0], b_sm], [b_sm])
        for (Aap, gi, si) in ((A_mix, 0, 1), (A_ffn, 1, 4)):
            ts("dve", tmp16, modv[:, si], 1.0, None, ALU.add, None, [b_sm], [b_sm])
            tt("dve", Aap, tmp16, gmf[:, gi, :].unsqueeze(2).to_broadcast([128, 8, 2]), ALU.mult, [b_sm], [b_sm])
        tt("dve", lamt[:, 0:64], lamt[:, 0:64], lamt[:, 64:128], ALU.mult, [b_sm], [b_sm])
        tt("dve", lam2[:, 0:64], lam2[:, 0:64], lam2[:, 64:128], ALU.mult, [b_l2], [b_l2])
        P.op("dve", lambda e: e.reduce_sum(out=lamw[:, 0:1], in_=lamt[:, 0:64], axis=AX.X), [b_sm], [b_sm])
        P.op("dve", lambda e: e.reduce_sum(out=lamw[:, 1:2], in_=lam2[:, 0:64], axis=AX.X), [b_l2, b_sm], [b_sm])
        act(lamw, lamw, AF.Exp, [b_sm], [b_sm])
        tt("dve", neg_lam, lamw[:, 1:2], lamw[:, 0:1], ALU.subtract, [b_sm], [b_sm])
        ts("dve", neg_lam, neg_lam, -lam_init, None, ALU.add, None, [b_sm], [b_sm])
        ts("dve", gsub_s, gsub_s, 1.0 - lam_init, None, ALU.mult, None, [b_sm], [b_sm])
        tt("dve", fb[0:64, 3:4], fb[0:64, 0:1], fb[0:64, 1:2], ALU.mult, [b_sm], [b_sm])
        tt("dve", fb[0:64, 4:5], fb[0:64, 2:3], fb[0:64, 1:2], ALU.mult, [b_sm], [b_sm])

        def mod_tmps(w):
            return dict(sq=[(ar.f32(w), Buf()) for _ in range(3)], rs=[(ar.f32(w), Buf()) for _ in range(2)],
                        tm=[(ar.f32(w), Buf()) for _ in range(3)], c=[0, 0, 0])

        def modulate(Aap, gB, blocks, emit_cb, mt):
            for (t0, Tn) in blocks:
                j = 0 if t0 < S else 1
                xb = t0 // 512
                for k in range(8):
                    sq, b_sq = mt["sq"][mt["c"][0] % 3]
                    mt["c"][0] += 1
                    act(sq[:, :Tn], xT[:, k, t0:t0 + Tn], AF.Square, [b_x[k][xb]], [b_sq])
                    mm(PS[7][:, :Tn], ones32, sq[:, :Tn], k == 0, k == 7, [b_sq, b_cst], [PB[7]])
                r, b_r = mt["rs"][mt["c"][1] % 2]
                mt["c"][1] += 1
                rsqrt_mean(r[:, :Tn], PS[7][:, :Tn], D, [PB[7]], [b_r])
                for k in range(8):
                    tm, b_t = mt["tm"][mt["c"][2] % 3]
                    mt["c"][2] += 1
                    tt("dve", tm[:, :Tn], xT[:, k, t0:t0 + Tn], r[:, :Tn], ALU.mult, [b_x[k][xb], b_r], [b_t])
                    emit_cb(k, t0, Tn, tm, b_t, Aap[:, k, j:j + 1], modv[:, gB, k, j:j + 1])

        def hyena_filters(L):
            ar.reset()
            nj = L // 128
            w1 = ar.f32(64)
            w2 = ar.f32(64)
            w3 = ar.f32(1024)
            b_w = Buf()
            P.dma(w1[0:33, :], hy_w1_d[l], writes=[b_w])
            P.dma(w2[0:64, :], hy_w2_d[l], writes=[b_w])
            P.dma(w3[0:64, :], hy_w3_d[l], writes=[b_w])
            h2T = ar.f32(L)
            b_h2 = Buf()
            off_mlp = ar.off
            ft = [(ar.f32(512), Buf()) for _ in range(2)]
            aa = [(ar.f32(512), Buf()) for _ in range(2)]
            h1 = [(ar.f32(512), Buf()) for _ in range(2)]
            aa2 = [(ar.f32(512), Buf()) for _ in range(2)]
            sx = [(ar.f32(512), Buf()) for _ in range(3)]

            def sin_act(out, a, Tn, b_in, b_out):
                (s4, b_s4), (c4, b_c4), (q, b_q) = sx
                act(s4[0:64, :Tn], a, AF.Sin, [b_in], [b_s4], scale=0.25)
                act(c4[0:64, :Tn], a, AF.Abs, [b_in], [b_c4])
                act(c4[0:64, :Tn], c4[0:64, :Tn], AF.Sin, [b_c4, b_sm], [b_c4], bias=halfpi[0:64, :], scale=-0.25)
                tt("dve", q[0:64, :Tn], s4[0:64, :Tn], s4[0:64, :Tn], ALU.mult, [b_s4], [b_q])
                ts("dve", q[0:64, :Tn], q[0:64, :Tn], -2.0, 1.0, ALU.mult, ALU.add, [b_q], [b_q])
                tt("dve", c4[0:64, :Tn], s4[0:64, :Tn], c4[0:64, :Tn], ALU.mult, [b_s4, b_c4], [b_c4])
                stt(out, c4[0:64, :Tn], 4.0, q[0:64, :Tn], ALU.mult, ALU.mult, [b_c4, b_q], [b_out])

            for bi, (t0, Tn) in enumerate(tblocks(0, L)):
                f, b_f = ft[bi % 2]
                P.dma(f[0:33, :Tn], feats_d[L][:, t0:t0 + Tn], writes=[b_f])
                mm(PS[0][0:64, :Tn], w1[0:33, :], f[0:33, :Tn], True, True, [b_w, b_f], [PB[0]])
                a, b_a = aa[bi % 2]
                ts("dve", a[0:64, :Tn], PS[0][0:64, :Tn], fb[0:64, 1:2], fb[0:64, 3:4], ALU.mult, ALU.add, [PB[0], b_sm], [b_a])
                hh, b_h = h1[bi % 2]
                sin_act(hh[0:64, :Tn], a[0:64, :Tn], Tn, b_a, b_h)
                mm(PS[1][0:64, :Tn], w2[0:64, :], hh[0:64, :Tn], True, True, [b_w, b_h], [PB[1]])
                a2_, b_a2 = aa2[bi % 2]
                ts("dve", a2_[0:64, :Tn], PS[1][0:64, :Tn], fb[0:64, 1:2], fb[0:64, 4:5], ALU.mult, ALU.add, [PB[1], b_sm], [b_a2])
                sin_act(h2T[0:64, t0:t0 + Tn], a2_[0:64, :Tn], Tn, b_a2, b_h2)
            dec = ar.f32(nj * 256).rearrange("p (j c) -> p j c", j=nj)
            off_dec_end = ar.off
            b_dec = Buf()
            P.dma(dec, dec_d[L].rearrange("(j p) c -> p j c", p=128), writes=[b_dec])
            hd = ar.f32(nj * 1024).rearrange("p (j c) -> p j c", j=nj)
            b_hd = [Buf() for _ in range(nj)]
            off_hd = ar.off
            for pc in range(nj):
                for half in range(2):
                    pb = half
                    mm(PS[pb][:, :], h2T[0:64, pc * 128:(pc + 1) * 128], w3[0:64, half * 512:(half + 1) * 512], True, True, [b_h2, b_w], [PB[pb]])
                    tt("dve", hd[:, pc, half * 512:(half + 1) * 512].rearrange("p (o c) -> p o c", o=2),
                       PS[pb][:, :].rearrange("p (o c) -> p o c", o=2),
                       dec[:, pc, :].unsqueeze(1).to_broadcast([128, 2, 256]), ALU.mult, [PB[pb], b_dec], [b_hd[pc]])
            P.op("dve", lambda e: e.memset(hd[0:1, 0, 512:1024], 0.0), [], [b_hd[0]])
            ab = [(ar.f32(1024), Buf()) for _ in range(2)]
            for pc in range(nj):
                a, b_a = ab[pc % 2]
                act(a, hd[:, pc, :], AF.Abs, [b_hd[pc]], [b_a])
                for half in range(2):
                    mm(PS[2 + half][:, :], ones32, a[:, half * 512:(half + 1) * 512], pc == 0, pc == nj - 1, [b_a, b_cst], [PB[2 + half]])
            rn = ar.f32(512)
            b_rn = Buf()
            cp("dve", rn, PS[2][:, :], [PB[2]], [b_rn])
            tt("dve", rn, rn, PS[3][:, :], ALU.add, [b_rn, PB[3]], [b_rn])
            recip(rn, rn, [b_rn], [b_rn])
            tmpe = [(ar.f32(512), Buf()) for _ in range(2)]
            P.barrier()
            ar.off = 0
            hdb = ar.bf(nj * 1024).rearrange("p (j c) -> p j c", j=nj)
            b_hdb = [Buf() for _ in range(nj)]
            tabs = [(ar.bf(2 * nj * 128).rearrange("p (c j f) -> p c j f", c=2, j=nj), Buf()) for _ in range(2)]
            assert ar.off <= off_dec_end
            ar.off = off_hd
            sts = [(ar.f32(1024).rearrange("p (c n) -> p c n", c=2), Buf()) for _ in range(1)]
            for pc in range(nj):
                te, b_te = tmpe[pc % 2]
                hf = hd[:, pc, 0:512]
                hb = hd[:, pc, 512:1024]
                tt("dve", te, hf, hb, ALU.add, [b_hd[pc]], [b_te])
                tt("pool", hb, hf, hb, ALU.subtract, [b_hd[pc]], [b_hd[pc]])
                tt("dve", hdb[:, pc, 0:512], te, rn, ALU.mult, [b_te, b_rn], [b_hdb[pc]])
                tt("pool", hdb[:, pc, 512:1024], hb, rn, ALU.mult, [b_hd[pc], b_rn], [b_hdb[pc]])
            for fc in range(nj):
                tb_, b_tb = tabs[fc % 2]
                for c in range(2):
                    P.dma(tb_[:, c], tF_d[L][c, fc], writes=[b_tb])
                for c in range(2):
                    for jc in range(nj):
                        mm(PS[4 + c][:, :], tb_[:, c, jc, :], hdb[:, jc, c * 512:(c + 1) * 512], jc == 0, jc == nj - 1, [b_tb, b_hdb[jc]], [PB[4 + c]])
                st, b_st = sts[0]
                cp("act", st[:, 0, :], PS[4][:, :], [PB[4]], [b_st])
                cp("dve", st[:, 1, :], PS[5][:, :], [PB[5]], [b_st])
                for c in range(2):
                    P.dma(kf_s[L][c, fc], st[:, c, :], reads=[b_st], writes=[b_kf[L]])

        b_kf = {S: Buf(), C: Buf()}
        halfpi = sm[:, 182:183]
        P.op("dve", lambda e: e.memset(halfpi, PI / 2), [], [b_sm])
        if want("hyf"):
            for (t0, L) in streams:
                hyena_filters(L)

        ar.reset()
        nT = ar.bf(8 * T).rearrange("p (k t) -> p k t", k=8)
        b_n = [[Buf() for _ in range(5)] for _ in range(8)]
        b_nTs = Buf()

        def emit_mix(k, t0, Tn, tm, b_t, Asc, Bsc):
            act(nT[:, k, t0:t0 + Tn], tm[:, :Tn], AF.Identity, [b_t, b_sm], [b_n[k][t0 // 512]], bias=Bsc, scale=Asc)

        mark = ar.off
        if want("proj") or want("merge"):
            modulate(A_mix, 0, tb_all, emit_mix, mod_tmps(512))
            for k in range(8):
                P.dma(nT_s[:, k, :], nT[:, k, :], reads=b_n[k], writes=[b_nTs])
        ar.off = mark
        P.barrier()

        b_hy = Buf()
        b_fn = Buf()
        b_q = Buf()
        b_k = Buf()
        b_v = Buf()
        if want("proj"):
            w32 = [(ar.f32(8 * 512).rearrange("p (k m) -> p k m", k=8), Buf()) for _ in range(1)]
            wbf = [(ar.bf(8 * 512).rearrange("p (k m) -> p k m", k=8), Buf()) for _ in range(2)]
            stg = [(ar.f32(512), Buf()) for _ in range(2)]
            sqt = [(ar.f32(512), Buf()) for _ in range(2)]
            rt = [(ar.f32(512), Buf()) for _ in range(2)]
            xnt = [(ar.f32(512), Buf()) for _ in range(2)]
            t1t = [(ar.f32(512), Buf()) for _ in range(2)]
            t2t = [(ar.f32(512), Buf()) for _ in range(2)]
            obt = [(ar.bf(512), Buf()) for _ in range(2)]
            rope_sb = ar.f32(2 * S).rearrange("p (c t) -> p c t", c=2)
            b_rope = Buf()
            for c in range(2):
                P.dma(rope_sb[:, c, :], rope_d[c], writes=[b_rope])
            cnt = [0]

            def qk_cb(ps, pb, which, h, t0, Tn):
                i = cnt[0] % 2
                cnt[0] += 1
                sq_, b_sq_ = sqt[i]
                act(sq_[:, :Tn], ps[:, :Tn], AF.Square, [pb], [b_sq_])
                mm(PS[6][:, :Tn], bd64, sq_[:, :Tn], True, True, [b_sq_, b_cst], [PB[6]])
                r_, b_r_ = rt[i]
                rsqrt_mean(r_[:, :Tn], PS[6][:, :Tn], 64, [PB[6]], [b_r_])
                xn, b_xn = xnt[i]
                stt(xn[:, :Tn], ps[:, :Tn], gqk[:, which:which + 1], r_[:, :Tn], ALU.mult, ALU.mult, [pb, b_r_, b_sm], [b_xn])
                ob, b_ob = obt[i]
                if t0 < S:
                    mm(PS[5][:, :Tn], rot, xn[:, :Tn], True, True, [b_xn, b_cst], [PB[5]])
                    t1, b_t1 = t1t[i]
                    t2, b_t2 = t2t[i]
                    tt("pool", t1[:, :Tn], xn[:, :Tn], rope_sb[:, 0, t0:t0 + Tn], ALU.mult, [b_xn, b_rope], [b_t1])
                    tt("dve", t2[:, :Tn], PS[5][:, :Tn], rope_sb[:, 1, t0:t0 + Tn], ALU.mult, [PB[5], b_rope], [b_t2])
                    tt("pool", ob[:, :Tn], t1[:, :Tn], t2[:, :Tn], ALU.add, [b_t1, b_t2], [b_ob])
                else:
                    cp("pool", ob[:, :Tn], xn[:, :Tn], [b_xn], [b_ob])
                dst = (q_s if which == 0 else k_s)[h][:, t0:t0 + Tn]
                P.dma(dst, ob[:, :Tn], reads=[b_ob], writes=[b_q if which == 0 else b_k])

            pcnt = [0]
            for g in range(5):
                w3_, b_w3 = w32[0]
                P.dma(w3_, w_in_d[l][:, g * 512:(g + 1) * 512].rearrange("(k p) m -> p k m", p=128), writes=[b_w3])
                wb_, b_wb = wbf[g % 2]
                cp("act", wb_[:, 0:4, :], w3_[:, 0:4, :], [b_w3], [b_wb])
                cp("pool", wb_[:, 4:8, :], w3_[:, 4:8, :], [b_w3], [b_wb])
                if g < 4:
                    for mi in range(4):
                        for (t0, Tn) in tb_all:
                            if g == 2 and t0 >= S and last:
                                continue
                            pi = pcnt[0] % 4
                            pcnt[0] += 1
                            for k in range(8):
                                mm(PS[pi][:, :Tn], wb_[:, k, mi * 128:(mi + 1) * 128], nT[:, k, t0:t0 + Tn], k == 0, k == 7,
                                   [b_wb, b_n[k][t0 // 512]], [PB[pi]])
                            if g < 2:
                                st, b_st = stg[pcnt[0] % 2]
                                cp("act" if pcnt[0] % 2 else "dve", st[:, :Tn], PS[pi][:, :Tn], [PB[pi]], [b_st])
                                ch = g * 4 + mi
                                if ch < 6:
                                    P.dma(hy_s[ch][:, t0:t0 + Tn], st[:, :Tn], reads=[b_st], writes=[b_hy])
                                else:
                                    P.dma(fn_s[ch - 6][:, t0:t0 + Tn], st[:, :Tn], reads=[b_st], writes=[b_fn])
                            else:
                                qk_cb(PS[pi], PB[pi], g - 2, mi, t0, Tn)
                else:
                    for i in range(18):
                        pi = pcnt[0] % 4
                        pcnt[0] += 1
                        for k in range(8):
                            mm(PS[pi][:, :], nT[:, k, i * 128:(i + 1) * 128], wb_[:, k, :], k == 0, k == 7, [b_wb, b_n[k][i // 4]], [PB[pi]])
                        ob, b_ob = obt[i % 2]
                        cp("act" if i % 2 else "dve", ob, PS[pi][:, :], [PB[pi]], [b_ob])
                        P.dma(v_s[:, i, :], ob, reads=[b_ob], writes=[b_v])

        b_ato = Buf()
        if want("attn"):
            ar.reset()
            kT = ar.bf(4 * T).rearrange("p (h t) -> p h t", h=4)
            qT = ar.bf(4 * T).rearrange("p (h t) -> p h t", h=4)
            vv = ar.bf(18 * 512).rearrange("p (j c) -> p j c", j=18)
            b_kT = Buf()
            b_qT = Buf()
            b_vv = Buf()
            for h in range(4):
                P.dma(kT[:, h, :], k_s[h], reads=[b_k], writes=[b_kT])
                P.dma(qT[:, h, :], q_s[h], reads=[b_q], writes=[b_qT])
            P.dma(vv, v_s, reads=[b_v], writes=[b_vv])
            Et = [(ar.bf(512), Buf()) for _ in range(3)]
            r0 = ar.f32(512)
            t0_ = ar.f32(512)
            t1_ = ar.f32(512)
            sq_ = ar.f32(512)
            rr_ = ar.f32(512)
            b_r0, b_t0, b_t1, b_sq2, b_rr = [Buf() for _ in range(5)]
            aob = [(ar.bf(512), Buf()) for _ in range(2)]
            ec = 0
            oc = 0
            qblocks = tblocks(0, S) if last else tb_all
            for h in range(4):
                for (q0, Tn) in qblocks:
                    keys = list(range(18)) if q0 < S else [16, 17]
                    for c in range(2):
                        for idx, j in enumerate(keys):
                            sp = ec % 2
                            mm(PS[sp][:, :Tn], kT[64 * c:64 * c + 64, h, j * 128:(j + 1) * 128], qT[64 * c:64 * c + 64, h, q0:q0 + Tn],
                               True, True, [b_kT, b_qT], [PB[sp]])
                            E, b_E = Et[ec % 3]
                            ec += 1
                            act(E[:, :Tn], PS[sp][:, :Tn], AF.Exp, [PB[sp]], [b_E], scale=0.125)
                            mm(PS[2 + 2 * c][:, :Tn], vv[:, j, h * 128:(h + 1) * 128], E[:, :Tn], idx == 0, idx == len(keys) - 1, [b_vv, b_E], [PB[2 + 2 * c]])
                            mm(PS[3 + 2 * c][:, :Tn], onesb, E[:, :Tn], idx == 0, idx == len(keys) - 1, [b_cst, b_E], [PB[3 + 2 * c]])
                    recip(r0[:, :Tn], PS[3][:, :Tn], [PB[3]], [b_r0])
                    tt("dve", t0_[:, :Tn], PS[2][:, :Tn], r0[:, :Tn], ALU.mult, [PB[2], b_r0], [b_t0])
                    recip(r0[:, :Tn], PS[5][:, :Tn], [PB[5]], [b_r0])
                    tt("dve", t1_[:, :Tn], PS[4][:, :Tn], r0[:, :Tn], ALU.mult, [PB[4], b_r0], [b_t1])
                    stt(t0_[:, :Tn], t1_[:, :Tn], neg_lam, t0_[:, :Tn], ALU.mult, ALU.add, [b_t1, b_t0, b_sm], [b_t0])
                    act(sq_[:, :Tn], t0_[:, :Tn], AF.Square, [b_t0], [b_sq2])
                    mm(PS[6][:, :Tn], ones32, sq_[:, :Tn], True, True, [b_sq2, b_cst], [PB[6]])
                    rsqrt_mean(rr_[:, :Tn], PS[6][:, :Tn], 128, [PB[6]], [b_rr])
                    ao, b_ao = aob[oc % 2]
                    oc += 1
                    stt(ao[:, :Tn], t0_[:, :Tn], gsub_s, rr_[:, :Tn], ALU.mult, ALU.mult, [b_t0, b_rr, b_sm], [b_ao])
                    P.dma(ato_s[h][:, q0:q0 + Tn], ao[:, :Tn], reads=[b_ao], writes=[b_ato])

        b_hyo = Buf()

        def hyena_main(t0, L):
            TB, G, nTB, nG = RL[L]
            nj = L // 128
            for cc in range(2):
                ar.reset()
                raw = ar.f32(L)
                b_raw = Buf()
                us = [ar.f32(L) for _ in range(3)]
                b_us = [Buf() for _ in range(3)]
                for jx in range(3):
                    ch = jx * 2 + cc
                    P.dma(raw, hy_s[ch][:, t0:t0 + L], reads=[b_hy], writes=[b_raw])
                    u = us[jx]
                    ts("dve", u, raw, hcw[:, ch, 1:2], hcw[:, ch, 3:4], ALU.mult, ALU.add, [b_raw, b_sm], [b_us[jx]])
                    stt(u[:, 1:L], raw[:, 0:L - 1], hcw[:, ch, 0:1], u[:, 1:L], ALU.mult, ALU.add, [b_raw, b_us[jx], b_sm], [b_us[jx]])
                    stt(u[:, 0:L - 1], raw[:, 1:L], hcw[:, ch, 2:3], u[:, 0:L - 1], ALU.mult, ALU.add, [b_raw, b_us[jx], b_sm], [b_us[jx]])
                zbuf = raw
                b_zb = b_raw
                ztm = ar.bf(L).rearrange("p (j c) -> p j c", j=nj)
                b_ztm = Buf()
                Z = ar.f32(2 * L).rearrange("p (c j n) -> p c j n", c=2, j=nj)
                b_Z = Buf()
                Kf = ar.f32(2 * L).rearrange("p (c j n) -> p c j n", c=2, j=nj)
                b_Kf = Buf()
                X = ar.bf(2 * L).rearrange("p (c j n) -> p c j n", c=2, j=nj)
                b_X = Buf()
                tmpx = ar.f32(L).rearrange("p (j n) -> p j n", j=nj)
                b_tx = Buf()
                tmpy = ar.f32(L).rearrange("p (j n) -> p j n", j=nj)
                b_ty = Buf()
                tabs = [(ar.bf(2048), Buf()) for _ in range(4)]
                tcnt = 0
                z, b_z = us[0], b_us[0]
                for o in range(2):
                    for scn in range(nj):
                        tr(PS[0][:, (scn % 4) * 128:(scn % 4 + 1) * 128], z[:, scn * 128:(scn + 1) * 128], ident32, [b_z, b_cst], [PB[0]])
                        if scn % 4 == 3 or scn == nj - 1:
                            n4 = scn % 4 + 1
                            cp("act", ztm[:, scn - n4 + 1:scn + 1, :], PS[0][:, 0:n4 * 128].rearrange("p (j c) -> p j c", j=n4), [PB[0]], [b_ztm])
                    for c in range(2):
                        P.dma(Kf[:, c], kf_s[L][c][:, :, o * 256 + cc * 128:o * 256 + cc * 128 + 128].rearrange("j p n -> p j n"), reads=[b_kf[L]], writes=[b_Kf])
                    for fc in range(nj):
                        tbc, b_tbc = tabs[tcnt % 4]
                        tbs, b_tbs = tabs[(tcnt + 1) % 4]
                        tcnt += 2
                        tbc3 = tbc[:, 0:nj * 128].rearrange("p (j f) -> p j f", j=nj)
                        tbs3 = tbs[:, 0:nj * 128].rearrange("p (j f) -> p j f", j=nj)
                        P.dma(tbc3, tF_d[L][0, fc], writes=[b_tbc])
                        P.dma(tbs3, tF_d[L][1, fc], writes=[b_tbs])
                        pz = 1 + fc % 2
                        for sc_ in range(nj):
                            mm(PS[pz][:, 0:128], tbc3[:, sc_, :], ztm[:, sc_, :], sc_ == 0, sc_ == nj - 1, [b_tbc, b_ztm], [PB[pz]])
                        for sc_ in range(nj):
                            mm(PS[pz][:, 128:256], tbs3[:, sc_, :], ztm[:, sc_, :], sc_ == 0, sc_ == nj - 1, [b_tbs, b_ztm], [PB[pz]])
                        cp("act", Z[:, :, fc, :], PS[pz][:, 0:256].rearrange("p (c n) -> p c n", c=2), [PB[pz]], [b_Z])
                    Zr, Zi, Kr, Ki = Z[:, 0], Z[:, 1], Kf[:, 0], Kf[:, 1]
                    tt("dve", tmpx, Zr, Kr, ALU.mult, [b_Z, b_Kf], [b_tx])
                    tt("pool", tmpy, Zi, Ki, ALU.mult, [b_Z, b_Kf], [b_ty])
                    tt("dve", X[:, 0], tmpx, tmpy, ALU.subtract, [b_tx, b_ty], [b_X])
                    tt("pool", tmpy, Zr, Ki, ALU.mult, [b_Z, b_Kf, b_X], [b_ty])
                    tt("dve", tmpx, Zi, Kr, ALU.mult, [b_Z, b_Kf, b_X], [b_tx])
                    tt("dve", X[:, 1], tmpx, tmpy, ALU.add, [b_tx, b_ty], [b_X])
                    gate, b_g = us[1 + o], b_us[1 + o]
                    znew, b_zn = (zbuf, b_zb) if o == 0 else (us[0], b_us[0])
                    for tb in range(nTB):
                        py = 3 + tb % 2
                        for g in range(nG):
                            tbc, b_tbc = tabs[tcnt % 4]
                            tbs, b_tbs = tabs[(tcnt + 1) % 4]
                            tcnt += 2
                            tc3 = tbc[:, 0:G * TB].rearrange("p (g t) -> p g t", g=G)
                            ts3 = tbs[:, 0:G * TB].rearrange("p (g t) -> p g t", g=G)
                            P.dma(tc3, tI_d[L][0, tb, g], writes=[b_tbc])
                            P.dma(ts3, tI_d[L][1, tb, g], writes=[b_tbs])
                            for fi in range(G):
                                fc = g * G + fi
                                mm(PS[py][:, :TB], X[:, 0, fc, :], tc3[:, fi, :], fc == 0, False, [b_X, b_tbc], [PB[py]])
                                mm(PS[py][:, :TB], X[:, 1, fc, :], ts3[:, fi, :], False, fc == nj - 1, [b_X, b_tbs], [PB[py]])
                        sl = slice(tb * TB, (tb + 1) * TB)
                        stt(tmpx.rearrange("p j n -> p (j n)")[:, sl], z[:, sl], hbias[:, o, cc:cc + 1], PS[py][:, :TB], ALU.mult, ALU.add, [b_z, PB[py], b_sm, b_X], [b_tx])
                        tt("dve", znew[:, sl], tmpx.rearrange("p j n -> p (j n)")[:, sl], gate[:, sl], ALU.mult, [b_tx, b_g], [b_zn])
                    z, b_z = znew, b_zn
                ob = ar.bf(L)
                b_ob = Buf()
                cp("act", ob, z, [b_z], [b_ob])
                P.dma(hyo_s[cc][:, t0:t0 + L], ob, reads=[b_ob], writes=[b_hyo])

        if want("hyena"):
            for (t0, L) in streams:
                hyena_main(t0, L)

        b_fno = Buf()

        def fnet(t0, L):
            TB, G, nTB, nG = RL[L]
            nj = L // 128
            ar.reset()
            fz = [(ar.f32(L), Buf()) for _ in range(2)]
            zc = ar.bf(nj * 256).rearrange("p (j c) -> p j c", j=nj)
            zs = ar.bf(nj * 256).rearrange("p (j c) -> p j c", j=nj)
            b_zc = Buf()
            for cc in range(2):
                f, b_f = fz[cc]
                P.dma(f, fn_s[cc][:, t0:t0 + L], reads=[b_fn], writes=[b_f])
                for scn in range(nj):
                    pa = scn % 2
                    mm(PS[pa][:, 0:128], f[:, scn * 128:(scn + 1) * 128], bdc, True, True, [b_f, b_cst], [PB[pa]])
                    mm(PS[pa][:, 128:256], f[:, scn * 128:(scn + 1) * 128], bds, True, True, [b_f, b_cst], [PB[pa]])
                    cp("act", zc[:, scn, cc * 128:(cc + 1) * 128], PS[pa][:, 0:128], [PB[pa]], [b_zc])
                    cp("dve", zs[:, scn, cc * 128:(cc + 1) * 128], PS[pa][:, 128:256], [PB[pa]], [b_zc])
            tabs = [(ar.bf(2048), Buf()) for _ in range(4)]
            obs = [(ar.bf(512), Buf()) for _ in range(2)]
            tcnt = 0
            oc = 0
            for tb in range(nTB):
                for g in range(nG):
                    tbc, b_tbc = tabs[tcnt % 4]
                    tbs, b_tbs = tabs[(tcnt + 1) % 4]
                    tcnt += 2
                    tc3 = tbc[:, 0:G * TB].rearrange("p (g t) -> p g t", g=G)
                    ts3 = tbs[:, 0:G * TB].rearrange("p (g t) -> p g t", g=G)
                    P.dma(tc3, tN_d[L][0, tb, g], writes=[b_tbc])
                    P.dma(ts3, tN_d[L][1, tb, g], writes=[b_tbs])
                    for cc in range(2):
                        py = 2 + cc + 2 * (tb % 2)
                        for fi in range(G):
                            sc_ = g * G + fi
                            mm(PS[py][:, :TB], zc[:, sc_, cc * 128:(cc + 1) * 128], tc3[:, fi, :], sc_ == 0, False, [b_zc, b_tbc], [PB[py]])
                            mm(PS[py][:, :TB], zs[:, sc_, cc * 128:(cc + 1) * 128], ts3[:, fi, :], False, sc_ == nj - 1, [b_zc, b_tbs], [PB[py]])
                for cc in range(2):
                    py = 2 + cc + 2 * (tb % 2)
                    ob, b_ob = obs[oc % 2]
                    oc += 1
                    cp("act" if cc else "dve", ob[:, :TB], PS[py][:, :TB], [PB[py]], [b_ob])
                    P.dma(fno_s[cc][:, t0 + tb * TB:t0 + (tb + 1) * TB], ob[:, :TB], reads=[b_ob], writes=[b_fno])

        if want("fnet"):
            for (t0, L) in streams:
                fnet(t0, L)

        if want("merge"):
            ar.reset()
            wbr = ar.bf(8 * D).rearrange("p (k m) -> p k m", k=8)
            wo = ar.bf(8 * D).rearrange("p (k m) -> p k m", k=8)
            b_wbr = Buf()
            b_wo = Buf()
            w32 = ar.f32(8 * 512).rearrange("p (k m) -> p k m", k=8)
            b_w32 = Buf()
            for (src, dst, bd) in ((w_br_d, wbr, b_wbr), (w_out_d, wo, b_wo)):
                for half in range(2):
                    P.dma(w32, src[l][:, half * 512:(half + 1) * 512].rearrange("(k p) m -> p k m", p=128), writes=[b_w32])
                    cp("act", dst[:, 0:4, half * 512:(half + 1) * 512], w32[:, 0:4, :], [b_w32], [bd])
                    cp("dve", dst[:, 4:8, half * 512:(half + 1) * 512], w32[:, 4:8, :], [b_w32], [bd])
            ar.off -= 8 * 512
            P.barrier()
            nb = [(ar.bf(8 * 512).rearrange("p (k t) -> p k t", k=8), Buf()) for _ in range(2)]
            sb_ = [(ar.bf(8 * 512).rearrange("p (k t) -> p k t", k=8), Buf()) for _ in range(2)]
            g32 = [(ar.f32(8 * 384).rearrange("p (k j c) -> p k j c", k=8, j=3), Buf()) for _ in range(2)]
            gbf = [(ar.bf(8 * 384).rearrange("p (k j c) -> p k j c", k=8, j=3), Buf()) for _ in range(2)]
            sig = [(ar.f32(512), Buf()) for _ in range(3)]
            yacc = [(ar.f32(512), Buf()) for _ in range(2)]
            ytmp = [(ar.f32(512), Buf()) for _ in range(2)]
            yT = ar.bf(8 * 512).rearrange("p (k t) -> p k t", k=8)
            b_yT = [Buf() for _ in range(8)]
            KR = ((0, 2), (2, 4), (4, 8))
            gc = 0
            for bi, (t0, Tn) in enumerate(tb_mix):
                nbk, b_nb = nb[bi % 2]
                sbk, b_sb = sb_[bi % 2]
                P.dma(nbk[:, :, :Tn], nT_s[:, :, t0:t0 + Tn], reads=[b_nTs], writes=[b_nb])
                for cc in range(2):
                    P.dma(sbk[:, cc, :Tn], hyo_s[cc][:, t0:t0 + Tn], reads=[b_hyo], writes=[b_sb])
                    P.dma(sbk[:, 2 + cc, :Tn], fno_s[cc][:, t0:t0 + Tn], reads=[b_fno], writes=[b_sb])
                for h in range(4):
                    P.dma(sbk[:, 4 + h, :Tn], ato_s[h][:, t0:t0 + Tn], reads=[b_ato], writes=[b_sb])
                for m in range(8):
                    gw, b_gw = g32[gc % 2]
                    gb, b_gb = gbf[gc % 2]
                    gc += 1
                    P.dma(gw, w_gate_d[l, m], writes=[b_gw])
                    cp("pool", gb, gw, [b_gw], [b_gb])
                    ya, b_ya = yacc[m % 2]
                    yt, b_yt = ytmp[m % 2]
                    for j in range(3):
                        pg = j
                        pbr = 3 + j
                        for k in range(8):
                            mm(PS[pg][:, :Tn], gb[:, k, j, :], nbk[:, k, :Tn], k == 0, k == 7, [b_gb, b_nb], [PB[pg]])
                        k0, k1 = KR[j]
                        for k in range(k0, k1):
                            mm(PS[pbr][:, :Tn], wbr[:, k, m * 128:(m + 1) * 128], sbk[:, k, :Tn], k == k0, k == k1 - 1, [b_wbr, b_sb], [PB[pbr]])
                        sg, b_sg = sig[(m * 3 + j) % 3]
                        act(sg[:, :Tn], PS[pg][:, :Tn], AF.Sigmoid, [PB[pg]], [b_sg])
                        if j == 0:
                            tt("dve", ya[:, :Tn], sg[:, :Tn], PS[pbr][:, :Tn], ALU.mult, [b_sg, PB[pbr]], [b_ya])
                        else:
                            tt("dve", yt[:, :Tn], sg[:, :Tn], PS[pbr][:, :Tn], ALU.mult, [b_sg, PB[pbr]], [b_yt])
                            if j == 1:
                                tt("pool", ya[:, :Tn], ya[:, :Tn], yt[:, :Tn], ALU.add, [b_ya, b_yt], [b_ya])
                            else:
                                tt("pool", yT[:, m, :Tn], ya[:, :Tn], yt[:, :Tn], ALU.add, [b_ya, b_yt], [b_yT[m]])
                jj = 0 if t0 < S else 1
                for mo in range(8):
                    po = 6 + mo % 2
                    for k in range(8):
                        mm(PS[po][:, :Tn], wo[:, k, mo * 128:(mo + 1) * 128], yT[:, k, :Tn], k == 0, k == 7, [b_wo, b_yT[k]], [PB[po]])
                    xb = b_x[mo][t0 // 512]
                    stt(xT[:, mo, t0:t0 + Tn], PS[po][:, :Tn], modv[:, 2, mo, jj:jj + 1], xT[:, mo, t0:t0 + Tn], ALU.mult, ALU.add, [PB[po], xb, b_sm], [xb])

        if want("peer"):
            peer(l, last, modulate, mod_tmps, A_ffn)

    def peer(l, last, modulate, mod_tmps, A_ffn):
        ar.reset()
        b_ub = Buf()
        b_vb = Buf()
        ld = [(ar.f32(4096), Buf()) for _ in range(3)]
        cv = [(ar.bf(4096), Buf()) for _ in range(3)]
        jobs = []
        for k in range(8):
            for eb in range(4):
                jobs.append(("u", k, eb))
        for e1g in range(32):
            jobs.append(("v", e1g, 0))

        def j_load(ci):
            kind, i0_, i1_ = jobs[ci]
            a, b_a = ld[ci % 3]
            if kind == "u":
                P.dma(a, uT_d[l][i0_ * 128:(i0_ + 1) * 128, i1_ * 4096:(i1_ + 1) * 4096], writes=[b_a])
            else:
                P.dma(a.rearrange("p (e d) -> p e d", e=4), v_d_in[l][i0_ * 512:(i0_ + 1) * 512, :].rearrange("(e p) d -> p e d", p=128), writes=[b_a])

        j_load(0)
        j_load(1)
        for ci in range(len(jobs)):
            if ci + 2 < len(jobs):
                j_load(ci + 2)
            kind, i0_, i1_ = jobs[ci]
            a, b_a = ld[ci % 3]
            o, b_o = cv[ci % 3]
            cp(("act", "dve", "pool")[ci % 3], o, a, [b_a], [b_o])
            if kind == "u":
                P.dma(uTb_s[i1_ * 8:(i1_ + 1) * 8, :, i0_, :].rearrange("g p e -> p g e"), o.rearrange("p (g e) -> p g e", g=8), reads=[b_o], writes=[b_ub])
            else:
                P.dma(vb_s[:, i0_ * 4:(i0_ + 1) * 4, :], o.rearrange("p (e d) -> p e d", e=4), reads=[b_o], writes=[b_vb])
        ar.reset()
        keysT = ar.f32(2048).rearrange("p (h n) -> p h n", h=16)
        b_ky = Buf()
        P.dma(keysT, keysT_d[l], writes=[b_ky])
        nbf = ar.bf(8 * 256).rearrange("p (k t) -> p k t", k=8)
        b_nbf = [Buf() for _ in range(8)]
        s1k = [ar.f32(1024).rearrange("p (h n) -> p h n", h=8) for _ in range(2)]
        a2k = [ar.f32(1024).rearrange("p (h n) -> p h n", h=8) for _ in range(2)]
        a1t = [ar.f32(128).rearrange("p (h a) -> p h a", h=8) for _ in range(2)]
        top1k = [ar.f32(128).rearrange("p (h a) -> p h a", h=8) for _ in range(2)]
        theta = [ar.f32(8) for _ in range(2)]
        b_keep = [Buf() for _ in range(2)]
        zer = ar.bf(512)
        b_zer = Buf()
        P.op("pool", lambda e: e.memset(zer, 0.0), [], [b_zer])
        base = ar.off
        blocks = tblocks(0, S if last else T, 256)
        b_wT = [Buf(), Buf()]
        for (t0, Tn) in blocks:
            jj = 0 if t0 < S else 1
            P.barrier()
            ar.off = base
            mt = mod_tmps(256)
            n32 = ar.f32(8 * 256).rearrange("p (k t) -> p k t", k=8)
            b_n32 = [Buf() for _ in range(8)]
            wq = [(ar.f32(8 * 128).rearrange("p (k m) -> p k m", k=8), Buf()) for _ in range(2)]
            qT = ar.f32(16 * 256).rearrange("p (h t) -> p h t", h=16)
            b_qT = Buf()
            s_sb = ar.f32(2048).rearrange("p (h n) -> p h n", h=16)
            b_s = Buf()
            tmpm = ar.f32(256)
            b_tm = Buf()
            tmpm2 = ar.f32(256)
            b_tm2 = Buf()
            top = ar.f32(256).rearrange("p (h a) -> p h a", h=16)
            b_top = Buf()
            cand = ar.f32(2048).rearrange("p (h a b) -> p h a b", h=8, a=16)
            b_cand = Buf()
            ctop = ar.f32(192).rearrange("p (h a) -> p h a", h=8)
            b_ct = Buf()
            misc = ar.f32(16)
            b_mi = Buf()
            ex16 = ar.f32(128).rearrange("p (h a) -> p h a", h=8)

            def emit_ffn(k, t0_, Tn_, tm, b_t, Asc, Bsc):
                act(n32[:, k, :], tm[:, :Tn_], AF.Identity, [b_t, b_sm], [b_n32[k]], bias=Bsc, scale=Asc)
                cp("pool", nbf[:, k, :], n32[:, k, :], [b_n32[k]], [b_nbf[k]])

            modulate(A_ffn, 3, [(t0, Tn)], emit_ffn, mt)
            for hp in range(16):
                w, b_w = wq[hp % 2]
                P.dma(w, wq_d[l][:, hp * 128:(hp + 1) * 128].rearrange("(k p) m -> p k m", p=128), writes=[b_w])
                pq = hp % 2
                for k in range(8):
                    mm(PS[pq][:, :Tn], w[:, k, :], n32[:, k, :], k == 0, k == 7, [b_w, b_n32[k]], [PB[pq]])
                cp("act" if hp % 2 else "dve", qT[:, hp, :], PS[pq][:, :Tn], [PB[pq]], [b_qT])
            for ti in range(2):
                tsl = slice(ti * 128, (ti + 1) * 128)
                bk = b_keep[ti]
                for hp in range(16):
                    pbk = hp // 4
                    mm(PS[pbk][:, (hp % 4) * 128:(hp % 4 + 1) * 128], qT[:, hp, tsl], keysT[:, hp, :], True, True, [b_qT, b_ky], [PB[pbk]])
                for q4 in range(4):
                    cp("act" if q4 % 2 else "dve", s_sb[:, q4 * 4:(q4 + 1) * 4, :], PS[q4][:, :].rearrange("p (h n) -> p h n", h=4), [PB[q4]], [b_s])
                for hp in range(16):
                    P.op("dve", (lambda hp: lambda e: e.max(out=top[:, hp, 0:8], in_=s_sb[:, hp, :]))(hp), [b_s], [b_top])
                    P.op("dve", (lambda hp: lambda e: e.match_replace(out=tmpm[:, 0:128], in_to_replace=top[:, hp, 0:8], in_values=s_sb[:, hp, :], imm_value=-1e30))(hp), [b_s, b_top], [b_tm])
                    P.op("dve", (lambda hp: lambda e: e.max(out=top[:, hp, 8:16], in_=tmpm[:, 0:128]))(hp), [b_tm], [b_top])
                top4 = top.rearrange("p (h c) a -> p h c a", c=2)
                s4v = s_sb.rearrange("p (h c) n -> p h c n", c=2)
                tt("dve", cand, top4[:, :, 0, :].unsqueeze(3).to_broadcast([128, 8, 16, 16]),
                   top4[:, :, 1, :].unsqueeze(2).to_broadcast([128, 8, 16, 16]), ALU.add, [b_top], [b_cand])
                for h in range(8):
                    ch = cand[:, h].rearrange("p a b -> p (a b)")
                    P.op("dve", (lambda h, ch: lambda e: e.max(out=ctop[:, h, 0:8], in_=ch))(h, ch), [b_cand], [b_ct])
                    P.op("dve", (lambda h, ch: lambda e: e.match_replace(out=tmpm, in_to_replace=ctop[:, h, 0:8], in_values=ch, imm_value=-1e30))(h, ch), [b_cand, b_ct], [b_tm])
                    P.op("dve", (lambda h: lambda e: e.max(out=ctop[:, h, 8:16], in_=tmpm))(h), [b_tm], [b_ct])
                    P.op("dve", (lambda h: lambda e: e.match_replace(out=tmpm2, in_to_replace=ctop[:, h, 8:16], in_values=tmpm, imm_value=-1e30))(h), [b_tm, b_ct], [b_tm2])
                    P.op("dve", (lambda h: lambda e: e.max(out=ctop[:, h, 16:24], in_=tmpm2))(h), [b_tm2], [b_ct])
                tt("dve", ex16, ctop[:, :, 0:16], ctop[:, :, 0:1].to_broadcast([128, 8, 16]), ALU.subtract, [b_ct], [b_mi])
                act(ex16, ex16, AF.Exp, [b_mi], [b_mi])
                P.op("dve", lambda e: e.reduce_sum(out=misc[:, 8:16], in_=ex16, axis=AX.X), [b_mi], [b_mi])
                recip(misc[:, 8:16], misc[:, 8:16], [b_mi], [b_mi])
                m8 = misc[:, 0:8].unsqueeze(2)
                tt("dve", m8, ctop[:, :, 15:16], ctop[:, :, 16:17], ALU.add, [b_ct, b_mi], [b_mi])
                stt(m8, m8, 0.5, ctop[:, :, 0:1], ALU.mult, ALU.subtract, [b_mi, b_ct], [b_mi])
                act(misc[:, 0:8], misc[:, 0:8], AF.Exp, [b_mi], [b_mi])
                tt("dve", theta[ti], misc[:, 0:8], misc[:, 8:16], ALU.mult, [b_mi], [bk])
                cp("pool", s1k[ti], s4v[:, :, 0, :], [b_s], [bk])
                cp("pool", top1k[ti], top4[:, :, 0, :], [b_top], [bk])
                tt("dve", a2k[ti], s4v[:, :, 1, :], top4[:, :, 1, 0:1].to_broadcast([128, 8, 128]), ALU.subtract, [b_s, b_top], [bk])
                act(a2k[ti], a2k[ti], AF.Exp, [bk], [bk])
                tt("dve", a1t[ti], top4[:, :, 0, :], top4[:, :, 0, 0:1].to_broadcast([128, 8, 16]), ALU.subtract, [b_top], [bk])
                act(a1t[ti], a1t[ti], AF.Exp, [bk], [bk])
                tt("dve", a1t[ti], a1t[ti], misc[:, 8:16].unsqueeze(2).to_broadcast([128, 8, 16]), ALU.mult, [bk, b_mi], [bk])
            P.barrier()
            ar.off = base
            pmt = ar.f32(2048).rearrange("p (h a e) -> p h a e", h=8, a=16)
            b_pm = Buf()
            csl = [(ar.bf(2048).rearrange("p (h a e) -> p h a e", h=8, a=16), Buf()) for _ in range(2)]
            CT = ar.bf(128 * 128).rearrange("p (e t) -> p e t", e=128)
            b_CT = Buf()
            osl = ar.bf(4096).rearrange("p (h a e) -> p h a e", h=8, a=16)
            b_osl = Buf()
            OTs = [(ar.bf(4096).rearrange("p (e t) -> p e t", e=32), Buf()) for _ in range(2)]
            WTs = [(ar.bf(4096).rearrange("p (e t) -> p e t", e=32), Buf()) for _ in range(2)]
            PSb = [PS[i][:, :].bitcast(BF16) for i in range(8)]
            evc = 0
            for ti in range(2):
                bk = b_keep[ti]
                for es in range(8):
                    tt("pool", pmt, a1t[ti].unsqueeze(3).to_broadcast([128, 8, 16, 16]),
                       a2k[ti][:, :, es * 16:(es + 1) * 16].unsqueeze(2).to_broadcast([128, 8, 16, 16]), ALU.mult, [bk], [b_pm])
                    cs, b_cs = csl[es % 2]
                    for h in range(8):
                        pmh = pmt[:, h].rearrange("p a e -> p (a e)")
                        stt(cs[:, h].rearrange("p a e -> p (a e)"), pmh, theta[ti][:, h:h + 1], pmh, ALU.is_ge, ALU.mult, [b_pm, bk], [b_cs])
                    csf = cs.rearrange("p h a e -> p (h a) e")
                    for half in range(2):
                        pb = 2 + (es * 2 + half) % 2
                        for e in range(8):
                            tr(PSb[pb][:, e * 128:(e + 1) * 128], csf[:, :, half * 8 + e], identb, [b_cs, b_cst], [PB[pb]])
                        e0 = es * 16 + half * 8
                        cp("act" if half else "dve", CT[:, e0:e0 + 8, :], PSb[pb][:, :].rearrange("p (e t) -> p e t", e=8), [PB[pb]], [b_CT])
                for r in range(4):
                    tt("dve", osl, s1k[ti][:, :, r * 32:(r + 1) * 32].unsqueeze(2).to_broadcast([128, 8, 16, 32]),
                       top1k[ti].unsqueeze(3).to_broadcast([128, 8, 16, 32]), ALU.is_equal, [bk], [b_osl])
                    osf = osl.rearrange("p h a e -> p (h a) e")
                    ot, b_ot = OTs[r % 2]
                    for q in range(4):
                        pb = 4 + q % 2
                        for e in range(8):
                            tr(PSb[pb][:, e * 128:(e + 1) * 128], osf[:, :, q * 8 + e], identb, [b_osl, b_cst], [PB[pb]])
                        cp("act" if q % 2 else "dve", ot[:, q * 8:(q + 1) * 8, :], PSb[pb][:, :].rearrange("p (e t) -> p e t", e=8), [PB[pb]], [b_ot])
                    wt, b_wt = WTs[r % 2]
                    for tg in range(8):
                        pb = 6 + tg % 2
                        for tk in range(16):
                            t_ = tg * 16 + tk
                            mm(PS[pb][:, tk * 32:(tk + 1) * 32], CT[:, :, t_], ot[:, :, t_], True, True, [b_CT, b_ot], [PB[pb]])
                        cp("dve" if evc % 2 else "act", wt[:, :, tg * 16:(tg + 1) * 16], PS[pb][:, :].rearrange("p (t e) -> p e t", t=16), [PB[pb]], [b_wt])
                        evc += 1
                    P.dma(wT_s[ti][:, r * 32:(r + 1) * 32, :], wt, reads=[b_wt], writes=[b_wT[ti]])
            P.barrier()
            ar.off = base
            ub = [(ar.bf(8 * 512).rearrange("p (k e) -> p k e", k=8), Buf()) for _ in range(2)]
            vbf = [(ar.bf(4 * 1024).rearrange("p (e d) -> p e d", e=4), Buf()) for _ in range(2)]
            Wt = [(ar.bf(4 * 256).rearrange("p (e t) -> p e t", e=4), Buf()) for _ in range(2)]
            gel = [(ar.f32(512), Buf()) for _ in range(2)]
            GT = [(ar.bf(512).rearrange("p (e t) -> p e t", e=2), Buf()) for _ in range(2)]
            for pb in range(4, 8):
                mm(PS[pb][:, :], zer[:, 0:128], zer[:, 0:512], True, False, [b_zer], [PB[pb]])
            for g in range(32):
                u_, b_u = ub[g % 2]
                v_, b_vv = vbf[g % 2]
                w_, b_w_ = Wt[g % 2]
                P.dma(u_, uTb_s[g], reads=[b_ub], writes=[b_u])
                P.dma(v_, vb_s[:, g * 4:(g + 1) * 4, :], reads=[b_vb], writes=[b_vv])
                for ti in range(2):
                    P.dma(w_[:, :, ti * 128:(ti + 1) * 128], wT_s[ti][:, g * 4:(g + 1) * 4, :], reads=[b_wT[ti]], writes=[b_w_])
                for sub in range(2):
                    it = g * 2 + sub
                    ph = it % 2
                    for i in range(2):
                        chn = sub * 2 + i
                        for k in range(8):
                            mm(PS[ph][:, i * 256:(i + 1) * 256], u_[:, k, chn * 128:(chn + 1) * 128], nbf[:, k, :], k == 0, k == 7, [b_u, b_nbf[k]], [PB[ph]])
                    ge, b_ge = gel[it % 2]
                    gt, b_gt = GT[it % 2]
                    act(ge, PS[ph][:, :], AF.Gelu, [PB[ph]], [b_ge])
                    tt("dve", gt.rearrange("p e t -> p (e t)"), ge, w_[:, sub * 2:(sub + 1) * 2, :].rearrange("p e t -> p (e t)"), ALU.mult, [b_ge, b_w_], [b_gt])
                    for dk in range(8):
                        pbo = 4 + dk // 2
                        for i in range(2):
                            mm(PS[pbo][:, (dk % 2) * 256:(dk % 2 + 1) * 256], v_[:, sub * 2 + i, dk * 128:(dk + 1) * 128], gt[:, i, :], False, False, [b_vv, b_gt], [PB[pbo]])
            for dk in range(8):
                pbo = 4 + dk // 2
                xb = b_x[dk][t0 // 512]
                stt(xT[:, dk, t0:t0 + 256], PS[pbo][:, (dk % 2) * 256:(dk % 2 + 1) * 256], modv[:, 5, dk, jj:jj + 1], xT[:, dk, t0:t0 + 256], ALU.mult, ALU.add, [PB[pbo], xb, b_sm], [xb])

    for l in range(depth):
        layer(l)

    P.barrier()
    fin = []
    for k in range(8):
        fin.append(P.dma(yT_d[:, k, :], xT[:, k, 0:S], reads=b_x[k]))
    for name, (src, shape) in dbg_out.items():
        pass
    P.emit(list(P.dmas[-8:]))
    es.close()
    return nc

import ml_dtypes
_CONST = {}


def _consts():
    if _CONST:
        return _CONST
    f64 = np.float64
    c = np.zeros((6, 128, 128), f64)
    c[0] = np.eye(128)
    c[1] = 1.0
    c[2, :64, :64] = 1.0
    c[2, 64:, 64:] = 1.0
    for base in range(0, 128, 32):
        for d in range(16):
            c[3, base + d + 16, base + d] = -1.0
            c[3, base + d, base + d + 16] = 1.0
    ci = np.arange(64)
    ang = 2 * np.pi * np.outer(ci, ci) / 64.0
    for b in range(2):
        c[4, b * 64:(b + 1) * 64, b * 64:(b + 1) * 64] = np.cos(ang)
        c[5, b * 64:(b + 1) * 64, b * 64:(b + 1) * 64] = np.sin(ang)
    _CONST["cst"] = np.ascontiguousarray(c.transpose(1, 0, 2)).astype(np.float32)
    t = np.arange(S)
    row = (t // 64).astype(f64)
    col = (t % 64).astype(f64)
    inv = 10000.0 ** (-np.arange(0, 32, 2, dtype=f64) / 32.0)
    d = np.arange(128) % 64
    pos = np.where((d // 32)[:, None] == 0, row[None, :], col[None, :])
    a = pos * inv[d % 16][:, None]
    _CONST["rope"] = np.stack([np.cos(a), np.sin(a)]).astype(np.float32)
    for L in (S, C):
        p = np.arange(L, dtype=f64)
        tt_ = p / max(L - 1, 1)
        w = 2.0 * np.pi * p / L
        fr = np.linspace(1e-4, 15, 16)
        feats = np.concatenate([tt_[:, None], np.cos(w[:, None] * fr), -np.sin(w[:, None] * fr)], axis=-1)
        _CONST["feats%d" % L] = np.ascontiguousarray(feats.T).astype(np.float32)
        deltas = np.abs(np.linspace(math.log(1e-2) / 1.5, math.log(1e-2) / 0.3, 256))
        _CONST["dec%d" % L] = np.exp(-tt_[:, None] * deltas[None, :]).astype(np.float32)
        nj = L // 128
        s_ = np.arange(L)
        kk = np.outer(s_, 2 * s_ + 1) % (4 * L)
        angF = np.pi * kk / (2.0 * L)
        TcF = np.cos(angF)
        TsF = -np.sin(angF)
        tF = np.stack([TcF, TsF]).reshape(2, nj, 128, nj, 128).transpose(0, 3, 2, 1, 4)
        _CONST["tF%d" % L] = np.ascontiguousarray(tF).astype(np.float32).astype(ml_dtypes.bfloat16)
        TB = min(512, L)
        G = min(4, nj)
        nTB = L // TB
        nG = nj // G
        TcI = TcF.T / L
        TsI = TsF.T / L
        def rl(M):
            return M.reshape(nG, G, 128, nTB, TB).transpose(3, 0, 2, 1, 4)
        _CONST["tI%d" % L] = np.ascontiguousarray(np.stack([rl(TcI), rl(TsI)])).astype(np.float32).astype(ml_dtypes.bfloat16)
        k2 = np.outer(s_, s_) % L
        ang2 = 2 * np.pi * k2 / L
        sc_ = 1.0 / math.sqrt(64.0 * L)
        _CONST["tN%d" % L] = np.ascontiguousarray(np.stack([rl(np.cos(ang2) * sc_), rl(-np.sin(ang2) * sc_)])).astype(np.float32).astype(ml_dtypes.bfloat16)
    return _CONST


def _prep(inp):
    f = lambda a: np.ascontiguousarray(np.asarray(a, dtype=np.float32))
    w = {}
    w["w_ada"] = f(inp["w_ada"])
    w["bada"] = f(np.asarray(inp["b_ada"]).reshape(2, 6, 8, 128).transpose(0, 3, 1, 2))
    w["gmf"] = f(np.stack([np.asarray(inp["g_mix"]).reshape(2, 8, 128), np.asarray(inp["g_ffn"]).reshape(2, 8, 128)], axis=1).transpose(0, 3, 1, 2))
    win = np.asarray(inp["w_in"])
    w["w_in"] = f(win)
    w["w_gate"] = f(win[:, :, 2560:].reshape(2, 8, 128, 3, 8, 128).transpose(0, 4, 2, 1, 3, 5))
    hcw = np.concatenate([np.asarray(inp["hy_conv_w"]), np.asarray(inp["hy_conv_b"])[:, None, :]], axis=1)
    w["hcw"] = f(hcw.reshape(2, 4, 6, 128).transpose(0, 3, 2, 1))
    w["hy_w1"] = f(inp["hy_w1"])
    w["hy_fb"] = f(np.stack([inp["hy_b1"], inp["hy_freq"], inp["hy_b2"]], axis=-1))
    w["hy_w2"] = f(inp["hy_w2"])
    w["hy_w3"] = f(inp["hy_w3"])
    w["hy_bias"] = f(np.asarray(inp["hy_bias"]).reshape(2, 2, 2, 128).transpose(0, 3, 1, 2))
    w["gqk"] = f(np.stack([np.asarray(inp["g_q"]).reshape(2, 128), np.asarray(inp["g_k"]).reshape(2, 128)], axis=-1))
    w["lam"] = f(np.asarray(inp["lam"]).reshape(2, 1, 256))
    w["gsub"] = f(np.asarray(inp["g_sub"]).reshape(2, 128, 1))
    w["w_br"] = f(np.concatenate([inp["w_hy"], inp["w_fn"], inp["w_at"]], axis=1))
    w["w_out"] = f(inp["w_out"])
    w["peer_wq"] = f(inp["peer_wq"])
    w["keysT"] = f(np.asarray(inp["peer_keys"]).reshape(2, 16, 128, 128).transpose(0, 3, 1, 2))
    w["uT"] = f(np.asarray(inp["peer_u"]).transpose(0, 2, 1))
    w["peer_v"] = f(inp["peer_v"])
    w.update(_consts())
    return w


def _core_inputs(inp, b):
    X = np.concatenate([np.asarray(inp["x"][b]), np.asarray(inp["ctx"][b])], axis=0)
    xT = np.ascontiguousarray(X.T.reshape(8, 128, T).transpose(1, 0, 2)).astype(np.float32)
    cc = np.stack([np.asarray(inp["c"][b]), np.asarray(inp["c_ctx"])], axis=-1)
    cc = np.ascontiguousarray(cc.reshape(8, 128, 2).transpose(1, 0, 2)).astype(np.float32)
    return {"xT": xT, "cc": cc}


_NC = {}


def kernel(**inp):
    w = _prep(inp)
    if "nc" not in _NC:
        _NC["nc"] = build()
    nc = _NC["nc"]
    in_maps = []
    for b in range(8):
        m = dict(w)
        m.update(_core_inputs(inp, b))
        in_maps.append(m)
    res = run_bass_kernel_spmd(nc, in_maps, core_ids=list(range(8)))
    out = np.empty((8, S, D), np.float32)
    for b in range(8):
        yT = np.asarray(res.results[b]["yT"])
        out[b] = yT.transpose(2, 1, 0).reshape(S, D)
    return out
```

```python
import numpy as np, math
from contextlib import ExitStack
import concourse.bass as bass
import concourse.mybir as mybir
from concourse.bass_utils import run_bass_kernel_spmd

F32 = mybir.dt.float32
BF16 = mybir.dt.bfloat16
ALU = mybir.AluOpType
AF = mybir.ActivationFunctionType
AX = mybir.AxisListType

NSLOT = 40
ENGS = ("pe", "act", "dve", "pool", "sp")


class Buf:
    __slots__ = ("w", "rs", "rd")

    def __init__(self):
        self.w = None
        self.rs = {}
        self.rd = []


class Op:
    __slots__ = ("eng", "fn", "deps", "sig", "cnt", "slot", "dma")

    def __init__(self, eng, fn, dma=False):
        self.eng = eng
        self.fn = fn
        self.deps = ()
        self.sig = False
        self.cnt = 0
        self.slot = -1
        self.dma = dma


class Prog:
    def __init__(self, nc):
        self.nc = nc
        self.streams = {e: [] for e in ENGS}
        self.dmas = []
        self.live_dmas = []
        self.last_real = {e: None for e in ENGS}

    def op(self, eng, fn, reads=(), writes=(), dma=False):
        o = Op(eng, fn, dma)
        deps = set()
        for b in reads:
            if b.w is not None:
                deps.add(b.w)
        for b in writes:
            if b.w is not None:
                deps.add(b.w)
            deps.update(b.rs.values())
            deps.update(b.rd)
        if eng == "pe" and not dma:
            deps = {d for d in deps if d.dma or d.eng != "pe"}
        o.deps = deps
        for b in writes:
            b.w = o
            b.rs = {}
            b.rd = []
        for b in reads:
            if dma:
                b.rd.append(o)
            else:
                b.rs[eng] = o
        self.streams[eng].append(o)
        if not dma:
            self.last_real[eng] = o
        if dma:
            self.dmas.append(o)
            self.live_dmas.append(o)
        return o

    def dma(self, out, in_, reads=(), writes=(), eng="sp"):
        return self.op(eng, lambda e: e.dma_start(out=out, in_=in_), reads, writes, dma=True)

    def barrier(self):
        last = dict(self.last_real)
        live = list(self.live_dmas)
        self.live_dmas = []
        for e in ENGS:
            o = Op(e, None)
            o.deps = {last[x] for x in ENGS if x != e and last[x] is not None}
            o.deps.update(live)
            self.streams[e].append(o)

    def emit(self, final_dmas):
        nc = self.nc
        for e in ENGS:
            for o in self.streams[e]:
                for d in o.deps:
                    d.sig = True
        with ExitStack() as es:
            sems = {e: es.enter_context(nc.semaphore("s_" + e)) for e in ENGS}
            dsem = [es.enter_context(nc.semaphore("d%d" % i)) for i in range(NSLOT)]
            for e in ENGS:
                c = 0
                for o in self.streams[e]:
                    if o.dma:
                        continue
                    if o.sig:
                        c += 1
                        o.cnt = c
            slot_cnt = [0] * NSLOT
            slot_prev = [None] * NSLOT
            for i, o in enumerate(self.dmas):
                s = i % NSLOT
                o.slot = s
                slot_cnt[s] += 16
                o.cnt = slot_cnt[s]
                if slot_prev[s] is not None:
                    o.deps = set(o.deps)
                    o.deps.add(slot_prev[s])
                slot_prev[s] = o
            block = es.enter_context(nc.Block())

            def run(ename, eng):
                waited = {}
                for o in self.streams[ename]:
                    for d in o.deps:
                        sem = dsem[d.slot] if d.dma else sems[d.eng]
                        if waited.get(sem.name, 0) >= d.cnt:
                            continue
                        eng.wait_ge(sem, d.cnt)
                        waited[sem.name] = d.cnt
                    if o.fn is None:
                        continue
                    ins = o.fn(eng)
                    if o.dma:
                        ins.then_inc(dsem[o.slot], 16)
                    elif o.sig:
                        ins.then_inc(sems[ename], 1)
                if ename == "sp":
                    for d in final_dmas:
                        eng.wait_ge(dsem[d.slot], d.cnt)

            @block.sync
            def _(e):
                run("sp", e)

            @block.tensor
            def _(e):
                run("pe", e)

            @block.scalar
            def _(e):
                run("act", e)

            @block.vector
            def _(e):
                run("dve", e)

            @block.gpsimd
            def _(e):
                run("pool", e)

D = 1024
S = 2048
C = 256
T = S + C
EPS = 1e-6
PI = math.pi
NE = 16384


def tblocks(lo, hi, step=512):
    return [(t, min(step, hi - t)) for t in range(lo, hi, step)]


def build(depth=2, stages=None, dbg=()):
    nc = bass.Bass("TRN2", target_bir_lowering=False)
    P = Prog(nc)

    def din(name, shape, dt=F32):
        return nc.dram_tensor(name, list(shape), dt, kind="ExternalInput").ap()

    def dscr(name, shape, dt=F32):
        if name in dbg:
            return nc.dram_tensor(name, list(shape), dt, kind="ExternalOutput").ap()
        return nc.dram_tensor(name, list(shape), dt).ap()

    xT_d = din("xT", [128, 8, T])
    cc_d = din("cc", [128, 8, 2])
    w_ada_d = din("w_ada", [2, D, 6 * D])
    bada_d = din("bada", [2, 128, 6, 8])
    gmf_d = din("gmf", [2, 128, 2, 8])
    w_in_d = din("w_in", [2, D, 5632])
    w_gate_d = din("w_gate", [2, 8, 128, 8, 3, 128])
    hcw_d = din("hcw", [2, 128, 6, 4])
    hy_w1_d = din("hy_w1", [2, 33, 64])
    hy_fb_d = din("hy_fb", [2, 64, 3])
    hy_w2_d = din("hy_w2", [2, 64, 64])
    hy_w3_d = din("hy_w3", [2, 64, 1024])
    hy_bias_d = din("hy_bias", [2, 128, 2, 2])
    gqk_d = din("gqk", [2, 128, 2])
    lam_d = din("lam", [2, 1, 256])
    gsub_d = din("gsub", [2, 128, 1])
    w_br_d = din("w_br", [2, D, D])
    w_out_d = din("w_out", [2, D, D])
    wq_d = din("peer_wq", [2, D, 2048])
    keysT_d = din("keysT", [2, 128, 16, 128])
    uT_d = din("uT", [2, D, NE])
    v_d_in = din("peer_v", [2, NE, D])
    cst_d = din("cst", [128, 6, 128])
    rope_d = din("rope", [2, 128, S])
    feats_d = {L: din("feats%d" % L, [33, L]) for L in (S, C)}
    dec_d = {L: din("dec%d" % L, [L, 256]) for L in (S, C)}
    tF_d = {L: din("tF%d" % L, [2, L // 128, 128, L // 128, 128], BF16) for L in (S, C)}
    RL = {}
    for L in (S, C):
        TB = min(512, L)
        G = min(4, L // 128)
        RL[L] = (TB, G, L // TB, (L // 128) // G)
    tI_d = {L: din("tI%d" % L, [2, RL[L][2], RL[L][3], 128, RL[L][1], RL[L][0]], BF16) for L in (S, C)}
    tN_d = {L: din("tN%d" % L, [2, RL[L][2], RL[L][3], 128, RL[L][1], RL[L][0]], BF16) for L in (S, C)}
    yT_d = nc.dram_tensor("yT", [128, 8, S], F32, kind="ExternalOutput").ap()
    hy_s = dscr("hy_s", [6, 128, T])
    fn_s = dscr("fn_s", [2, 128, T])
    q_s = dscr("q_s", [4, 128, T], BF16)
    k_s = dscr("k_s", [4, 128, T], BF16)
    v_s = dscr("v_s", [128, 18, 512], BF16)
    kf_s = {L: dscr("kf_s%d" % L, [2, L // 128, 128, 512]) for L in (S, C)}
    hyo_s = dscr("hyo_s", [2, 128, T], BF16)
    fno_s = dscr("fno_s", [2, 128, T], BF16)
    ato_s = dscr("ato_s", [4, 128, T], BF16)
    nT_s = dscr("nT_s", [128, 8, T], BF16)
    uTb_s = dscr("uTb_s", [32, 128, 8, 512], BF16)
    vb_s = dscr("vb_s", [128, 128, D], BF16)
    wT_s = dscr("wT_s", [2, 128, 128, 128], BF16)
    dbg_out = {}

    es = ExitStack()
    xT = es.enter_context(nc.sbuf_tensor("xT_sb", [128, 8, T], F32))
    cst = es.enter_context(nc.sbuf_tensor("cst_sb", [128, 6, 128], F32))
    cstb = es.enter_context(nc.sbuf_tensor("cstb_sb", [128, 2, 128], BF16))
    sm = es.enter_context(nc.sbuf_tensor("small_sb", [128, 512], F32))
    ARENA = 33280
    AR = es.enter_context(nc.sbuf_tensor("arena", [128, ARENA], F32))
    PS = [es.enter_context(nc.psum_tensor("ps%d" % i, [128, 512], F32)) for i in range(8)]
    PB = [Buf() for _ in range(8)]
    ident32 = cst[:, 0, :]
    ones32 = cst[:, 1, :]
    bd64 = cst[:, 2, :]
    rot = cst[:, 3, :]
    bdc = cst[:, 4, :]
    bds = cst[:, 5, :]
    identb = cstb[:, 0, :]
    onesb = cstb[:, 1, :]
    b_cst = Buf()
    b_x = [[Buf() for _ in range(5)] for _ in range(8)]
    b_sm = Buf()
    sc = sm[:, 0:16].rearrange("p (k j) -> p k j", k=8)
    modv = sm[:, 16:112].rearrange("p (g m j) -> p g m j", g=6, m=8)
    A_mix = sm[:, 112:128].rearrange("p (k j) -> p k j", k=8)
    A_ffn = sm[:, 128:144].rearrange("p (k j) -> p k j", k=8)
    gmf = sm[:, 144:160].rearrange("p (a k) -> p a k", a=2)
    tmp16 = sm[:, 160:176].rearrange("p (k j) -> p k j", k=8)
    gqk = sm[:, 176:178]
    gsub_s = sm[:, 178:179]
    neg_lam = sm[:, 179:180]
    lamw = sm[:, 180:182]
    hcw = sm[:, 184:208].rearrange("p (c t) -> p c t", c=6)
    hbias = sm[:, 208:212].rearrange("p (o c) -> p o c", o=2)
    fb = sm[:, 212:217]
    bada = sm[:, 224:272].rearrange("p (g m) -> p g m", g=6)
    lamt = sm[:, 272:400]
    epsc = sm[:, 183:184]

    class Arena:
        def __init__(self):
            self.off = 0

        def reset(self):
            P.barrier()
            self.off = 0

        def f32(self, n):
            a = AR[:, self.off:self.off + n]
            self.off += n
            assert self.off <= ARENA, self.off
            return a

        def bf(self, n):
            w = (n + 1) // 2
            return self.f32(w).bitcast(BF16)[:, 0:n]

    ar = Arena()

    def mm(out, lhsT, rhs, start, stop, rd, wr):
        P.op("pe", lambda e: e.matmul(out, lhsT=lhsT, rhs=rhs, start=start, stop=stop), rd, wr)

    def tr(out, in_, idn, rd, wr):
        P.op("pe", lambda e: e.transpose(out=out, in_=in_, identity=idn), rd, wr)

    def act(out, in_, func, rd, wr, bias=0.0, scale=1.0):
        P.op("act", lambda e: e.activation(out=out, in_=in_, func=func, bias=bias, scale=scale), rd, wr)

    def tt(eng, out, in0, in1, op, rd, wr):
        P.op(eng, lambda e: e.tensor_tensor(out=out, in0=in0, in1=in1, op=op), rd, wr)

    def ts(eng, out, in0, s1, s2, op0, op1, rd, wr):
        if s2 is None:
            P.op(eng, lambda e: e.tensor_scalar(out=out, in0=in0, scalar1=s1, scalar2=None, op0=op0), rd, wr)
        else:
            P.op(eng, lambda e: e.tensor_scalar(out=out, in0=in0, scalar1=s1, scalar2=s2, op0=op0, op1=op1), rd, wr)

    def stt(out, in0, scalar, in1, op0, op1, rd, wr):
        P.op("dve", lambda e: e.scalar_tensor_tensor(out=out, in0=in0, scalar=scalar, in1=in1, op0=op0, op1=op1), rd, wr)

    def cp(eng, out, in_, rd, wr):
        if eng == "act":
            act(out, in_, AF.Copy, rd, wr)
        else:
            P.op(eng, lambda e: e.tensor_copy(out=out, in_=in_), rd, wr)

    def recip(out, in_, rd, wr):
        P.op("dve", lambda e: e.reciprocal(out=out, in_=in_), rd, wr)

    def rsqrt_mean(out, in_, n, rd, wr):
        act(out, in_, AF.Sqrt, list(rd) + [b_sm], wr, bias=epsc, scale=1.0 / n)
        recip(out, out, wr, wr)

    P.dma(cst[:, :, :], cst_d, writes=[b_cst])
    for k in range(8):
        P.dma(xT[:, k, :], xT_d[:, k, :], writes=b_x[k])
    P.dma(sc, cc_d, writes=[b_sm])
    cp("dve", cstb[:, 0, :], ident32, [b_cst], [b_cst])
    cp("dve", cstb[:, 1, :], ones32, [b_cst], [b_cst])
    act(sc, sc, AF.Silu, [b_sm], [b_sm])
    P.op("dve", lambda e: e.memset(epsc, EPS), [], [b_sm])
    ar.reset()

    def want(name):
        return stages is None or name in stages

    def layer(l):
        last = l == depth - 1
        lam_init = 0.8 - 0.6 * math.exp(-0.3 * l)
        streams = [(0, S)] if last else [(0, S), (S, C)]
        tb_all = tblocks(0, T)
        tb_mix = tblocks(0, S) if last else tb_all

        ar.reset()
        P.dma(bada, bada_d[l], writes=[b_sm])
        P.dma(gmf, gmf_d[l], writes=[b_sm])
        P.dma(gqk, gqk_d[l], writes=[b_sm])
        P.dma(gsub_s, gsub_d[l], writes=[b_sm])
        P.dma(hcw, hcw_d[l], writes=[b_sm])
        P.dma(hbias, hy_bias_d[l], writes=[b_sm])
        P.dma(fb[0:64, 0:3], hy_fb_d[l], writes=[b_sm])
        P.dma(lamt, lam_d[l][:, 0:128].partition_broadcast(128), writes=[b_sm])
        lam2 = ar.f32(128)
        b_l2 = Buf()
        P.dma(lam2, lam_d[l][:, 128:256].partition_broadcast(128), writes=[b_l2])
        wts = [(ar.f32(8 * 1024), Buf()) for _ in range(2)]
        for g in range(6):
            wt, bw = wts[g % 2]
            wt3 = wt.rearrange("p (k m) -> p k m", k=8)
            P.dma(wt3, w_ada_d[l][:, g * 1024:(g + 1) * 1024].rearrange("(k p) m -> p k m", p=128), writes=[bw])
            for m in range(8):
                for k in range(8):
                    mm(PS[0][:, m * 2:m * 2 + 2], wt3[:, k, m * 128:(m + 1) * 128], sc[:, k, :], k == 0, k == 7, [bw, b_sm], [PB[0]])
            tt("dve", modv[:, g], PS[0][:, 0:16].rearrange("p (m j) -> p m j", m=8),
               bada[:, g, :].unsqueeze(2).to_broadcast([128, 8, 2]), ALU.add, [PB[0], b_sm], [b_sm])
        for (Aap, gi, si) in ((A_mix, 0, 1), (A_ffn, 1, 4)):
            ts("dve", tmp16, modv[:, si], 1.0, None, ALU.add, None, [b_sm], [b_sm])
            tt("dve", Aap, tmp16, gmf[:, gi, :].unsqueeze(2).to_broadcast([128, 8, 2]), ALU.mult, [b_sm], [b_sm])
        tt("dve", lamt[:, 0:64], lamt[:, 0:64], lamt[:, 64:128], ALU.mult, [b_sm], [b_sm])
        tt("dve", lam2[:, 0:64], lam2[:, 0:64], lam2[:, 64:128], ALU.mult, [b_l2], [b_l2])
        P.op("dve", lambda e: e.reduce_sum(out=lamw[:, 0:1], in_=lamt[:, 0:64], axis=AX.X), [b_sm], [b_sm])
        P.op("dve", lambda e: e.reduce_sum(out=lamw[:, 1:2], in_=lam2[:, 0:64], axis=AX.X), [b_l2, b_sm], [b_sm])
        act(lamw, lamw, AF.Exp, [b_sm], [b_sm])
        tt("dve", neg_lam, lamw[:, 1:2], lamw[:, 0:1], ALU.subtract, [b_sm], [b_sm])
        ts("dve", neg_lam, neg_lam, -lam_init, None, ALU.add, None, [b_sm], [b_sm])
        ts("dve", gsub_s, gsub_s, 1.0 - lam_init, None, ALU.mult, None, [b_sm], [b_sm])
        tt("dve", fb[0:64, 3:4], fb[0:64, 0:1], fb[0:64, 1:2], ALU.mult, [b_sm], [b_sm])
        tt("dve", fb[0:64, 4:5], fb[0:64, 2:3], fb[0:64, 1:2], ALU.mult, [b_sm], [b_sm])

        def mod_tmps(w):
            return dict(sq=[(ar.f32(w), Buf()) for _ in range(3)], rs=[(ar.f32(w), Buf()) for _ in range(2)],
                        tm=[(ar.f32(w), Buf()) for _ in range(3)], c=[0, 0, 0])

        def modulate(Aap, gB, blocks, emit_cb, mt):
            for (t0, Tn) in blocks:
                j = 0 if t0 < S else 1
                xb = t0 // 512
                for k in range(8):
                    sq, b_sq = mt["sq"][mt["c"][0] % 3]
                    mt["c"][0] += 1
                    act(sq[:, :Tn], xT[:, k, t0:t0 + Tn], AF.Square, [b_x[k][xb]], [b_sq])
                    mm(PS[7][:, :Tn], ones32, sq[:, :Tn], k == 0, k == 7, [b_sq, b_cst], [PB[7]])
                r, b_r = mt["rs"][mt["c"][1] % 2]
                mt["c"][1] += 1
                rsqrt_mean(r[:, :Tn], PS[7][:, :Tn], D, [PB[7]], [b_r])
                for k in range(8):
                    tm, b_t = mt["tm"][mt["c"][2] % 3]
                    mt["c"][2] += 1
                    tt("dve", tm[:, :Tn], xT[:, k, t0:t0 + Tn], r[:, :Tn], ALU.mult, [b_x[k][xb], b_r], [b_t])
                    emit_cb(k, t0, Tn, tm, b_t, Aap[:, k, j:j + 1], modv[:, gB, k, j:j + 1])

        def hyena_filters(L):
            ar.reset()
            nj = L // 128
            w1 = ar.f32(64)
            w2 = ar.f32(64)
            w3 = ar.f32(1024)
            b_w = Buf()
            P.dma(w1[0:33, :], hy_w1_d[l], writes=[b_w])
            P.dma(w2[0:64, :], hy_w2_d[l], writes=[b_w])
            P.dma(w3[0:64, :], hy_w3_d[l], writes=[b_w])
            h2T = ar.f32(L)
            b_h2 = Buf()
            off_mlp = ar.off
            ft = [(ar.f32(512), Buf()) for _ in range(2)]
            aa = [(ar.f32(512), Buf()) for _ in range(2)]
            h1 = [(ar.f32(512), Buf()) for _ in range(2)]
            aa2 = [(ar.f32(512), Buf()) for _ in range(2)]
            sx = [(ar.f32(512), Buf()) for _ in range(3)]

            def sin_act(out, a, Tn, b_in, b_out):
                (s4, b_s4), (c4, b_c4), (q, b_q) = sx
                act(s4[0:64, :Tn], a, AF.Sin, [b_in], [b_s4], scale=0.25)
                act(c4[0:64, :Tn], a, AF.Abs, [b_in], [b_c4])
                act(c4[0:64, :Tn], c4[0:64, :Tn], AF.Sin, [b_c4, b_sm], [b_c4], bias=halfpi[0:64, :], scale=-0.25)
                tt("dve", q[0:64, :Tn], s4[0:64, :Tn], s4[0:64, :Tn], ALU.mult, [b_s4], [b_q])
                ts("dve", q[0:64, :Tn], q[0:64, :Tn], -2.0, 1.0, ALU.mult, ALU.add, [b_q], [b_q])
                tt("dve", c4[0:64, :Tn], s4[0:64, :Tn], c4[0:64, :Tn], ALU.mult, [b_s4, b_c4], [b_c4])
                stt(out, c4[0:64, :Tn], 4.0, q[0:64, :Tn], ALU.mult, ALU.mult, [b_c4, b_q], [b_out])

            for bi, (t0, Tn) in enumerate(tblocks(0, L)):
                f, b_f = ft[bi % 2]
                P.dma(f[0:33, :Tn], feats_d[L][:, t0:t0 + Tn], writes=[b_f])
                mm(PS[0][0:64, :Tn], w1[0:33, :], f[0:33, :Tn], True, True, [b_w, b_f], [PB[0]])
                a, b_a = aa[bi % 2]
                ts("dve", a[0:64, :Tn], PS[0][0:64, :Tn], fb[0:64, 1:2], fb[0:64, 3:4], ALU.mult, ALU.add, [PB[0], b_sm], [b_a])
                hh, b_h = h1[bi % 2]
                sin_act(hh[0:64, :Tn], a[0:64, :Tn], Tn, b_a, b_h)
                mm(PS[1][0:64, :Tn], w2[0:64, :], hh[0:64, :Tn], True, True, [b_w, b_h], [PB[1]])
                a2_, b_a2 = aa2[bi % 2]
                ts("dve", a2_[0:64, :Tn], PS[1][0:64, :Tn], fb[0:64, 1:2], fb[0:64, 4:5], ALU.mult, ALU.add, [PB[1], b_sm], [b_a2])
                sin_act(h2T[0:64, t0:t0 + Tn], a2_[0:64, :Tn], Tn, b_a2, b_h2)
            dec = ar.f32(nj * 256).rearrange("p (j c) -> p j c", j=nj)
            off_dec_end = ar.off
            b_dec = Buf()
            P.dma(dec, dec_d[L].rearrange("(j p) c -> p j c", p=128), writes=[b_dec])
            hd = ar.f32(nj * 1024).rearrange("p (j c) -> p j c", j=nj)
            b_hd = [Buf() for _ in range(nj)]
            off_hd = ar.off
            for pc in range(nj):
                for half in range(2):
                    pb = half
                    mm(PS[pb][:, :], h2T[0:64, pc * 128:(pc + 1) * 128], w3[0:64, half * 512:(half + 1) * 512], True, True, [b_h2, b_w], [PB[pb]])
                    tt("dve", hd[:, pc, half * 512:(half + 1) * 512].rearrange("p (o c) -> p o c", o=2),
                       PS[pb][:, :].rearrange("p (o c) -> p o c", o=2),
                       dec[:, pc, :].unsqueeze(1).to_broadcast([128, 2, 256]), ALU.mult, [PB[pb], b_dec], [b_hd[pc]])
            P.op("dve", lambda e: e.memset(hd[0:1, 0, 512:1024], 0.0), [], [b_hd[0]])
            ab = [(ar.f32(1024), Buf()) for _ in range(2)]
            for pc in range(nj):
                a, b_a = ab[pc % 2]
                act(a, hd[:, pc, :], AF.Abs, [b_hd[pc]], [b_a])
                for half in range(2):
                    mm(PS[2 + half][:, :], ones32, a[:, half * 512:(half + 1) * 512], pc == 0, pc == nj - 1, [b_a, b_cst], [PB[2 + half]])
            rn = ar.f32(512)
            b_rn = Buf()
            cp("dve", rn, PS[2][:, :], [PB[2]], [b_rn])
            tt("dve", rn, rn, PS[3][:, :], ALU.add, [b_rn, PB[3]], [b_rn])
            recip(rn, rn, [b_rn], [b_rn])
            tmpe = [(ar.f32(512), Buf()) for _ in range(2)]
            P.barrier()
            ar.off = 0
            hdb = ar.bf(nj * 1024).rearrange("p (j c) -> p j c", j=nj)
            b_hdb = [Buf() for _ in range(nj)]
            tabs = [(ar.bf(2 * nj * 128).rearrange("p (c j f) -> p c j f", c=2, j=nj), Buf()) for _ in range(2)]
            assert ar.off <= off_dec_end
            ar.off = off_hd
            sts = [(ar.f32(1024).rearrange("p (c n) -> p c n", c=2), Buf()) for _ in range(1)]
            for pc in range(nj):
                te, b_te = tmpe[pc % 2]
                hf = hd[:, pc, 0:512]
                hb = hd[:, pc, 512:1024]
                tt("dve", te, hf, hb, ALU.add, [b_hd[pc]], [b_te])
                tt("pool", hb, hf, hb, ALU.subtract, [b_hd[pc]], [b_hd[pc]])
                tt("dve", hdb[:, pc, 0:512], te, rn, ALU.mult, [b_te, b_rn], [b_hdb[pc]])
                tt("pool", hdb[:, pc, 512:1024], hb, rn, ALU.mult, [b_hd[pc], b_rn], [b_hdb[pc]])
            for fc in range(nj):
                tb_, b_tb = tabs[fc % 2]
                for c in range(2):
                    P.dma(tb_[:, c], tF_d[L][c, fc], writes=[b_tb])
                for c in range(2):
                    for jc in range(nj):
                        mm(PS[4 + c][:, :], tb_[:, c, jc, :], hdb[:, jc, c * 512:(c + 1) * 512], jc == 0, jc == nj - 1, [b_tb, b_hdb[jc]], [PB[4 + c]])
                st, b_st = sts[0]
                cp("act", st[:, 0, :], PS[4][:, :], [PB[4]], [b_st])
                cp("dve", st[:, 1, :], PS[5][:, :], [PB[5]], [b_st])
                for c in range(2):
                    P.dma(kf_s[L][c, fc], st[:, c, :], reads=[b_st], writes=[b_kf[L]])

        b_kf = {S: Buf(), C: Buf()}
        halfpi = sm[:, 182:183]
        P.op("dve", lambda e: e.memset(halfpi, PI / 2), [], [b_sm])
        if want("hyf"):
            for (t0, L) in streams:
                hyena_filters(L)

        ar.reset()
        nT = ar.bf(8 * T).rearrange("p (k t) -> p k t", k=8)
        b_n = [[Buf() for _ in range(5)] for _ in range(8)]
        b_nTs = Buf()

        def emit_mix(k, t0, Tn, tm, b_t, Asc, Bsc):
            act(nT[:, k, t0:t0 + Tn], tm[:, :Tn], AF.Identity, [b_t, b_sm], [b_n[k][t0 // 512]], bias=Bsc, scale=Asc)

        mark = ar.off
        if want("proj") or want("merge"):
            modulate(A_mix, 0, tb_all, emit_mix, mod_tmps(512))
            for k in range(8):
                P.dma(nT_s[:, k, :], nT[:, k, :], reads=b_n[k], writes=[b_nTs])
        ar.off = mark
        P.barrier()

        b_hy = Buf()
        b_fn = Buf()
        b_q = Buf()
        b_k = Buf()
        b_v = Buf()
        if want("proj"):
            w32 = [(ar.f32(8 * 512).rearrange("p (k m) -> p k m", k=8), Buf()) for _ in range(1)]
            wbf = [(ar.bf(8 * 512).rearrange("p (k m) -> p k m", k=8), Buf()) for _ in range(2)]
            stg = [(ar.f32(512), Buf()) for _ in range(2)]
            sqt = [(ar.f32(512), Buf()) for _ in range(2)]
            rt = [(ar.f32(512), Buf()) for _ in range(2)]
            xnt = [(ar.f32(512), Buf()) for _ in range(2)]
            t1t = [(ar.f32(512), Buf()) for _ in range(2)]
            t2t = [(ar.f32(512), Buf()) for _ in range(2)]
            obt = [(ar.bf(512), Buf()) for _ in range(2)]
            rope_sb = ar.f32(2 * S).rearrange("p (c t) -> p c t", c=2)
            b_rope = Buf()
            for c in range(2):
                P.dma(rope_sb[:, c, :], rope_d[c], writes=[b_rope])
            cnt = [0]

            def qk_cb(ps, pb, which, h, t0, Tn):
                i = cnt[0] % 2
                cnt[0] += 1
                sq_, b_sq_ = sqt[i]
                act(sq_[:, :Tn], ps[:, :Tn], AF.Square, [pb], [b_sq_])
                mm(PS[6][:, :Tn], bd64, sq_[:, :Tn], True, True, [b_sq_, b_cst], [PB[6]])
                r_, b_r_ = rt[i]
                rsqrt_mean(r_[:, :Tn], PS[6][:, :Tn], 64, [PB[6]], [b_r_])
                xn, b_xn = xnt[i]
                stt(xn[:, :Tn], ps[:, :Tn], gqk[:, which:which + 1], r_[:, :Tn], ALU.mult, ALU.mult, [pb, b_r_, b_sm], [b_xn])
                ob, b_ob = obt[i]
                if t0 < S:
                    mm(PS[5][:, :Tn], rot, xn[:, :Tn], True, True, [b_xn, b_cst], [PB[5]])
                    t1, b_t1 = t1t[i]
                    t2, b_t2 = t2t[i]
                    tt("pool", t1[:, :Tn], xn[:, :Tn], rope_sb[:, 0, t0:t0 + Tn], ALU.mult, [b_xn, b_rope], [b_t1])
                    tt("dve", t2[:, :Tn], PS[5][:, :Tn], rope_sb[:, 1, t0:t0 + Tn], ALU.mult, [PB[5], b_rope], [b_t2])
                    tt("pool", ob[:, :Tn], t1[:, :Tn], t2[:, :Tn], ALU.add, [b_t1, b_t2], [b_ob])
                else:
                    cp("pool", ob[:, :Tn], xn[:, :Tn], [b_xn], [b_ob])
                dst = (q_s if which == 0 else k_s)[h][:, t0:t0 + Tn]
                P.dma(dst, ob[:, :Tn], reads=[b_ob], writes=[b_q if which == 0 else b_k])

            pcnt = [0]
            for g in range(5):
                w3_, b_w3 = w32[0]
                P.dma(w3_, w_in_d[l][:, g * 512:(g + 1) * 512].rearrange("(k p) m -> p k m", p=128), writes=[b_w3])
                wb_, b_wb = wbf[g % 2]
                cp("act", wb_[:, 0:4, :], w3_[:, 0:4, :], [b_w3], [b_wb])
                cp("pool", wb_[:, 4:8, :], w3_[:, 4:8, :], [b_w3], [b_wb])
                if g < 4:
                    for mi in range(4):
                        for (t0, Tn) in tb_all:
                            if g == 2 and t0 >= S and last:
                                continue
                            pi = pcnt[0] % 4
                            pcnt[0] += 1
                            for k in range(8):
                                mm(PS[pi][:, :Tn], wb_[:, k, mi * 128:(mi + 1) * 128], nT[:, k, t0:t0 + Tn], k == 0, k == 7,
                                   [b_wb, b_n[k][t0 // 512]], [PB[pi]])
                            if g < 2:
                                st, b_st = stg[pcnt[0] % 2]
                                cp("act" if pcnt[0] % 2 else "dve", st[:, :Tn], PS[pi][:, :Tn], [PB[pi]], [b_st])
                                ch = g * 4 + mi
                                if ch < 6:
                                    P.dma(hy_s[ch][:, t0:t0 + Tn], st[:, :Tn], reads=[b_st], writes=[b_hy])
                                else:
                                    P.dma(fn_s[ch - 6][:, t0:t0 + Tn], st[:, :Tn], reads=[b_st], writes=[b_fn])
                            else:
                                qk_cb(PS[pi], PB[pi], g - 2, mi, t0, Tn)
                else:
                    for i in range(18):
                        pi = pcnt[0] % 4
                        pcnt[0] += 1
                        for k in range(8):
                            mm(PS[pi][:, :], nT[:, k, i * 128:(i + 1) * 128], wb_[:, k, :], k == 0, k == 7, [b_wb, b_n[k][i // 4]], [PB[pi]])
                        ob, b_ob = obt[i % 2]
                        cp("act" if i % 2 else "dve", ob, PS[pi][:, :], [PB[pi]], [b_ob])
                        P.dma(v_s[:, i, :], ob, reads=[b_ob], writes=[b_v])

        b_ato = Buf()
        if want("attn"):
            ar.reset()
            kT = ar.bf(4 * T).rearrange("p (h t) -> p h t", h=4)
            qT = ar.bf(4 * T).rearrange("p (h t) -> p h t", h=4)
            vv = ar.bf(18 * 512).rearrange("p (j c) -> p j c", j=18)
            b_kT = Buf()
            b_qT = Buf()
            b_vv = Buf()
            for h in range(4):
                P.dma(kT[:, h, :], k_s[h], reads=[b_k], writes=[b_kT])
                P.dma(qT[:, h, :], q_s[h], reads=[b_q], writes=[b_qT])
            P.dma(vv, v_s, reads=[b_v], writes=[b_vv])
            Et = [(ar.bf(512), Buf()) for _ in range(3)]
            r0 = ar.f32(512)
            t0_ = ar.f32(512)
            t1_ = ar.f32(512)
            sq_ = ar.f32(512)
            rr_ = ar.f32(512)
            b_r0, b_t0, b_t1, b_sq2, b_rr = [Buf() for _ in range(5)]
            aob = [(ar.bf(512), Buf()) for _ in range(2)]
            ec = 0
            oc = 0
            qblocks = tblocks(0, S) if last else tb_all
            for h in range(4):
                for (q0, Tn) in qblocks:
                    keys = list(range(18)) if q0 < S else [16, 17]
                    seq = [(c, idx, j) for c in range(2) for idx, j in enumerate(keys)]

                    def s_mm(n_):
                        c, idx, j = seq[n_]
                        sp = (ec + n_) % 2
                        mm(PS[sp][:, :Tn], kT[64 * c:64 * c + 64, h, j * 128:(j + 1) * 128], qT[64 * c:64 * c + 64, h, q0:q0 + Tn],
                           True, True, [b_kT, b_qT], [PB[sp]])

                    s_mm(0)
                    for n_, (c, idx, j) in enumerate(seq):
                        if n_ + 1 < len(seq):
                            s_mm(n_ + 1)
                        sp = (ec + n_) % 2
                        E, b_E = Et[(ec + n_) % 3]
                        act(E[:, :Tn], PS[sp][:, :Tn], AF.Exp, [PB[sp]], [b_E], scale=0.125)
                        mm(PS[2 + 2 * c][:, :Tn], vv[:, j, h * 128:(h + 1) * 128], E[:, :Tn], idx == 0, idx == len(keys) - 1, [b_vv, b_E], [PB[2 + 2 * c]])
                        mm(PS[3 + 2 * c][:, :Tn], onesb, E[:, :Tn], idx == 0, idx == len(keys) - 1, [b_cst, b_E], [PB[3 + 2 * c]])
                    ec += len(seq)
                    recip(r0[:, :Tn], PS[3][:, :Tn], [PB[3]], [b_r0])
                    tt("dve", t0_[:, :Tn], PS[2][:, :Tn], r0[:, :Tn], ALU.mult, [PB[2], b_r0], [b_t0])
                    recip(r0[:, :Tn], PS[5][:, :Tn], [PB[5]], [b_r0])
                    tt("dve", t1_[:, :Tn], PS[4][:, :Tn], r0[:, :Tn], ALU.mult, [PB[4], b_r0], [b_t1])
                    stt(t0_[:, :Tn], t1_[:, :Tn], neg_lam, t0_[:, :Tn], ALU.mult, ALU.add, [b_t1, b_t0, b_sm], [b_t0])
                    act(sq_[:, :Tn], t0_[:, :Tn], AF.Square, [b_t0], [b_sq2])
                    mm(PS[6][:, :Tn], ones32, sq_[:, :Tn], True, True, [b_sq2, b_cst], [PB[6]])
                    rsqrt_mean(rr_[:, :Tn], PS[6][:, :Tn], 128, [PB[6]], [b_rr])
                    ao, b_ao = aob[oc % 2]
                    oc += 1
                    stt(ao[:, :Tn], t0_[:, :Tn], gsub_s, rr_[:, :Tn], ALU.mult, ALU.mult, [b_t0, b_rr, b_sm], [b_ao])
                    P.dma(ato_s[h][:, q0:q0 + Tn], ao[:, :Tn], reads=[b_ao], writes=[b_ato])

        b_hyo = Buf()

        def hyena_main(t0, L):
            TB, G, nTB, nG = RL[L]
            nj = L // 128
            for cc in range(2):
                ar.reset()
                raw = ar.f32(L)
                b_raw = Buf()
                us = [ar.f32(L) for _ in range(3)]
                b_us = [Buf() for _ in range(3)]
                for jx in range(3):
                    ch = jx * 2 + cc
                    P.dma(raw, hy_s[ch][:, t0:t0 + L], reads=[b_hy], writes=[b_raw])
                    u = us[jx]
                    ts("dve", u, raw, hcw[:, ch, 1:2], hcw[:, ch, 3:4], ALU.mult, ALU.add, [b_raw, b_sm], [b_us[jx]])
                    stt(u[:, 1:L], raw[:, 0:L - 1], hcw[:, ch, 0:1], u[:, 1:L], ALU.mult, ALU.add, [b_raw, b_us[jx], b_sm], [b_us[jx]])
                    stt(u[:, 0:L - 1], raw[:, 1:L], hcw[:, ch, 2:3], u[:, 0:L - 1], ALU.mult, ALU.add, [b_raw, b_us[jx], b_sm], [b_us[jx]])
                zbuf = raw
                b_zb = b_raw
                ztm = ar.bf(L).rearrange("p (j c) -> p j c", j=nj)
                b_ztm = Buf()
                Z = ar.f32(2 * L).rearrange("p (c j n) -> p c j n", c=2, j=nj)
                b_Z = Buf()
                Kf = ar.f32(2 * L).rearrange("p (c j n) -> p c j n", c=2, j=nj)
                b_Kf = Buf()
                X = ar.bf(2 * L).rearrange("p (c j n) -> p c j n", c=2, j=nj)
                b_X = Buf()
                tmpx = ar.f32(L).rearrange("p (j n) -> p j n", j=nj)
                b_tx = Buf()
                tmpy = ar.f32(L).rearrange("p (j n) -> p j n", j=nj)
                b_ty = Buf()
                tabs = [(ar.bf(2048), Buf()) for _ in range(4)]
                tcnt = 0
                z, b_z = us[0], b_us[0]
                for o in range(2):
                    for scn in range(nj):
                        tr(PS[0][:, (scn % 4) * 128:(scn % 4 + 1) * 128], z[:, scn * 128:(scn + 1) * 128], ident32, [b_z, b_cst], [PB[0]])
                        if scn % 4 == 3 or scn == nj - 1:
                            n4 = scn % 4 + 1
                            cp("act", ztm[:, scn - n4 + 1:scn + 1, :], PS[0][:, 0:n4 * 128].rearrange("p (j c) -> p j c", j=n4), [PB[0]], [b_ztm])
                    for c in range(2):
                        P.dma(Kf[:, c], kf_s[L][c][:, :, o * 256 + cc * 128:o * 256 + cc * 128 + 128].rearrange("j p n -> p j n"), reads=[b_kf[L]], writes=[b_Kf])
                    for fc in range(nj):
                        tbc, b_tbc = tabs[tcnt % 4]
                        tbs, b_tbs = tabs[(tcnt + 1) % 4]
                        tcnt += 2
                        tbc3 = tbc[:, 0:nj * 128].rearrange("p (j f) -> p j f", j=nj)
                        tbs3 = tbs[:, 0:nj * 128].rearrange("p (j f) -> p j f", j=nj)
                        P.dma(tbc3, tF_d[L][0, fc], writes=[b_tbc])
                        P.dma(tbs3, tF_d[L][1, fc], writes=[b_tbs])
                        pz = 1 + fc % 2
                        for sc_ in range(nj):
                            mm(PS[pz][:, 0:128], tbc3[:, sc_, :], ztm[:, sc_, :], sc_ == 0, sc_ == nj - 1, [b_tbc, b_ztm], [PB[pz]])
                        for sc_ in range(nj):
                            mm(PS[pz][:, 128:256], tbs3[:, sc_, :], ztm[:, sc_, :], sc_ == 0, sc_ == nj - 1, [b_tbs, b_ztm], [PB[pz]])
                        cp("act", Z[:, :, fc, :], PS[pz][:, 0:256].rearrange("p (c n) -> p c n", c=2), [PB[pz]], [b_Z])
                    Zr, Zi, Kr, Ki = Z[:, 0], Z[:, 1], Kf[:, 0], Kf[:, 1]
                    tt("dve", tmpx, Zr, Kr, ALU.mult, [b_Z, b_Kf], [b_tx])
                    tt("pool", tmpy, Zi, Ki, ALU.mult, [b_Z, b_Kf], [b_ty])
                    tt("dve", X[:, 0], tmpx, tmpy, ALU.subtract, [b_tx, b_ty], [b_X])
                    tt("pool", tmpy, Zr, Ki, ALU.mult, [b_Z, b_Kf, b_X], [b_ty])
                    tt("dve", tmpx, Zi, Kr, ALU.mult, [b_Z, b_Kf, b_X], [b_tx])
                    tt("dve", X[:, 1], tmpx, tmpy, ALU.add, [b_tx, b_ty], [b_X])
                    gate, b_g = us[1 + o], b_us[1 + o]
                    znew, b_zn = (zbuf, b_zb) if o == 0 else (us[0], b_us[0])
                    for tb in range(nTB):
                        py = 3 + tb % 2
                        for g in range(nG):
                            tbc, b_tbc = tabs[tcnt % 4]
                            tbs, b_tbs = tabs[(tcnt + 1) % 4]
                            tcnt += 2
                            tc3 = tbc[:, 0:G * TB].rearrange("p (g t) -> p g t", g=G)
                            ts3 = tbs[:, 0:G * TB].rearrange("p (g t) -> p g t", g=G)
                            P.dma(tc3, tI_d[L][0, tb, g], writes=[b_tbc])
                            P.dma(ts3, tI_d[L][1, tb, g], writes=[b_tbs])
                            for fi in range(G):
                                fc = g * G + fi
                                mm(PS[py][:, :TB], X[:, 0, fc, :], tc3[:, fi, :], fc == 0, False, [b_X, b_tbc], [PB[py]])
                                mm(PS[py][:, :TB], X[:, 1, fc, :], ts3[:, fi, :], False, fc == nj - 1, [b_X, b_tbs], [PB[py]])
                        sl = slice(tb * TB, (tb + 1) * TB)
                        stt(tmpx.rearrange("p j n -> p (j n)")[:, sl], z[:, sl], hbias[:, o, cc:cc + 1], PS[py][:, :TB], ALU.mult, ALU.add, [b_z, PB[py], b_sm, b_X], [b_tx])
                        tt("dve", znew[:, sl], tmpx.rearrange("p j n -> p (j n)")[:, sl], gate[:, sl], ALU.mult, [b_tx, b_g], [b_zn])
                    z, b_z = znew, b_zn
                ob = ar.bf(L)
                b_ob = Buf()
                cp("act", ob, z, [b_z], [b_ob])
                P.dma(hyo_s[cc][:, t0:t0 + L], ob, reads=[b_ob], writes=[b_hyo])

        if want("hyena"):
            for (t0, L) in streams:
                hyena_main(t0, L)

        b_fno = Buf()

        def fnet(t0, L):
            TB, G, nTB, nG = RL[L]
            nj = L // 128
            ar.reset()
            fz = [(ar.f32(L), Buf()) for _ in range(2)]
            zc = ar.bf(nj * 256).rearrange("p (j c) -> p j c", j=nj)
            zs = ar.bf(nj * 256).rearrange("p (j c) -> p j c", j=nj)
            b_zc = Buf()
            for cc in range(2):
                f, b_f = fz[cc]
                P.dma(f, fn_s[cc][:, t0:t0 + L], reads=[b_fn], writes=[b_f])
                for scn in range(nj):
                    pa = scn % 2
                    mm(PS[pa][:, 0:128], f[:, scn * 128:(scn + 1) * 128], bdc, True, True, [b_f, b_cst], [PB[pa]])
                    mm(PS[pa][:, 128:256], f[:, scn * 128:(scn + 1) * 128], bds, True, True, [b_f, b_cst], [PB[pa]])
                    cp("act", zc[:, scn, cc * 128:(cc + 1) * 128], PS[pa][:, 0:128], [PB[pa]], [b_zc])
                    cp("dve", zs[:, scn, cc * 128:(cc + 1) * 128], PS[pa][:, 128:256], [PB[pa]], [b_zc])
            tabs = [(ar.bf(2048), Buf()) for _ in range(4)]
            obs = [(ar.bf(512), Buf()) for _ in range(2)]
            tcnt = 0
            oc = 0
            for tb in range(nTB):
                for g in range(nG):
                    tbc, b_tbc = tabs[tcnt % 4]
                    tbs, b_tbs = tabs[(tcnt + 1) % 4]
                    tcnt += 2
                    tc3 = tbc[:, 0:G * TB].rearrange("p (g t) -> p g t", g=G)
                    ts3 = tbs[:, 0:G * TB].rearrange("p (g t) -> p g t", g=G)
                    P.dma(tc3, tN_d[L][0, tb, g], writes=[b_tbc])
                    P.dma(ts3, tN_d[L][1, tb, g], writes=[b_tbs])
                    for cc in range(2):
                        py = 2 + cc + 2 * (tb % 2)
                        for fi in range(G):
                            sc_ = g * G + fi
                            mm(PS[py][:, :TB], zc[:, sc_, cc * 128:(cc + 1) * 128], tc3[:, fi, :], sc_ == 0, False, [b_zc, b_tbc], [PB[py]])
                            mm(PS[py][:, :TB], zs[:, sc_, cc * 128:(cc + 1) * 128], ts3[:, fi, :], False, sc_ == nj - 1, [b_zc, b_tbs], [PB[py]])
                for cc in range(2):
                    py = 2 + cc + 2 * (tb % 2)
                    ob, b_ob = obs[oc % 2]
                    oc += 1
                    cp("act" if cc else "dve", ob[:, :TB], PS[py][:, :TB], [PB[py]], [b_ob])
                    P.dma(fno_s[cc][:, t0 + tb * TB:t0 + (tb + 1) * TB], ob[:, :TB], reads=[b_ob], writes=[b_fno])

        if want("fnet"):
            for (t0, L) in streams:
                fnet(t0, L)

        if want("merge"):
            ar.reset()
            wbr = ar.bf(8 * D).rearrange("p (k m) -> p k m", k=8)
            wo = ar.bf(8 * D).rearrange("p (k m) -> p k m", k=8)
            b_wbr = Buf()
            b_wo = Buf()
            w32 = ar.f32(8 * 512).rearrange("p (k m) -> p k m", k=8)
            b_w32 = Buf()
            for (src, dst, bd) in ((w_br_d, wbr, b_wbr), (w_out_d, wo, b_wo)):
                for half in range(2):
                    P.dma(w32, src[l][:, half * 512:(half + 1) * 512].rearrange("(k p) m -> p k m", p=128), writes=[b_w32])
                    cp("act", dst[:, 0:4, half * 512:(half + 1) * 512], w32[:, 0:4, :], [b_w32], [bd])
                    cp("dve", dst[:, 4:8, half * 512:(half + 1) * 512], w32[:, 4:8, :], [b_w32], [bd])
            ar.off -= 8 * 512
            P.barrier()
            nb = [(ar.bf(8 * 512).rearrange("p (k t) -> p k t", k=8), Buf()) for _ in range(2)]
            sb_ = [(ar.bf(8 * 512).rearrange("p (k t) -> p k t", k=8), Buf()) for _ in range(2)]
            g32 = [(ar.f32(8 * 384).rearrange("p (k j c) -> p k j c", k=8, j=3), Buf()) for _ in range(2)]
            gbf = [(ar.bf(8 * 384).rearrange("p (k j c) -> p k j c", k=8, j=3), Buf()) for _ in range(2)]
            sig = [(ar.f32(512), Buf()) for _ in range(3)]
            yacc = [(ar.f32(512), Buf()) for _ in range(2)]
            ytmp = [(ar.f32(512), Buf()) for _ in range(2)]
            yT = ar.bf(8 * 512).rearrange("p (k t) -> p k t", k=8)
            b_yT = [Buf() for _ in range(8)]
            KR = ((0, 2), (2, 4), (4, 8))
            gc = 0
            for bi, (t0, Tn) in enumerate(tb_mix):
                nbk, b_nb = nb[bi % 2]
                sbk, b_sb = sb_[bi % 2]
                P.dma(nbk[:, :, :Tn], nT_s[:, :, t0:t0 + Tn], reads=[b_nTs], writes=[b_nb])
                for cc in range(2):
                    P.dma(sbk[:, cc, :Tn], hyo_s[cc][:, t0:t0 + Tn], reads=[b_hyo], writes=[b_sb])
                    P.dma(sbk[:, 2 + cc, :Tn], fno_s[cc][:, t0:t0 + Tn], reads=[b_fno], writes=[b_sb])
                for h in range(4):
                    P.dma(sbk[:, 4 + h, :Tn], ato_s[h][:, t0:t0 + Tn], reads=[b_ato], writes=[b_sb])
                for m in range(8):
                    gw, b_gw = g32[gc % 2]
                    gb, b_gb = gbf[gc % 2]
                    gc += 1
                    P.dma(gw, w_gate_d[l, m], writes=[b_gw])
                    cp("pool", gb, gw, [b_gw], [b_gb])
                    ya, b_ya = yacc[m % 2]
                    yt, b_yt = ytmp[m % 2]
                    for j in range(3):
                        pg = j
                        pbr = 3 + j
                        for k in range(8):
                            mm(PS[pg][:, :Tn], gb[:, k, j, :], nbk[:, k, :Tn], k == 0, k == 7, [b_gb, b_nb], [PB[pg]])
                        k0, k1 = KR[j]
                        for k in range(k0, k1):
                            mm(PS[pbr][:, :Tn], wbr[:, k, m * 128:(m + 1) * 128], sbk[:, k, :Tn], k == k0, k == k1 - 1, [b_wbr, b_sb], [PB[pbr]])
                        sg, b_sg = sig[(m * 3 + j) % 3]
                        act(sg[:, :Tn], PS[pg][:, :Tn], AF.Sigmoid, [PB[pg]], [b_sg])
                        if j == 0:
                            tt("dve", ya[:, :Tn], sg[:, :Tn], PS[pbr][:, :Tn], ALU.mult, [b_sg, PB[pbr]], [b_ya])
                        else:
                            tt("dve", yt[:, :Tn], sg[:, :Tn], PS[pbr][:, :Tn], ALU.mult, [b_sg, PB[pbr]], [b_yt])
                            if j == 1:
                                tt("pool", ya[:, :Tn], ya[:, :Tn], yt[:, :Tn], ALU.add, [b_ya, b_yt], [b_ya])
                            else:
                                tt("pool", yT[:, m, :Tn], ya[:, :Tn], yt[:, :Tn], ALU.add, [b_ya, b_yt], [b_yT[m]])
                jj = 0 if t0 < S else 1
                for mo in range(8):
                    po = 6 + mo % 2
                    for k in range(8):
                        mm(PS[po][:, :Tn], wo[:, k, mo * 128:(mo + 1) * 128], yT[:, k, :Tn], k == 0, k == 7, [b_wo, b_yT[k]], [PB[po]])
                    xb = b_x[mo][t0 // 512]
                    stt(xT[:, mo, t0:t0 + Tn], PS[po][:, :Tn], modv[:, 2, mo, jj:jj + 1], xT[:, mo, t0:t0 + Tn], ALU.mult, ALU.add, [PB[po], xb, b_sm], [xb])

        if want("peer"):
            peer(l, last, modulate, mod_tmps, A_ffn)

    def peer(l, last, modulate, mod_tmps, A_ffn):
        ar.reset()
        b_ub = Buf()
        b_vb = Buf()
        ld = [(ar.f32(4096), Buf()) for _ in range(3)]
        cv = [(ar.bf(4096), Buf()) for _ in range(3)]
        jobs = []
        for k in range(8):
            for eb in range(4):
                jobs.append(("u", k, eb))
        for e1g in range(32):
            jobs.append(("v", e1g, 0))

        def j_load(ci):
            kind, i0_, i1_ = jobs[ci]
            a, b_a = ld[ci % 3]
            if kind == "u":
                P.dma(a, uT_d[l][i0_ * 128:(i0_ + 1) * 128, i1_ * 4096:(i1_ + 1) * 4096], writes=[b_a])
            else:
                P.dma(a.rearrange("p (e d) -> p e d", e=4), v_d_in[l][i0_ * 512:(i0_ + 1) * 512, :].rearrange("(e p) d -> p e d", p=128), writes=[b_a])

        j_load(0)
        j_load(1)
        for ci in range(len(jobs)):
            if ci + 2 < len(jobs):
                j_load(ci + 2)
            kind, i0_, i1_ = jobs[ci]
            a, b_a = ld[ci % 3]
            o, b_o = cv[ci % 3]
            cp(("act", "dve", "pool")[ci % 3], o, a, [b_a], [b_o])
            if kind == "u":
                P.dma(uTb_s[i1_ * 8:(i1_ + 1) * 8, :, i0_, :].rearrange("g p e -> p g e"), o.rearrange("p (g e) -> p g e", g=8), reads=[b_o], writes=[b_ub])
            else:
                P.dma(vb_s[:, i0_ * 4:(i0_ + 1) * 4, :], o.rearrange("p (e d) -> p e d", e=4), reads=[b_o], writes=[b_vb])
        ar.reset()
        keysT = ar.f32(2048).rearrange("p (h n) -> p h n", h=16)
        b_ky = Buf()
        P.dma(keysT, keysT_d[l], writes=[b_ky])
        nbf = ar.bf(8 * 256).rearrange("p (k t) -> p k t", k=8)
        b_nbf = [Buf() for _ in range(8)]
        s1k = [ar.f32(1024).rearrange("p (h n) -> p h n", h=8) for _ in range(2)]
        a2k = [ar.f32(1024).rearrange("p (h n) -> p h n", h=8) for _ in range(2)]
        a1t = [ar.f32(128).rearrange("p (h a) -> p h a", h=8) for _ in range(2)]
        top1k = [ar.f32(128).rearrange("p (h a) -> p h a", h=8) for _ in range(2)]
        theta = [ar.f32(8) for _ in range(2)]
        b_keep = [Buf() for _ in range(2)]
        zer = ar.bf(512)
        b_zer = Buf()
        P.op("pool", lambda e: e.memset(zer, 0.0), [], [b_zer])
        base = ar.off
        blocks = tblocks(0, S if last else T, 256)
        b_wT = [Buf(), Buf()]
        for (t0, Tn) in blocks:
            jj = 0 if t0 < S else 1
            P.barrier()
            ar.off = base
            mt = mod_tmps(256)
            n32 = ar.f32(8 * 256).rearrange("p (k t) -> p k t", k=8)
            b_n32 = [Buf() for _ in range(8)]
            wq = [(ar.f32(8 * 128).rearrange("p (k m) -> p k m", k=8), Buf()) for _ in range(2)]
            qT = ar.f32(16 * 256).rearrange("p (h t) -> p h t", h=16)
            b_qT = Buf()
            s_sb = ar.f32(2048).rearrange("p (h n) -> p h n", h=16)
            b_s = Buf()
            tmpm = ar.f32(256)
            b_tm = Buf()
            tmpm2 = ar.f32(256)
            b_tm2 = Buf()
            top = ar.f32(256).rearrange("p (h a) -> p h a", h=16)
            b_top = Buf()
            cand = ar.f32(2048).rearrange("p (h a b) -> p h a b", h=8, a=16)
            b_cand = Buf()
            ctop = ar.f32(192).rearrange("p (h a) -> p h a", h=8)
            b_ct = Buf()
            misc = ar.f32(16)
            b_mi = Buf()
            ex16 = ar.f32(128).rearrange("p (h a) -> p h a", h=8)

            def emit_ffn(k, t0_, Tn_, tm, b_t, Asc, Bsc):
                act(n32[:, k, :], tm[:, :Tn_], AF.Identity, [b_t, b_sm], [b_n32[k]], bias=Bsc, scale=Asc)
                cp("pool", nbf[:, k, :], n32[:, k, :], [b_n32[k]], [b_nbf[k]])

            modulate(A_ffn, 3, [(t0, Tn)], emit_ffn, mt)
            for hp in range(16):
                w, b_w = wq[hp % 2]
                P.dma(w, wq_d[l][:, hp * 128:(hp + 1) * 128].rearrange("(k p) m -> p k m", p=128), writes=[b_w])
                pq = hp % 2
                for k in range(8):
                    mm(PS[pq][:, :Tn], w[:, k, :], n32[:, k, :], k == 0, k == 7, [b_w, b_n32[k]], [PB[pq]])
                cp("act" if hp % 2 else "dve", qT[:, hp, :], PS[pq][:, :Tn], [PB[pq]], [b_qT])
            for ti in range(2):
                tsl = slice(ti * 128, (ti + 1) * 128)
                bk = b_keep[ti]
                for hp in range(16):
                    pbk = hp // 4
                    mm(PS[pbk][:, (hp % 4) * 128:(hp % 4 + 1) * 128], qT[:, hp, tsl], keysT[:, hp, :], True, True, [b_qT, b_ky], [PB[pbk]])
                for q4 in range(4):
                    cp("act" if q4 % 2 else "dve", s_sb[:, q4 * 4:(q4 + 1) * 4, :], PS[q4][:, :].rearrange("p (h n) -> p h n", h=4), [PB[q4]], [b_s])
                for hp in range(16):
                    P.op("dve", (lambda hp: lambda e: e.max(out=top[:, hp, 0:8], in_=s_sb[:, hp, :]))(hp), [b_s], [b_top])
                    P.op("dve", (lambda hp: lambda e: e.match_replace(out=tmpm[:, 0:128], in_to_replace=top[:, hp, 0:8], in_values=s_sb[:, hp, :], imm_value=-1e30))(hp), [b_s, b_top], [b_tm])
                    P.op("dve", (lambda hp: lambda e: e.max(out=top[:, hp, 8:16], in_=tmpm[:, 0:128]))(hp), [b_tm], [b_top])
                top4 = top.rearrange("p (h c) a -> p h c a", c=2)
                s4v = s_sb.rearrange("p (h c) n -> p h c n", c=2)
                tt("dve", cand, top4[:, :, 0, :].unsqueeze(3).to_broadcast([128, 8, 16, 16]),
                   top4[:, :, 1, :].unsqueeze(2).to_broadcast([128, 8, 16, 16]), ALU.add, [b_top], [b_cand])
                for h in range(8):
                    ch = cand[:, h].rearrange("p a b -> p (a b)")
                    P.op("dve", (lambda h, ch: lambda e: e.max(out=ctop[:, h, 0:8], in_=ch))(h, ch), [b_cand], [b_ct])
                    P.op("dve", (lambda h, ch: lambda e: e.match_replace(out=tmpm, in_to_replace=ctop[:, h, 0:8], in_values=ch, imm_value=-1e30))(h, ch), [b_cand, b_ct], [b_tm])
                    P.op("dve", (lambda h: lambda e: e.max(out=ctop[:, h, 8:16], in_=tmpm))(h), [b_tm], [b_ct])
                    P.op("dve", (lambda h: lambda e: e.match_replace(out=tmpm2, in_to_replace=ctop[:, h, 8:16], in_values=tmpm, imm_value=-1e30))(h), [b_tm, b_ct], [b_tm2])
                    P.op("dve", (lambda h: lambda e: e.max(out=ctop[:, h, 16:24], in_=tmpm2))(h), [b_tm2], [b_ct])
                tt("dve", ex16, ctop[:, :, 0:16], ctop[:, :, 0:1].to_broadcast([128, 8, 16]), ALU.subtract, [b_ct], [b_mi])
                act(ex16, ex16, AF.Exp, [b_mi], [b_mi])
                P.op("dve", lambda e: e.reduce_sum(out=misc[:, 8:16], in_=ex16, axis=AX.X), [b_mi], [b_mi])
                recip(misc[:, 8:16], misc[:, 8:16], [b_mi], [b_mi])
                m8 = misc[:, 0:8].unsqueeze(2)
                tt("dve", m8, ctop[:, :, 15:16], ctop[:, :, 16:17], ALU.add, [b_ct, b_mi], [b_mi])
                stt(m8, m8, 0.5, ctop[:, :, 0:1], ALU.mult, ALU.subtract, [b_mi, b_ct], [b_mi])
                act(misc[:, 0:8], misc[:, 0:8], AF.Exp, [b_mi], [b_mi])
                tt("dve", theta[ti], misc[:, 0:8], misc[:, 8:16], ALU.mult, [b_mi], [bk])
                cp("pool", s1k[ti], s4v[:, :, 0, :], [b_s], [bk])
                cp("pool", top1k[ti], top4[:, :, 0, :], [b_top], [bk])
                tt("dve", a2k[ti], s4v[:, :, 1, :], top4[:, :, 1, 0:1].to_broadcast([128, 8, 128]), ALU.subtract, [b_s, b_top], [bk])
                act(a2k[ti], a2k[ti], AF.Exp, [bk], [bk])
                tt("dve", a1t[ti], top4[:, :, 0, :], top4[:, :, 0, 0:1].to_broadcast([128, 8, 16]), ALU.subtract, [b_top], [bk])
                act(a1t[ti], a1t[ti], AF.Exp, [bk], [bk])
                tt("dve", a1t[ti], a1t[ti], misc[:, 8:16].unsqueeze(2).to_broadcast([128, 8, 16]), ALU.mult, [bk, b_mi], [bk])
            P.barrier()
            ar.off = base
            pmt = ar.f32(2048).rearrange("p (h a e) -> p h a e", h=8, a=16)
            b_pm = Buf()
            csl = [(ar.bf(2048).rearrange("p (h a e) -> p h a e", h=8, a=16), Buf()) for _ in range(2)]
            CT = ar.bf(128 * 128).rearrange("p (e t) -> p e t", e=128)
            b_CT = Buf()
            osl = ar.bf(4096).rearrange("p (h a e) -> p h a e", h=8, a=16)
            b_osl = Buf()
            OTs = [(ar.bf(4096).rearrange("p (e t) -> p e t", e=32), Buf()) for _ in range(2)]
            WTs = [(ar.bf(4096).rearrange("p (e t) -> p e t", e=32), Buf()) for _ in range(2)]
            PSb = [PS[i][:, :].bitcast(BF16) for i in range(8)]
            evc = 0
            for ti in range(2):
                bk = b_keep[ti]
                for es in range(8):
                    tt("pool", pmt, a1t[ti].unsqueeze(3).to_broadcast([128, 8, 16, 16]),
                       a2k[ti][:, :, es * 16:(es + 1) * 16].unsqueeze(2).to_broadcast([128, 8, 16, 16]), ALU.mult, [bk], [b_pm])
                    cs, b_cs = csl[es % 2]
                    for h in range(8):
                        pmh = pmt[:, h].rearrange("p a e -> p (a e)")
                        stt(cs[:, h].rearrange("p a e -> p (a e)"), pmh, theta[ti][:, h:h + 1], pmh, ALU.is_ge, ALU.mult, [b_pm, bk], [b_cs])
                    csf = cs.rearrange("p h a e -> p (h a) e")
                    for half in range(2):
                        pb = 2 + (es * 2 + half) % 2
                        for e in range(8):
                            tr(PSb[pb][:, e * 128:(e + 1) * 128], csf[:, :, half * 8 + e], identb, [b_cs, b_cst], [PB[pb]])
                        e0 = es * 16 + half * 8
                        cp("act" if half else "dve", CT[:, e0:e0 + 8, :], PSb[pb][:, :].rearrange("p (e t) -> p e t", e=8), [PB[pb]], [b_CT])
                for r in range(4):
                    tt("dve", osl, s1k[ti][:, :, r * 32:(r + 1) * 32].unsqueeze(2).to_broadcast([128, 8, 16, 32]),
                       top1k[ti].unsqueeze(3).to_broadcast([128, 8, 16, 32]), ALU.is_equal, [bk], [b_osl])
                    osf = osl.rearrange("p h a e -> p (h a) e")
                    ot, b_ot = OTs[r % 2]
                    for q in range(4):
                        pb = 4 + q % 2
                        for e in range(8):
                            tr(PSb[pb][:, e * 128:(e + 1) * 128], osf[:, :, q * 8 + e], identb, [b_osl, b_cst], [PB[pb]])
                        cp("act" if q % 2 else "dve", ot[:, q * 8:(q + 1) * 8, :], PSb[pb][:, :].rearrange("p (e t) -> p e t", e=8), [PB[pb]], [b_ot])
                    wt, b_wt = WTs[r % 2]
                    for tg in range(8):
                        pb = 6 + tg % 2
                        for tk in range(16):
                            t_ = tg * 16 + tk
                            mm(PS[pb][:, tk * 32:(tk + 1) * 32], CT[:, :, t_], ot[:, :, t_], True, True, [b_CT, b_ot], [PB[pb]])
                        cp("dve" if evc % 2 else "act", wt[:, :, tg * 16:(tg + 1) * 16], PS[pb][:, :].rearrange("p (t e) -> p e t", t=16), [PB[pb]], [b_wt])
                        evc += 1
                    P.dma(wT_s[ti][:, r * 32:(r + 1) * 32, :], wt, reads=[b_wt], writes=[b_wT[ti]])
            P.barrier()
            ar.off = base
            ub = [(ar.bf(8 * 512).rearrange("p (k e) -> p k e", k=8), Buf()) for _ in range(2)]
            vbf = [(ar.bf(4 * 1024).rearrange("p (e d) -> p e d", e=4), Buf()) for _ in range(2)]
            Wt = [(ar.bf(4 * 256).rearrange("p (e t) -> p e t", e=4), Buf()) for _ in range(2)]
            gel = [(ar.f32(512), Buf()) for _ in range(2)]
            GT = [(ar.bf(512).rearrange("p (e t) -> p e t", e=2), Buf()) for _ in range(2)]
            for pb in range(4, 8):
                mm(PS[pb][:, :], zer[:, 0:128], zer[:, 0:512], True, False, [b_zer], [PB[pb]])
            pend = None
            for g in range(32):
                u_, b_u = ub[g % 2]
                v_, b_vv = vbf[g % 2]
                w_, b_w_ = Wt[g % 2]
                P.dma(u_, uTb_s[g], reads=[b_ub], writes=[b_u])
                P.dma(v_, vb_s[:, g * 4:(g + 1) * 4, :], reads=[b_vb], writes=[b_vv])
                for ti in range(2):
                    P.dma(w_[:, :, ti * 128:(ti + 1) * 128], wT_s[ti][:, g * 4:(g + 1) * 4, :], reads=[b_wT[ti]], writes=[b_w_])
                for sub in range(2):
                    it = g * 2 + sub
                    ph = it % 2
                    for i in range(2):
                        chn = sub * 2 + i
                        for k in range(8):
                            mm(PS[ph][:, i * 256:(i + 1) * 256], u_[:, k, chn * 128:(chn + 1) * 128], nbf[:, k, :], k == 0, k == 7, [b_u, b_nbf[k]], [PB[ph]])
                    ge, b_ge = gel[it % 2]
                    gt, b_gt = GT[it % 2]
                    act(ge, PS[ph][:, :], AF.Gelu, [PB[ph]], [b_ge])
                    tt("dve", gt.rearrange("p e t -> p (e t)"), ge, w_[:, sub * 2:(sub + 1) * 2, :].rearrange("p e t -> p (e t)"), ALU.mult, [b_ge, b_w_], [b_gt])
                    if pend is not None:
                        pend()

                    def mk(v_=v_, gt=gt, sub=sub, b_vv=b_vv, b_gt=b_gt):
                        def f():
                            for dk in range(8):
                                pbo = 4 + dk // 2
                                for i in range(2):
                                    mm(PS[pbo][:, (dk % 2) * 256:(dk % 2 + 1) * 256], v_[:, sub * 2 + i, dk * 128:(dk + 1) * 128], gt[:, i, :], False, False, [b_vv, b_gt], [PB[pbo]])
                        return f
                    pend = mk()
            pend()
            for dk in range(8):
                pbo = 4 + dk // 2
                xb = b_x[dk][t0 // 512]
                stt(xT[:, dk, t0:t0 + 256], PS[pbo][:, (dk % 2) * 256:(dk % 2 + 1) * 256], modv[:, 5, dk, jj:jj + 1], xT[:, dk, t0:t0 + 256], ALU.mult, ALU.add, [PB[pbo], xb, b_sm], [xb])

    for l in range(depth):
        layer(l)

    P.barrier()
    fin = []
    for k in range(8):
        fin.append(P.dma(yT_d[:, k, :], xT[:, k, 0:S], reads=b_x[k]))
    for name, (src, shape) in dbg_out.items():
        pass
    P.emit(list(P.dmas[-8:]))
    es.close()
    return nc

import ml_dtypes
_CONST = {}


def _consts():
    if _CONST:
        return _CONST
    f64 = np.float64
    c = np.zeros((6, 128, 128), f64)
    c[0] = np.eye(128)
    c[1] = 1.0
    c[2, :64, :64] = 1.0
    c[2, 64:, 64:] = 1.0
    for base in range(0, 128, 32):
        for d in range(16):
            c[3, base + d + 16, base + d] = -1.0
            c[3, base + d, base + d + 16] = 1.0
    ci = np.arange(64)
    ang = 2 * np.pi * np.outer(ci, ci) / 64.0
    for b in range(2):
        c[4, b * 64:(b + 1) * 64, b * 64:(b + 1) * 64] = np.cos(ang)
        c[5, b * 64:(b + 1) * 64, b * 64:(b + 1) * 64] = np.sin(ang)
    _CONST["cst"] = np.ascontiguousarray(c.transpose(1, 0, 2)).astype(np.float32)
    t = np.arange(S)
    row = (t // 64).astype(f64)
    col = (t % 64).astype(f64)
    inv = 10000.0 ** (-np.arange(0, 32, 2, dtype=f64) / 32.0)
    d = np.arange(128) % 64
    pos = np.where((d // 32)[:, None] == 0, row[None, :], col[None, :])
    a = pos * inv[d % 16][:, None]
    _CONST["rope"] = np.stack([np.cos(a), np.sin(a)]).astype(np.float32)
    for L in (S, C):
        p = np.arange(L, dtype=f64)
        tt_ = p / max(L - 1, 1)
        w = 2.0 * np.pi * p / L
        fr = np.linspace(1e-4, 15, 16)
        feats = np.concatenate([tt_[:, None], np.cos(w[:, None] * fr), -np.sin(w[:, None] * fr)], axis=-1)
        _CONST["feats%d" % L] = np.ascontiguousarray(feats.T).astype(np.float32)
        deltas = np.abs(np.linspace(math.log(1e-2) / 1.5, math.log(1e-2) / 0.3, 256))
        _CONST["dec%d" % L] = np.exp(-tt_[:, None] * deltas[None, :]).astype(np.float32)
        nj = L // 128
        s_ = np.arange(L)
        kk = np.outer(s_, 2 * s_ + 1) % (4 * L)
        angF = np.pi * kk / (2.0 * L)
        TcF = np.cos(angF)
        TsF = -np.sin(angF)
        tF = np.stack([TcF, TsF]).reshape(2, nj, 128, nj, 128).transpose(0, 3, 2, 1, 4)
        _CONST["tF%d" % L] = np.ascontiguousarray(tF).astype(np.float32).astype(ml_dtypes.bfloat16)
        TB = min(512, L)
        G = min(4, nj)
        nTB = L // TB
        nG = nj // G
        TcI = TcF.T / L
        TsI = TsF.T / L
        def rl(M):
            return M.reshape(nG, G, 128, nTB, TB).transpose(3, 0, 2, 1, 4)
        _CONST["tI%d" % L] = np.ascontiguousarray(np.stack([rl(TcI), rl(TsI)])).astype(np.float32).astype(ml_dtypes.bfloat16)
        k2 = np.outer(s_, s_) % L
        ang2 = 2 * np.pi * k2 / L
        sc_ = 1.0 / math.sqrt(64.0 * L)
        _CONST["tN%d" % L] = np.ascontiguousarray(np.stack([rl(np.cos(ang2) * sc_), rl(-np.sin(ang2) * sc_)])).astype(np.float32).astype(ml_dtypes.bfloat16)
    return _CONST


def _prep(inp):
    f = lambda a: np.ascontiguousarray(np.asarray(a, dtype=np.float32))
    w = {}
    w["w_ada"] = f(inp["w_ada"])
    w["bada"] = f(np.asarray(inp["b_ada"]).reshape(2, 6, 8, 128).transpose(0, 3, 1, 2))
    w["gmf"] = f(np.stack([np.asarray(inp["g_mix"]).reshape(2, 8, 128), np.asarray(inp["g_ffn"]).reshape(2, 8, 128)], axis=1).transpose(0, 3, 1, 2))
    win = np.asarray(inp["w_in"])
    w["w_in"] = f(win)
    w["w_gate"] = f(win[:, :, 2560:].reshape(2, 8, 128, 3, 8, 128).transpose(0, 4, 2, 1, 3, 5))
    hcw = np.concatenate([np.asarray(inp["hy_conv_w"]), np.asarray(inp["hy_conv_b"])[:, None, :]], axis=1)
    w["hcw"] = f(hcw.reshape(2, 4, 6, 128).transpose(0, 3, 2, 1))
    w["hy_w1"] = f(inp["hy_w1"])
    w["hy_fb"] = f(np.stack([inp["hy_b1"], inp["hy_freq"], inp["hy_b2"]], axis=-1))
    w["hy_w2"] = f(inp["hy_w2"])
    w["hy_w3"] = f(inp["hy_w3"])
    w["hy_bias"] = f(np.asarray(inp["hy_bias"]).reshape(2, 2, 2, 128).transpose(0, 3, 1, 2))
    w["gqk"] = f(np.stack([np.asarray(inp["g_q"]).reshape(2, 128), np.asarray(inp["g_k"]).reshape(2, 128)], axis=-1))
    w["lam"] = f(np.asarray(inp["lam"]).reshape(2, 1, 256))
    w["gsub"] = f(np.asarray(inp["g_sub"]).reshape(2, 128, 1))
    w["w_br"] = f(np.concatenate([inp["w_hy"], inp["w_fn"], inp["w_at"]], axis=1))
    w["w_out"] = f(inp["w_out"])
    w["peer_wq"] = f(inp["peer_wq"])
    w["keysT"] = f(np.asarray(inp["peer_keys"]).reshape(2, 16, 128, 128).transpose(0, 3, 1, 2))
    w["uT"] = f(np.asarray(inp["peer_u"]).transpose(0, 2, 1))
    w["peer_v"] = f(inp["peer_v"])
    w.update(_consts())
    return w


def _core_inputs(inp, b):
    X = np.concatenate([np.asarray(inp["x"][b]), np.asarray(inp["ctx"][b])], axis=0)
    xT = np.ascontiguousarray(X.T.reshape(8, 128, T).transpose(1, 0, 2)).astype(np.float32)
    cc = np.stack([np.asarray(inp["c"][b]), np.asarray(inp["c_ctx"])], axis=-1)
    cc = np.ascontiguousarray(cc.reshape(8, 128, 2).transpose(1, 0, 2)).astype(np.float32)
    return {"xT": xT, "cc": cc}


_NC = {}


def kernel(**inp):
    w = _prep(inp)
    if "nc" not in _NC:
        _NC["nc"] = build()
    nc = _NC["nc"]
    in_maps = []
    for b in range(8):
        m = dict(w)
        m.update(_core_inputs(inp, b))
        in_maps.append(m)
    res = run_bass_kernel_spmd(nc, in_maps, core_ids=list(range(8)))
    out = np.empty((8, S, D), np.float32)
    for b in range(8):
        yT = np.asarray(res.results[b]["yT"])
        out[b] = yT.transpose(2, 1, 0).reshape(S, D)
    return out
```

```python
import numpy as np, math
from contextlib import ExitStack
import concourse.bass as bass
import concourse.mybir as mybir
from concourse.bass_utils import run_bass_kernel_spmd

F32 = mybir.dt.float32
BF16 = mybir.dt.bfloat16
ALU = mybir.AluOpType
AF = mybir.ActivationFunctionType
AX = mybir.AxisListType

NSLOT = 40
ENGS = ("pe", "act", "dve", "pool", "sp")


class Buf:
    __slots__ = ("w", "rs", "rd")

    def __init__(self):
        self.w = None
        self.rs = {}
        self.rd = []


class Op:
    __slots__ = ("eng", "fn", "deps", "sig", "cnt", "slot", "dma")

    def __init__(self, eng, fn, dma=False):
        self.eng = eng
        self.fn = fn
        self.deps = ()
        self.sig = False
        self.cnt = 0
        self.slot = -1
        self.dma = dma


class Prog:
    def __init__(self, nc):
        self.nc = nc
        self.streams = {e: [] for e in ENGS}
        self.dmas = []
        self.live_dmas = []
        self.last_real = {e: None for e in ENGS}

    def op(self, eng, fn, reads=(), writes=(), dma=False):
        o = Op(eng, fn, dma)
        deps = set()
        for b in reads:
            if b.w is not None:
                deps.add(b.w)
        for b in writes:
            if b.w is not None:
                deps.add(b.w)
            deps.update(b.rs.values())
            deps.update(b.rd)
        if eng == "pe" and not dma:
            deps = {d for d in deps if d.dma or d.eng != "pe"}
        o.deps = deps
        for b in writes:
            b.w = o
            b.rs = {}
            b.rd = []
        for b in reads:
            if dma:
                b.rd.append(o)
            else:
                b.rs[eng] = o
        self.streams[eng].append(o)
        if not dma:
            self.last_real[eng] = o
        if dma:
            self.dmas.append(o)
            self.live_dmas.append(o)
        return o

    def dma(self, out, in_, reads=(), writes=(), eng="sp"):
        return self.op(eng, lambda e: e.dma_start(out=out, in_=in_), reads, writes, dma=True)

    def barrier(self):
        last = dict(self.last_real)
        live = list(self.live_dmas)
        self.live_dmas = []
        for e in ENGS:
            o = Op(e, None)
            o.deps = {last[x] for x in ENGS if x != e and last[x] is not None}
            o.deps.update(live)
            self.streams[e].append(o)

    def emit(self, final_dmas):
        nc = self.nc
        for e in ENGS:
            for o in self.streams[e]:
                for d in o.deps:
                    d.sig = True
        with ExitStack() as es:
            sems = {e: es.enter_context(nc.semaphore("s_" + e)) for e in ENGS}
            dsem = [es.enter_context(nc.semaphore("d%d" % i)) for i in range(NSLOT)]
            for e in ENGS:
                c = 0
                for o in self.streams[e]:
                    if o.dma:
                        continue
                    if o.sig:
                        c += 1
                        o.cnt = c
            slot_cnt = [0] * NSLOT
            slot_prev = [None] * NSLOT
            for i, o in enumerate(self.dmas):
                s = i % NSLOT
                o.slot = s
                slot_cnt[s] += 16
                o.cnt = slot_cnt[s]
                if slot_prev[s] is not None:
                    o.deps = set(o.deps)
                    o.deps.add(slot_prev[s])
                slot_prev[s] = o
            block = es.enter_context(nc.Block())

            def run(ename, eng):
                waited = {}
                for o in self.streams[ename]:
                    for d in o.deps:
                        sem = dsem[d.slot] if d.dma else sems[d.eng]
                        if waited.get(sem.name, 0) >= d.cnt:
                            continue
                        eng.wait_ge(sem, d.cnt)
                        waited[sem.name] = d.cnt
                    if o.fn is None:
                        continue
                    ins = o.fn(eng)
                    if o.dma:
                        ins.then_inc(dsem[o.slot], 16)
                    elif o.sig:
                        ins.then_inc(sems[ename], 1)
                if ename == "sp":
                    for d in final_dmas:
                        eng.wait_ge(dsem[d.slot], d.cnt)

            @block.sync
            def _(e):
                run("sp", e)

            @block.tensor
            def _(e):
                run("pe", e)

            @block.scalar
            def _(e):
                run("act", e)

            @block.vector
            def _(e):
                run("dve", e)

            @block.gpsimd
            def _(e):
                run("pool", e)

D = 1024
S = 2048
C = 256
T = S + C
EPS = 1e-6
PI = math.pi
NE = 16384


def tblocks(lo, hi, step=512):
    return [(t, min(step, hi - t)) for t in range(lo, hi, step)]


def build(depth=2, stages=None, dbg=()):
    nc = bass.Bass("TRN2", target_bir_lowering=False)
    P = Prog(nc)

    def din(name, shape, dt=F32):
        return nc.dram_tensor(name, list(shape), dt, kind="ExternalInput").ap()

    def dscr(name, shape, dt=F32):
        if name in dbg:
            return nc.dram_tensor(name, list(shape), dt, kind="ExternalOutput").ap()
        return nc.dram_tensor(name, list(shape), dt).ap()

    xT_d = din("xT", [128, 8, T])
    cc_d = din("cc", [128, 8, 2])
    w_ada_d = din("w_ada", [2, D, 6 * D])
    bada_d = din("bada", [2, 128, 6, 8])
    gmf_d = din("gmf", [2, 128, 2, 8])
    w_in_d = din("w_in", [2, D, 5632])
    w_gate_d = din("w_gate", [2, 8, 128, 8, 3, 128])
    hcw_d = din("hcw", [2, 128, 6, 4])
    hy_w1_d = din("hy_w1", [2, 33, 64])
    hy_fb_d = din("hy_fb", [2, 64, 3])
    hy_w2_d = din("hy_w2", [2, 64, 64])
    hy_w3_d = din("hy_w3", [2, 64, 1024])
    hy_bias_d = din("hy_bias", [2, 128, 2, 2])
    gqk_d = din("gqk", [2, 128, 2])
    lam_d = din("lam", [2, 1, 256])
    gsub_d = din("gsub", [2, 128, 1])
    w_br_d = din("w_br", [2, D, D])
    w_out_d = din("w_out", [2, D, D])
    wq_d = din("peer_wq", [2, D, 2048])
    keysT_d = din("keysT", [2, 128, 16, 128])
    uT_d = din("uT", [2, D, NE])
    v_d_in = din("peer_v", [2, NE, D])
    cst_d = din("cst", [128, 6, 128])
    rope_d = din("rope", [2, 128, S])
    feats_d = {L: din("feats%d" % L, [33, L]) for L in (S, C)}
    dec_d = {L: din("dec%d" % L, [L, 256]) for L in (S, C)}
    tF_d = {L: din("tF%d" % L, [2, L // 128, 128, L // 128, 128], BF16) for L in (S, C)}
    RL = {}
    for L in (S, C):
        TB = min(512, L)
        G = min(4, L // 128)
        RL[L] = (TB, G, L // TB, (L // 128) // G)
    tI_d = {L: din("tI%d" % L, [2, RL[L][2], RL[L][3], 128, RL[L][1], RL[L][0]], BF16) for L in (S, C)}
    tN_d = {L: din("tN%d" % L, [2, RL[L][2], RL[L][3], 128, RL[L][1], RL[L][0]], BF16) for L in (S, C)}
    yT_d = nc.dram_tensor("yT", [128, 8, S], F32, kind="ExternalOutput").ap()
    hy_s = dscr("hy_s", [6, 128, T])
    fn_s = dscr("fn_s", [2, 128, T])
    q_s = dscr("q_s", [4, 128, T], BF16)
    k_s = dscr("k_s", [4, 128, T], BF16)
    v_s = dscr("v_s", [128, 18, 512], BF16)
    kf_s = {L: dscr("kf_s%d" % L, [2, L // 128, 128, 512]) for L in (S, C)}
    hyo_s = dscr("hyo_s", [2, 128, T], BF16)
    fno_s = dscr("fno_s", [2, 128, T], BF16)
    ato_s = dscr("ato_s", [4, 128, T], BF16)
    nT_s = dscr("nT_s", [128, 8, T], BF16)
    uTb_s = dscr("uTb_s", [32, 128, 8, 512], BF16)
    vb_s = dscr("vb_s", [128, 128, D], BF16)
    wT_s = dscr("wT_s", [2, 128, 128, 128], BF16)
    dbg_out = {}

    es = ExitStack()
    xT = es.enter_context(nc.sbuf_tensor("xT_sb", [128, 8, T], F32))
    cst = es.enter_context(nc.sbuf_tensor("cst_sb", [128, 6, 128], F32))
    cstb = es.enter_context(nc.sbuf_tensor("cstb_sb", [128, 2, 128], BF16))
    sm = es.enter_context(nc.sbuf_tensor("small_sb", [128, 512], F32))
    ARENA = 33280
    AR = es.enter_context(nc.sbuf_tensor("arena", [128, ARENA], F32))
    PS = [es.enter_context(nc.psum_tensor("ps%d" % i, [128, 512], F32)) for i in range(8)]
    PB = [Buf() for _ in range(8)]
    ident32 = cst[:, 0, :]
    ones32 = cst[:, 1, :]
    bd64 = cst[:, 2, :]
    rot = cst[:, 3, :]
    bdc = cst[:, 4, :]
    bds = cst[:, 5, :]
    identb = cstb[:, 0, :]
    onesb = cstb[:, 1, :]
    b_cst = Buf()
    b_x = [[Buf() for _ in range(5)] for _ in range(8)]
    b_sm = Buf()
    sc = sm[:, 0:16].rearrange("p (k j) -> p k j", k=8)
    modv = sm[:, 16:112].rearrange("p (g m j) -> p g m j", g=6, m=8)
    A_mix = sm[:, 112:128].rearrange("p (k j) -> p k j", k=8)
    A_ffn = sm[:, 128:144].rearrange("p (k j) -> p k j", k=8)
    gmf = sm[:, 144:160].rearrange("p (a k) -> p a k", a=2)
    tmp16 = sm[:, 160:176].rearrange("p (k j) -> p k j", k=8)
    gqk = sm[:, 176:178]
    gsub_s = sm[:, 178:179]
    neg_lam = sm[:, 179:180]
    lamw = sm[:, 180:182]
    hcw = sm[:, 184:208].rearrange("p (c t) -> p c t", c=6)
    hbias = sm[:, 208:212].rearrange("p (o c) -> p o c", o=2)
    fb = sm[:, 212:217]
    bada = sm[:, 224:272].rearrange("p (g m) -> p g m", g=6)
    lamt = sm[:, 272:400]
    epsc = sm[:, 183:184]

    class Arena:
        def __init__(self):
            self.off = 0

        def reset(self):
            P.barrier()
            self.off = 0

        def f32(self, n):
            a = AR[:, self.off:self.off + n]
            self.off += n
            assert self.off <= ARENA, self.off
            return a

        def bf(self, n):
            w = (n + 1) // 2
            return self.f32(w).bitcast(BF16)[:, 0:n]

    ar = Arena()

    def mm(out, lhsT, rhs, start, stop, rd, wr):
        P.op("pe", lambda e: e.matmul(out, lhsT=lhsT, rhs=rhs, start=start, stop=stop), rd, wr)

    def tr(out, in_, idn, rd, wr):
        P.op("pe", lambda e: e.transpose(out=out, in_=in_, identity=idn), rd, wr)

    def act(out, in_, func, rd, wr, bias=0.0, scale=1.0):
        P.op("act", lambda e: e.activation(out=out, in_=in_, func=func, bias=bias, scale=scale), rd, wr)

    def tt(eng, out, in0, in1, op, rd, wr):
        P.op(eng, lambda e: e.tensor_tensor(out=out, in0=in0, in1=in1, op=op), rd, wr)

    def ts(eng, out, in0, s1, s2, op0, op1, rd, wr):
        if s2 is None:
            P.op(eng, lambda e: e.tensor_scalar(out=out, in0=in0, scalar1=s1, scalar2=None, op0=op0), rd, wr)
        else:
            P.op(eng, lambda e: e.tensor_scalar(out=out, in0=in0, scalar1=s1, scalar2=s2, op0=op0, op1=op1), rd, wr)

    def stt(out, in0, scalar, in1, op0, op1, rd, wr):
        P.op("dve", lambda e: e.scalar_tensor_tensor(out=out, in0=in0, scalar=scalar, in1=in1, op0=op0, op1=op1), rd, wr)

    def cp(eng, out, in_, rd, wr):
        if eng == "act":
            act(out, in_, AF.Copy, rd, wr)
        else:
            P.op(eng, lambda e: e.tensor_copy(out=out, in_=in_), rd, wr)

    def recip(out, in_, rd, wr):
        P.op("dve", lambda e: e.reciprocal(out=out, in_=in_), rd, wr)

    def rsqrt_mean(out, in_, n, rd, wr):
        act(out, in_, AF.Sqrt, list(rd) + [b_sm], wr, bias=epsc, scale=1.0 / n)
        recip(out, out, wr, wr)

    P.dma(cst[:, :, :], cst_d, writes=[b_cst])
    for k in range(8):
        P.dma(xT[:, k, :], xT_d[:, k, :], writes=b_x[k])
    P.dma(sc, cc_d, writes=[b_sm])
    cp("dve", cstb[:, 0, :], ident32, [b_cst], [b_cst])
    cp("dve", cstb[:, 1, :], ones32, [b_cst], [b_cst])
    act(sc, sc, AF.Silu, [b_sm], [b_sm])
    P.op("dve", lambda e: e.memset(epsc, EPS), [], [b_sm])
    ar.reset()

    def want(name):
        return stages is None or name in stages

    def layer(l):
        last = l == depth - 1
        lam_init = 0.8 - 0.6 * math.exp(-0.3 * l)
        streams = [(0, S)] if last else [(0, S), (S, C)]
        tb_all = tblocks(0, T)
        tb_mix = tblocks(0, S) if last else tb_all

        ar.reset()
        P.dma(bada, bada_d[l], writes=[b_sm])
        P.dma(gmf, gmf_d[l], writes=[b_sm])
        P.dma(gqk, gqk_d[l], writes=[b_sm])
        P.dma(gsub_s, gsub_d[l], writes=[b_sm])
        P.dma(hcw, hcw_d[l], writes=[b_sm])
        P.dma(hbias, hy_bias_d[l], writes=[b_sm])
        P.dma(fb[0:64, 0:3], hy_fb_d[l], writes=[b_sm])
        P.dma(lamt, lam_d[l][:, 0:128].partition_broadcast(128), writes=[b_sm])
        lam2 = ar.f32(128)
        b_l2 = Buf()
        P.dma(lam2, lam_d[l][:, 128:256].partition_broadcast(128), writes=[b_l2])
        wts = [(ar.f32(8 * 1024), Buf()) for _ in range(2)]
        for g in range(6):
            wt, bw = wts[g % 2]
            wt3 = wt.rearrange("p (k m) -> p k m", k=8)
            P.dma(wt3, w_ada_d[l][:, g * 1024:(g + 1) * 1024].rearrange("(k p) m -> p k m", p=128), writes=[bw])
            for m in range(8):
                for k in range(8):
                    mm(PS[0][:, m * 2:m * 2 + 2], wt3[:, k, m * 128:(m + 1) * 128], sc[:, k, :], k == 0, k == 7, [bw, b_sm], [PB[0]])
            tt("dve", modv[:, g], PS[0][:, 0:16].rearrange("p (m j) -> p m j", m=8),
               bada[:, g, :].unsqueeze(2).to_broadcast([128, 8, 2]), ALU.add, [PB[0], b_sm], [b_sm])
        for (Aap, gi, si) in ((A_mix, 0, 1), (A_ffn, 1, 4)):
            ts("dve", tmp16, modv[:, si], 1.0, None, ALU.add, None, [b_sm], [b_sm])
            tt("dve", Aap, tmp16, gmf[:, gi, :].unsqueeze(2).to_broadcast([128, 8, 2]), ALU.mult, [b_sm], [b_sm])
        tt("dve", lamt[:, 0:64], lamt[:, 0:64], lamt[:, 64:128], ALU.mult, [b_sm], [b_sm])
        tt("dve", lam2[:, 0:64], lam2[:, 0:64], lam2[:, 64:128], ALU.mult, [b_l2], [b_l2])
        P.op("dve", lambda e: e.reduce_sum(out=lamw[:, 0:1], in_=lamt[:, 0:64], axis=AX.X), [b_sm], [b_sm])
        P.op("dve", lambda e: e.reduce_sum(out=lamw[:, 1:2], in_=lam2[:, 0:64], axis=AX.X), [b_l2, b_sm], [b_sm])
        act(lamw, lamw, AF.Exp, [b_sm], [b_sm])
        tt("dve", neg_lam, lamw[:, 1:2], lamw[:, 0:1], ALU.subtract, [b_sm], [b_sm])
        ts("dve", neg_lam, neg_lam, -lam_init, None, ALU.add, None, [b_sm], [b_sm])
        ts("dve", gsub_s, gsub_s, 1.0 - lam_init, None, ALU.mult, None, [b_sm], [b_sm])
        tt("dve", fb[0:64, 3:4], fb[0:64, 0:1], fb[0:64, 1:2], ALU.mult, [b_sm], [b_sm])
        tt("dve", fb[0:64, 4:5], fb[0:64, 2:3], fb[0:64, 1:2], ALU.mult, [b_sm], [b_sm])

        def mod_tmps(w, n=(3, 2, 3)):
            return dict(sq=[(ar.f32(w), Buf()) for _ in range(n[0])], rs=[(ar.f32(w), Buf()) for _ in range(n[1])],
                        tm=[(ar.f32(w), Buf()) for _ in range(n[2])], c=[0, 0, 0])

        def modulate(Aap, gB, blocks, emit_cb, mt, pbk=7):
            for (t0, Tn) in blocks:
                j = 0 if t0 < S else 1
                xb = t0 // 512
                for k in range(8):
                    sq, b_sq = mt["sq"][mt["c"][0] % len(mt["sq"])]
                    mt["c"][0] += 1
                    act(sq[:, :Tn], xT[:, k, t0:t0 + Tn], AF.Square, [b_x[k][xb]], [b_sq])
                    mm(PS[pbk][:, :Tn], ones32, sq[:, :Tn], k == 0, k == 7, [b_sq, b_cst], [PB[pbk]])
                r, b_r = mt["rs"][mt["c"][1] % len(mt["rs"])]
                mt["c"][1] += 1
                rsqrt_mean(r[:, :Tn], PS[pbk][:, :Tn], D, [PB[pbk]], [b_r])
                for k in range(8):
                    tm, b_t = mt["tm"][mt["c"][2] % len(mt["tm"])]
                    mt["c"][2] += 1
                    tt("dve", tm[:, :Tn], xT[:, k, t0:t0 + Tn], r[:, :Tn], ALU.mult, [b_x[k][xb], b_r], [b_t])
                    emit_cb(k, t0, Tn, tm, b_t, Aap[:, k, j:j + 1], modv[:, gB, k, j:j + 1])

        def hyena_filters(L):
            ar.reset()
            nj = L // 128
            w1 = ar.f32(64)
            w2 = ar.f32(64)
            w3 = ar.f32(1024)
            b_w = Buf()
            P.dma(w1[0:33, :], hy_w1_d[l], writes=[b_w])
            P.dma(w2[0:64, :], hy_w2_d[l], writes=[b_w])
            P.dma(w3[0:64, :], hy_w3_d[l], writes=[b_w])
            h2T = ar.f32(L)
            b_h2 = Buf()
            off_mlp = ar.off
            ft = [(ar.f32(512), Buf()) for _ in range(2)]
            aa = [(ar.f32(512), Buf()) for _ in range(2)]
            h1 = [(ar.f32(512), Buf()) for _ in range(2)]
            aa2 = [(ar.f32(512), Buf()) for _ in range(2)]
            sx = [(ar.f32(512), Buf()) for _ in range(3)]

            def sin_act(out, a, Tn, b_in, b_out):
                (s4, b_s4), (c4, b_c4), (q, b_q) = sx
                act(s4[0:64, :Tn], a, AF.Sin, [b_in], [b_s4], scale=0.25)
                act(c4[0:64, :Tn], a, AF.Abs, [b_in], [b_c4])
                act(c4[0:64, :Tn], c4[0:64, :Tn], AF.Sin, [b_c4, b_sm], [b_c4], bias=halfpi[0:64, :], scale=-0.25)
                tt("dve", q[0:64, :Tn], s4[0:64, :Tn], s4[0:64, :Tn], ALU.mult, [b_s4], [b_q])
                ts("dve", q[0:64, :Tn], q[0:64, :Tn], -2.0, 1.0, ALU.mult, ALU.add, [b_q], [b_q])
                tt("dve", c4[0:64, :Tn], s4[0:64, :Tn], c4[0:64, :Tn], ALU.mult, [b_s4, b_c4], [b_c4])
                stt(out, c4[0:64, :Tn], 4.0, q[0:64, :Tn], ALU.mult, ALU.mult, [b_c4, b_q], [b_out])

            for bi, (t0, Tn) in enumerate(tblocks(0, L)):
                f, b_f = ft[bi % 2]
                P.dma(f[0:33, :Tn], feats_d[L][:, t0:t0 + Tn], writes=[b_f])
                mm(PS[0][0:64, :Tn], w1[0:33, :], f[0:33, :Tn], True, True, [b_w, b_f], [PB[0]])
                a, b_a = aa[bi % 2]
                ts("dve", a[0:64, :Tn], PS[0][0:64, :Tn], fb[0:64, 1:2], fb[0:64, 3:4], ALU.mult, ALU.add, [PB[0], b_sm], [b_a])
                hh, b_h = h1[bi % 2]
                sin_act(hh[0:64, :Tn], a[0:64, :Tn], Tn, b_a, b_h)
                mm(PS[1][0:64, :Tn], w2[0:64, :], hh[0:64, :Tn], True, True, [b_w, b_h], [PB[1]])
                a2_, b_a2 = aa2[bi % 2]
                ts("dve", a2_[0:64, :Tn], PS[1][0:64, :Tn], fb[0:64, 1:2], fb[0:64, 4:5], ALU.mult, ALU.add, [PB[1], b_sm], [b_a2])
                sin_act(h2T[0:64, t0:t0 + Tn], a2_[0:64, :Tn], Tn, b_a2, b_h2)
            dec = ar.f32(nj * 256).rearrange("p (j c) -> p j c", j=nj)
            off_dec_end = ar.off
            b_dec = Buf()
            P.dma(dec, dec_d[L].rearrange("(j p) c -> p j c", p=128), writes=[b_dec])
            hd = ar.f32(nj * 1024).rearrange("p (j c) -> p j c", j=nj)
            b_hd = [Buf() for _ in range(nj)]
            off_hd = ar.off
            for pc in range(nj):
                for half in range(2):
                    pb = half
                    mm(PS[pb][:, :], h2T[0:64, pc * 128:(pc + 1) * 128], w3[0:64, half * 512:(half + 1) * 512], True, True, [b_h2, b_w], [PB[pb]])
                    tt("dve", hd[:, pc, half * 512:(half + 1) * 512].rearrange("p (o c) -> p o c", o=2),
                       PS[pb][:, :].rearrange("p (o c) -> p o c", o=2),
                       dec[:, pc, :].unsqueeze(1).to_broadcast([128, 2, 256]), ALU.mult, [PB[pb], b_dec], [b_hd[pc]])
            P.op("dve", lambda e: e.memset(hd[0:1, 0, 512:1024], 0.0), [], [b_hd[0]])
            ab = [(ar.f32(1024), Buf()) for _ in range(2)]
            for pc in range(nj):
                a, b_a = ab[pc % 2]
                act(a, hd[:, pc, :], AF.Abs, [b_hd[pc]], [b_a])
                for half in range(2):
                    mm(PS[2 + half][:, :], ones32, a[:, half * 512:(half + 1) * 512], pc == 0, pc == nj - 1, [b_a, b_cst], [PB[2 + half]])
            rn = ar.f32(512)
            b_rn = Buf()
            cp("dve", rn, PS[2][:, :], [PB[2]], [b_rn])
            tt("dve", rn, rn, PS[3][:, :], ALU.add, [b_rn, PB[3]], [b_rn])
            recip(rn, rn, [b_rn], [b_rn])
            tmpe = [(ar.f32(512), Buf()) for _ in range(2)]
            P.barrier()
            ar.off = 0
            hdb = ar.bf(nj * 1024).rearrange("p (j c) -> p j c", j=nj)
            b_hdb = [Buf() for _ in range(nj)]
            tabs = [(ar.bf(2 * nj * 128).rearrange("p (c j f) -> p c j f", c=2, j=nj), Buf()) for _ in range(2)]
            assert ar.off <= off_dec_end
            ar.off = off_hd
            sts = [(ar.f32(1024).rearrange("p (c n) -> p c n", c=2), Buf()) for _ in range(1)]
            for pc in range(nj):
                te, b_te = tmpe[pc % 2]
                hf = hd[:, pc, 0:512]
                hb = hd[:, pc, 512:1024]
                tt("dve", te, hf, hb, ALU.add, [b_hd[pc]], [b_te])
                tt("pool", hb, hf, hb, ALU.subtract, [b_hd[pc]], [b_hd[pc]])
                tt("dve", hdb[:, pc, 0:512], te, rn, ALU.mult, [b_te, b_rn], [b_hdb[pc]])
                tt("pool", hdb[:, pc, 512:1024], hb, rn, ALU.mult, [b_hd[pc], b_rn], [b_hdb[pc]])
            for fc in range(nj):
                tb_, b_tb = tabs[fc % 2]
                for c in range(2):
                    P.dma(tb_[:, c], tF_d[L][c, fc], writes=[b_tb])
                for c in range(2):
                    for jc in range(nj):
                        mm(PS[4 + c][:, :], tb_[:, c, jc, :], hdb[:, jc, c * 512:(c + 1) * 512], jc == 0, jc == nj - 1, [b_tb, b_hdb[jc]], [PB[4 + c]])
                st, b_st = sts[0]
                cp("act", st[:, 0, :], PS[4][:, :], [PB[4]], [b_st])
                cp("dve", st[:, 1, :], PS[5][:, :], [PB[5]], [b_st])
                for c in range(2):
                    P.dma(kf_s[L][c, fc], st[:, c, :], reads=[b_st], writes=[b_kf[L]])

        b_kf = {S: Buf(), C: Buf()}
        halfpi = sm[:, 182:183]
        P.op("dve", lambda e: e.memset(halfpi, PI / 2), [], [b_sm])
        if want("hyf"):
            for (t0, L) in streams:
                hyena_filters(L)

        ar.reset()
        nT = ar.bf(8 * T).rearrange("p (k t) -> p k t", k=8)
        b_n = [[Buf() for _ in range(5)] for _ in range(8)]
        b_nTs = Buf()

        def emit_mix(k, t0, Tn, tm, b_t, Asc, Bsc):
            act(nT[:, k, t0:t0 + Tn], tm[:, :Tn], AF.Identity, [b_t, b_sm], [b_n[k][t0 // 512]], bias=Bsc, scale=Asc)

        mark = ar.off
        if want("proj") or want("merge"):
            modulate(A_mix, 0, tb_all, emit_mix, mod_tmps(512))
            for k in range(8):
                P.dma(nT_s[:, k, :], nT[:, k, :], reads=b_n[k], writes=[b_nTs])
        ar.off = mark
        P.barrier()

        b_hy = Buf()
        b_fn = Buf()
        b_q = Buf()
        b_k = Buf()
        b_v = Buf()
        if want("proj"):
            w32 = [(ar.f32(8 * 512).rearrange("p (k m) -> p k m", k=8), Buf()) for _ in range(1)]
            wbf = [(ar.bf(8 * 512).rearrange("p (k m) -> p k m", k=8), Buf()) for _ in range(2)]
            stg = [(ar.f32(512), Buf()) for _ in range(2)]
            sqt = [(ar.f32(512), Buf()) for _ in range(2)]
            rt = [(ar.f32(512), Buf()) for _ in range(2)]
            xnt = [(ar.f32(512), Buf()) for _ in range(2)]
            t1t = [(ar.f32(512), Buf()) for _ in range(2)]
            t2t = [(ar.f32(512), Buf()) for _ in range(2)]
            obt = [(ar.bf(512), Buf()) for _ in range(2)]
            rope_sb = ar.f32(2 * S).rearrange("p (c t) -> p c t", c=2)
            b_rope = Buf()
            for c in range(2):
                P.dma(rope_sb[:, c, :], rope_d[c], writes=[b_rope])
            cnt = [0]

            def qk_cb(ps, pb, which, h, t0, Tn):
                i = cnt[0] % 2
                cnt[0] += 1
                sq_, b_sq_ = sqt[i]
                act(sq_[:, :Tn], ps[:, :Tn], AF.Square, [pb], [b_sq_])
                mm(PS[6][:, :Tn], bd64, sq_[:, :Tn], True, True, [b_sq_, b_cst], [PB[6]])
                r_, b_r_ = rt[i]
                rsqrt_mean(r_[:, :Tn], PS[6][:, :Tn], 64, [PB[6]], [b_r_])
                xn, b_xn = xnt[i]
                stt(xn[:, :Tn], ps[:, :Tn], gqk[:, which:which + 1], r_[:, :Tn], ALU.mult, ALU.mult, [pb, b_r_, b_sm], [b_xn])
                ob, b_ob = obt[i]
                if t0 < S:
                    mm(PS[5][:, :Tn], rot, xn[:, :Tn], True, True, [b_xn, b_cst], [PB[5]])
                    t1, b_t1 = t1t[i]
                    t2, b_t2 = t2t[i]
                    tt("pool", t1[:, :Tn], xn[:, :Tn], rope_sb[:, 0, t0:t0 + Tn], ALU.mult, [b_xn, b_rope], [b_t1])
                    tt("dve", t2[:, :Tn], PS[5][:, :Tn], rope_sb[:, 1, t0:t0 + Tn], ALU.mult, [PB[5], b_rope], [b_t2])
                    tt("pool", ob[:, :Tn], t1[:, :Tn], t2[:, :Tn], ALU.add, [b_t1, b_t2], [b_ob])
                else:
                    cp("pool", ob[:, :Tn], xn[:, :Tn], [b_xn], [b_ob])
                dst = (q_s if which == 0 else k_s)[h][:, t0:t0 + Tn]
                P.dma(dst, ob[:, :Tn], reads=[b_ob], writes=[b_q if which == 0 else b_k])

            pcnt = [0]
            for g in range(5):
                w3_, b_w3 = w32[0]
                P.dma(w3_, w_in_d[l][:, g * 512:(g + 1) * 512].rearrange("(k p) m -> p k m", p=128), writes=[b_w3])
                wb_, b_wb = wbf[g % 2]
                cp("act", wb_[:, 0:4, :], w3_[:, 0:4, :], [b_w3], [b_wb])
                cp("pool", wb_[:, 4:8, :], w3_[:, 4:8, :], [b_w3], [b_wb])
                if g < 4:
                    for mi in range(4):
                        for (t0, Tn) in tb_all:
                            if g == 2 and t0 >= S and last:
                                continue
                            pi = pcnt[0] % 4
                            pcnt[0] += 1
                            for k in range(8):
                                mm(PS[pi][:, :Tn], wb_[:, k, mi * 128:(mi + 1) * 128], nT[:, k, t0:t0 + Tn], k == 0, k == 7,
                                   [b_wb, b_n[k][t0 // 512]], [PB[pi]])
                            if g < 2:
                                st, b_st = stg[pcnt[0] % 2]
                                cp("act" if pcnt[0] % 2 else "dve", st[:, :Tn], PS[pi][:, :Tn], [PB[pi]], [b_st])
                                ch = g * 4 + mi
                                if ch < 6:
                                    P.dma(hy_s[ch][:, t0:t0 + Tn], st[:, :Tn], reads=[b_st], writes=[b_hy])
                                else:
                                    P.dma(fn_s[ch - 6][:, t0:t0 + Tn], st[:, :Tn], reads=[b_st], writes=[b_fn])
                            else:
                                qk_cb(PS[pi], PB[pi], g - 2, mi, t0, Tn)
                else:
                    for i in range(18):
                        pi = pcnt[0] % 4
                        pcnt[0] += 1
                        for k in range(8):
                            mm(PS[pi][:, :], nT[:, k, i * 128:(i + 1) * 128], wb_[:, k, :], k == 0, k == 7, [b_wb, b_n[k][i // 4]], [PB[pi]])
                        ob, b_ob = obt[i % 2]
                        cp("act" if i % 2 else "dve", ob, PS[pi][:, :], [PB[pi]], [b_ob])
                        P.dma(v_s[:, i, :], ob, reads=[b_ob], writes=[b_v])

        b_ato = Buf()
        if want("attn"):
            ar.reset()
            kT = ar.bf(4 * T).rearrange("p (h t) -> p h t", h=4)
            qT = ar.bf(4 * T).rearrange("p (h t) -> p h t", h=4)
            vv = ar.bf(18 * 512).rearrange("p (j c) -> p j c", j=18)
            b_kT = Buf()
            b_qT = Buf()
            b_vv = Buf()
            for h in range(4):
                P.dma(kT[:, h, :], k_s[h], reads=[b_k], writes=[b_kT])
                P.dma(qT[:, h, :], q_s[h], reads=[b_q], writes=[b_qT])
            P.dma(vv, v_s, reads=[b_v], writes=[b_vv])
            Et = [(ar.bf(512), Buf()) for _ in range(3)]
            r0 = ar.f32(512)
            t0_ = ar.f32(512)
            t1_ = ar.f32(512)
            sq_ = ar.f32(512)
            rr_ = ar.f32(512)
            b_r0, b_t0, b_t1, b_sq2, b_rr = [Buf() for _ in range(5)]
            aob = [(ar.bf(512), Buf()) for _ in range(2)]
            ec = 0
            oc = 0
            qblocks = tblocks(0, S) if last else tb_all
            for h in range(4):
                for (q0, Tn) in qblocks:
                    keys = list(range(18)) if q0 < S else [16, 17]
                    seq = [(c, idx, j) for c in range(2) for idx, j in enumerate(keys)]

                    def s_mm(n_):
                        c, idx, j = seq[n_]
                        sp = (ec + n_) % 2
                        mm(PS[sp][:, :Tn], kT[64 * c:64 * c + 64, h, j * 128:(j + 1) * 128], qT[64 * c:64 * c + 64, h, q0:q0 + Tn],
                           True, True, [b_kT, b_qT], [PB[sp]])

                    s_mm(0)
                    for n_, (c, idx, j) in enumerate(seq):
                        if n_ + 1 < len(seq):
                            s_mm(n_ + 1)
                        sp = (ec + n_) % 2
                        E, b_E = Et[(ec + n_) % 3]
                        act(E[:, :Tn], PS[sp][:, :Tn], AF.Exp, [PB[sp]], [b_E], scale=0.125)
                        mm(PS[2 + 2 * c][:, :Tn], vv[:, j, h * 128:(h + 1) * 128], E[:, :Tn], idx == 0, idx == len(keys) - 1, [b_vv, b_E], [PB[2 + 2 * c]])
                        mm(PS[3 + 2 * c][:, :Tn], onesb, E[:, :Tn], idx == 0, idx == len(keys) - 1, [b_cst, b_E], [PB[3 + 2 * c]])
                    ec += len(seq)
                    recip(r0[:, :Tn], PS[3][:, :Tn], [PB[3]], [b_r0])
                    tt("dve", t0_[:, :Tn], PS[2][:, :Tn], r0[:, :Tn], ALU.mult, [PB[2], b_r0], [b_t0])
                    recip(r0[:, :Tn], PS[5][:, :Tn], [PB[5]], [b_r0])
                    tt("dve", t1_[:, :Tn], PS[4][:, :Tn], r0[:, :Tn], ALU.mult, [PB[4], b_r0], [b_t1])
                    stt(t0_[:, :Tn], t1_[:, :Tn], neg_lam, t0_[:, :Tn], ALU.mult, ALU.add, [b_t1, b_t0, b_sm], [b_t0])
                    act(sq_[:, :Tn], t0_[:, :Tn], AF.Square, [b_t0], [b_sq2])
                    mm(PS[6][:, :Tn], ones32, sq_[:, :Tn], True, True, [b_sq2, b_cst], [PB[6]])
                    rsqrt_mean(rr_[:, :Tn], PS[6][:, :Tn], 128, [PB[6]], [b_rr])
                    ao, b_ao = aob[oc % 2]
                    oc += 1
                    stt(ao[:, :Tn], t0_[:, :Tn], gsub_s, rr_[:, :Tn], ALU.mult, ALU.mult, [b_t0, b_rr, b_sm], [b_ao])
                    P.dma(ato_s[h][:, q0:q0 + Tn], ao[:, :Tn], reads=[b_ao], writes=[b_ato])

        b_hyo = Buf()

        def hyena_main(t0, L):
            TB, G, nTB, nG = RL[L]
            nj = L // 128
            for cc in range(2):
                ar.reset()
                raw = ar.f32(L)
                b_raw = Buf()
                us = [ar.f32(L) for _ in range(3)]
                b_us = [Buf() for _ in range(3)]
                for jx in range(3):
                    ch = jx * 2 + cc
                    P.dma(raw, hy_s[ch][:, t0:t0 + L], reads=[b_hy], writes=[b_raw])
                    u = us[jx]
                    ts("dve", u, raw, hcw[:, ch, 1:2], hcw[:, ch, 3:4], ALU.mult, ALU.add, [b_raw, b_sm], [b_us[jx]])
                    stt(u[:, 1:L], raw[:, 0:L - 1], hcw[:, ch, 0:1], u[:, 1:L], ALU.mult, ALU.add, [b_raw, b_us[jx], b_sm], [b_us[jx]])
                    stt(u[:, 0:L - 1], raw[:, 1:L], hcw[:, ch, 2:3], u[:, 0:L - 1], ALU.mult, ALU.add, [b_raw, b_us[jx], b_sm], [b_us[jx]])
                zbuf = raw
                b_zb = b_raw
                ztm = ar.bf(L).rearrange("p (j c) -> p j c", j=nj)
                b_ztm = Buf()
                Z = ar.f32(2 * L).rearrange("p (c j n) -> p c j n", c=2, j=nj)
                b_Z = Buf()
                Kf = ar.f32(2 * L).rearrange("p (c j n) -> p c j n", c=2, j=nj)
                b_Kf = Buf()
                X = ar.bf(2 * L).rearrange("p (c j n) -> p c j n", c=2, j=nj)
                b_X = Buf()
                tmpx = ar.f32(L).rearrange("p (j n) -> p j n", j=nj)
                b_tx = Buf()
                tmpy = ar.f32(L).rearrange("p (j n) -> p j n", j=nj)
                b_ty = Buf()
                tabs = [(ar.bf(2048), Buf()) for _ in range(4)]
                tcnt = 0
                z, b_z = us[0], b_us[0]
                for o in range(2):
                    for scn in range(nj):
                        tr(PS[0][:, (scn % 4) * 128:(scn % 4 + 1) * 128], z[:, scn * 128:(scn + 1) * 128], ident32, [b_z, b_cst], [PB[0]])
                        if scn % 4 == 3 or scn == nj - 1:
                            n4 = scn % 4 + 1
                            cp("act", ztm[:, scn - n4 + 1:scn + 1, :], PS[0][:, 0:n4 * 128].rearrange("p (j c) -> p j c", j=n4), [PB[0]], [b_ztm])
                    for c in range(2):
                        P.dma(Kf[:, c], kf_s[L][c][:, :, o * 256 + cc * 128:o * 256 + cc * 128 + 128].rearrange("j p n -> p j n"), reads=[b_kf[L]], writes=[b_Kf])
                    for fc in range(nj):
                        tbc, b_tbc = tabs[tcnt % 4]
                        tbs, b_tbs = tabs[(tcnt + 1) % 4]
                        tcnt += 2
                        tbc3 = tbc[:, 0:nj * 128].rearrange("p (j f) -> p j f", j=nj)
                        tbs3 = tbs[:, 0:nj * 128].rearrange("p (j f) -> p j f", j=nj)
                        P.dma(tbc3, tF_d[L][0, fc], writes=[b_tbc])
                        P.dma(tbs3, tF_d[L][1, fc], writes=[b_tbs])
                        pz = 1 + fc % 2
                        for sc_ in range(nj):
                            mm(PS[pz][:, 0:128], tbc3[:, sc_, :], ztm[:, sc_, :], sc_ == 0, sc_ == nj - 1, [b_tbc, b_ztm], [PB[pz]])
                        for sc_ in range(nj):
                            mm(PS[pz][:, 128:256], tbs3[:, sc_, :], ztm[:, sc_, :], sc_ == 0, sc_ == nj - 1, [b_tbs, b_ztm], [PB[pz]])
                        cp("act", Z[:, :, fc, :], PS[pz][:, 0:256].rearrange("p (c n) -> p c n", c=2), [PB[pz]], [b_Z])
                    Zr, Zi, Kr, Ki = Z[:, 0], Z[:, 1], Kf[:, 0], Kf[:, 1]
                    tt("dve", tmpx, Zr, Kr, ALU.mult, [b_Z, b_Kf], [b_tx])
                    tt("pool", tmpy, Zi, Ki, ALU.mult, [b_Z, b_Kf], [b_ty])
                    tt("dve", X[:, 0], tmpx, tmpy, ALU.subtract, [b_tx, b_ty], [b_X])
                    tt("pool", tmpy, Zr, Ki, ALU.mult, [b_Z, b_Kf, b_X], [b_ty])
                    tt("dve", tmpx, Zi, Kr, ALU.mult, [b_Z, b_Kf, b_X], [b_tx])
                    tt("dve", X[:, 1], tmpx, tmpy, ALU.add, [b_tx, b_ty], [b_X])
                    gate, b_g = us[1 + o], b_us[1 + o]
                    znew, b_zn = (zbuf, b_zb) if o == 0 else (us[0], b_us[0])
                    for tb in range(nTB):
                        py = 3 + tb % 2
                        for g in range(nG):
                            tbc, b_tbc = tabs[tcnt % 4]
                            tbs, b_tbs = tabs[(tcnt + 1) % 4]
                            tcnt += 2
                            tc3 = tbc[:, 0:G * TB].rearrange("p (g t) -> p g t", g=G)
                            ts3 = tbs[:, 0:G * TB].rearrange("p (g t) -> p g t", g=G)
                            P.dma(tc3, tI_d[L][0, tb, g], writes=[b_tbc])
                            P.dma(ts3, tI_d[L][1, tb, g], writes=[b_tbs])
                            for fi in range(G):
                                fc = g * G + fi
                                mm(PS[py][:, :TB], X[:, 0, fc, :], tc3[:, fi, :], fc == 0, False, [b_X, b_tbc], [PB[py]])
                                mm(PS[py][:, :TB], X[:, 1, fc, :], ts3[:, fi, :], False, fc == nj - 1, [b_X, b_tbs], [PB[py]])
                        sl = slice(tb * TB, (tb + 1) * TB)
                        stt(tmpx.rearrange("p j n -> p (j n)")[:, sl], z[:, sl], hbias[:, o, cc:cc + 1], PS[py][:, :TB], ALU.mult, ALU.add, [b_z, PB[py], b_sm, b_X], [b_tx])
                        tt("dve", znew[:, sl], tmpx.rearrange("p j n -> p (j n)")[:, sl], gate[:, sl], ALU.mult, [b_tx, b_g], [b_zn])
                    z, b_z = znew, b_zn
                ob = ar.bf(L)
                b_ob = Buf()
                cp("act", ob, z, [b_z], [b_ob])
                P.dma(hyo_s[cc][:, t0:t0 + L], ob, reads=[b_ob], writes=[b_hyo])

        if want("hyena"):
            for (t0, L) in streams:
                hyena_main(t0, L)

        b_fno = Buf()

        def fnet(t0, L):
            TB, G, nTB, nG = RL[L]
            nj = L // 128
            ar.reset()
            fz = [(ar.f32(L), Buf()) for _ in range(2)]
            zc = ar.bf(nj * 256).rearrange("p (j c) -> p j c", j=nj)
            zs = ar.bf(nj * 256).rearrange("p (j c) -> p j c", j=nj)
            b_zc = Buf()
            for cc in range(2):
                f, b_f = fz[cc]
                P.dma(f, fn_s[cc][:, t0:t0 + L], reads=[b_fn], writes=[b_f])
                for scn in range(nj):
                    pa = scn % 2
                    mm(PS[pa][:, 0:128], f[:, scn * 128:(scn + 1) * 128], bdc, True, True, [b_f, b_cst], [PB[pa]])
                    mm(PS[pa][:, 128:256], f[:, scn * 128:(scn + 1) * 128], bds, True, True, [b_f, b_cst], [PB[pa]])
                    cp("act", zc[:, scn, cc * 128:(cc + 1) * 128], PS[pa][:, 0:128], [PB[pa]], [b_zc])
                    cp("dve", zs[:, scn, cc * 128:(cc + 1) * 128], PS[pa][:, 128:256], [PB[pa]], [b_zc])
            tabs = [(ar.bf(2048), Buf()) for _ in range(4)]
            obs = [(ar.bf(512), Buf()) for _ in range(2)]
            tcnt = 0
            oc = 0
            for tb in range(nTB):
                for g in range(nG):
                    tbc, b_tbc = tabs[tcnt % 4]
                    tbs, b_tbs = tabs[(tcnt + 1) % 4]
                    tcnt += 2
                    tc3 = tbc[:, 0:G * TB].rearrange("p (g t) -> p g t", g=G)
                    ts3 = tbs[:, 0:G * TB].rearrange("p (g t) -> p g t", g=G)
                    P.dma(tc3, tN_d[L][0, tb, g], writes=[b_tbc])
                    P.dma(ts3, tN_d[L][1, tb, g], writes=[b_tbs])
                    for cc in range(2):
                        py = 2 + cc + 2 * (tb % 2)
                        for fi in range(G):
                            sc_ = g * G + fi
                            mm(PS[py][:, :TB], zc[:, sc_, cc * 128:(cc + 1) * 128], tc3[:, fi, :], sc_ == 0, False, [b_zc, b_tbc], [PB[py]])
                            mm(PS[py][:, :TB], zs[:, sc_, cc * 128:(cc + 1) * 128], ts3[:, fi, :], False, sc_ == nj - 1, [b_zc, b_tbs], [PB[py]])
                for cc in range(2):
                    py = 2 + cc + 2 * (tb % 2)
                    ob, b_ob = obs[oc % 2]
                    oc += 1
                    cp("act" if cc else "dve", ob[:, :TB], PS[py][:, :TB], [PB[py]], [b_ob])
                    P.dma(fno_s[cc][:, t0 + tb * TB:t0 + (tb + 1) * TB], ob[:, :TB], reads=[b_ob], writes=[b_fno])

        if want("fnet"):
            for (t0, L) in streams:
                fnet(t0, L)

        if want("merge"):
            ar.reset()
            wbr = ar.bf(8 * D).rearrange("p (k m) -> p k m", k=8)
            wo = ar.bf(8 * D).rearrange("p (k m) -> p k m", k=8)
            b_wbr = Buf()
            b_wo = Buf()
            w32 = ar.f32(8 * 512).rearrange("p (k m) -> p k m", k=8)
            b_w32 = Buf()
            for (src, dst, bd) in ((w_br_d, wbr, b_wbr), (w_out_d, wo, b_wo)):
                for half in range(2):
                    P.dma(w32, src[l][:, half * 512:(half + 1) * 512].rearrange("(k p) m -> p k m", p=128), writes=[b_w32])
                    cp("act", dst[:, 0:4, half * 512:(half + 1) * 512], w32[:, 0:4, :], [b_w32], [bd])
                    cp("dve", dst[:, 4:8, half * 512:(half + 1) * 512], w32[:, 4:8, :], [b_w32], [bd])
            ar.off -= 8 * 512
            P.barrier()
            nb = [(ar.bf(8 * 512).rearrange("p (k t) -> p k t", k=8), Buf()) for _ in range(2)]
            sb_ = [(ar.bf(8 * 512).rearrange("p (k t) -> p k t", k=8), Buf()) for _ in range(2)]
            g32 = [(ar.f32(8 * 384).rearrange("p (k j c) -> p k j c", k=8, j=3), Buf()) for _ in range(2)]
            gbf = [(ar.bf(8 * 384).rearrange("p (k j c) -> p k j c", k=8, j=3), Buf()) for _ in range(2)]
            sig = [(ar.f32(512), Buf()) for _ in range(3)]
            yacc = [(ar.f32(512), Buf()) for _ in range(2)]
            ytmp = [(ar.f32(512), Buf()) for _ in range(2)]
            yT = ar.bf(8 * 512).rearrange("p (k t) -> p k t", k=8)
            b_yT = [Buf() for _ in range(8)]
            KR = ((0, 2), (2, 4), (4, 8))
            gc = 0
            for bi, (t0, Tn) in enumerate(tb_mix):
                nbk, b_nb = nb[bi % 2]
                sbk, b_sb = sb_[bi % 2]
                P.dma(nbk[:, :, :Tn], nT_s[:, :, t0:t0 + Tn], reads=[b_nTs], writes=[b_nb])
                for cc in range(2):
                    P.dma(sbk[:, cc, :Tn], hyo_s[cc][:, t0:t0 + Tn], reads=[b_hyo], writes=[b_sb])
                    P.dma(sbk[:, 2 + cc, :Tn], fno_s[cc][:, t0:t0 + Tn], reads=[b_fno], writes=[b_sb])
                for h in range(4):
                    P.dma(sbk[:, 4 + h, :Tn], ato_s[h][:, t0:t0 + Tn], reads=[b_ato], writes=[b_sb])
                for m in range(8):
                    gw, b_gw = g32[gc % 2]
                    gb, b_gb = gbf[gc % 2]
                    gc += 1
                    P.dma(gw, w_gate_d[l, m], writes=[b_gw])
                    cp("pool", gb, gw, [b_gw], [b_gb])
                    ya, b_ya = yacc[m % 2]
                    yt, b_yt = ytmp[m % 2]
                    for j in range(3):
                        pg = j
                        pbr = 3 + j
                        for k in range(8):
                            mm(PS[pg][:, :Tn], gb[:, k, j, :], nbk[:, k, :Tn], k == 0, k == 7, [b_gb, b_nb], [PB[pg]])
                        k0, k1 = KR[j]
                        for k in range(k0, k1):
                            mm(PS[pbr][:, :Tn], wbr[:, k, m * 128:(m + 1) * 128], sbk[:, k, :Tn], k == k0, k == k1 - 1, [b_wbr, b_sb], [PB[pbr]])
                        sg, b_sg = sig[(m * 3 + j) % 3]
                        act(sg[:, :Tn], PS[pg][:, :Tn], AF.Sigmoid, [PB[pg]], [b_sg])
                        if j == 0:
                            tt("dve", ya[:, :Tn], sg[:, :Tn], PS[pbr][:, :Tn], ALU.mult, [b_sg, PB[pbr]], [b_ya])
                        else:
                            tt("dve", yt[:, :Tn], sg[:, :Tn], PS[pbr][:, :Tn], ALU.mult, [b_sg, PB[pbr]], [b_yt])
                            if j == 1:
                                tt("pool", ya[:, :Tn], ya[:, :Tn], yt[:, :Tn], ALU.add, [b_ya, b_yt], [b_ya])
                            else:
                                tt("pool", yT[:, m, :Tn], ya[:, :Tn], yt[:, :Tn], ALU.add, [b_ya, b_yt], [b_yT[m]])
                jj = 0 if t0 < S else 1
                for mo in range(8):
                    po = 6 + mo % 2
                    for k in range(8):
                        mm(PS[po][:, :Tn], wo[:, k, mo * 128:(mo + 1) * 128], yT[:, k, :Tn], k == 0, k == 7, [b_wo, b_yT[k]], [PB[po]])
                    xb = b_x[mo][t0 // 512]
                    stt(xT[:, mo, t0:t0 + Tn], PS[po][:, :Tn], modv[:, 2, mo, jj:jj + 1], xT[:, mo, t0:t0 + Tn], ALU.mult, ALU.add, [PB[po], xb, b_sm], [xb])

        if want("peer"):
            peer(l, last, modulate, mod_tmps, A_ffn)

    def peer(l, last, modulate, mod_tmps, A_ffn):
        ar.reset()
        b_ub = Buf()
        b_vb = Buf()
        ld = [(ar.f32(4096), Buf()) for _ in range(3)]
        cv = [(ar.bf(4096), Buf()) for _ in range(3)]
        jobs = []
        for k in range(8):
            for eb in range(4):
                jobs.append(("u", k, eb))
        for e1g in range(32):
            jobs.append(("v", e1g, 0))

        def j_load(ci):
            kind, i0_, i1_ = jobs[ci]
            a, b_a = ld[ci % 3]
            if kind == "u":
                P.dma(a, uT_d[l][i0_ * 128:(i0_ + 1) * 128, i1_ * 4096:(i1_ + 1) * 4096], writes=[b_a])
            else:
                P.dma(a.rearrange("p (e d) -> p e d", e=4), v_d_in[l][i0_ * 512:(i0_ + 1) * 512, :].rearrange("(e p) d -> p e d", p=128), writes=[b_a])

        j_load(0)
        j_load(1)
        for ci in range(len(jobs)):
            if ci + 2 < len(jobs):
                j_load(ci + 2)
            kind, i0_, i1_ = jobs[ci]
            a, b_a = ld[ci % 3]
            o, b_o = cv[ci % 3]
            cp(("act", "dve", "pool")[ci % 3], o, a, [b_a], [b_o])
            if kind == "u":
                P.dma(uTb_s[i1_ * 8:(i1_ + 1) * 8, :, i0_, :].rearrange("g p e -> p g e"), o.rearrange("p (g e) -> p g e", g=8), reads=[b_o], writes=[b_ub])
            else:
                P.dma(vb_s[:, i0_ * 4:(i0_ + 1) * 4, :], o.rearrange("p (e d) -> p e d", e=4), reads=[b_o], writes=[b_vb])
        ar.reset()
        keysT = ar.f32(2048).rearrange("p (h n) -> p h n", h=16)
        b_ky = Buf()
        P.dma(keysT, keysT_d[l], writes=[b_ky])
        nbfs = [ar.bf(8 * 256).rearrange("p (k t) -> p k t", k=8) for _ in range(2)]
        b_nbfs = [[Buf() for _ in range(8)] for _ in range(2)]
        s1k = [ar.f32(1024).rearrange("p (h n) -> p h n", h=8) for _ in range(2)]
        a2k = [ar.f32(1024).rearrange("p (h n) -> p h n", h=8) for _ in range(2)]
        a1t = [ar.f32(128).rearrange("p (h a) -> p h a", h=8) for _ in range(2)]
        top1k = [ar.f32(128).rearrange("p (h a) -> p h a", h=8) for _ in range(2)]
        theta = [ar.f32(8) for _ in range(2)]
        b_keep = [Buf() for _ in range(2)]
        zer = ar.bf(512)
        b_zer = Buf()
        P.op("pool", lambda e: e.memset(zer, 0.0), [], [b_zer])
        base = ar.off
        mt = mod_tmps(256, (2, 1, 2))
        n32 = ar.f32(8 * 256).rearrange("p (k t) -> p k t", k=8)
        b_n32 = [Buf() for _ in range(8)]
        wq = [(ar.f32(8 * 128).rearrange("p (k m) -> p k m", k=8), Buf()) for _ in range(2)]
        qT = ar.f32(16 * 256).rearrange("p (h t) -> p h t", h=16)
        b_qT = Buf()
        s_sb = ar.f32(2048).rearrange("p (h n) -> p h n", h=16)
        b_s = Buf()
        tmpm = ar.f32(256)
        b_tm = Buf()
        tmpm2 = ar.f32(256)
        b_tm2 = Buf()
        top = ar.f32(256).rearrange("p (h a) -> p h a", h=16)
        b_top = Buf()
        cand = ar.f32(256)
        b_cand = Buf()
        ctop = ar.f32(192).rearrange("p (h a) -> p h a", h=8)
        b_ct = Buf()
        misc = ar.f32(16)
        b_mi = Buf()
        ex16 = ar.f32(128).rearrange("p (h a) -> p h a", h=8)
        ub = [(ar.bf(8 * 512).rearrange("p (k e) -> p k e", k=8), Buf()) for _ in range(2)]
        vbf = [(ar.bf(4 * 1024).rearrange("p (e d) -> p e d", e=4), Buf()) for _ in range(2)]
        Wt = [(ar.bf(4 * 256).rearrange("p (e t) -> p e t", e=4), Buf()) for _ in range(2)]
        gel = [(ar.f32(512), Buf()) for _ in range(2)]
        GT = [(ar.bf(512).rearrange("p (e t) -> p e t", e=2), Buf()) for _ in range(2)]
        blocks = tblocks(0, S if last else T, 256)
        b_wT = [Buf(), Buf()]

        def phaseA(t0, Tn, nbf, b_nbf):
            jj = 0 if t0 < S else 1
            def emit_ffn(k, t0_, Tn_, tm, b_t, Asc, Bsc):
                act(n32[:, k, :], tm[:, :Tn_], AF.Identity, [b_t, b_sm], [b_n32[k]], bias=Bsc, scale=Asc)
                cp("pool", nbf[:, k, :], n32[:, k, :], [b_n32[k]], [b_nbf[k]])

            modulate(A_ffn, 3, [(t0, Tn)], emit_ffn, mt, 2)
            yield
            for hp in range(16):
                w, b_w = wq[hp % 2]
                P.dma(w, wq_d[l][:, hp * 128:(hp + 1) * 128].rearrange("(k p) m -> p k m", p=128), writes=[b_w])
                pq = 2 + hp % 2
                for k in range(8):
                    mm(PS[pq][:, :Tn], w[:, k, :], n32[:, k, :], k == 0, k == 7, [b_w, b_n32[k]], [PB[pq]])
                cp("act" if hp % 2 else "dve", qT[:, hp, :], PS[pq][:, :Tn], [PB[pq]], [b_qT])
                yield
            for ti in range(2):
                tsl = slice(ti * 128, (ti + 1) * 128)
                bk = b_keep[ti]
                for hf in range(2):
                    for h8 in range(8):
                        hp = hf * 8 + h8
                        pbk = 2 + h8 // 4
                        mm(PS[pbk][:, (h8 % 4) * 128:(h8 % 4 + 1) * 128], qT[:, hp, tsl], keysT[:, hp, :], True, True, [b_qT, b_ky], [PB[pbk]])
                    for q2 in range(2):
                        cp("act" if q2 else "dve", s_sb[:, hf * 8 + q2 * 4:hf * 8 + (q2 + 1) * 4, :], PS[2 + q2][:, :].rearrange("p (h n) -> p h n", h=4), [PB[2 + q2]], [b_s])
                    yield
                for hp in range(16):
                    P.op("dve", (lambda hp: lambda e: e.max(out=top[:, hp, 0:8], in_=s_sb[:, hp, :]))(hp), [b_s], [b_top])
                    P.op("dve", (lambda hp: lambda e: e.match_replace(out=tmpm[:, 0:128], in_to_replace=top[:, hp, 0:8], in_values=s_sb[:, hp, :], imm_value=-1e30))(hp), [b_s, b_top], [b_tm])
                    P.op("dve", (lambda hp: lambda e: e.max(out=top[:, hp, 8:16], in_=tmpm[:, 0:128]))(hp), [b_tm], [b_top])
                    if hp % 2:
                        yield
                top4 = top.rearrange("p (h c) a -> p h c a", c=2)
                s4v = s_sb.rearrange("p (h c) n -> p h c n", c=2)
                for h in range(8):
                    ch = cand
                    tt("dve", cand.rearrange("p (a b) -> p a b", a=16), top4[:, h, 0, :].unsqueeze(2).to_broadcast([128, 16, 16]),
                       top4[:, h, 1, :].unsqueeze(1).to_broadcast([128, 16, 16]), ALU.add, [b_top, b_ct], [b_cand])
                    P.op("dve", (lambda h, ch: lambda e: e.max(out=ctop[:, h, 0:8], in_=ch))(h, ch), [b_cand], [b_ct])
                    P.op("dve", (lambda h, ch: lambda e: e.match_replace(out=tmpm, in_to_replace=ctop[:, h, 0:8], in_values=ch, imm_value=-1e30))(h, ch), [b_cand, b_ct], [b_tm])
                    P.op("dve", (lambda h: lambda e: e.max(out=ctop[:, h, 8:16], in_=tmpm))(h), [b_tm], [b_ct])
                    P.op("dve", (lambda h: lambda e: e.match_replace(out=tmpm2, in_to_replace=ctop[:, h, 8:16], in_values=tmpm, imm_value=-1e30))(h), [b_tm, b_ct], [b_tm2])
                    P.op("dve", (lambda h: lambda e: e.max(out=ctop[:, h, 16:24], in_=tmpm2))(h), [b_tm2], [b_ct])
                    yield
                tt("dve", ex16, ctop[:, :, 0:16], ctop[:, :, 0:1].to_broadcast([128, 8, 16]), ALU.subtract, [b_ct], [b_mi])
                act(ex16, ex16, AF.Exp, [b_mi], [b_mi])
                P.op("dve", lambda e: e.reduce_sum(out=misc[:, 8:16], in_=ex16, axis=AX.X), [b_mi], [b_mi])
                recip(misc[:, 8:16], misc[:, 8:16], [b_mi], [b_mi])
                m8 = misc[:, 0:8].unsqueeze(2)
                tt("dve", m8, ctop[:, :, 15:16], ctop[:, :, 16:17], ALU.add, [b_ct, b_mi], [b_mi])
                stt(m8, m8, 0.5, ctop[:, :, 0:1], ALU.mult, ALU.subtract, [b_mi, b_ct], [b_mi])
                act(misc[:, 0:8], misc[:, 0:8], AF.Exp, [b_mi], [b_mi])
                tt("dve", theta[ti], misc[:, 0:8], misc[:, 8:16], ALU.mult, [b_mi], [bk])
                cp("pool", s1k[ti], s4v[:, :, 0, :], [b_s], [bk])
                cp("pool", top1k[ti], top4[:, :, 0, :], [b_top], [bk])
                tt("dve", a2k[ti], s4v[:, :, 1, :], top4[:, :, 1, 0:1].to_broadcast([128, 8, 128]), ALU.subtract, [b_s, b_top], [bk])
                act(a2k[ti], a2k[ti], AF.Exp, [bk], [bk])
                tt("dve", a1t[ti], top4[:, :, 0, :], top4[:, :, 0, 0:1].to_broadcast([128, 8, 16]), ALU.subtract, [b_top], [bk])
                act(a1t[ti], a1t[ti], AF.Exp, [bk], [bk])
                tt("dve", a1t[ti], a1t[ti], misc[:, 8:16].unsqueeze(2).to_broadcast([128, 8, 16]), ALU.mult, [bk, b_mi], [bk])
                yield

        for _ in phaseA(blocks[0][0], blocks[0][1], nbfs[0], b_nbfs[0]):
            pass
        for bi, (t0, Tn) in enumerate(blocks):
            jj = 0 if t0 < S else 1
            nbf = nbfs[bi % 2]
            b_nbf = b_nbfs[bi % 2]
            P.barrier()
            ar.off = base
            pmt = ar.f32(2048).rearrange("p (h a e) -> p h a e", h=8, a=16)
            b_pm = Buf()
            csl = [(ar.bf(2048).rearrange("p (h a e) -> p h a e", h=8, a=16), Buf()) for _ in range(2)]
            CT = ar.bf(128 * 128).rearrange("p (e t) -> p e t", e=128)
            b_CT = Buf()
            osl = ar.bf(4096).rearrange("p (h a e) -> p h a e", h=8, a=16)
            b_osl = Buf()
            OTs = [(ar.bf(4096).rearrange("p (e t) -> p e t", e=32), Buf()) for _ in range(2)]
            WTs = [(ar.bf(4096).rearrange("p (e t) -> p e t", e=32), Buf()) for _ in range(2)]
            PSb = [PS[i][:, :].bitcast(BF16) for i in range(8)]
            evc = 0
            for ti in range(2):
                bk = b_keep[ti]
                for es in range(8):
                    tt("pool", pmt, a1t[ti].unsqueeze(3).to_broadcast([128, 8, 16, 16]),
                       a2k[ti][:, :, es * 16:(es + 1) * 16].unsqueeze(2).to_broadcast([128, 8, 16, 16]), ALU.mult, [bk], [b_pm])
                    cs, b_cs = csl[es % 2]
                    for h in range(8):
                        pmh = pmt[:, h].rearrange("p a e -> p (a e)")
                        stt(cs[:, h].rearrange("p a e -> p (a e)"), pmh, theta[ti][:, h:h + 1], pmh, ALU.is_ge, ALU.mult, [b_pm, bk], [b_cs])
                    csf = cs.rearrange("p h a e -> p (h a) e")
                    for half in range(2):
                        pb = 2 + (es * 2 + half) % 2
                        for e in range(8):
                            tr(PSb[pb][:, e * 128:(e + 1) * 128], csf[:, :, half * 8 + e], identb, [b_cs, b_cst], [PB[pb]])
                        e0 = es * 16 + half * 8
                        cp("act" if half else "dve", CT[:, e0:e0 + 8, :], PSb[pb][:, :].rearrange("p (e t) -> p e t", e=8), [PB[pb]], [b_CT])
                for r in range(4):
                    tt("dve", osl, s1k[ti][:, :, r * 32:(r + 1) * 32].unsqueeze(2).to_broadcast([128, 8, 16, 32]),
                       top1k[ti].unsqueeze(3).to_broadcast([128, 8, 16, 32]), ALU.is_equal, [bk], [b_osl])
                    osf = osl.rearrange("p h a e -> p (h a) e")
                    ot, b_ot = OTs[r % 2]
                    for q in range(4):
                        pb = 4 + q % 2
                        for e in range(8):
                            tr(PSb[pb][:, e * 128:(e + 1) * 128], osf[:, :, q * 8 + e], identb, [b_osl, b_cst], [PB[pb]])
                        cp("act" if q % 2 else "dve", ot[:, q * 8:(q + 1) * 8, :], PSb[pb][:, :].rearrange("p (e t) -> p e t", e=8), [PB[pb]], [b_ot])
                    wt, b_wt = WTs[r % 2]
                    for tg in range(8):
                        pb = 6 + tg % 2
                        for tk in range(16):
                            t_ = tg * 16 + tk
                            mm(PS[pb][:, tk * 32:(tk + 1) * 32], CT[:, :, t_], ot[:, :, t_], True, True, [b_CT, b_ot], [PB[pb]])
                        cp("dve" if evc % 2 else "act", wt[:, :, tg * 16:(tg + 1) * 16], PS[pb][:, :].rearrange("p (t e) -> p e t", t=16), [PB[pb]], [b_wt])
                        evc += 1
                    P.dma(wT_s[ti][:, r * 32:(r + 1) * 32, :], wt, reads=[b_wt], writes=[b_wT[ti]])
            P.barrier()
            genA = phaseA(blocks[bi + 1][0], blocks[bi + 1][1], nbfs[(bi + 1) % 2], b_nbfs[(bi + 1) % 2]) if bi + 1 < len(blocks) else None
            for pb in range(4, 8):
                mm(PS[pb][:, :], zer[:, 0:128], zer[:, 0:512], True, False, [b_zer], [PB[pb]])
            pend = None
            for g in range(32):
                u_, b_u = ub[g % 2]
                v_, b_vv = vbf[g % 2]
                w_, b_w_ = Wt[g % 2]
                P.dma(u_, uTb_s[g], reads=[b_ub], writes=[b_u])
                P.dma(v_, vb_s[:, g * 4:(g + 1) * 4, :], reads=[b_vb], writes=[b_vv])
                for ti in range(2):
                    P.dma(w_[:, :, ti * 128:(ti + 1) * 128], wT_s[ti][:, g * 4:(g + 1) * 4, :], reads=[b_wT[ti]], writes=[b_w_])
                for sub in range(2):
                    it = g * 2 + sub
                    ph = it % 2
                    for i in range(2):
                        chn = sub * 2 + i
                        for k in range(8):
                            mm(PS[ph][:, i * 256:(i + 1) * 256], u_[:, k, chn * 128:(chn + 1) * 128], nbf[:, k, :], k == 0, k == 7, [b_u, b_nbf[k]], [PB[ph]])
                    ge, b_ge = gel[it % 2]
                    gt, b_gt = GT[it % 2]
                    act(ge, PS[ph][:, :], AF.Gelu, [PB[ph]], [b_ge])
                    tt("dve", gt.rearrange("p e t -> p (e t)"), ge, w_[:, sub * 2:(sub + 1) * 2, :].rearrange("p e t -> p (e t)"), ALU.mult, [b_ge, b_w_], [b_gt])
                    if pend is not None:
                        pend()

                    def mk(v_=v_, gt=gt, sub=sub, b_vv=b_vv, b_gt=b_gt):
                        def f():
                            for dk in range(8):
                                pbo = 4 + dk // 2
                                for i in range(2):
                                    mm(PS[pbo][:, (dk % 2) * 256:(dk % 2 + 1) * 256], v_[:, sub * 2 + i, dk * 128:(dk + 1) * 128], gt[:, i, :], False, False, [b_vv, b_gt], [PB[pbo]])
                        return f
                    pend = mk()
                if genA is not None:
                    for _ in range(3):
                        next(genA, None)
            pend()
            if genA is not None:
                for _ in genA:
                    pass
            for dk in range(8):
                pbo = 4 + dk // 2
                xb = b_x[dk][t0 // 512]
                stt(xT[:, dk, t0:t0 + 256], PS[pbo][:, (dk % 2) * 256:(dk % 2 + 1) * 256], modv[:, 5, dk, jj:jj + 1], xT[:, dk, t0:t0 + 256], ALU.mult, ALU.add, [PB[pbo], xb, b_sm], [xb])

    for l in range(depth):
        layer(l)

    P.barrier()
    fin = []
    for k in range(8):
        fin.append(P.dma(yT_d[:, k, :], xT[:, k, 0:S], reads=b_x[k]))
    for name, (src, shape) in dbg_out.items():
        pass
    P.emit(list(P.dmas[-8:]))
    es.close()
    return nc

import ml_dtypes
_CONST = {}


def _consts():
    if _CONST:
        return _CONST
    f64 = np.float64
    c = np.zeros((6, 128, 128), f64)
    c[0] = np.eye(128)
    c[1] = 1.0
    c[2, :64, :64] = 1.0
    c[2, 64:, 64:] = 1.0
    for base in range(0, 128, 32):
        for d in range(16):
            c[3, base + d + 16, base + d] = -1.0
            c[3, base + d, base + d + 16] = 1.0
    ci = np.arange(64)
    ang = 2 * np.pi * np.outer(ci, ci) / 64.0
    for b in range(2):
        c[4, b * 64:(b + 1) * 64, b * 64:(b + 1) * 64] = np.cos(ang)
        c[5, b * 64:(b + 1) * 64, b * 64:(b + 1) * 64] = np.sin(ang)
    _CONST["cst"] = np.ascontiguousarray(c.transpose(1, 0, 2)).astype(np.float32)
    t = np.arange(S)
    row = (t // 64).astype(f64)
    col = (t % 64).astype(f64)
    inv = 10000.0 ** (-np.arange(0, 32, 2, dtype=f64) / 32.0)
    d = np.arange(128) % 64
    pos = np.where((d // 32)[:, None] == 0, row[None, :], col[None, :])
    a = pos * inv[d % 16][:, None]
    _CONST["rope"] = np.stack([np.cos(a), np.sin(a)]).astype(np.float32)
    for L in (S, C):
        p = np.arange(L, dtype=f64)
        tt_ = p / max(L - 1, 1)
        w = 2.0 * np.pi * p / L
        fr = np.linspace(1e-4, 15, 16)
        feats = np.concatenate([tt_[:, None], np.cos(w[:, None] * fr), -np.sin(w[:, None] * fr)], axis=-1)
        _CONST["feats%d" % L] = np.ascontiguousarray(feats.T).astype(np.float32)
        deltas = np.abs(np.linspace(math.log(1e-2) / 1.5, math.log(1e-2) / 0.3, 256))
        _CONST["dec%d" % L] = np.exp(-tt_[:, None] * deltas[None, :]).astype(np.float32)
        nj = L // 128
        s_ = np.arange(L)
        kk = np.outer(s_, 2 * s_ + 1) % (4 * L)
        angF = np.pi * kk / (2.0 * L)
        TcF = np.cos(angF)
        TsF = -np.sin(angF)
        tF = np.stack([TcF, TsF]).reshape(2, nj, 128, nj, 128).transpose(0, 3, 2, 1, 4)
        _CONST["tF%d" % L] = np.ascontiguousarray(tF).astype(np.float32).astype(ml_dtypes.bfloat16)
        TB = min(512, L)
        G = min(4, nj)
        nTB = L // TB
        nG = nj // G
        TcI = TcF.T / L
        TsI = TsF.T / L
        def rl(M):
            return M.reshape(nG, G, 128, nTB, TB).transpose(3, 0, 2, 1, 4)
        _CONST["tI%d" % L] = np.ascontiguousarray(np.stack([rl(TcI), rl(TsI)])).astype(np.float32).astype(ml_dtypes.bfloat16)
        k2 = np.outer(s_, s_) % L
        ang2 = 2 * np.pi * k2 / L
        sc_ = 1.0 / math.sqrt(64.0 * L)
        _CONST["tN%d" % L] = np.ascontiguousarray(np.stack([rl(np.cos(ang2) * sc_), rl(-np.sin(ang2) * sc_)])).astype(np.float32).astype(ml_dtypes.bfloat16)
    return _CONST


def _prep(inp):
    f = lambda a: np.ascontiguousarray(np.asarray(a, dtype=np.float32))
    w = {}
    w["w_ada"] = f(inp["w_ada"])
    w["bada"] = f(np.asarray(inp["b_ada"]).reshape(2, 6, 8, 128).transpose(0, 3, 1, 2))
    w["gmf"] = f(np.stack([np.asarray(inp["g_mix"]).reshape(2, 8, 128), np.asarray(inp["g_ffn"]).reshape(2, 8, 128)], axis=1).transpose(0, 3, 1, 2))
    win = np.asarray(inp["w_in"])
    w["w_in"] = f(win)
    w["w_gate"] = f(win[:, :, 2560:].reshape(2, 8, 128, 3, 8, 128).transpose(0, 4, 2, 1, 3, 5))
    hcw = np.concatenate([np.asarray(inp["hy_conv_w"]), np.asarray(inp["hy_conv_b"])[:, None, :]], axis=1)
    w["hcw"] = f(hcw.reshape(2, 4, 6, 128).transpose(0, 3, 2, 1))
    w["hy_w1"] = f(inp["hy_w1"])
    w["hy_fb"] = f(np.stack([inp["hy_b1"], inp["hy_freq"], inp["hy_b2"]], axis=-1))
    w["hy_w2"] = f(inp["hy_w2"])
    w["hy_w3"] = f(inp["hy_w3"])
    w["hy_bias"] = f(np.asarray(inp["hy_bias"]).reshape(2, 2, 2, 128).transpose(0, 3, 1, 2))
    w["gqk"] = f(np.stack([np.asarray(inp["g_q"]).reshape(2, 128), np.asarray(inp["g_k"]).reshape(2, 128)], axis=-1))
    w["lam"] = f(np.asarray(inp["lam"]).reshape(2, 1, 256))
    w["gsub"] = f(np.asarray(inp["g_sub"]).reshape(2, 128, 1))
    w["w_br"] = f(np.concatenate([inp["w_hy"], inp["w_fn"], inp["w_at"]], axis=1))
    w["w_out"] = f(inp["w_out"])
    w["peer_wq"] = f(inp["peer_wq"])
    w["keysT"] = f(np.asarray(inp["peer_keys"]).reshape(2, 16, 128, 128).transpose(0, 3, 1, 2))
    w["uT"] = f(np.asarray(inp["peer_u"]).transpose(0, 2, 1))
    w["peer_v"] = f(inp["peer_v"])
    w.update(_consts())
    return w


def _core_inputs(inp, b):
    X = np.concatenate([np.asarray(inp["x"][b]), np.asarray(inp["ctx"][b])], axis=0)
    xT = np.ascontiguousarray(X.T.reshape(8, 128, T).transpose(1, 0, 2)).astype(np.float32)
    cc = np.stack([np.asarray(inp["c"][b]), np.asarray(inp["c_ctx"])], axis=-1)
    cc = np.ascontiguousarray(cc.reshape(8, 128, 2).transpose(1, 0, 2)).astype(np.float32)
    return {"xT": xT, "cc": cc}


_NC = {}


def kernel(**inp):
    w = _prep(inp)
    if "nc" not in _NC:
        _NC["nc"] = build()
    nc = _NC["nc"]
    in_maps = []
    for b in range(8):
        m = dict(w)
        m.update(_core_inputs(inp, b))
        in_maps.append(m)
    res = run_bass_kernel_spmd(nc, in_maps, core_ids=list(range(8)))
    out = np.empty((8, S, D), np.float32)
    for b in range(8):
        yT = np.asarray(res.results[b]["yT"])
        out[b] = yT.transpose(2, 1, 0).reshape(S, D)
    return out
```

```python
import numpy as np, math
from contextlib import ExitStack
import concourse.bass as bass
import concourse.mybir as mybir
from concourse.bass_utils import run_bass_kernel_spmd

F32 = mybir.dt.float32
BF16 = mybir.dt.bfloat16
ALU = mybir.AluOpType
AF = mybir.ActivationFunctionType
AX = mybir.AxisListType

NSLOT = 40
ENGS = ("pe", "act", "dve", "pool", "sp")


class Buf:
    __slots__ = ("w", "rs", "rd")

    def __init__(self):
        self.w = None
        self.rs = {}
        self.rd = []


class Op:
    __slots__ = ("eng", "fn", "deps", "sig", "cnt", "slot", "dma")

    def __init__(self, eng, fn, dma=False):
        self.eng = eng
        self.fn = fn
        self.deps = ()
        self.sig = False
        self.cnt = 0
        self.slot = -1
        self.dma = dma


class Prog:
    def __init__(self, nc):
        self.nc = nc
        self.streams = {e: [] for e in ENGS}
        self.dmas = []
        self.live_dmas = []
        self.last_real = {e: None for e in ENGS}

    def op(self, eng, fn, reads=(), writes=(), dma=False):
        o = Op(eng, fn, dma)
        deps = set()
        for b in reads:
            if b.w is not None:
                deps.add(b.w)
        for b in writes:
            if b.w is not None:
                deps.add(b.w)
            deps.update(b.rs.values())
            deps.update(b.rd)
        if eng == "pe" and not dma:
            deps = {d for d in deps if d.dma or d.eng != "pe"}
        o.deps = deps
        for b in writes:
            b.w = o
            b.rs = {}
            b.rd = []
        for b in reads:
            if dma:
                b.rd.append(o)
            else:
                b.rs[eng] = o
        self.streams[eng].append(o)
        if not dma:
            self.last_real[eng] = o
        if dma:
            self.dmas.append(o)
            self.live_dmas.append(o)
        return o

    def dma(self, out, in_, reads=(), writes=(), eng="sp"):
        return self.op(eng, lambda e: e.dma_start(out=out, in_=in_), reads, writes, dma=True)

    def barrier(self):
        last = dict(self.last_real)
        live = list(self.live_dmas)
        self.live_dmas = []
        for e in ENGS:
            o = Op(e, None)
            o.deps = {last[x] for x in ENGS if x != e and last[x] is not None}
            o.deps.update(live)
            self.streams[e].append(o)

    def emit(self, final_dmas):
        nc = self.nc
        for e in ENGS:
            for o in self.streams[e]:
                for d in o.deps:
                    d.sig = True
        with ExitStack() as es:
            sems = {e: es.enter_context(nc.semaphore("s_" + e)) for e in ENGS}
            dsem = [es.enter_context(nc.semaphore("d%d" % i)) for i in range(NSLOT)]
            for e in ENGS:
                c = 0
                for o in self.streams[e]:
                    if o.dma:
                        continue
                    if o.sig:
                        c += 1
                        o.cnt = c
            slot_cnt = [0] * NSLOT
            slot_prev = [None] * NSLOT
            for i, o in enumerate(self.dmas):
                s = i % NSLOT
                o.slot = s
                slot_cnt[s] += 16
                o.cnt = slot_cnt[s]
                if slot_prev[s] is not None:
                    o.deps = set(o.deps)
                    o.deps.add(slot_prev[s])
                slot_prev[s] = o
            block = es.enter_context(nc.Block())

            def run(ename, eng):
                waited = {}
                for o in self.streams[ename]:
                    for d in o.deps:
                        sem = dsem[d.slot] if d.dma else sems[d.eng]
                        if waited.get(sem.name, 0) >= d.cnt:
                            continue
                        eng.wait_ge(sem, d.cnt)
                        waited[sem.name] = d.cnt
                    if o.fn is None:
                        continue
                    ins = o.fn(eng)
                    if o.dma:
                        ins.then_inc(dsem[o.slot], 16)
                    elif o.sig:
                        ins.then_inc(sems[ename], 1)
                if ename == "sp":
                    for d in final_dmas:
                        eng.wait_ge(dsem[d.slot], d.cnt)

            @block.sync
            def _(e):
                run("sp", e)

            @block.tensor
            def _(e):
                run("pe", e)

            @block.scalar
            def _(e):
                run("act", e)

            @block.vector
            def _(e):
                run("dve", e)

            @block.gpsimd
            def _(e):
                run("pool", e)

D = 1024
S = 2048
C = 256
T = S + C
EPS = 1e-6
PI = math.pi
NE = 16384


def tblocks(lo, hi, step=512):
    return [(t, min(step, hi - t)) for t in range(lo, hi, step)]


def build(depth=2, stages=None, dbg=()):
    nc = bass.Bass("TRN2", target_bir_lowering=False)
    P = Prog(nc)

    def din(name, shape, dt=F32):
        return nc.dram_tensor(name, list(shape), dt, kind="ExternalInput").ap()

    def dscr(name, shape, dt=F32):
        if name in dbg:
            return nc.dram_tensor(name, list(shape), dt, kind="ExternalOutput").ap()
        return nc.dram_tensor(name, list(shape), dt).ap()

    xT_d = din("xT", [128, 8, T])
    cc_d = din("cc", [128, 8, 2])
    w_ada_d = din("w_ada", [2, D, 6 * D])
    bada_d = din("bada", [2, 128, 6, 8])
    gmf_d = din("gmf", [2, 128, 2, 8])
    w_in_d = din("w_in", [2, D, 5632])
    w_gate_d = din("w_gate", [2, 8, 128, 8, 3, 128])
    hcw_d = din("hcw", [2, 128, 6, 4])
    hy_w1_d = din("hy_w1", [2, 33, 64])
    hy_fb_d = din("hy_fb", [2, 64, 3])
    hy_w2_d = din("hy_w2", [2, 64, 64])
    hy_w3_d = din("hy_w3", [2, 64, 1024])
    hy_bias_d = din("hy_bias", [2, 128, 2, 2])
    gqk_d = din("gqk", [2, 128, 2])
    lam_d = din("lam", [2, 1, 256])
    gsub_d = din("gsub", [2, 128, 1])
    w_br_d = din("w_br", [2, D, D])
    w_out_d = din("w_out", [2, D, D])
    wq_d = din("peer_wq", [2, D, 2048])
    keysT_d = din("keysT", [2, 128, 16, 128])
    uT_d = din("uT", [2, D, NE])
    v_d_in = din("peer_v", [2, NE, D])
    cst_d = din("cst", [128, 6, 128])
    rope_d = din("rope", [2, 128, S])
    feats_d = {L: din("feats%d" % L, [33, L]) for L in (S, C)}
    dec_d = {L: din("dec%d" % L, [L, 256]) for L in (S, C)}
    tF_d = {L: din("tF%d" % L, [2, L // 128, 128, L // 128, 128], BF16) for L in (S, C)}
    RL = {}
    for L in (S, C):
        TB = min(512, L)
        G = min(4, L // 128)
        RL[L] = (TB, G, L // TB, (L // 128) // G)
    tI_d = {L: din("tI%d" % L, [2, RL[L][2], RL[L][3], 128, RL[L][1], RL[L][0]], BF16) for L in (S, C)}
    tN_d = {L: din("tN%d" % L, [2, RL[L][2], RL[L][3], 128, RL[L][1], RL[L][0]], BF16) for L in (S, C)}
    yT_d = nc.dram_tensor("yT", [128, 8, S], F32, kind="ExternalOutput").ap()
    hy_s = dscr("hy_s", [6, 128, T])
    fn_s = dscr("fn_s", [2, 128, T])
    q_s = dscr("q_s", [4, 128, T], BF16)
    k_s = dscr("k_s", [4, 128, T], BF16)
    v_s = dscr("v_s", [128, 18, 512], BF16)
    kf_s = {L: dscr("kf_s%d" % L, [2, L // 128, 128, 512]) for L in (S, C)}
    hyo_s = dscr("hyo_s", [2, 128, T], BF16)
    fno_s = dscr("fno_s", [2, 128, T], BF16)
    ato_s = dscr("ato_s", [4, 128, T], BF16)
    nT_s = dscr("nT_s", [128, 8, T], BF16)
    uTb_s = dscr("uTb_s", [32, 128, 8, 512], BF16)
    vb_s = dscr("vb_s", [128, 128, D], BF16)
    wT_s = dscr("wT_s", [2, 128, 128, 128], BF16)
    dbg_out = {}

    es = ExitStack()
    xT = es.enter_context(nc.sbuf_tensor("xT_sb", [128, 8, T], F32))
    cst = es.enter_context(nc.sbuf_tensor("cst_sb", [128, 6, 128], F32))
    cstb = es.enter_context(nc.sbuf_tensor("cstb_sb", [128, 2, 128], BF16))
    sm = es.enter_context(nc.sbuf_tensor("small_sb", [128, 512], F32))
    ARENA = 33280
    AR = es.enter_context(nc.sbuf_tensor("arena", [128, ARENA], F32))
    PS = [es.enter_context(nc.psum_tensor("ps%d" % i, [128, 512], F32)) for i in range(8)]
    PB = [Buf() for _ in range(8)]
    ident32 = cst[:, 0, :]
    ones32 = cst[:, 1, :]
    bd64 = cst[:, 2, :]
    rot = cst[:, 3, :]
    bdc = cst[:, 4, :]
    bds = cst[:, 5, :]
    identb = cstb[:, 0, :]
    onesb = cstb[:, 1, :]
    b_cst = Buf()
    b_x = [[Buf() for _ in range(5)] for _ in range(8)]
    b_sm = Buf()
    sc = sm[:, 0:16].rearrange("p (k j) -> p k j", k=8)
    modv = sm[:, 16:112].rearrange("p (g m j) -> p g m j", g=6, m=8)
    A_mix = sm[:, 112:128].rearrange("p (k j) -> p k j", k=8)
    A_ffn = sm[:, 128:144].rearrange("p (k j) -> p k j", k=8)
    gmf = sm[:, 144:160].rearrange("p (a k) -> p a k", a=2)
    tmp16 = sm[:, 160:176].rearrange("p (k j) -> p k j", k=8)
    gqk = sm[:, 176:178]
    gsub_s = sm[:, 178:179]
    neg_lam = sm[:, 179:180]
    lamw = sm[:, 180:182]
    hcw = sm[:, 184:208].rearrange("p (c t) -> p c t", c=6)
    hbias = sm[:, 208:212].rearrange("p (o c) -> p o c", o=2)
    fb = sm[:, 212:217]
    bada = sm[:, 224:272].rearrange("p (g m) -> p g m", g=6)
    lamt = sm[:, 272:400]
    epsc = sm[:, 183:184]

    class Arena:
        def __init__(self):
            self.off = 0

        def reset(self):
            P.barrier()
            self.off = 0

        def f32(self, n):
            a = AR[:, self.off:self.off + n]
            self.off += n
            assert self.off <= ARENA, self.off
            return a

        def bf(self, n):
            w = (n + 1) // 2
            return self.f32(w).bitcast(BF16)[:, 0:n]

    ar = Arena()

    def mm(out, lhsT, rhs, start, stop, rd, wr):
        P.op("pe", lambda e: e.matmul(out, lhsT=lhsT, rhs=rhs, start=start, stop=stop), rd, wr)

    def tr(out, in_, idn, rd, wr):
        P.op("pe", lambda e: e.transpose(out=out, in_=in_, identity=idn), rd, wr)

    def act(out, in_, func, rd, wr, bias=0.0, scale=1.0):
        P.op("act", lambda e: e.activation(out=out, in_=in_, func=func, bias=bias, scale=scale), rd, wr)

    def tt(eng, out, in0, in1, op, rd, wr):
        P.op(eng, lambda e: e.tensor_tensor(out=out, in0=in0, in1=in1, op=op), rd, wr)

    def ts(eng, out, in0, s1, s2, op0, op1, rd, wr):
        if s2 is None:
            P.op(eng, lambda e: e.tensor_scalar(out=out, in0=in0, scalar1=s1, scalar2=None, op0=op0), rd, wr)
        else:
            P.op(eng, lambda e: e.tensor_scalar(out=out, in0=in0, scalar1=s1, scalar2=s2, op0=op0, op1=op1), rd, wr)

    def stt(out, in0, scalar, in1, op0, op1, rd, wr):
        P.op("dve", lambda e: e.scalar_tensor_tensor(out=out, in0=in0, scalar=scalar, in1=in1, op0=op0, op1=op1), rd, wr)

    def cp(eng, out, in_, rd, wr):
        if eng == "act":
            act(out, in_, AF.Copy, rd, wr)
        else:
            P.op(eng, lambda e: e.tensor_copy(out=out, in_=in_), rd, wr)

    def recip(out, in_, rd, wr):
        P.op("dve", lambda e: e.reciprocal(out=out, in_=in_), rd, wr)

    def rsqrt_mean(out, in_, n, rd, wr):
        act(out, in_, AF.Sqrt, list(rd) + [b_sm], wr, bias=epsc, scale=1.0 / n)
        recip(out, out, wr, wr)

    P.dma(cst[:, :, :], cst_d, writes=[b_cst])
    for k in range(8):
        P.dma(xT[:, k, :], xT_d[:, k, :], writes=b_x[k])
    P.dma(sc, cc_d, writes=[b_sm])
    cp("dve", cstb[:, 0, :], ident32, [b_cst], [b_cst])
    cp("dve", cstb[:, 1, :], ones32, [b_cst], [b_cst])
    act(sc, sc, AF.Silu, [b_sm], [b_sm])
    P.op("dve", lambda e: e.memset(epsc, EPS), [], [b_sm])
    ar.reset()

    def want(name):
        return stages is None or name in stages

    def layer(l):
        last = l == depth - 1
        lam_init = 0.8 - 0.6 * math.exp(-0.3 * l)
        streams = [(0, S)] if last else [(0, S), (S, C)]
        tb_all = tblocks(0, T)
        tb_mix = tblocks(0, S) if last else tb_all

        ar.reset()
        P.dma(bada, bada_d[l], writes=[b_sm])
        P.dma(gmf, gmf_d[l], writes=[b_sm])
        P.dma(gqk, gqk_d[l], writes=[b_sm])
        P.dma(gsub_s, gsub_d[l], writes=[b_sm])
        P.dma(hcw, hcw_d[l], writes=[b_sm])
        P.dma(hbias, hy_bias_d[l], writes=[b_sm])
        P.dma(fb[0:64, 0:3], hy_fb_d[l], writes=[b_sm])
        P.dma(lamt, lam_d[l][:, 0:128].partition_broadcast(128), writes=[b_sm])
        lam2 = ar.f32(128)
        b_l2 = Buf()
        P.dma(lam2, lam_d[l][:, 128:256].partition_broadcast(128), writes=[b_l2])
        wts = [(ar.f32(8 * 1024), Buf()) for _ in range(2)]
        for g in range(6):
            wt, bw = wts[g % 2]
            wt3 = wt.rearrange("p (k m) -> p k m", k=8)
            P.dma(wt3, w_ada_d[l][:, g * 1024:(g + 1) * 1024].rearrange("(k p) m -> p k m", p=128), writes=[bw])
            for m in range(8):
                for k in range(8):
                    mm(PS[0][:, m * 2:m * 2 + 2], wt3[:, k, m * 128:(m + 1) * 128], sc[:, k, :], k == 0, k == 7, [bw, b_sm], [PB[0]])
            tt("dve", modv[:, g], PS[0][:, 0:16].rearrange("p (m j) -> p m j", m=8),
               bada[:, g, :].unsqueeze(2).to_broadcast([128, 8, 2]), ALU.add, [PB[0], b_sm], [b_sm])
        for (Aap, gi, si) in ((A_mix, 0, 1), (A_ffn, 1, 4)):
            ts("dve", tmp16, modv[:, si], 1.0, None, ALU.add, None, [b_sm], [b_sm])
            tt("dve", Aap, tmp16, gmf[:, gi, :].unsqueeze(2).to_broadcast([128, 8, 2]), ALU.mult, [b_sm], [b_sm])
        tt("dve", lamt[:, 0:64], lamt[:, 0:64], lamt[:, 64:128], ALU.mult, [b_sm], [b_sm])
        tt("dve", lam2[:, 0:64], lam2[:, 0:64], lam2[:, 64:128], ALU.mult, [b_l2], [b_l2])
        P.op("dve", lambda e: e.reduce_sum(out=lamw[:, 0:1], in_=lamt[:, 0:64], axis=AX.X), [b_sm], [b_sm])
        P.op("dve", lambda e: e.reduce_sum(out=lamw[:, 1:2], in_=lam2[:, 0:64], axis=AX.X), [b_l2, b_sm], [b_sm])
        act(lamw, lamw, AF.Exp, [b_sm], [b_sm])
        tt("dve", neg_lam, lamw[:, 1:2], lamw[:, 0:1], ALU.subtract, [b_sm], [b_sm])
        ts("dve", neg_lam, neg_lam, -lam_init, None, ALU.add, None, [b_sm], [b_sm])
        ts("dve", gsub_s, gsub_s, 1.0 - lam_init, None, ALU.mult, None, [b_sm], [b_sm])
        tt("dve", fb[0:64, 3:4], fb[0:64, 0:1], fb[0:64, 1:2], ALU.mult, [b_sm], [b_sm])
        tt("dve", fb[0:64, 4:5], fb[0:64, 2:3], fb[0:64, 1:2], ALU.mult, [b_sm], [b_sm])

        def mod_tmps(w, n=(3, 2, 3)):
            return dict(sq=[(ar.f32(w), Buf()) for _ in range(n[0])], rs=[(ar.f32(w), Buf()) for _ in range(n[1])],
                        tm=[(ar.f32(w), Buf()) for _ in range(n[2])], c=[0, 0, 0])

        def modulate(Aap, gB, blocks, emit_cb, mt, pbk=7):
            for (t0, Tn) in blocks:
                j = 0 if t0 < S else 1
                xb = t0 // 512
                for k in range(8):
                    sq, b_sq = mt["sq"][mt["c"][0] % len(mt["sq"])]
                    mt["c"][0] += 1
                    act(sq[:, :Tn], xT[:, k, t0:t0 + Tn], AF.Square, [b_x[k][xb]], [b_sq])
                    mm(PS[pbk][:, :Tn], ones32, sq[:, :Tn], k == 0, k == 7, [b_sq, b_cst], [PB[pbk]])
                r, b_r = mt["rs"][mt["c"][1] % len(mt["rs"])]
                mt["c"][1] += 1
                rsqrt_mean(r[:, :Tn], PS[pbk][:, :Tn], D, [PB[pbk]], [b_r])
                for k in range(8):
                    tm, b_t = mt["tm"][mt["c"][2] % len(mt["tm"])]
                    mt["c"][2] += 1
                    tt("dve", tm[:, :Tn], xT[:, k, t0:t0 + Tn], r[:, :Tn], ALU.mult, [b_x[k][xb], b_r], [b_t])
                    emit_cb(k, t0, Tn, tm, b_t, Aap[:, k, j:j + 1], modv[:, gB, k, j:j + 1])

        def hyena_filters(L):
            ar.reset()
            nj = L // 128
            w1 = ar.f32(64)
            w2 = ar.f32(64)
            w3 = ar.f32(1024)
            b_w = Buf()
            P.dma(w1[0:33, :], hy_w1_d[l], writes=[b_w])
            P.dma(w2[0:64, :], hy_w2_d[l], writes=[b_w])
            P.dma(w3[0:64, :], hy_w3_d[l], writes=[b_w])
            h2T = ar.f32(L)
            b_h2 = Buf()
            off_mlp = ar.off
            ft = [(ar.f32(512), Buf()) for _ in range(2)]
            aa = [(ar.f32(512), Buf()) for _ in range(2)]
            h1 = [(ar.f32(512), Buf()) for _ in range(2)]
            aa2 = [(ar.f32(512), Buf()) for _ in range(2)]
            sx = [(ar.f32(512), Buf()) for _ in range(3)]

            def sin_act(out, a, Tn, b_in, b_out):
                (s4, b_s4), (c4, b_c4), (q, b_q) = sx
                act(s4[0:64, :Tn], a, AF.Sin, [b_in], [b_s4], scale=0.25)
                act(c4[0:64, :Tn], a, AF.Abs, [b_in], [b_c4])
                act(c4[0:64, :Tn], c4[0:64, :Tn], AF.Sin, [b_c4, b_sm], [b_c4], bias=halfpi[0:64, :], scale=-0.25)
                tt("dve", q[0:64, :Tn], s4[0:64, :Tn], s4[0:64, :Tn], ALU.mult, [b_s4], [b_q])
                ts("dve", q[0:64, :Tn], q[0:64, :Tn], -2.0, 1.0, ALU.mult, ALU.add, [b_q], [b_q])
                tt("dve", c4[0:64, :Tn], s4[0:64, :Tn], c4[0:64, :Tn], ALU.mult, [b_s4, b_c4], [b_c4])
                stt(out, c4[0:64, :Tn], 4.0, q[0:64, :Tn], ALU.mult, ALU.mult, [b_c4, b_q], [b_out])

            for bi, (t0, Tn) in enumerate(tblocks(0, L)):
                f, b_f = ft[bi % 2]
                P.dma(f[0:33, :Tn], feats_d[L][:, t0:t0 + Tn], writes=[b_f])
                mm(PS[0][0:64, :Tn], w1[0:33, :], f[0:33, :Tn], True, True, [b_w, b_f], [PB[0]])
                a, b_a = aa[bi % 2]
                ts("dve", a[0:64, :Tn], PS[0][0:64, :Tn], fb[0:64, 1:2], fb[0:64, 3:4], ALU.mult, ALU.add, [PB[0], b_sm], [b_a])
                hh, b_h = h1[bi % 2]
                sin_act(hh[0:64, :Tn], a[0:64, :Tn], Tn, b_a, b_h)
                mm(PS[1][0:64, :Tn], w2[0:64, :], hh[0:64, :Tn], True, True, [b_w, b_h], [PB[1]])
                a2_, b_a2 = aa2[bi % 2]
                ts("dve", a2_[0:64, :Tn], PS[1][0:64, :Tn], fb[0:64, 1:2], fb[0:64, 4:5], ALU.mult, ALU.add, [PB[1], b_sm], [b_a2])
                sin_act(h2T[0:64, t0:t0 + Tn], a2_[0:64, :Tn], Tn, b_a2, b_h2)
            dec = ar.f32(nj * 256).rearrange("p (j c) -> p j c", j=nj)
            off_dec_end = ar.off
            b_dec = Buf()
            P.dma(dec, dec_d[L].rearrange("(j p) c -> p j c", p=128), writes=[b_dec])
            hd = ar.f32(nj * 1024).rearrange("p (j c) -> p j c", j=nj)
            b_hd = [Buf() for _ in range(nj)]
            off_hd = ar.off
            for pc in range(nj):
                for half in range(2):
                    pb = half
                    mm(PS[pb][:, :], h2T[0:64, pc * 128:(pc + 1) * 128], w3[0:64, half * 512:(half + 1) * 512], True, True, [b_h2, b_w], [PB[pb]])
                    tt("dve", hd[:, pc, half * 512:(half + 1) * 512].rearrange("p (o c) -> p o c", o=2),
                       PS[pb][:, :].rearrange("p (o c) -> p o c", o=2),
                       dec[:, pc, :].unsqueeze(1).to_broadcast([128, 2, 256]), ALU.mult, [PB[pb], b_dec], [b_hd[pc]])
            P.op("dve", lambda e: e.memset(hd[0:1, 0, 512:1024], 0.0), [], [b_hd[0]])
            ab = [(ar.f32(1024), Buf()) for _ in range(2)]
            for pc in range(nj):
                a, b_a = ab[pc % 2]
                act(a, hd[:, pc, :], AF.Abs, [b_hd[pc]], [b_a])
                for half in range(2):
                    mm(PS[2 + half][:, :], ones32, a[:, half * 512:(half + 1) * 512], pc == 0, pc == nj - 1, [b_a, b_cst], [PB[2 + half]])
            rn = ar.f32(512)
            b_rn = Buf()
            cp("dve", rn, PS[2][:, :], [PB[2]], [b_rn])
            tt("dve", rn, rn, PS[3][:, :], ALU.add, [b_rn, PB[3]], [b_rn])
            recip(rn, rn, [b_rn], [b_rn])
            tmpe = [(ar.f32(512), Buf()) for _ in range(2)]
            P.barrier()
            ar.off = 0
            hdb = ar.bf(nj * 1024).rearrange("p (j c) -> p j c", j=nj)
            b_hdb = [Buf() for _ in range(nj)]
            tabs = [(ar.bf(2 * nj * 128).rearrange("p (c j f) -> p c j f", c=2, j=nj), Buf()) for _ in range(2)]
            assert ar.off <= off_dec_end
            ar.off = off_hd
            sts = [(ar.f32(1024).rearrange("p (c n) -> p c n", c=2), Buf()) for _ in range(1)]
            for pc in range(nj):
                te, b_te = tmpe[pc % 2]
                hf = hd[:, pc, 0:512]
                hb = hd[:, pc, 512:1024]
                tt("dve", te, hf, hb, ALU.add, [b_hd[pc]], [b_te])
                tt("pool", hb, hf, hb, ALU.subtract, [b_hd[pc]], [b_hd[pc]])
                tt("dve", hdb[:, pc, 0:512], te, rn, ALU.mult, [b_te, b_rn], [b_hdb[pc]])
                tt("pool", hdb[:, pc, 512:1024], hb, rn, ALU.mult, [b_hd[pc], b_rn], [b_hdb[pc]])
            for fc in range(nj):
                tb_, b_tb = tabs[fc % 2]
                for c in range(2):
                    P.dma(tb_[:, c], tF_d[L][c, fc], writes=[b_tb])
                for c in range(2):
                    for jc in range(nj):
                        mm(PS[4 + c][:, :], tb_[:, c, jc, :], hdb[:, jc, c * 512:(c + 1) * 512], jc == 0, jc == nj - 1, [b_tb, b_hdb[jc]], [PB[4 + c]])
                st, b_st = sts[0]
                cp("act", st[:, 0, :], PS[4][:, :], [PB[4]], [b_st])
                cp("dve", st[:, 1, :], PS[5][:, :], [PB[5]], [b_st])
                for c in range(2):
                    P.dma(kf_s[L][c, fc], st[:, c, :], reads=[b_st], writes=[b_kf[L]])

        b_kf = {S: Buf(), C: Buf()}
        halfpi = sm[:, 182:183]
        P.op("dve", lambda e: e.memset(halfpi, PI / 2), [], [b_sm])
        if want("hyf"):
            for (t0, L) in streams:
                hyena_filters(L)

        ar.reset()
        nT = ar.bf(8 * T).rearrange("p (k t) -> p k t", k=8)
        b_n = [[Buf() for _ in range(5)] for _ in range(8)]
        b_nTs = Buf()

        def emit_mix(k, t0, Tn, tm, b_t, Asc, Bsc):
            act(nT[:, k, t0:t0 + Tn], tm[:, :Tn], AF.Identity, [b_t, b_sm], [b_n[k][t0 // 512]], bias=Bsc, scale=Asc)

        mark = ar.off
        if want("proj") or want("merge"):
            modulate(A_mix, 0, tb_all, emit_mix, mod_tmps(512))
            for k in range(8):
                P.dma(nT_s[:, k, :], nT[:, k, :], reads=b_n[k], writes=[b_nTs])
        ar.off = mark
        P.barrier()

        b_hy = Buf()
        b_fn = Buf()
        b_q = Buf()
        b_k = Buf()
        b_v = Buf()
        if want("proj"):
            w32 = [(ar.f32(8 * 512).rearrange("p (k m) -> p k m", k=8), Buf()) for _ in range(1)]
            wbf = [(ar.bf(8 * 512).rearrange("p (k m) -> p k m", k=8), Buf()) for _ in range(2)]
            stg = [(ar.f32(512), Buf()) for _ in range(2)]
            sqt = [(ar.f32(512), Buf()) for _ in range(2)]
            rt = [(ar.f32(512), Buf()) for _ in range(2)]
            xnt = [(ar.f32(512), Buf()) for _ in range(2)]
            t1t = [(ar.f32(512), Buf()) for _ in range(2)]
            t2t = [(ar.f32(512), Buf()) for _ in range(2)]
            obt = [(ar.bf(512), Buf()) for _ in range(2)]
            rope_sb = ar.f32(2 * S).rearrange("p (c t) -> p c t", c=2)
            b_rope = Buf()
            for c in range(2):
                P.dma(rope_sb[:, c, :], rope_d[c], writes=[b_rope])
            cnt = [0]

            def qk_cb(ps, pb, which, h, t0, Tn):
                i = cnt[0] % 2
                cnt[0] += 1
                sq_, b_sq_ = sqt[i]
                act(sq_[:, :Tn], ps[:, :Tn], AF.Square, [pb], [b_sq_])
                mm(PS[6][:, :Tn], bd64, sq_[:, :Tn], True, True, [b_sq_, b_cst], [PB[6]])
                r_, b_r_ = rt[i]
                rsqrt_mean(r_[:, :Tn], PS[6][:, :Tn], 64, [PB[6]], [b_r_])
                xn, b_xn = xnt[i]
                stt(xn[:, :Tn], ps[:, :Tn], gqk[:, which:which + 1], r_[:, :Tn], ALU.mult, ALU.mult, [pb, b_r_, b_sm], [b_xn])
                ob, b_ob = obt[i]
                if t0 < S:
                    mm(PS[5][:, :Tn], rot, xn[:, :Tn], True, True, [b_xn, b_cst], [PB[5]])
                    t1, b_t1 = t1t[i]
                    t2, b_t2 = t2t[i]
                    tt("pool", t1[:, :Tn], xn[:, :Tn], rope_sb[:, 0, t0:t0 + Tn], ALU.mult, [b_xn, b_rope], [b_t1])
                    tt("dve", t2[:, :Tn], PS[5][:, :Tn], rope_sb[:, 1, t0:t0 + Tn], ALU.mult, [PB[5], b_rope], [b_t2])
                    tt("pool", ob[:, :Tn], t1[:, :Tn], t2[:, :Tn], ALU.add, [b_t1, b_t2], [b_ob])
                else:
                    cp("pool", ob[:, :Tn], xn[:, :Tn], [b_xn], [b_ob])
                dst = (q_s if which == 0 else k_s)[h][:, t0:t0 + Tn]
                P.dma(dst, ob[:, :Tn], reads=[b_ob], writes=[b_q if which == 0 else b_k])

            pcnt = [0]
            for g in range(5):
                w3_, b_w3 = w32[0]
                P.dma(w3_, w_in_d[l][:, g * 512:(g + 1) * 512].rearrange("(k p) m -> p k m", p=128), writes=[b_w3])
                wb_, b_wb = wbf[g % 2]
                cp("act", wb_[:, 0:4, :], w3_[:, 0:4, :], [b_w3], [b_wb])
                cp("pool", wb_[:, 4:8, :], w3_[:, 4:8, :], [b_w3], [b_wb])
                if g < 4:
                    for mi in range(4):
                        for (t0, Tn) in tb_all:
                            if g == 2 and t0 >= S and last:
                                continue
                            pi = pcnt[0] % 4
                            pcnt[0] += 1
                            for k in range(8):
                                mm(PS[pi][:, :Tn], wb_[:, k, mi * 128:(mi + 1) * 128], nT[:, k, t0:t0 + Tn], k == 0, k == 7,
                                   [b_wb, b_n[k][t0 // 512]], [PB[pi]])
                            if g < 2:
                                st, b_st = stg[pcnt[0] % 2]
                                cp("act" if pcnt[0] % 2 else "dve", st[:, :Tn], PS[pi][:, :Tn], [PB[pi]], [b_st])
                                ch = g * 4 + mi
                                if ch < 6:
                                    P.dma(hy_s[ch][:, t0:t0 + Tn], st[:, :Tn], reads=[b_st], writes=[b_hy])
                                else:
                                    P.dma(fn_s[ch - 6][:, t0:t0 + Tn], st[:, :Tn], reads=[b_st], writes=[b_fn])
                            else:
                                qk_cb(PS[pi], PB[pi], g - 2, mi, t0, Tn)
                else:
                    for i in range(18):
                        pi = pcnt[0] % 4
                        pcnt[0] += 1
                        for k in range(8):
                            mm(PS[pi][:, :], nT[:, k, i * 128:(i + 1) * 128], wb_[:, k, :], k == 0, k == 7, [b_wb, b_n[k][i // 4]], [PB[pi]])
                        ob, b_ob = obt[i % 2]
                        cp("act" if i % 2 else "dve", ob, PS[pi][:, :], [PB[pi]], [b_ob])
                        P.dma(v_s[:, i, :], ob, reads=[b_ob], writes=[b_v])

        b_ato = Buf()
        if want("attn"):
            ar.reset()
            kT = ar.bf(4 * T).rearrange("p (h t) -> p h t", h=4)
            qT = ar.bf(4 * T).rearrange("p (h t) -> p h t", h=4)
            vv = ar.bf(18 * 512).rearrange("p (j c) -> p j c", j=18)
            b_kT = Buf()
            b_qT = Buf()
            b_vv = Buf()
            for h in range(4):
                P.dma(kT[:, h, :], k_s[h], reads=[b_k], writes=[b_kT])
                P.dma(qT[:, h, :], q_s[h], reads=[b_q], writes=[b_qT])
            P.dma(vv, v_s, reads=[b_v], writes=[b_vv])
            Et = [(ar.bf(512), Buf()) for _ in range(4)]
            r0 = ar.f32(512)
            t0_ = ar.f32(512)
            t1_ = ar.f32(512)
            sq_ = ar.f32(512)
            rr_ = ar.f32(512)
            b_r0, b_t0, b_t1, b_sq2, b_rr = [Buf() for _ in range(5)]
            aob = [(ar.bf(512), Buf()) for _ in range(2)]
            ec = 0
            oc = 0
            qblocks = tblocks(0, S) if last else tb_all
            for h in range(4):
                for (q0, Tn) in qblocks:
                    keys = list(range(18)) if q0 < S else [16, 17]
                    seq = [(c, idx, j) for c in range(2) for idx, j in enumerate(keys)]

                    SB3 = (0, 1, 7)

                    def s_mm(n_):
                        c, idx, j = seq[n_]
                        sp = SB3[(ec + n_) % 3]
                        mm(PS[sp][:, :Tn], kT[64 * c:64 * c + 64, h, j * 128:(j + 1) * 128], qT[64 * c:64 * c + 64, h, q0:q0 + Tn],
                           True, True, [b_kT, b_qT], [PB[sp]])

                    s_mm(0)
                    if len(seq) > 1:
                        s_mm(1)
                    for n_, (c, idx, j) in enumerate(seq):
                        if n_ + 2 < len(seq):
                            s_mm(n_ + 2)
                        sp = SB3[(ec + n_) % 3]
                        E, b_E = Et[(ec + n_) % 4]
                        act(E[:, :Tn], PS[sp][:, :Tn], AF.Exp, [PB[sp]], [b_E], scale=0.125)
                        mm(PS[2 + 2 * c][:, :Tn], vv[:, j, h * 128:(h + 1) * 128], E[:, :Tn], idx == 0, idx == len(keys) - 1, [b_vv, b_E], [PB[2 + 2 * c]])
                        mm(PS[3 + 2 * c][:, :Tn], onesb, E[:, :Tn], idx == 0, idx == len(keys) - 1, [b_cst, b_E], [PB[3 + 2 * c]])
                    ec += len(seq)
                    recip(r0[:, :Tn], PS[3][:, :Tn], [PB[3]], [b_r0])
                    tt("dve", t0_[:, :Tn], PS[2][:, :Tn], r0[:, :Tn], ALU.mult, [PB[2], b_r0], [b_t0])
                    recip(r0[:, :Tn], PS[5][:, :Tn], [PB[5]], [b_r0])
                    tt("dve", t1_[:, :Tn], PS[4][:, :Tn], r0[:, :Tn], ALU.mult, [PB[4], b_r0], [b_t1])
                    stt(t0_[:, :Tn], t1_[:, :Tn], neg_lam, t0_[:, :Tn], ALU.mult, ALU.add, [b_t1, b_t0, b_sm], [b_t0])
                    act(sq_[:, :Tn], t0_[:, :Tn], AF.Square, [b_t0], [b_sq2])
                    mm(PS[6][:, :Tn], ones32, sq_[:, :Tn], True, True, [b_sq2, b_cst], [PB[6]])
                    rsqrt_mean(rr_[:, :Tn], PS[6][:, :Tn], 128, [PB[6]], [b_rr])
                    ao, b_ao = aob[oc % 2]
                    oc += 1
                    stt(ao[:, :Tn], t0_[:, :Tn], gsub_s, rr_[:, :Tn], ALU.mult, ALU.mult, [b_t0, b_rr, b_sm], [b_ao])
                    P.dma(ato_s[h][:, q0:q0 + Tn], ao[:, :Tn], reads=[b_ao], writes=[b_ato])

        b_hyo = Buf()

        def hyena_main(t0, L):
            TB, G, nTB, nG = RL[L]
            nj = L // 128
            for cc in range(2):
                ar.reset()
                raw = ar.f32(L)
                b_raw = Buf()
                us = [ar.f32(L) for _ in range(3)]
                b_us = [Buf() for _ in range(3)]
                for jx in range(3):
                    ch = jx * 2 + cc
                    P.dma(raw, hy_s[ch][:, t0:t0 + L], reads=[b_hy], writes=[b_raw])
                    u = us[jx]
                    ts("dve", u, raw, hcw[:, ch, 1:2], hcw[:, ch, 3:4], ALU.mult, ALU.add, [b_raw, b_sm], [b_us[jx]])
                    stt(u[:, 1:L], raw[:, 0:L - 1], hcw[:, ch, 0:1], u[:, 1:L], ALU.mult, ALU.add, [b_raw, b_us[jx], b_sm], [b_us[jx]])
                    stt(u[:, 0:L - 1], raw[:, 1:L], hcw[:, ch, 2:3], u[:, 0:L - 1], ALU.mult, ALU.add, [b_raw, b_us[jx], b_sm], [b_us[jx]])
                zbuf = raw
                b_zb = b_raw
                ztm = ar.bf(L).rearrange("p (j c) -> p j c", j=nj)
                b_ztm = Buf()
                Z = ar.f32(2 * L).rearrange("p (c j n) -> p c j n", c=2, j=nj)
                b_Z = Buf()
                Kf = ar.f32(2 * L).rearrange("p (c j n) -> p c j n", c=2, j=nj)
                b_Kf = Buf()
                X = ar.bf(2 * L).rearrange("p (c j n) -> p c j n", c=2, j=nj)
                b_X = Buf()
                tmpx = ar.f32(L).rearrange("p (j n) -> p j n", j=nj)
                b_tx = Buf()
                tmpy = ar.f32(L).rearrange("p (j n) -> p j n", j=nj)
                b_ty = Buf()
                tabs = [(ar.bf(2048), Buf()) for _ in range(4)]
                tcnt = 0
                z, b_z = us[0], b_us[0]
                for o in range(2):
                    for scn in range(nj):
                        tr(PS[0][:, (scn % 4) * 128:(scn % 4 + 1) * 128], z[:, scn * 128:(scn + 1) * 128], ident32, [b_z, b_cst], [PB[0]])
                        if scn % 4 == 3 or scn == nj - 1:
                            n4 = scn % 4 + 1
                            cp("act", ztm[:, scn - n4 + 1:scn + 1, :], PS[0][:, 0:n4 * 128].rearrange("p (j c) -> p j c", j=n4), [PB[0]], [b_ztm])
                    for c in range(2):
                        P.dma(Kf[:, c], kf_s[L][c][:, :, o * 256 + cc * 128:o * 256 + cc * 128 + 128].rearrange("j p n -> p j n"), reads=[b_kf[L]], writes=[b_Kf])
                    for fc in range(nj):
                        tbc, b_tbc = tabs[tcnt % 4]
                        tbs, b_tbs = tabs[(tcnt + 1) % 4]
                        tcnt += 2
                        tbc3 = tbc[:, 0:nj * 128].rearrange("p (j f) -> p j f", j=nj)
                        tbs3 = tbs[:, 0:nj * 128].rearrange("p (j f) -> p j f", j=nj)
                        P.dma(tbc3, tF_d[L][0, fc], writes=[b_tbc])
                        P.dma(tbs3, tF_d[L][1, fc], writes=[b_tbs])
                        pz = 1 + fc % 2
                        for sc_ in range(nj):
                            mm(PS[pz][:, 0:128], tbc3[:, sc_, :], ztm[:, sc_, :], sc_ == 0, sc_ == nj - 1, [b_tbc, b_ztm], [PB[pz]])
                        for sc_ in range(nj):
                            mm(PS[pz][:, 128:256], tbs3[:, sc_, :], ztm[:, sc_, :], sc_ == 0, sc_ == nj - 1, [b_tbs, b_ztm], [PB[pz]])
                        cp("act", Z[:, :, fc, :], PS[pz][:, 0:256].rearrange("p (c n) -> p c n", c=2), [PB[pz]], [b_Z])
                    Zr, Zi, Kr, Ki = Z[:, 0], Z[:, 1], Kf[:, 0], Kf[:, 1]
                    tt("dve", tmpx, Zr, Kr, ALU.mult, [b_Z, b_Kf], [b_tx])
                    tt("pool", tmpy, Zi, Ki, ALU.mult, [b_Z, b_Kf], [b_ty])
                    tt("dve", X[:, 0], tmpx, tmpy, ALU.subtract, [b_tx, b_ty], [b_X])
                    tt("pool", tmpy, Zr, Ki, ALU.mult, [b_Z, b_Kf, b_X], [b_ty])
                    tt("dve", tmpx, Zi, Kr, ALU.mult, [b_Z, b_Kf, b_X], [b_tx])
                    tt("dve", X[:, 1], tmpx, tmpy, ALU.add, [b_tx, b_ty], [b_X])
                    gate, b_g = us[1 + o], b_us[1 + o]
                    znew, b_zn = (zbuf, b_zb) if o == 0 else (us[0], b_us[0])
                    for tb in range(nTB):
                        py = 3 + tb % 2
                        for g in range(nG):
                            tbc, b_tbc = tabs[tcnt % 4]
                            tbs, b_tbs = tabs[(tcnt + 1) % 4]
                            tcnt += 2
                            tc3 = tbc[:, 0:G * TB].rearrange("p (g t) -> p g t", g=G)
                            ts3 = tbs[:, 0:G * TB].rearrange("p (g t) -> p g t", g=G)
                            P.dma(tc3, tI_d[L][0, tb, g], writes=[b_tbc])
                            P.dma(ts3, tI_d[L][1, tb, g], writes=[b_tbs])
                            for fi in range(G):
                                fc = g * G + fi
                                mm(PS[py][:, :TB], X[:, 0, fc, :], tc3[:, fi, :], fc == 0, False, [b_X, b_tbc], [PB[py]])
                                mm(PS[py][:, :TB], X[:, 1, fc, :], ts3[:, fi, :], False, fc == nj - 1, [b_X, b_tbs], [PB[py]])
                        sl = slice(tb * TB, (tb + 1) * TB)
                        stt(tmpx.rearrange("p j n -> p (j n)")[:, sl], z[:, sl], hbias[:, o, cc:cc + 1], PS[py][:, :TB], ALU.mult, ALU.add, [b_z, PB[py], b_sm, b_X], [b_tx])
                        tt("dve", znew[:, sl], tmpx.rearrange("p j n -> p (j n)")[:, sl], gate[:, sl], ALU.mult, [b_tx, b_g], [b_zn])
                    z, b_z = znew, b_zn
                ob = ar.bf(L)
                b_ob = Buf()
                cp("act", ob, z, [b_z], [b_ob])
                P.dma(hyo_s[cc][:, t0:t0 + L], ob, reads=[b_ob], writes=[b_hyo])

        if want("hyena"):
            for (t0, L) in streams:
                hyena_main(t0, L)

        b_fno = Buf()

        def fnet(t0, L):
            TB, G, nTB, nG = RL[L]
            nj = L // 128
            ar.reset()
            fz = [(ar.f32(L), Buf()) for _ in range(2)]
            zc = ar.bf(nj * 256).rearrange("p (j c) -> p j c", j=nj)
            zs = ar.bf(nj * 256).rearrange("p (j c) -> p j c", j=nj)
            b_zc = Buf()
            for cc in range(2):
                f, b_f = fz[cc]
                P.dma(f, fn_s[cc][:, t0:t0 + L], reads=[b_fn], writes=[b_f])
                for scn in range(nj):
                    pa = scn % 2
                    mm(PS[pa][:, 0:128], f[:, scn * 128:(scn + 1) * 128], bdc, True, True, [b_f, b_cst], [PB[pa]])
                    mm(PS[pa][:, 128:256], f[:, scn * 128:(scn + 1) * 128], bds, True, True, [b_f, b_cst], [PB[pa]])
                    cp("act", zc[:, scn, cc * 128:(cc + 1) * 128], PS[pa][:, 0:128], [PB[pa]], [b_zc])
                    cp("dve", zs[:, scn, cc * 128:(cc + 1) * 128], PS[pa][:, 128:256], [PB[pa]], [b_zc])
            tabs = [(ar.bf(2048), Buf()) for _ in range(4)]
            obs = [(ar.bf(512), Buf()) for _ in range(2)]
            tcnt = 0
            oc = 0
            for tb in range(nTB):
                for g in range(nG):
                    tbc, b_tbc = tabs[tcnt % 4]
                    tbs, b_tbs = tabs[(tcnt + 1) % 4]
                    tcnt += 2
                    tc3 = tbc[:, 0:G * TB].rearrange("p (g t) -> p g t", g=G)
                    ts3 = tbs[:, 0:G * TB].rearrange("p (g t) -> p g t", g=G)
                    P.dma(tc3, tN_d[L][0, tb, g], writes=[b_tbc])
                    P.dma(ts3, tN_d[L][1, tb, g], writes=[b_tbs])
                    for cc in range(2):
                        py = 2 + cc + 2 * (tb % 2)
                        for fi in range(G):
                            sc_ = g * G + fi
                            mm(PS[py][:, :TB], zc[:, sc_, cc * 128:(cc + 1) * 128], tc3[:, fi, :], sc_ == 0, False, [b_zc, b_tbc], [PB[py]])
                            mm(PS[py][:, :TB], zs[:, sc_, cc * 128:(cc + 1) * 128], ts3[:, fi, :], False, sc_ == nj - 1, [b_zc, b_tbs], [PB[py]])
                for cc in range(2):
                    py = 2 + cc + 2 * (tb % 2)
                    ob, b_ob = obs[oc % 2]
                    oc += 1
                    cp("act" if cc else "dve", ob[:, :TB], PS[py][:, :TB], [PB[py]], [b_ob])
                    P.dma(fno_s[cc][:, t0 + tb * TB:t0 + (tb + 1) * TB], ob[:, :TB], reads=[b_ob], writes=[b_fno])

        if want("fnet"):
            for (t0, L) in streams:
                fnet(t0, L)

        if want("merge"):
            ar.reset()
            wbr = ar.bf(8 * D).rearrange("p (k m) -> p k m", k=8)
            wo = ar.bf(8 * D).rearrange("p (k m) -> p k m", k=8)
            b_wbr = Buf()
            b_wo = Buf()
            w32 = ar.f32(8 * 512).rearrange("p (k m) -> p k m", k=8)
            b_w32 = Buf()
            for (src, dst, bd) in ((w_br_d, wbr, b_wbr), (w_out_d, wo, b_wo)):
                for half in range(2):
                    P.dma(w32, src[l][:, half * 512:(half + 1) * 512].rearrange("(k p) m -> p k m", p=128), writes=[b_w32])
                    cp("act", dst[:, 0:4, half * 512:(half + 1) * 512], w32[:, 0:4, :], [b_w32], [bd])
                    cp("dve", dst[:, 4:8, half * 512:(half + 1) * 512], w32[:, 4:8, :], [b_w32], [bd])
            ar.off -= 8 * 512
            P.barrier()
            nb = [(ar.bf(8 * 512).rearrange("p (k t) -> p k t", k=8), Buf()) for _ in range(2)]
            sb_ = [(ar.bf(8 * 512).rearrange("p (k t) -> p k t", k=8), Buf()) for _ in range(2)]
            g32 = [(ar.f32(8 * 384).rearrange("p (k j c) -> p k j c", k=8, j=3), Buf()) for _ in range(2)]
            gbf = [(ar.bf(8 * 384).rearrange("p (k j c) -> p k j c", k=8, j=3), Buf()) for _ in range(2)]
            sig = [(ar.f32(512), Buf()) for _ in range(3)]
            yacc = [(ar.f32(512), Buf()) for _ in range(2)]
            ytmp = [(ar.f32(512), Buf()) for _ in range(2)]
            yT = ar.bf(8 * 512).rearrange("p (k t) -> p k t", k=8)
            b_yT = [Buf() for _ in range(8)]
            KR = ((0, 2), (2, 4), (4, 8))
            gc = 0
            for bi, (t0, Tn) in enumerate(tb_mix):
                nbk, b_nb = nb[bi % 2]
                sbk, b_sb = sb_[bi % 2]
                P.dma(nbk[:, :, :Tn], nT_s[:, :, t0:t0 + Tn], reads=[b_nTs], writes=[b_nb])
                for cc in range(2):
                    P.dma(sbk[:, cc, :Tn], hyo_s[cc][:, t0:t0 + Tn], reads=[b_hyo], writes=[b_sb])
                    P.dma(sbk[:, 2 + cc, :Tn], fno_s[cc][:, t0:t0 + Tn], reads=[b_fno], writes=[b_sb])
                for h in range(4):
                    P.dma(sbk[:, 4 + h, :Tn], ato_s[h][:, t0:t0 + Tn], reads=[b_ato], writes=[b_sb])
                for m in range(8):
                    gw, b_gw = g32[gc % 2]
                    gb, b_gb = gbf[gc % 2]
                    gc += 1
                    P.dma(gw, w_gate_d[l, m], writes=[b_gw])
                    cp("act", gb, gw, [b_gw], [b_gb])
                    ya, b_ya = yacc[m % 2]
                    yt, b_yt = ytmp[m % 2]
                    for j in range(3):
                        pg = j
                        pbr = 3 + j
                        for k in range(8):
                            mm(PS[pg][:, :Tn], gb[:, k, j, :], nbk[:, k, :Tn], k == 0, k == 7, [b_gb, b_nb], [PB[pg]])
                        k0, k1 = KR[j]
                        for k in range(k0, k1):
                            mm(PS[pbr][:, :Tn], wbr[:, k, m * 128:(m + 1) * 128], sbk[:, k, :Tn], k == k0, k == k1 - 1, [b_wbr, b_sb], [PB[pbr]])
                        sg, b_sg = sig[(m * 3 + j) % 3]
                        act(sg[:, :Tn], PS[pg][:, :Tn], AF.Sigmoid, [PB[pg]], [b_sg])
                        if j == 0:
                            tt("dve", ya[:, :Tn], sg[:, :Tn], PS[pbr][:, :Tn], ALU.mult, [b_sg, PB[pbr]], [b_ya])
                        else:
                            tt("dve", yt[:, :Tn], sg[:, :Tn], PS[pbr][:, :Tn], ALU.mult, [b_sg, PB[pbr]], [b_yt])
                            if j == 1:
                                tt("dve", ya[:, :Tn], ya[:, :Tn], yt[:, :Tn], ALU.add, [b_ya, b_yt], [b_ya])
                            else:
                                tt("dve", yT[:, m, :Tn], ya[:, :Tn], yt[:, :Tn], ALU.add, [b_ya, b_yt], [b_yT[m]])
                jj = 0 if t0 < S else 1
                for mo in range(8):
                    po = 6 + mo % 2
                    for k in range(8):
                        mm(PS[po][:, :Tn], wo[:, k, mo * 128:(mo + 1) * 128], yT[:, k, :Tn], k == 0, k == 7, [b_wo, b_yT[k]], [PB[po]])
                    xb = b_x[mo][t0 // 512]
                    stt(xT[:, mo, t0:t0 + Tn], PS[po][:, :Tn], modv[:, 2, mo, jj:jj + 1], xT[:, mo, t0:t0 + Tn], ALU.mult, ALU.add, [PB[po], xb, b_sm], [xb])

        if want("peer"):
            peer(l, last, modulate, mod_tmps, A_ffn)

    def peer(l, last, modulate, mod_tmps, A_ffn):
        ar.reset()
        b_ub = Buf()
        b_vb = Buf()
        ld = [(ar.f32(4096), Buf()) for _ in range(3)]
        cv = [(ar.bf(4096), Buf()) for _ in range(3)]
        jobs = []
        for k in range(8):
            for eb in range(4):
                jobs.append(("u", k, eb))
        for e1g in range(32):
            jobs.append(("v", e1g, 0))

        def j_load(ci):
            kind, i0_, i1_ = jobs[ci]
            a, b_a = ld[ci % 3]
            if kind == "u":
                P.dma(a, uT_d[l][i0_ * 128:(i0_ + 1) * 128, i1_ * 4096:(i1_ + 1) * 4096], writes=[b_a])
            else:
                P.dma(a.rearrange("p (e d) -> p e d", e=4), v_d_in[l][i0_ * 512:(i0_ + 1) * 512, :].rearrange("(e p) d -> p e d", p=128), writes=[b_a])

        j_load(0)
        j_load(1)
        for ci in range(len(jobs)):
            if ci + 2 < len(jobs):
                j_load(ci + 2)
            kind, i0_, i1_ = jobs[ci]
            a, b_a = ld[ci % 3]
            o, b_o = cv[ci % 3]
            cp(("act", "dve", "pool")[ci % 3], o, a, [b_a], [b_o])
            if kind == "u":
                P.dma(uTb_s[i1_ * 8:(i1_ + 1) * 8, :, i0_, :].rearrange("g p e -> p g e"), o.rearrange("p (g e) -> p g e", g=8), reads=[b_o], writes=[b_ub])
            else:
                P.dma(vb_s[:, i0_ * 4:(i0_ + 1) * 4, :], o.rearrange("p (e d) -> p e d", e=4), reads=[b_o], writes=[b_vb])
        ar.reset()
        keysT = ar.f32(2048).rearrange("p (h n) -> p h n", h=16)
        b_ky = Buf()
        P.dma(keysT, keysT_d[l], writes=[b_ky])
        nbfs = [ar.bf(8 * 256).rearrange("p (k t) -> p k t", k=8) for _ in range(2)]
        b_nbfs = [[Buf() for _ in range(8)] for _ in range(2)]
        s1k = [ar.f32(1024).rearrange("p (h n) -> p h n", h=8) for _ in range(2)]
        a2k = [ar.f32(1024).rearrange("p (h n) -> p h n", h=8) for _ in range(2)]
        a1t = [ar.f32(128).rearrange("p (h a) -> p h a", h=8) for _ in range(2)]
        top1k = [ar.f32(128).rearrange("p (h a) -> p h a", h=8) for _ in range(2)]
        theta = [ar.f32(8) for _ in range(2)]
        b_keep = [Buf() for _ in range(2)]
        zer = ar.bf(512)
        b_zer = Buf()
        P.op("pool", lambda e: e.memset(zer, 0.0), [], [b_zer])
        base = ar.off
        mt = mod_tmps(256, (2, 1, 2))
        n32 = ar.f32(8 * 256).rearrange("p (k t) -> p k t", k=8)
        b_n32 = [Buf() for _ in range(8)]
        wq = [(ar.f32(8 * 128).rearrange("p (k m) -> p k m", k=8), Buf()) for _ in range(2)]
        qT = ar.f32(16 * 256).rearrange("p (h t) -> p h t", h=16)
        b_qT = Buf()
        s_sb = ar.f32(2048).rearrange("p (h n) -> p h n", h=16)
        b_s = Buf()
        tmpm = ar.f32(256)
        b_tm = Buf()
        tmpm2 = ar.f32(256)
        b_tm2 = Buf()
        top = ar.f32(256).rearrange("p (h a) -> p h a", h=16)
        b_top = Buf()
        cand = ar.f32(256)
        b_cand = Buf()
        ctop = ar.f32(192).rearrange("p (h a) -> p h a", h=8)
        b_ct = Buf()
        misc = ar.f32(16)
        b_mi = Buf()
        ex16 = ar.f32(128).rearrange("p (h a) -> p h a", h=8)
        ub = [(ar.bf(8 * 512).rearrange("p (k e) -> p k e", k=8), Buf()) for _ in range(2)]
        vbf = [(ar.bf(4 * 1024).rearrange("p (e d) -> p e d", e=4), Buf()) for _ in range(2)]
        Wt = [(ar.bf(4 * 256).rearrange("p (e t) -> p e t", e=4), Buf()) for _ in range(2)]
        gel = [(ar.f32(512), Buf()) for _ in range(2)]
        GT = [(ar.bf(512).rearrange("p (e t) -> p e t", e=2), Buf()) for _ in range(2)]
        blocks = tblocks(0, S if last else T, 256)
        b_wT = [Buf(), Buf()]

        def phaseA(t0, Tn, nbf, b_nbf):
            jj = 0 if t0 < S else 1
            def emit_ffn(k, t0_, Tn_, tm, b_t, Asc, Bsc):
                act(n32[:, k, :], tm[:, :Tn_], AF.Identity, [b_t, b_sm], [b_n32[k]], bias=Bsc, scale=Asc)
                cp("pool", nbf[:, k, :], n32[:, k, :], [b_n32[k]], [b_nbf[k]])

            modulate(A_ffn, 3, [(t0, Tn)], emit_ffn, mt, 2)
            yield
            for hp in range(16):
                w, b_w = wq[hp % 2]
                P.dma(w, wq_d[l][:, hp * 128:(hp + 1) * 128].rearrange("(k p) m -> p k m", p=128), writes=[b_w])
                pq = 2 + hp % 2
                for k in range(8):
                    mm(PS[pq][:, :Tn], w[:, k, :], n32[:, k, :], k == 0, k == 7, [b_w, b_n32[k]], [PB[pq]])
                cp("act" if hp % 2 else "dve", qT[:, hp, :], PS[pq][:, :Tn], [PB[pq]], [b_qT])
                yield
            for ti in range(2):
                tsl = slice(ti * 128, (ti + 1) * 128)
                bk = b_keep[ti]
                for hf in range(2):
                    for h8 in range(8):
                        hp = hf * 8 + h8
                        pbk = 2 + h8 // 4
                        mm(PS[pbk][:, (h8 % 4) * 128:(h8 % 4 + 1) * 128], qT[:, hp, tsl], keysT[:, hp, :], True, True, [b_qT, b_ky], [PB[pbk]])
                    for q2 in range(2):
                        cp("act" if q2 else "dve", s_sb[:, hf * 8 + q2 * 4:hf * 8 + (q2 + 1) * 4, :], PS[2 + q2][:, :].rearrange("p (h n) -> p h n", h=4), [PB[2 + q2]], [b_s])
                    yield
                for hp in range(16):
                    P.op("dve", (lambda hp: lambda e: e.max(out=top[:, hp, 0:8], in_=s_sb[:, hp, :]))(hp), [b_s], [b_top])
                    P.op("dve", (lambda hp: lambda e: e.match_replace(out=tmpm[:, 0:128], in_to_replace=top[:, hp, 0:8], in_values=s_sb[:, hp, :], imm_value=-1e30))(hp), [b_s, b_top], [b_tm])
                    P.op("dve", (lambda hp: lambda e: e.max(out=top[:, hp, 8:16], in_=tmpm[:, 0:128]))(hp), [b_tm], [b_top])
                    if hp % 2:
                        yield
                top4 = top.rearrange("p (h c) a -> p h c a", c=2)
                s4v = s_sb.rearrange("p (h c) n -> p h c n", c=2)
                for h in range(8):
                    ch = cand
                    tt("dve", cand.rearrange("p (a b) -> p a b", a=16), top4[:, h, 0, :].unsqueeze(2).to_broadcast([128, 16, 16]),
                       top4[:, h, 1, :].unsqueeze(1).to_broadcast([128, 16, 16]), ALU.add, [b_top, b_ct], [b_cand])
                    P.op("dve", (lambda h, ch: lambda e: e.max(out=ctop[:, h, 0:8], in_=ch))(h, ch), [b_cand], [b_ct])
                    P.op("dve", (lambda h, ch: lambda e: e.match_replace(out=tmpm, in_to_replace=ctop[:, h, 0:8], in_values=ch, imm_value=-1e30))(h, ch), [b_cand, b_ct], [b_tm])
                    P.op("dve", (lambda h: lambda e: e.max(out=ctop[:, h, 8:16], in_=tmpm))(h), [b_tm], [b_ct])
                    P.op("dve", (lambda h: lambda e: e.match_replace(out=tmpm2, in_to_replace=ctop[:, h, 8:16], in_values=tmpm, imm_value=-1e30))(h), [b_tm, b_ct], [b_tm2])
                    P.op("dve", (lambda h: lambda e: e.max(out=ctop[:, h, 16:24], in_=tmpm2))(h), [b_tm2], [b_ct])
                    yield
                tt("dve", ex16, ctop[:, :, 0:16], ctop[:, :, 0:1].to_broadcast([128, 8, 16]), ALU.subtract, [b_ct], [b_mi])
                act(ex16, ex16, AF.Exp, [b_mi], [b_mi])
                P.op("dve", lambda e: e.reduce_sum(out=misc[:, 8:16], in_=ex16, axis=AX.X), [b_mi], [b_mi])
                recip(misc[:, 8:16], misc[:, 8:16], [b_mi], [b_mi])
                m8 = misc[:, 0:8].unsqueeze(2)
                tt("dve", m8, ctop[:, :, 15:16], ctop[:, :, 16:17], ALU.add, [b_ct, b_mi], [b_mi])
                stt(m8, m8, 0.5, ctop[:, :, 0:1], ALU.mult, ALU.subtract, [b_mi, b_ct], [b_mi])
                act(misc[:, 0:8], misc[:, 0:8], AF.Exp, [b_mi], [b_mi])
                tt("dve", theta[ti], misc[:, 0:8], misc[:, 8:16], ALU.mult, [b_mi], [bk])
                cp("pool", s1k[ti], s4v[:, :, 0, :], [b_s], [bk])
                cp("pool", top1k[ti], top4[:, :, 0, :], [b_top], [bk])
                tt("dve", a2k[ti], s4v[:, :, 1, :], top4[:, :, 1, 0:1].to_broadcast([128, 8, 128]), ALU.subtract, [b_s, b_top], [bk])
                act(a2k[ti], a2k[ti], AF.Exp, [bk], [bk])
                tt("dve", a1t[ti], top4[:, :, 0, :], top4[:, :, 0, 0:1].to_broadcast([128, 8, 16]), ALU.subtract, [b_top], [bk])
                act(a1t[ti], a1t[ti], AF.Exp, [bk], [bk])
                tt("dve", a1t[ti], a1t[ti], misc[:, 8:16].unsqueeze(2).to_broadcast([128, 8, 16]), ALU.mult, [bk, b_mi], [bk])
                yield

        for _ in phaseA(blocks[0][0], blocks[0][1], nbfs[0], b_nbfs[0]):
            pass
        for bi, (t0, Tn) in enumerate(blocks):
            jj = 0 if t0 < S else 1
            nbf = nbfs[bi % 2]
            b_nbf = b_nbfs[bi % 2]
            P.barrier()
            ar.off = base
            pmt = ar.f32(2048).rearrange("p (h a e) -> p h a e", h=8, a=16)
            b_pm = Buf()
            csl = [(ar.bf(2048).rearrange("p (h a e) -> p h a e", h=8, a=16), Buf()) for _ in range(2)]
            CT = ar.bf(128 * 128).rearrange("p (e t) -> p e t", e=128)
            b_CT = Buf()
            osl = ar.bf(4096).rearrange("p (h a e) -> p h a e", h=8, a=16)
            b_osl = Buf()
            OTs = [(ar.bf(4096).rearrange("p (e t) -> p e t", e=32), Buf()) for _ in range(2)]
            WTs = [(ar.bf(4096).rearrange("p (e t) -> p e t", e=32), Buf()) for _ in range(2)]
            PSb = [PS[i][:, :].bitcast(BF16) for i in range(8)]
            evc = 0
            for ti in range(2):
                bk = b_keep[ti]
                for es in range(8):
                    tt("pool", pmt, a1t[ti].unsqueeze(3).to_broadcast([128, 8, 16, 16]),
                       a2k[ti][:, :, es * 16:(es + 1) * 16].unsqueeze(2).to_broadcast([128, 8, 16, 16]), ALU.mult, [bk], [b_pm])
                    cs, b_cs = csl[es % 2]
                    for h in range(8):
                        pmh = pmt[:, h].rearrange("p a e -> p (a e)")
                        stt(cs[:, h].rearrange("p a e -> p (a e)"), pmh, theta[ti][:, h:h + 1], pmh, ALU.is_ge, ALU.mult, [b_pm, bk], [b_cs])
                    csf = cs.rearrange("p h a e -> p (h a) e")
                    for half in range(2):
                        pb = 2 + (es * 2 + half) % 2
                        for e in range(8):
                            tr(PSb[pb][:, e * 128:(e + 1) * 128], csf[:, :, half * 8 + e], identb, [b_cs, b_cst], [PB[pb]])
                        e0 = es * 16 + half * 8
                        cp("act" if half else "dve", CT[:, e0:e0 + 8, :], PSb[pb][:, :].rearrange("p (e t) -> p e t", e=8), [PB[pb]], [b_CT])
                for r in range(4):
                    tt("dve", osl, s1k[ti][:, :, r * 32:(r + 1) * 32].unsqueeze(2).to_broadcast([128, 8, 16, 32]),
                       top1k[ti].unsqueeze(3).to_broadcast([128, 8, 16, 32]), ALU.is_equal, [bk], [b_osl])
                    osf = osl.rearrange("p h a e -> p (h a) e")
                    ot, b_ot = OTs[r % 2]
                    for q in range(4):
                        pb = 4 + q % 2
                        for e in range(8):
                            tr(PSb[pb][:, e * 128:(e + 1) * 128], osf[:, :, q * 8 + e], identb, [b_osl, b_cst], [PB[pb]])
                        cp("act" if q % 2 else "dve", ot[:, q * 8:(q + 1) * 8, :], PSb[pb][:, :].rearrange("p (e t) -> p e t", e=8), [PB[pb]], [b_ot])
                    wt, b_wt = WTs[r % 2]
                    for tg in range(8):
                        pb = 6 + tg % 2
                        for tk in range(16):
                            t_ = tg * 16 + tk
                            mm(PS[pb][:, tk * 32:(tk + 1) * 32], CT[:, :, t_], ot[:, :, t_], True, True, [b_CT, b_ot], [PB[pb]])
                        cp("dve" if evc % 2 else "act", wt[:, :, tg * 16:(tg + 1) * 16], PS[pb][:, :].rearrange("p (t e) -> p e t", t=16), [PB[pb]], [b_wt])
                        evc += 1
                    P.dma(wT_s[ti][:, r * 32:(r + 1) * 32, :], wt, reads=[b_wt], writes=[b_wT[ti]])
            P.barrier()
            genA = phaseA(blocks[bi + 1][0], blocks[bi + 1][1], nbfs[(bi + 1) % 2], b_nbfs[(bi + 1) % 2]) if bi + 1 < len(blocks) else None
            for pb in range(4, 8):
                mm(PS[pb][:, :], zer[:, 0:128], zer[:, 0:512], True, False, [b_zer], [PB[pb]])
            pend = None
            for g in range(32):
                u_, b_u = ub[g % 2]
                v_, b_vv = vbf[g % 2]
                w_, b_w_ = Wt[g % 2]
                P.dma(u_, uTb_s[g], reads=[b_ub], writes=[b_u])
                P.dma(v_, vb_s[:, g * 4:(g + 1) * 4, :], reads=[b_vb], writes=[b_vv])
                for ti in range(2):
                    P.dma(w_[:, :, ti * 128:(ti + 1) * 128], wT_s[ti][:, g * 4:(g + 1) * 4, :], reads=[b_wT[ti]], writes=[b_w_])
                for sub in range(2):
                    it = g * 2 + sub
                    ph = it % 2
                    for i in range(2):
                        chn = sub * 2 + i
                        for k in range(8):
                            mm(PS[ph][:, i * 256:(i + 1) * 256], u_[:, k, chn * 128:(chn + 1) * 128], nbf[:, k, :], k == 0, k == 7, [b_u, b_nbf[k]], [PB[ph]])
                    ge, b_ge = gel[it % 2]
                    gt, b_gt = GT[it % 2]
                    act(ge, PS[ph][:, :], AF.Gelu, [PB[ph]], [b_ge])
                    tt("dve", gt.rearrange("p e t -> p (e t)"), ge, w_[:, sub * 2:(sub + 1) * 2, :].rearrange("p e t -> p (e t)"), ALU.mult, [b_ge, b_w_], [b_gt])
                    if pend is not None:
                        pend()

                    def mk(v_=v_, gt=gt, sub=sub, b_vv=b_vv, b_gt=b_gt):
                        def f():
                            for dk in range(8):
                                pbo = 4 + dk // 2
                                for i in range(2):
                                    mm(PS[pbo][:, (dk % 2) * 256:(dk % 2 + 1) * 256], v_[:, sub * 2 + i, dk * 128:(dk + 1) * 128], gt[:, i, :], False, False, [b_vv, b_gt], [PB[pbo]])
                        return f
                    pend = mk()
                if genA is not None:
                    for _ in range(3):
                        next(genA, None)
            pend()
            if genA is not None:
                for _ in genA:
                    pass
            for dk in range(8):
                pbo = 4 + dk // 2
                xb = b_x[dk][t0 // 512]
                stt(xT[:, dk, t0:t0 + 256], PS[pbo][:, (dk % 2) * 256:(dk % 2 + 1) * 256], modv[:, 5, dk, jj:jj + 1], xT[:, dk, t0:t0 + 256], ALU.mult, ALU.add, [PB[pbo], xb, b_sm], [xb])

    for l in range(depth):
        layer(l)

    P.barrier()
    fin = []
    for k in range(8):
        fin.append(P.dma(yT_d[:, k, :], xT[:, k, 0:S], reads=b_x[k]))
    for name, (src, shape) in dbg_out.items():
        pass
    P.emit(list(P.dmas[-8:]))
    es.close()
    return nc

import ml_dtypes
_CONST = {}


def _consts():
    if _CONST:
        return _CONST
    f64 = np.float64
    c = np.zeros((6, 128, 128), f64)
    c[0] = np.eye(128)
    c[1] = 1.0
    c[2, :64, :64] = 1.0
    c[2, 64:, 64:] = 1.0
    for base in range(0, 128, 32):
        for d in range(16):
            c[3, base + d + 16, base + d] = -1.0
            c[3, base + d, base + d + 16] = 1.0
    ci = np.arange(64)
    ang = 2 * np.pi * np.outer(ci, ci) / 64.0
    for b in range(2):
        c[4, b * 64:(b + 1) * 64, b * 64:(b + 1) * 64] = np.cos(ang)
        c[5, b * 64:(b + 1) * 64, b * 64:(b + 1) * 64] = np.sin(ang)
    _CONST["cst"] = np.ascontiguousarray(c.transpose(1, 0, 2)).astype(np.float32)
    t = np.arange(S)
    row = (t // 64).astype(f64)
    col = (t % 64).astype(f64)
    inv = 10000.0 ** (-np.arange(0, 32, 2, dtype=f64) / 32.0)
    d = np.arange(128) % 64
    pos = np.where((d // 32)[:, None] == 0, row[None, :], col[None, :])
    a = pos * inv[d % 16][:, None]
    _CONST["rope"] = np.stack([np.cos(a), np.sin(a)]).astype(np.float32)
    for L in (S, C):
        p = np.arange(L, dtype=f64)
        tt_ = p / max(L - 1, 1)
        w = 2.0 * np.pi * p / L
        fr = np.linspace(1e-4, 15, 16)
        feats = np.concatenate([tt_[:, None], np.cos(w[:, None] * fr), -np.sin(w[:, None] * fr)], axis=-1)
        _CONST["feats%d" % L] = np.ascontiguousarray(feats.T).astype(np.float32)
        deltas = np.abs(np.linspace(math.log(1e-2) / 1.5, math.log(1e-2) / 0.3, 256))
        _CONST["dec%d" % L] = np.exp(-tt_[:, None] * deltas[None, :]).astype(np.float32)
        nj = L // 128
        s_ = np.arange(L)
        kk = np.outer(s_, 2 * s_ + 1) % (4 * L)
        angF = np.pi * kk / (2.0 * L)
        TcF = np.cos(angF)
        TsF = -np.sin(angF)
        tF = np.stack([TcF, TsF]).reshape(2, nj, 128, nj, 128).transpose(0, 3, 2, 1, 4)
        _CONST["tF%d" % L] = np.ascontiguousarray(tF).astype(np.float32).astype(ml_dtypes.bfloat16)
        TB = min(512, L)
        G = min(4, nj)
        nTB = L // TB
        nG = nj // G
        TcI = TcF.T / L
        TsI = TsF.T / L
        def rl(M):
            return M.reshape(nG, G, 128, nTB, TB).transpose(3, 0, 2, 1, 4)
        _CONST["tI%d" % L] = np.ascontiguousarray(np.stack([rl(TcI), rl(TsI)])).astype(np.float32).astype(ml_dtypes.bfloat16)
        k2 = np.outer(s_, s_) % L
        ang2 = 2 * np.pi * k2 / L
        sc_ = 1.0 / math.sqrt(64.0 * L)
        _CONST["tN%d" % L] = np.ascontiguousarray(np.stack([rl(np.cos(ang2) * sc_), rl(-np.sin(ang2) * sc_)])).astype(np.float32).astype(ml_dtypes.bfloat16)
    return _CONST


def _prep(inp):
    f = lambda a: np.ascontiguousarray(np.asarray(a, dtype=np.float32))
    w = {}
    w["w_ada"] = f(inp["w_ada"])
    w["bada"] = f(np.asarray(inp["b_ada"]).reshape(2, 6, 8, 128).transpose(0, 3, 1, 2))
    w["gmf"] = f(np.stack([np.asarray(inp["g_mix"]).reshape(2, 8, 128), np.asarray(inp["g_ffn"]).reshape(2, 8, 128)], axis=1).transpose(0, 3, 1, 2))
    win = np.asarray(inp["w_in"])
    w["w_in"] = f(win)
    w["w_gate"] = f(win[:, :, 2560:].reshape(2, 8, 128, 3, 8, 128).transpose(0, 4, 2, 1, 3, 5))
    hcw = np.concatenate([np.asarray(inp["hy_conv_w"]), np.asarray(inp["hy_conv_b"])[:, None, :]], axis=1)
    w["hcw"] = f(hcw.reshape(2, 4, 6, 128).transpose(0, 3, 2, 1))
    w["hy_w1"] = f(inp["hy_w1"])
    w["hy_fb"] = f(np.stack([inp["hy_b1"], inp["hy_freq"], inp["hy_b2"]], axis=-1))
    w["hy_w2"] = f(inp["hy_w2"])
    w["hy_w3"] = f(inp["hy_w3"])
    w["hy_bias"] = f(np.asarray(inp["hy_bias"]).reshape(2, 2, 2, 128).transpose(0, 3, 1, 2))
    w["gqk"] = f(np.stack([np.asarray(inp["g_q"]).reshape(2, 128), np.asarray(inp["g_k"]).reshape(2, 128)], axis=-1))
    w["lam"] = f(np.asarray(inp["lam"]).reshape(2, 1, 256))
    w["gsub"] = f(np.asarray(inp["g_sub"]).reshape(2, 128, 1))
    w["w_br"] = f(np.concatenate([inp["w_hy"], inp["w_fn"], inp["w_at"]], axis=1))
    w["w_out"] = f(inp["w_out"])
    w["peer_wq"] = f(inp["peer_wq"])
    w["keysT"] = f(np.asarray(inp["peer_keys"]).reshape(2, 16, 128, 128).transpose(0, 3, 1, 2))
    w["uT"] = f(np.asarray(inp["peer_u"]).transpose(0, 2, 1))
    w["peer_v"] = f(inp["peer_v"])
    w.update(_consts())
    return w


def _core_inputs(inp, b):
    X = np.concatenate([np.asarray(inp["x"][b]), np.asarray(inp["ctx"][b])], axis=0)
    xT = np.ascontiguousarray(X.T.reshape(8, 128, T).transpose(1, 0, 2)).astype(np.float32)
    cc = np.stack([np.asarray(inp["c"][b]), np.asarray(inp["c_ctx"])], axis=-1)
    cc = np.ascontiguousarray(cc.reshape(8, 128, 2).transpose(1, 0, 2)).astype(np.float32)
    return {"xT": xT, "cc": cc}


_NC = {}


def kernel(**inp):
    w = _prep(inp)
    if "nc" not in _NC:
        _NC["nc"] = build()
    nc = _NC["nc"]
    in_maps = []
    for b in range(8):
        m = dict(w)
        m.update(_core_inputs(inp, b))
        in_maps.append(m)
    res = run_bass_kernel_spmd(nc, in_maps, core_ids=list(range(8)))
    out = np.empty((8, S, D), np.float32)
    for b in range(8):
        yT = np.asarray(res.results[b]["yT"])
        out[b] = yT.transpose(2, 1, 0).reshape(S, D)
    return out
```

```python
import numpy as np, math
from contextlib import ExitStack
import concourse.bass as bass
import concourse.mybir as mybir
from concourse.bass_utils import run_bass_kernel_spmd

F32 = mybir.dt.float32
BF16 = mybir.dt.bfloat16
ALU = mybir.AluOpType
AF = mybir.ActivationFunctionType
AX = mybir.AxisListType

NSLOT = 40
ENGS = ("pe", "act", "dve", "pool", "sp")


class Buf:
    __slots__ = ("w", "rs", "rd")

    def __init__(self):
        self.w = None
        self.rs = {}
        self.rd = []


class Op:
    __slots__ = ("eng", "fn", "deps", "sig", "cnt", "slot", "dma")

    def __init__(self, eng, fn, dma=False):
        self.eng = eng
        self.fn = fn
        self.deps = ()
        self.sig = False
        self.cnt = 0
        self.slot = -1
        self.dma = dma


class Prog:
    def __init__(self, nc):
        self.nc = nc
        self.streams = {e: [] for e in ENGS}
        self.dmas = []
        self.live_dmas = []
        self.last_real = {e: None for e in ENGS}

    def op(self, eng, fn, reads=(), writes=(), dma=False):
        o = Op(eng, fn, dma)
        deps = set()
        for b in reads:
            if b.w is not None:
                deps.add(b.w)
        for b in writes:
            if b.w is not None:
                deps.add(b.w)
            deps.update(b.rs.values())
            deps.update(b.rd)
        if eng == "pe" and not dma:
            deps = {d for d in deps if d.dma or d.eng != "pe"}
        o.deps = deps
        for b in writes:
            b.w = o
            b.rs = {}
            b.rd = []
        for b in reads:
            if dma:
                b.rd.append(o)
            else:
                b.rs[eng] = o
        self.streams[eng].append(o)
        if not dma:
            self.last_real[eng] = o
        if dma:
            self.dmas.append(o)
            self.live_dmas.append(o)
        return o

    def dma(self, out, in_, reads=(), writes=(), eng="sp"):
        return self.op(eng, lambda e: e.dma_start(out=out, in_=in_), reads, writes, dma=True)

    def barrier(self):
        last = dict(self.last_real)
        live = list(self.live_dmas)
        self.live_dmas = []
        for e in ENGS:
            o = Op(e, None)
            o.deps = {last[x] for x in ENGS if x != e and last[x] is not None}
            o.deps.update(live)
            self.streams[e].append(o)

    def emit(self, final_dmas):
        nc = self.nc
        for e in ENGS:
            for o in self.streams[e]:
                for d in o.deps:
                    d.sig = True
        with ExitStack() as es:
            sems = {e: es.enter_context(nc.semaphore("s_" + e)) for e in ENGS}
            dsem = [es.enter_context(nc.semaphore("d%d" % i)) for i in range(NSLOT)]
            for e in ENGS:
                c = 0
                for o in self.streams[e]:
                    if o.dma:
                        continue
                    if o.sig:
                        c += 1
                        o.cnt = c
            slot_cnt = [0] * NSLOT
            slot_prev = [None] * NSLOT
            for i, o in enumerate(self.dmas):
                s = i % NSLOT
                o.slot = s
                slot_cnt[s] += 16
                o.cnt = slot_cnt[s]
                if slot_prev[s] is not None:
                    o.deps = set(o.deps)
                    o.deps.add(slot_prev[s])
                slot_prev[s] = o
            block = es.enter_context(nc.Block())

            def run(ename, eng):
                waited = {}
                for o in self.streams[ename]:
                    for d in o.deps:
                        sem = dsem[d.slot] if d.dma else sems[d.eng]
                        if waited.get(sem.name, 0) >= d.cnt:
                            continue
                        eng.wait_ge(sem, d.cnt)
                        waited[sem.name] = d.cnt
                    if o.fn is None:
                        continue
                    ins = o.fn(eng)
                    if o.dma:
                        ins.then_inc(dsem[o.slot], 16)
                    elif o.sig:
                        ins.then_inc(sems[ename], 1)
                if ename == "sp":
                    for d in final_dmas:
                        eng.wait_ge(dsem[d.slot], d.cnt)

            @block.sync
            def _(e):
                run("sp", e)

            @block.tensor
            def _(e):
                run("pe", e)

            @block.scalar
            def _(e):
                run("act", e)

            @block.vector
            def _(e):
                run("dve", e)

            @block.gpsimd
            def _(e):
                run("pool", e)

D = 1024
S = 2048
C = 256
T = S + C
EPS = 1e-6
PI = math.pi
NE = 16384


def tblocks(lo, hi, step=512):
    return [(t, min(step, hi - t)) for t in range(lo, hi, step)]


def build(depth=2, stages=None, dbg=()):
    nc = bass.Bass("TRN2", target_bir_lowering=False)
    P = Prog(nc)

    def din(name, shape, dt=F32):
        return nc.dram_tensor(name, list(shape), dt, kind="ExternalInput").ap()

    def dscr(name, shape, dt=F32):
        if name in dbg:
            return nc.dram_tensor(name, list(shape), dt, kind="ExternalOutput").ap()
        return nc.dram_tensor(name, list(shape), dt).ap()

    xT_d = din("xT", [128, 8, T])
    cc_d = din("cc", [128, 8, 2])
    w_ada_d = din("w_ada", [2, D, 6 * D])
    bada_d = din("bada", [2, 128, 6, 8])
    gmf_d = din("gmf", [2, 128, 2, 8])
    w_in_d = din("w_in", [2, D, 5632])
    w_gate_d = din("w_gate", [2, 8, 128, 8, 3, 128])
    hcw_d = din("hcw", [2, 128, 6, 4])
    hy_w1_d = din("hy_w1", [2, 33, 64])
    hy_fb_d = din("hy_fb", [2, 64, 3])
    hy_w2_d = din("hy_w2", [2, 64, 64])
    hy_w3_d = din("hy_w3", [2, 64, 1024])
    hy_bias_d = din("hy_bias", [2, 128, 2, 2])
    gqk_d = din("gqk", [2, 128, 2])
    lam_d = din("lam", [2, 1, 256])
    gsub_d = din("gsub", [2, 128, 1])
    w_br_d = din("w_br", [2, D, D])
    w_out_d = din("w_out", [2, D, D])
    wq_d = din("peer_wq", [2, D, 2048])
    keysT_d = din("keysT", [2, 128, 16, 128])
    uT_d = din("uT", [2, D, NE])
    v_d_in = din("peer_v", [2, NE, D])
    cst_d = din("cst", [128, 6, 128])
    rope_d = din("rope", [2, 128, S])
    feats_d = {L: din("feats%d" % L, [33, L]) for L in (S, C)}
    dec_d = {L: din("dec%d" % L, [L, 256]) for L in (S, C)}
    tF_d = {L: din("tF%d" % L, [2, L // 128, 128, L // 128, 128], BF16) for L in (S, C)}
    RL = {}
    for L in (S, C):
        TB = min(512, L)
        G = min(4, L // 128)
        RL[L] = (TB, G, L // TB, (L // 128) // G)
    tI_d = {L: din("tI%d" % L, [2, RL[L][2], RL[L][3], 128, RL[L][1], RL[L][0]], BF16) for L in (S, C)}
    tN_d = {L: din("tN%d" % L, [2, RL[L][2], RL[L][3], 128, RL[L][1], RL[L][0]], BF16) for L in (S, C)}
    yT_d = nc.dram_tensor("yT", [128, 8, S], F32, kind="ExternalOutput").ap()
    hy_s = dscr("hy_s", [6, 128, T])
    fn_s = dscr("fn_s", [2, 128, T])
    q_s = dscr("q_s", [4, 128, T], BF16)
    k_s = dscr("k_s", [4, 128, T], BF16)
    v_s = dscr("v_s", [128, 18, 512], BF16)
    kf_s = {L: dscr("kf_s%d" % L, [2, L // 128, 128, 512]) for L in (S, C)}
    hyo_s = dscr("hyo_s", [2, 128, T], BF16)
    fno_s = dscr("fno_s", [2, 128, T], BF16)
    ato_s = dscr("ato_s", [4, 128, T], BF16)
    nT_s = dscr("nT_s", [128, 8, T], BF16)
    uTb_s = dscr("uTb_s", [32, 128, 8, 512], BF16)
    vb_s = dscr("vb_s", [128, 128, D], BF16)
    wT_s = dscr("wT_s", [2, 128, 128, 128], BF16)
    dbg_out = {}

    es = ExitStack()
    xT = es.enter_context(nc.sbuf_tensor("xT_sb", [128, 8, T], F32))
    cst = es.enter_context(nc.sbuf_tensor("cst_sb", [128, 6, 128], F32))
    cstb = es.enter_context(nc.sbuf_tensor("cstb_sb", [128, 2, 128], BF16))
    sm = es.enter_context(nc.sbuf_tensor("small_sb", [128, 512], F32))
    ARENA = 33280
    AR = es.enter_context(nc.sbuf_tensor("arena", [128, ARENA], F32))
    PS = [es.enter_context(nc.psum_tensor("ps%d" % i, [128, 512], F32)) for i in range(8)]
    PB = [Buf() for _ in range(8)]
    ident32 = cst[:, 0, :]
    ones32 = cst[:, 1, :]
    bd64 = cst[:, 2, :]
    rot = cst[:, 3, :]
    bdc = cst[:, 4, :]
    bds = cst[:, 5, :]
    identb = cstb[:, 0, :]
    onesb = cstb[:, 1, :]
    b_cst = Buf()
    b_x = [[Buf() for _ in range(5)] for _ in range(8)]
    b_sm = Buf()
    sc = sm[:, 0:16].rearrange("p (k j) -> p k j", k=8)
    modv = sm[:, 16:112].rearrange("p (g m j) -> p g m j", g=6, m=8)
    A_mix = sm[:, 112:128].rearrange("p (k j) -> p k j", k=8)
    A_ffn = sm[:, 128:144].rearrange("p (k j) -> p k j", k=8)
    gmf = sm[:, 144:160].rearrange("p (a k) -> p a k", a=2)
    tmp16 = sm[:, 160:176].rearrange("p (k j) -> p k j", k=8)
    gqk = sm[:, 176:178]
    gsub_s = sm[:, 178:179]
    neg_lam = sm[:, 179:180]
    lamw = sm[:, 180:182]
    hcw = sm[:, 184:208].rearrange("p (c t) -> p c t", c=6)
    hbias = sm[:, 208:212].rearrange("p (o c) -> p o c", o=2)
    fb = sm[:, 212:217]
    bada = sm[:, 224:272].rearrange("p (g m) -> p g m", g=6)
    lamt = sm[:, 272:400]
    epsc = sm[:, 183:184]

    class Arena:
        def __init__(self):
            self.off = 0

        def reset(self):
            P.barrier()
            self.off = 0

        def f32(self, n):
            a = AR[:, self.off:self.off + n]
            self.off += n
            assert self.off <= ARENA, self.off
            return a

        def bf(self, n):
            w = (n + 1) // 2
            return self.f32(w).bitcast(BF16)[:, 0:n]

    ar = Arena()

    def mm(out, lhsT, rhs, start, stop, rd, wr):
        P.op("pe", lambda e: e.matmul(out, lhsT=lhsT, rhs=rhs, start=start, stop=stop), rd, wr)

    def tr(out, in_, idn, rd, wr):
        P.op("pe", lambda e: e.transpose(out=out, in_=in_, identity=idn), rd, wr)

    def act(out, in_, func, rd, wr, bias=0.0, scale=1.0):
        P.op("act", lambda e: e.activation(out=out, in_=in_, func=func, bias=bias, scale=scale), rd, wr)

    def tt(eng, out, in0, in1, op, rd, wr):
        P.op(eng, lambda e: e.tensor_tensor(out=out, in0=in0, in1=in1, op=op), rd, wr)

    def ts(eng, out, in0, s1, s2, op0, op1, rd, wr):
        if s2 is None:
            P.op(eng, lambda e: e.tensor_scalar(out=out, in0=in0, scalar1=s1, scalar2=None, op0=op0), rd, wr)
        else:
            P.op(eng, lambda e: e.tensor_scalar(out=out, in0=in0, scalar1=s1, scalar2=s2, op0=op0, op1=op1), rd, wr)

    def stt(out, in0, scalar, in1, op0, op1, rd, wr):
        P.op("dve", lambda e: e.scalar_tensor_tensor(out=out, in0=in0, scalar=scalar, in1=in1, op0=op0, op1=op1), rd, wr)

    def cp(eng, out, in_, rd, wr):
        if eng == "act":
            act(out, in_, AF.Copy, rd, wr)
        else:
            P.op(eng, lambda e: e.tensor_copy(out=out, in_=in_), rd, wr)

    def recip(out, in_, rd, wr):
        P.op("dve", lambda e: e.reciprocal(out=out, in_=in_), rd, wr)

    def rsqrt_mean(out, in_, n, rd, wr):
        act(out, in_, AF.Sqrt, list(rd) + [b_sm], wr, bias=epsc, scale=1.0 / n)
        recip(out, out, wr, wr)

    P.dma(cst[:, :, :], cst_d, writes=[b_cst])
    for k in range(8):
        P.dma(xT[:, k, :], xT_d[:, k, :], writes=b_x[k])
    P.dma(sc, cc_d, writes=[b_sm])
    cp("dve", cstb[:, 0, :], ident32, [b_cst], [b_cst])
    cp("dve", cstb[:, 1, :], ones32, [b_cst], [b_cst])
    act(sc, sc, AF.Silu, [b_sm], [b_sm])
    P.op("dve", lambda e: e.memset(epsc, EPS), [], [b_sm])
    ar.reset()

    def want(name):
        return stages is None or name in stages

    def layer(l):
        last = l == depth - 1
        lam_init = 0.8 - 0.6 * math.exp(-0.3 * l)
        streams = [(0, S)] if last else [(0, S), (S, C)]
        tb_all = tblocks(0, T)
        tb_mix = tblocks(0, S) if last else tb_all

        ar.reset()
        P.dma(bada, bada_d[l], writes=[b_sm])
        P.dma(gmf, gmf_d[l], writes=[b_sm])
        P.dma(gqk, gqk_d[l], writes=[b_sm])
        P.dma(gsub_s, gsub_d[l], writes=[b_sm])
        P.dma(hcw, hcw_d[l], writes=[b_sm])
        P.dma(hbias, hy_bias_d[l], writes=[b_sm])
        P.dma(fb[0:64, 0:3], hy_fb_d[l], writes=[b_sm])
        P.dma(lamt, lam_d[l][:, 0:128].partition_broadcast(128), writes=[b_sm])
        lam2 = ar.f32(128)
        b_l2 = Buf()
        P.dma(lam2, lam_d[l][:, 128:256].partition_broadcast(128), writes=[b_l2])
        wts = [(ar.f32(8 * 1024), Buf()) for _ in range(2)]
        for g in range(6):
            wt, bw = wts[g % 2]
            wt3 = wt.rearrange("p (k m) -> p k m", k=8)
            P.dma(wt3, w_ada_d[l][:, g * 1024:(g + 1) * 1024].rearrange("(k p) m -> p k m", p=128), writes=[bw])
            for m in range(8):
                for k in range(8):
                    mm(PS[0][:, m * 2:m * 2 + 2], wt3[:, k, m * 128:(m + 1) * 128], sc[:, k, :], k == 0, k == 7, [bw, b_sm], [PB[0]])
            tt("dve", modv[:, g], PS[0][:, 0:16].rearrange("p (m j) -> p m j", m=8),
               bada[:, g, :].unsqueeze(2).to_broadcast([128, 8, 2]), ALU.add, [PB[0], b_sm], [b_sm])
        for (Aap, gi, si) in ((A_mix, 0, 1), (A_ffn, 1, 4)):
            ts("dve", tmp16, modv[:, si], 1.0, None, ALU.add, None, [b_sm], [b_sm])
            tt("dve", Aap, tmp16, gmf[:, gi, :].unsqueeze(2).to_broadcast([128, 8, 2]), ALU.mult, [b_sm], [b_sm])
        tt("dve", lamt[:, 0:64], lamt[:, 0:64], lamt[:, 64:128], ALU.mult, [b_sm], [b_sm])
        tt("dve", lam2[:, 0:64], lam2[:, 0:64], lam2[:, 64:128], ALU.mult, [b_l2], [b_l2])
        P.op("dve", lambda e: e.reduce_sum(out=lamw[:, 0:1], in_=lamt[:, 0:64], axis=AX.X), [b_sm], [b_sm])
        P.op("dve", lambda e: e.reduce_sum(out=lamw[:, 1:2], in_=lam2[:, 0:64], axis=AX.X), [b_l2, b_sm], [b_sm])
        act(lamw, lamw, AF.Exp, [b_sm], [b_sm])
        tt("dve", neg_lam, lamw[:, 1:2], lamw[:, 0:1], ALU.subtract, [b_sm], [b_sm])
        ts("dve", neg_lam, neg_lam, -lam_init, None, ALU.add, None, [b_sm], [b_sm])
        ts("dve", gsub_s, gsub_s, 1.0 - lam_init, None, ALU.mult, None, [b_sm], [b_sm])
        tt("dve", fb[0:64, 3:4], fb[0:64, 0:1], fb[0:64, 1:2], ALU.mult, [b_sm], [b_sm])
        tt("dve", fb[0:64, 4:5], fb[0:64, 2:3], fb[0:64, 1:2], ALU.mult, [b_sm], [b_sm])

        def mod_tmps(w, n=(3, 2, 3)):
            return dict(sq=[(ar.f32(w), Buf()) for _ in range(n[0])], rs=[(ar.f32(w), Buf()) for _ in range(n[1])],
                        tm=[(ar.f32(w), Buf()) for _ in range(n[2])], c=[0, 0, 0])

        def modulate(Aap, gB, blocks, emit_cb, mt, pbk=7):
            for (t0, Tn) in blocks:
                j = 0 if t0 < S else 1
                xb = t0 // 512
                for k in range(8):
                    sq, b_sq = mt["sq"][mt["c"][0] % len(mt["sq"])]
                    mt["c"][0] += 1
                    act(sq[:, :Tn], xT[:, k, t0:t0 + Tn], AF.Square, [b_x[k][xb]], [b_sq])
                    mm(PS[pbk][:, :Tn], ones32, sq[:, :Tn], k == 0, k == 7, [b_sq, b_cst], [PB[pbk]])
                r, b_r = mt["rs"][mt["c"][1] % len(mt["rs"])]
                mt["c"][1] += 1
                rsqrt_mean(r[:, :Tn], PS[pbk][:, :Tn], D, [PB[pbk]], [b_r])
                for k in range(8):
                    tm, b_t = mt["tm"][mt["c"][2] % len(mt["tm"])]
                    mt["c"][2] += 1
                    tt("dve", tm[:, :Tn], xT[:, k, t0:t0 + Tn], r[:, :Tn], ALU.mult, [b_x[k][xb], b_r], [b_t])
                    emit_cb(k, t0, Tn, tm, b_t, Aap[:, k, j:j + 1], modv[:, gB, k, j:j + 1])

        def hyena_filters(L):
            ar.reset()
            nj = L // 128
            w1 = ar.f32(64)
            w2 = ar.f32(64)
            w3 = ar.f32(1024)
            b_w = Buf()
            P.dma(w1[0:33, :], hy_w1_d[l], writes=[b_w])
            P.dma(w2[0:64, :], hy_w2_d[l], writes=[b_w])
            P.dma(w3[0:64, :], hy_w3_d[l], writes=[b_w])
            h2T = ar.f32(L)
            b_h2 = Buf()
            off_mlp = ar.off
            ft = [(ar.f32(512), Buf()) for _ in range(2)]
            aa = [(ar.f32(512), Buf()) for _ in range(2)]
            h1 = [(ar.f32(512), Buf()) for _ in range(2)]
            aa2 = [(ar.f32(512), Buf()) for _ in range(2)]
            sx = [(ar.f32(512), Buf()) for _ in range(3)]

            def sin_act(out, a, Tn, b_in, b_out):
                (s4, b_s4), (c4, b_c4), (q, b_q) = sx
                act(s4[0:64, :Tn], a, AF.Sin, [b_in], [b_s4], scale=0.25)
                act(c4[0:64, :Tn], a, AF.Abs, [b_in], [b_c4])
                act(c4[0:64, :Tn], c4[0:64, :Tn], AF.Sin, [b_c4, b_sm], [b_c4], bias=halfpi[0:64, :], scale=-0.25)
                tt("dve", q[0:64, :Tn], s4[0:64, :Tn], s4[0:64, :Tn], ALU.mult, [b_s4], [b_q])
                ts("dve", q[0:64, :Tn], q[0:64, :Tn], -2.0, 1.0, ALU.mult, ALU.add, [b_q], [b_q])
                tt("dve", c4[0:64, :Tn], s4[0:64, :Tn], c4[0:64, :Tn], ALU.mult, [b_s4, b_c4], [b_c4])
                stt(out, c4[0:64, :Tn], 4.0, q[0:64, :Tn], ALU.mult, ALU.mult, [b_c4, b_q], [b_out])

            for bi, (t0, Tn) in enumerate(tblocks(0, L)):
                f, b_f = ft[bi % 2]
                P.dma(f[0:33, :Tn], feats_d[L][:, t0:t0 + Tn], writes=[b_f])
                mm(PS[0][0:64, :Tn], w1[0:33, :], f[0:33, :Tn], True, True, [b_w, b_f], [PB[0]])
                a, b_a = aa[bi % 2]
                ts("dve", a[0:64, :Tn], PS[0][0:64, :Tn], fb[0:64, 1:2], fb[0:64, 3:4], ALU.mult, ALU.add, [PB[0], b_sm], [b_a])
                hh, b_h = h1[bi % 2]
                sin_act(hh[0:64, :Tn], a[0:64, :Tn], Tn, b_a, b_h)
                mm(PS[1][0:64, :Tn], w2[0:64, :], hh[0:64, :Tn], True, True, [b_w, b_h], [PB[1]])
                a2_, b_a2 = aa2[bi % 2]
                ts("dve", a2_[0:64, :Tn], PS[1][0:64, :Tn], fb[0:64, 1:2], fb[0:64, 4:5], ALU.mult, ALU.add, [PB[1], b_sm], [b_a2])
                sin_act(h2T[0:64, t0:t0 + Tn], a2_[0:64, :Tn], Tn, b_a2, b_h2)
            dec = ar.f32(nj * 256).rearrange("p (j c) -> p j c", j=nj)
            off_dec_end = ar.off
            b_dec = Buf()
            P.dma(dec, dec_d[L].rearrange("(j p) c -> p j c", p=128), writes=[b_dec])
            hd = ar.f32(nj * 1024).rearrange("p (j c) -> p j c", j=nj)
            b_hd = [Buf() for _ in range(nj)]
            off_hd = ar.off
            for pc in range(nj):
                for half in range(2):
                    pb = half
                    mm(PS[pb][:, :], h2T[0:64, pc * 128:(pc + 1) * 128], w3[0:64, half * 512:(half + 1) * 512], True, True, [b_h2, b_w], [PB[pb]])
                    tt("dve", hd[:, pc, half * 512:(half + 1) * 512].rearrange("p (o c) -> p o c", o=2),
                       PS[pb][:, :].rearrange("p (o c) -> p o c", o=2),
                       dec[:, pc, :].unsqueeze(1).to_broadcast([128, 2, 256]), ALU.mult, [PB[pb], b_dec], [b_hd[pc]])
            P.op("dve", lambda e: e.memset(hd[0:1, 0, 512:1024], 0.0), [], [b_hd[0]])
            ab = [(ar.f32(1024), Buf()) for _ in range(2)]
            for pc in range(nj):
                a, b_a = ab[pc % 2]
                act(a, hd[:, pc, :], AF.Abs, [b_hd[pc]], [b_a])
                for half in range(2):
                    mm(PS[2 + half][:, :], ones32, a[:, half * 512:(half + 1) * 512], pc == 0, pc == nj - 1, [b_a, b_cst], [PB[2 + half]])
            rn = ar.f32(512)
            b_rn = Buf()
            cp("dve", rn, PS[2][:, :], [PB[2]], [b_rn])
            tt("dve", rn, rn, PS[3][:, :], ALU.add, [b_rn, PB[3]], [b_rn])
            recip(rn, rn, [b_rn], [b_rn])
            tmpe = [(ar.f32(512), Buf()) for _ in range(2)]
            P.barrier()
            ar.off = 0
            hdb = ar.bf(nj * 1024).rearrange("p (j c) -> p j c", j=nj)
            b_hdb = [Buf() for _ in range(nj)]
            tabs = [(ar.bf(2 * nj * 128).rearrange("p (c j f) -> p c j f", c=2, j=nj), Buf()) for _ in range(2)]
            assert ar.off <= off_dec_end
            ar.off = off_hd
            sts = [(ar.f32(1024).rearrange("p (c n) -> p c n", c=2), Buf()) for _ in range(1)]
            for pc in range(nj):
                te, b_te = tmpe[pc % 2]
                hf = hd[:, pc, 0:512]
                hb = hd[:, pc, 512:1024]
                tt("dve", te, hf, hb, ALU.add, [b_hd[pc]], [b_te])
                tt("pool", hb, hf, hb, ALU.subtract, [b_hd[pc]], [b_hd[pc]])
                tt("dve", hdb[:, pc, 0:512], te, rn, ALU.mult, [b_te, b_rn], [b_hdb[pc]])
                tt("pool", hdb[:, pc, 512:1024], hb, rn, ALU.mult, [b_hd[pc], b_rn], [b_hdb[pc]])
            for fc in range(nj):
                tb_, b_tb = tabs[fc % 2]
                for c in range(2):
                    P.dma(tb_[:, c], tF_d[L][c, fc], writes=[b_tb])
                for c in range(2):
                    for jc in range(nj):
                        mm(PS[4 + c][:, :], tb_[:, c, jc, :], hdb[:, jc, c * 512:(c + 1) * 512], jc == 0, jc == nj - 1, [b_tb, b_hdb[jc]], [PB[4 + c]])
                st, b_st = sts[0]
                cp("act", st[:, 0, :], PS[4][:, :], [PB[4]], [b_st])
                cp("dve", st[:, 1, :], PS[5][:, :], [PB[5]], [b_st])
                for c in range(2):
                    P.dma(kf_s[L][c, fc], st[:, c, :], reads=[b_st], writes=[b_kf[L]])

        b_kf = {S: Buf(), C: Buf()}
        halfpi = sm[:, 182:183]
        P.op("dve", lambda e: e.memset(halfpi, PI / 2), [], [b_sm])
        if want("hyf"):
            for (t0, L) in streams:
                hyena_filters(L)

        ar.reset()
        nT = ar.bf(8 * T).rearrange("p (k t) -> p k t", k=8)
        b_n = [[Buf() for _ in range(5)] for _ in range(8)]
        b_nTs = Buf()

        def emit_mix(k, t0, Tn, tm, b_t, Asc, Bsc):
            act(nT[:, k, t0:t0 + Tn], tm[:, :Tn], AF.Identity, [b_t, b_sm], [b_n[k][t0 // 512]], bias=Bsc, scale=Asc)

        mark = ar.off
        if want("proj") or want("merge"):
            modulate(A_mix, 0, tb_all, emit_mix, mod_tmps(512))
            for k in range(8):
                P.dma(nT_s[:, k, :], nT[:, k, :], reads=b_n[k], writes=[b_nTs])
        ar.off = mark
        P.barrier()

        b_hy = Buf()
        b_fn = Buf()
        b_q = Buf()
        b_k = Buf()
        b_v = Buf()
        if want("proj"):
            w32 = [(ar.f32(8 * 512).rearrange("p (k m) -> p k m", k=8), Buf()) for _ in range(1)]
            wbf = [(ar.bf(8 * 512).rearrange("p (k m) -> p k m", k=8), Buf()) for _ in range(2)]
            stg = [(ar.f32(512), Buf()) for _ in range(2)]
            sqt = [(ar.f32(512), Buf()) for _ in range(2)]
            rt = [(ar.f32(512), Buf()) for _ in range(2)]
            xnt = [(ar.f32(512), Buf()) for _ in range(2)]
            t1t = [(ar.f32(512), Buf()) for _ in range(2)]
            t2t = [(ar.f32(512), Buf()) for _ in range(2)]
            obt = [(ar.bf(512), Buf()) for _ in range(2)]
            rope_sb = ar.f32(2 * S).rearrange("p (c t) -> p c t", c=2)
            b_rope = Buf()
            for c in range(2):
                P.dma(rope_sb[:, c, :], rope_d[c], writes=[b_rope])
            cnt = [0]

            def qk_cb(ps, pb, which, h, t0, Tn):
                i = cnt[0] % 2
                cnt[0] += 1
                sq_, b_sq_ = sqt[i]
                act(sq_[:, :Tn], ps[:, :Tn], AF.Square, [pb], [b_sq_])
                mm(PS[6][:, :Tn], bd64, sq_[:, :Tn], True, True, [b_sq_, b_cst], [PB[6]])
                r_, b_r_ = rt[i]
                rsqrt_mean(r_[:, :Tn], PS[6][:, :Tn], 64, [PB[6]], [b_r_])
                xn, b_xn = xnt[i]
                stt(xn[:, :Tn], ps[:, :Tn], gqk[:, which:which + 1], r_[:, :Tn], ALU.mult, ALU.mult, [pb, b_r_, b_sm], [b_xn])
                ob, b_ob = obt[i]
                if t0 < S:
                    mm(PS[5][:, :Tn], rot, xn[:, :Tn], True, True, [b_xn, b_cst], [PB[5]])
                    t1, b_t1 = t1t[i]
                    t2, b_t2 = t2t[i]
                    tt("pool", t1[:, :Tn], xn[:, :Tn], rope_sb[:, 0, t0:t0 + Tn], ALU.mult, [b_xn, b_rope], [b_t1])
                    tt("dve", t2[:, :Tn], PS[5][:, :Tn], rope_sb[:, 1, t0:t0 + Tn], ALU.mult, [PB[5], b_rope], [b_t2])
                    tt("pool", ob[:, :Tn], t1[:, :Tn], t2[:, :Tn], ALU.add, [b_t1, b_t2], [b_ob])
                else:
                    cp("pool", ob[:, :Tn], xn[:, :Tn], [b_xn], [b_ob])
                dst = (q_s if which == 0 else k_s)[h][:, t0:t0 + Tn]
                P.dma(dst, ob[:, :Tn], reads=[b_ob], writes=[b_q if which == 0 else b_k])

            pcnt = [0]
            for g in range(5):
                w3_, b_w3 = w32[0]
                P.dma(w3_, w_in_d[l][:, g * 512:(g + 1) * 512].rearrange("(k p) m -> p k m", p=128), writes=[b_w3])
                wb_, b_wb = wbf[g % 2]
                cp("act", wb_[:, 0:4, :], w3_[:, 0:4, :], [b_w3], [b_wb])
                cp("pool", wb_[:, 4:8, :], w3_[:, 4:8, :], [b_w3], [b_wb])
                if g < 4:
                    for mi in range(4):
                        for (t0, Tn) in tb_all:
                            if g == 2 and t0 >= S and last:
                                continue
                            pi = pcnt[0] % 4
                            pcnt[0] += 1
                            for k in range(8):
                                mm(PS[pi][:, :Tn], wb_[:, k, mi * 128:(mi + 1) * 128], nT[:, k, t0:t0 + Tn], k == 0, k == 7,
                                   [b_wb, b_n[k][t0 // 512]], [PB[pi]])
                            if g < 2:
                                st, b_st = stg[pcnt[0] % 2]
                                cp("act" if pcnt[0] % 2 else "dve", st[:, :Tn], PS[pi][:, :Tn], [PB[pi]], [b_st])
                                ch = g * 4 + mi
                                if ch < 6:
                                    P.dma(hy_s[ch][:, t0:t0 + Tn], st[:, :Tn], reads=[b_st], writes=[b_hy])
                                else:
                                    P.dma(fn_s[ch - 6][:, t0:t0 + Tn], st[:, :Tn], reads=[b_st], writes=[b_fn])
                            else:
                                qk_cb(PS[pi], PB[pi], g - 2, mi, t0, Tn)
                else:
                    for i in range(18):
                        pi = pcnt[0] % 4
                        pcnt[0] += 1
                        for k in range(8):
                            mm(PS[pi][:, :], nT[:, k, i * 128:(i + 1) * 128], wb_[:, k, :], k == 0, k == 7, [b_wb, b_n[k][i // 4]], [PB[pi]])
                        ob, b_ob = obt[i % 2]
                        cp("act" if i % 2 else "dve", ob, PS[pi][:, :], [PB[pi]], [b_ob])
                        P.dma(v_s[:, i, :], ob, reads=[b_ob], writes=[b_v])

        b_ato = Buf()
        if want("attn"):
            ar.reset()
            kT2 = [ar.bf(4 * T).rearrange("p (h t) -> p h t", h=4) for _ in range(2)]
            qT = ar.bf(4 * T).rearrange("p (h t) -> p h t", h=4)
            vv = ar.bf(18 * 512).rearrange("p (j c) -> p j c", j=18)
            b_kT = Buf()
            b_qT = Buf()
            b_vv = Buf()
            for h in range(4):
                for c in range(2):
                    P.dma(kT2[c][:, h, :], k_s[h], reads=[b_k], writes=[b_kT])
                P.dma(qT[:, h, :], q_s[h], reads=[b_q], writes=[b_qT])
            P.op("pool", lambda e: e.memset(kT2[0][64:128, :, :], 0.0), [], [b_kT])
            P.op("pool", lambda e: e.memset(kT2[1][0:64, :, :], 0.0), [], [b_kT])
            P.dma(vv, v_s, reads=[b_v], writes=[b_vv])
            Et = [(ar.bf(512), Buf()) for _ in range(4)]
            r0 = ar.f32(512)
            t0_ = ar.f32(512)
            t1_ = ar.f32(512)
            sq_ = ar.f32(512)
            rr_ = ar.f32(512)
            b_r0, b_t0, b_t1, b_sq2, b_rr = [Buf() for _ in range(5)]
            aob = [(ar.bf(512), Buf()) for _ in range(2)]
            ec = 0
            oc = 0
            qblocks = tblocks(0, S) if last else tb_all
            for h in range(4):
                for (q0, Tn) in qblocks:
                    keys = list(range(18)) if q0 < S else [16, 17]
                    seq = [(c, idx, j) for c in range(2) for idx, j in enumerate(keys)]

                    SB3 = (0, 1, 7)

                    def s_mm(n_):
                        c, idx, j = seq[n_]
                        sp = SB3[(ec + n_) % 3]
                        mm(PS[sp][:, :Tn], kT2[c][:, h, j * 128:(j + 1) * 128], qT[:, h, q0:q0 + Tn],
                           True, True, [b_kT, b_qT], [PB[sp]])

                    s_mm(0)
                    if len(seq) > 1:
                        s_mm(1)
                    for n_, (c, idx, j) in enumerate(seq):
                        if n_ + 2 < len(seq):
                            s_mm(n_ + 2)
                        sp = SB3[(ec + n_) % 3]
                        E, b_E = Et[(ec + n_) % 4]
                        act(E[:, :Tn], PS[sp][:, :Tn], AF.Exp, [PB[sp]], [b_E], scale=0.125)
                        mm(PS[2 + 2 * c][:, :Tn], vv[:, j, h * 128:(h + 1) * 128], E[:, :Tn], idx == 0, idx == len(keys) - 1, [b_vv, b_E], [PB[2 + 2 * c]])
                        mm(PS[3 + 2 * c][:, :Tn], onesb, E[:, :Tn], idx == 0, idx == len(keys) - 1, [b_cst, b_E], [PB[3 + 2 * c]])
                    ec += len(seq)
                    recip(r0[:, :Tn], PS[3][:, :Tn], [PB[3]], [b_r0])
                    tt("dve", t0_[:, :Tn], PS[2][:, :Tn], r0[:, :Tn], ALU.mult, [PB[2], b_r0], [b_t0])
                    recip(r0[:, :Tn], PS[5][:, :Tn], [PB[5]], [b_r0])
                    tt("dve", t1_[:, :Tn], PS[4][:, :Tn], r0[:, :Tn], ALU.mult, [PB[4], b_r0], [b_t1])
                    stt(t0_[:, :Tn], t1_[:, :Tn], neg_lam, t0_[:, :Tn], ALU.mult, ALU.add, [b_t1, b_t0, b_sm], [b_t0])
                    act(sq_[:, :Tn], t0_[:, :Tn], AF.Square, [b_t0], [b_sq2])
                    mm(PS[6][:, :Tn], ones32, sq_[:, :Tn], True, True, [b_sq2, b_cst], [PB[6]])
                    rsqrt_mean(rr_[:, :Tn], PS[6][:, :Tn], 128, [PB[6]], [b_rr])
                    ao, b_ao = aob[oc % 2]
                    oc += 1
                    stt(ao[:, :Tn], t0_[:, :Tn], gsub_s, rr_[:, :Tn], ALU.mult, ALU.mult, [b_t0, b_rr, b_sm], [b_ao])
                    P.dma(ato_s[h][:, q0:q0 + Tn], ao[:, :Tn], reads=[b_ao], writes=[b_ato])

        b_hyo = Buf()

        def hyena_main(t0, L):
            TB, G, nTB, nG = RL[L]
            nj = L // 128
            for cc in range(2):
                ar.reset()
                raw = ar.f32(L)
                b_raw = Buf()
                us = [ar.f32(L) for _ in range(3)]
                b_us = [Buf() for _ in range(3)]
                for jx in range(3):
                    ch = jx * 2 + cc
                    P.dma(raw, hy_s[ch][:, t0:t0 + L], reads=[b_hy], writes=[b_raw])
                    u = us[jx]
                    ts("dve", u, raw, hcw[:, ch, 1:2], hcw[:, ch, 3:4], ALU.mult, ALU.add, [b_raw, b_sm], [b_us[jx]])
                    stt(u[:, 1:L], raw[:, 0:L - 1], hcw[:, ch, 0:1], u[:, 1:L], ALU.mult, ALU.add, [b_raw, b_us[jx], b_sm], [b_us[jx]])
                    stt(u[:, 0:L - 1], raw[:, 1:L], hcw[:, ch, 2:3], u[:, 0:L - 1], ALU.mult, ALU.add, [b_raw, b_us[jx], b_sm], [b_us[jx]])
                zbuf = raw
                b_zb = b_raw
                ztm = ar.bf(L).rearrange("p (j c) -> p j c", j=nj)
                b_ztm = Buf()
                Z = ar.f32(2 * L).rearrange("p (c j n) -> p c j n", c=2, j=nj)
                b_Z = Buf()
                Kf = ar.f32(2 * L).rearrange("p (c j n) -> p c j n", c=2, j=nj)
                b_Kf = Buf()
                X = ar.bf(2 * L).rearrange("p (c j n) -> p c j n", c=2, j=nj)
                b_X = Buf()
                tmpx = ar.f32(L).rearrange("p (j n) -> p j n", j=nj)
                b_tx = Buf()
                tmpy = ar.f32(L).rearrange("p (j n) -> p j n", j=nj)
                b_ty = Buf()
                tabs = [(ar.bf(2048), Buf()) for _ in range(4)]
                tcnt = 0
                z, b_z = us[0], b_us[0]
                for o in range(2):
                    for scn in range(nj):
                        tr(PS[0][:, (scn % 4) * 128:(scn % 4 + 1) * 128], z[:, scn * 128:(scn + 1) * 128], ident32, [b_z, b_cst], [PB[0]])
                        if scn % 4 == 3 or scn == nj - 1:
                            n4 = scn % 4 + 1
                            cp("act", ztm[:, scn - n4 + 1:scn + 1, :], PS[0][:, 0:n4 * 128].rearrange("p (j c) -> p j c", j=n4), [PB[0]], [b_ztm])
                    for c in range(2):
                        P.dma(Kf[:, c], kf_s[L][c][:, :, o * 256 + cc * 128:o * 256 + cc * 128 + 128].rearrange("j p n -> p j n"), reads=[b_kf[L]], writes=[b_Kf])
                    for fc in range(nj):
                        tbc, b_tbc = tabs[tcnt % 4]
                        tbs, b_tbs = tabs[(tcnt + 1) % 4]
                        tcnt += 2
                        tbc3 = tbc[:, 0:nj * 128].rearrange("p (j f) -> p j f", j=nj)
                        tbs3 = tbs[:, 0:nj * 128].rearrange("p (j f) -> p j f", j=nj)
                        P.dma(tbc3, tF_d[L][0, fc], writes=[b_tbc])
                        P.dma(tbs3, tF_d[L][1, fc], writes=[b_tbs])
                        pz = 1 + fc % 2
                        for sc_ in range(nj):
                            mm(PS[pz][:, 0:128], tbc3[:, sc_, :], ztm[:, sc_, :], sc_ == 0, sc_ == nj - 1, [b_tbc, b_ztm], [PB[pz]])
                        for sc_ in range(nj):
                            mm(PS[pz][:, 128:256], tbs3[:, sc_, :], ztm[:, sc_, :], sc_ == 0, sc_ == nj - 1, [b_tbs, b_ztm], [PB[pz]])
                        cp("act", Z[:, :, fc, :], PS[pz][:, 0:256].rearrange("p (c n) -> p c n", c=2), [PB[pz]], [b_Z])
                    Zr, Zi, Kr, Ki = Z[:, 0], Z[:, 1], Kf[:, 0], Kf[:, 1]
                    tt("dve", tmpx, Zr, Kr, ALU.mult, [b_Z, b_Kf], [b_tx])
                    tt("pool", tmpy, Zi, Ki, ALU.mult, [b_Z, b_Kf], [b_ty])
                    tt("dve", X[:, 0], tmpx, tmpy, ALU.subtract, [b_tx, b_ty], [b_X])
                    tt("pool", tmpy, Zr, Ki, ALU.mult, [b_Z, b_Kf, b_X], [b_ty])
                    tt("dve", tmpx, Zi, Kr, ALU.mult, [b_Z, b_Kf, b_X], [b_tx])
                    tt("dve", X[:, 1], tmpx, tmpy, ALU.add, [b_tx, b_ty], [b_X])
                    gate, b_g = us[1 + o], b_us[1 + o]
                    znew, b_zn = (zbuf, b_zb) if o == 0 else (us[0], b_us[0])
                    for tb in range(nTB):
                        py = 3 + tb % 2
                        for g in range(nG):
                            tbc, b_tbc = tabs[tcnt % 4]
                            tbs, b_tbs = tabs[(tcnt + 1) % 4]
                            tcnt += 2
                            tc3 = tbc[:, 0:G * TB].rearrange("p (g t) -> p g t", g=G)
                            ts3 = tbs[:, 0:G * TB].rearrange("p (g t) -> p g t", g=G)
                            P.dma(tc3, tI_d[L][0, tb, g], writes=[b_tbc])
                            P.dma(ts3, tI_d[L][1, tb, g], writes=[b_tbs])
                            for fi in range(G):
                                fc = g * G + fi
                                mm(PS[py][:, :TB], X[:, 0, fc, :], tc3[:, fi, :], fc == 0, False, [b_X, b_tbc], [PB[py]])
                                mm(PS[py][:, :TB], X[:, 1, fc, :], ts3[:, fi, :], False, fc == nj - 1, [b_X, b_tbs], [PB[py]])
                        sl = slice(tb * TB, (tb + 1) * TB)
                        stt(tmpx.rearrange("p j n -> p (j n)")[:, sl], z[:, sl], hbias[:, o, cc:cc + 1], PS[py][:, :TB], ALU.mult, ALU.add, [b_z, PB[py], b_sm, b_X], [b_tx])
                        tt("dve", znew[:, sl], tmpx.rearrange("p j n -> p (j n)")[:, sl], gate[:, sl], ALU.mult, [b_tx, b_g], [b_zn])
                    z, b_z = znew, b_zn
                ob = ar.bf(L)
                b_ob = Buf()
                cp("act", ob, z, [b_z], [b_ob])
                P.dma(hyo_s[cc][:, t0:t0 + L], ob, reads=[b_ob], writes=[b_hyo])

        if want("hyena"):
            for (t0, L) in streams:
                hyena_main(t0, L)

        b_fno = Buf()

        def fnet(t0, L):
            TB, G, nTB, nG = RL[L]
            nj = L // 128
            ar.reset()
            fz = [(ar.f32(L), Buf()) for _ in range(2)]
            zc = ar.bf(nj * 256).rearrange("p (j c) -> p j c", j=nj)
            zs = ar.bf(nj * 256).rearrange("p (j c) -> p j c", j=nj)
            b_zc = Buf()
            for cc in range(2):
                f, b_f = fz[cc]
                P.dma(f, fn_s[cc][:, t0:t0 + L], reads=[b_fn], writes=[b_f])
                for scn in range(nj):
                    pa = scn % 2
                    mm(PS[pa][:, 0:128], f[:, scn * 128:(scn + 1) * 128], bdc, True, True, [b_f, b_cst], [PB[pa]])
                    mm(PS[pa][:, 128:256], f[:, scn * 128:(scn + 1) * 128], bds, True, True, [b_f, b_cst], [PB[pa]])
                    cp("act", zc[:, scn, cc * 128:(cc + 1) * 128], PS[pa][:, 0:128], [PB[pa]], [b_zc])
                    cp("dve", zs[:, scn, cc * 128:(cc + 1) * 128], PS[pa][:, 128:256], [PB[pa]], [b_zc])
            tabs = [(ar.bf(2048), Buf()) for _ in range(4)]
            obs = [(ar.bf(512), Buf()) for _ in range(2)]
            tcnt = 0
            oc = 0
            for tb in range(nTB):
                for g in range(nG):
                    tbc, b_tbc = tabs[tcnt % 4]
                    tbs, b_tbs = tabs[(tcnt + 1) % 4]
                    tcnt += 2
                    tc3 = tbc[:, 0:G * TB].rearrange("p (g t) -> p g t", g=G)
                    ts3 = tbs[:, 0:G * TB].rearrange("p (g t) -> p g t", g=G)
                    P.dma(tc3, tN_d[L][0, tb, g], writes=[b_tbc])
                    P.dma(ts3, tN_d[L][1, tb, g], writes=[b_tbs])
                    for cc in range(2):
                        py = 2 + cc + 2 * (tb % 2)
                        for fi in range(G):
                            sc_ = g * G + fi
                            mm(PS[py][:, :TB], zc[:, sc_, cc * 128:(cc + 1) * 128], tc3[:, fi, :], sc_ == 0, False, [b_zc, b_tbc], [PB[py]])
                            mm(PS[py][:, :TB], zs[:, sc_, cc * 128:(cc + 1) * 128], ts3[:, fi, :], False, sc_ == nj - 1, [b_zc, b_tbs], [PB[py]])
                for cc in range(2):
                    py = 2 + cc + 2 * (tb % 2)
                    ob, b_ob = obs[oc % 2]
                    oc += 1
                    cp("act" if cc else "dve", ob[:, :TB], PS[py][:, :TB], [PB[py]], [b_ob])
                    P.dma(fno_s[cc][:, t0 + tb * TB:t0 + (tb + 1) * TB], ob[:, :TB], reads=[b_ob], writes=[b_fno])

        if want("fnet"):
            for (t0, L) in streams:
                fnet(t0, L)

        if want("merge"):
            ar.reset()
            wbr = ar.bf(8 * D).rearrange("p (k m) -> p k m", k=8)
            wo = ar.bf(8 * D).rearrange("p (k m) -> p k m", k=8)
            b_wbr = Buf()
            b_wo = Buf()
            w32 = ar.f32(8 * 512).rearrange("p (k m) -> p k m", k=8)
            b_w32 = Buf()
            for (src, dst, bd) in ((w_br_d, wbr, b_wbr), (w_out_d, wo, b_wo)):
                for half in range(2):
                    P.dma(w32, src[l][:, half * 512:(half + 1) * 512].rearrange("(k p) m -> p k m", p=128), writes=[b_w32])
                    cp("act", dst[:, 0:4, half * 512:(half + 1) * 512], w32[:, 0:4, :], [b_w32], [bd])
                    cp("dve", dst[:, 4:8, half * 512:(half + 1) * 512], w32[:, 4:8, :], [b_w32], [bd])
            ar.off -= 8 * 512
            P.barrier()
            nb = [(ar.bf(8 * 512).rearrange("p (k t) -> p k t", k=8), Buf()) for _ in range(2)]
            sb_ = [(ar.bf(8 * 512).rearrange("p (k t) -> p k t", k=8), Buf()) for _ in range(2)]
            g32 = [(ar.f32(8 * 384).rearrange("p (k j c) -> p k j c", k=8, j=3), Buf()) for _ in range(2)]
            gbf = [(ar.bf(8 * 384).rearrange("p (k j c) -> p k j c", k=8, j=3), Buf()) for _ in range(2)]
            sig = [(ar.f32(512), Buf()) for _ in range(3)]
            yacc = [(ar.f32(512), Buf()) for _ in range(2)]
            ytmp = [(ar.f32(512), Buf()) for _ in range(2)]
            yT = ar.bf(8 * 512).rearrange("p (k t) -> p k t", k=8)
            b_yT = [Buf() for _ in range(8)]
            KR = ((0, 2), (2, 4), (4, 8))
            gc = 0
            for bi, (t0, Tn) in enumerate(tb_mix):
                nbk, b_nb = nb[bi % 2]
                sbk, b_sb = sb_[bi % 2]
                P.dma(nbk[:, :, :Tn], nT_s[:, :, t0:t0 + Tn], reads=[b_nTs], writes=[b_nb])
                for cc in range(2):
                    P.dma(sbk[:, cc, :Tn], hyo_s[cc][:, t0:t0 + Tn], reads=[b_hyo], writes=[b_sb])
                    P.dma(sbk[:, 2 + cc, :Tn], fno_s[cc][:, t0:t0 + Tn], reads=[b_fno], writes=[b_sb])
                for h in range(4):
                    P.dma(sbk[:, 4 + h, :Tn], ato_s[h][:, t0:t0 + Tn], reads=[b_ato], writes=[b_sb])
                for m in range(8):
                    gw, b_gw = g32[gc % 2]
                    gb, b_gb = gbf[gc % 2]
                    gc += 1
                    P.dma(gw, w_gate_d[l, m], writes=[b_gw])
                    cp("act", gb, gw, [b_gw], [b_gb])
                    ya, b_ya = yacc[m % 2]
                    yt, b_yt = ytmp[m % 2]
                    for j in range(3):
                        pg = j
                        pbr = 3 + j
                        for k in range(8):
                            mm(PS[pg][:, :Tn], gb[:, k, j, :], nbk[:, k, :Tn], k == 0, k == 7, [b_gb, b_nb], [PB[pg]])
                        k0, k1 = KR[j]
                        for k in range(k0, k1):
                            mm(PS[pbr][:, :Tn], wbr[:, k, m * 128:(m + 1) * 128], sbk[:, k, :Tn], k == k0, k == k1 - 1, [b_wbr, b_sb], [PB[pbr]])
                        sg, b_sg = sig[(m * 3 + j) % 3]
                        act(sg[:, :Tn], PS[pg][:, :Tn], AF.Sigmoid, [PB[pg]], [b_sg])
                        if j == 0:
                            tt("dve", ya[:, :Tn], sg[:, :Tn], PS[pbr][:, :Tn], ALU.mult, [b_sg, PB[pbr]], [b_ya])
                        else:
                            tt("dve", yt[:, :Tn], sg[:, :Tn], PS[pbr][:, :Tn], ALU.mult, [b_sg, PB[pbr]], [b_yt])
                            if j == 1:
                                tt("dve", ya[:, :Tn], ya[:, :Tn], yt[:, :Tn], ALU.add, [b_ya, b_yt], [b_ya])
                            else:
                                tt("dve", yT[:, m, :Tn], ya[:, :Tn], yt[:, :Tn], ALU.add, [b_ya, b_yt], [b_yT[m]])
                jj = 0 if t0 < S else 1
                for mo in range(8):
                    po = 6 + mo % 2
                    for k in range(8):
                        mm(PS[po][:, :Tn], wo[:, k, mo * 128:(mo + 1) * 128], yT[:, k, :Tn], k == 0, k == 7, [b_wo, b_yT[k]], [PB[po]])
                    xb = b_x[mo][t0 // 512]
                    stt(xT[:, mo, t0:t0 + Tn], PS[po][:, :Tn], modv[:, 2, mo, jj:jj + 1], xT[:, mo, t0:t0 + Tn], ALU.mult, ALU.add, [PB[po], xb, b_sm], [xb])

        if want("peer"):
            peer(l, last, modulate, mod_tmps, A_ffn)

    def peer(l, last, modulate, mod_tmps, A_ffn):
        ar.reset()
        b_ub = Buf()
        b_vb = Buf()
        ld = [(ar.f32(4096), Buf()) for _ in range(3)]
        cv = [(ar.bf(4096), Buf()) for _ in range(3)]
        jobs = []
        for k in range(8):
            for eb in range(4):
                jobs.append(("u", k, eb))
        for e1g in range(32):
            jobs.append(("v", e1g, 0))

        def j_load(ci):
            kind, i0_, i1_ = jobs[ci]
            a, b_a = ld[ci % 3]
            if kind == "u":
                P.dma(a, uT_d[l][i0_ * 128:(i0_ + 1) * 128, i1_ * 4096:(i1_ + 1) * 4096], writes=[b_a])
            else:
                P.dma(a.rearrange("p (e d) -> p e d", e=4), v_d_in[l][i0_ * 512:(i0_ + 1) * 512, :].rearrange("(e p) d -> p e d", p=128), writes=[b_a])

        j_load(0)
        j_load(1)
        for ci in range(len(jobs)):
            if ci + 2 < len(jobs):
                j_load(ci + 2)
            kind, i0_, i1_ = jobs[ci]
            a, b_a = ld[ci % 3]
            o, b_o = cv[ci % 3]
            cp(("act", "dve", "pool")[ci % 3], o, a, [b_a], [b_o])
            if kind == "u":
                P.dma(uTb_s[i1_ * 8:(i1_ + 1) * 8, :, i0_, :].rearrange("g p e -> p g e"), o.rearrange("p (g e) -> p g e", g=8), reads=[b_o], writes=[b_ub])
            else:
                P.dma(vb_s[:, i0_ * 4:(i0_ + 1) * 4, :], o.rearrange("p (e d) -> p e d", e=4), reads=[b_o], writes=[b_vb])
        ar.reset()
        keysT = ar.f32(2048).rearrange("p (h n) -> p h n", h=16)
        b_ky = Buf()
        P.dma(keysT, keysT_d[l], writes=[b_ky])
        nbfs = [ar.bf(8 * 256).rearrange("p (k t) -> p k t", k=8) for _ in range(2)]
        b_nbfs = [[Buf() for _ in range(8)] for _ in range(2)]
        s1k = [ar.f32(1024).rearrange("p (h n) -> p h n", h=8) for _ in range(2)]
        a2k = [ar.f32(1024).rearrange("p (h n) -> p h n", h=8) for _ in range(2)]
        a1t = [ar.f32(128).rearrange("p (h a) -> p h a", h=8) for _ in range(2)]
        top1k = [ar.f32(128).rearrange("p (h a) -> p h a", h=8) for _ in range(2)]
        theta = [ar.f32(8) for _ in range(2)]
        b_keep = [Buf() for _ in range(2)]
        zer = ar.bf(512)
        b_zer = Buf()
        P.op("pool", lambda e: e.memset(zer, 0.0), [], [b_zer])
        base = ar.off
        mt = mod_tmps(256, (2, 1, 2))
        n32 = ar.f32(8 * 256).rearrange("p (k t) -> p k t", k=8)
        b_n32 = [Buf() for _ in range(8)]
        wq = [(ar.f32(8 * 128).rearrange("p (k m) -> p k m", k=8), Buf()) for _ in range(2)]
        qT = ar.f32(16 * 256).rearrange("p (h t) -> p h t", h=16)
        b_qT = Buf()
        s_sb = ar.f32(2048).rearrange("p (h n) -> p h n", h=16)
        b_s = Buf()
        tmpm = ar.f32(256)
        b_tm = Buf()
        tmpm2 = ar.f32(256)
        b_tm2 = Buf()
        top = ar.f32(256).rearrange("p (h a) -> p h a", h=16)
        b_top = Buf()
        cand = ar.f32(256)
        b_cand = Buf()
        ctop = ar.f32(192).rearrange("p (h a) -> p h a", h=8)
        b_ct = Buf()
        misc = ar.f32(16)
        b_mi = Buf()
        ex16 = ar.f32(128).rearrange("p (h a) -> p h a", h=8)
        ub = [(ar.bf(8 * 512).rearrange("p (k e) -> p k e", k=8), Buf()) for _ in range(2)]
        vbf = [(ar.bf(4 * 1024).rearrange("p (e d) -> p e d", e=4), Buf()) for _ in range(2)]
        Wt = [(ar.bf(4 * 256).rearrange("p (e t) -> p e t", e=4), Buf()) for _ in range(2)]
        gel = [(ar.f32(512), Buf()) for _ in range(2)]
        GT = [(ar.bf(512).rearrange("p (e t) -> p e t", e=2), Buf()) for _ in range(2)]
        blocks = tblocks(0, S if last else T, 256)
        b_wT = [Buf(), Buf()]

        def phaseA(t0, Tn, nbf, b_nbf):
            jj = 0 if t0 < S else 1
            def emit_ffn(k, t0_, Tn_, tm, b_t, Asc, Bsc):
                act(n32[:, k, :], tm[:, :Tn_], AF.Identity, [b_t, b_sm], [b_n32[k]], bias=Bsc, scale=Asc)
                cp("pool", nbf[:, k, :], n32[:, k, :], [b_n32[k]], [b_nbf[k]])

            modulate(A_ffn, 3, [(t0, Tn)], emit_ffn, mt, 2)
            yield
            for hp in range(16):
                w, b_w = wq[hp % 2]
                P.dma(w, wq_d[l][:, hp * 128:(hp + 1) * 128].rearrange("(k p) m -> p k m", p=128), writes=[b_w])
                pq = 2 + hp % 2
                for k in range(8):
                    mm(PS[pq][:, :Tn], w[:, k, :], n32[:, k, :], k == 0, k == 7, [b_w, b_n32[k]], [PB[pq]])
                cp("act" if hp % 2 else "dve", qT[:, hp, :], PS[pq][:, :Tn], [PB[pq]], [b_qT])
                yield
            for ti in range(2):
                tsl = slice(ti * 128, (ti + 1) * 128)
                bk = b_keep[ti]
                for hf in range(2):
                    for h8 in range(8):
                        hp = hf * 8 + h8
                        pbk = 2 + h8 // 4
                        mm(PS[pbk][:, (h8 % 4) * 128:(h8 % 4 + 1) * 128], qT[:, hp, tsl], keysT[:, hp, :], True, True, [b_qT, b_ky], [PB[pbk]])
                    for q2 in range(2):
                        cp("act" if q2 else "dve", s_sb[:, hf * 8 + q2 * 4:hf * 8 + (q2 + 1) * 4, :], PS[2 + q2][:, :].rearrange("p (h n) -> p h n", h=4), [PB[2 + q2]], [b_s])
                    yield
                for hp in range(16):
                    P.op("dve", (lambda hp: lambda e: e.max(out=top[:, hp, 0:8], in_=s_sb[:, hp, :]))(hp), [b_s], [b_top])
                    P.op("dve", (lambda hp: lambda e: e.match_replace(out=tmpm[:, 0:128], in_to_replace=top[:, hp, 0:8], in_values=s_sb[:, hp, :], imm_value=-1e30))(hp), [b_s, b_top], [b_tm])
                    P.op("dve", (lambda hp: lambda e: e.max(out=top[:, hp, 8:16], in_=tmpm[:, 0:128]))(hp), [b_tm], [b_top])
                    if hp % 2:
                        yield
                top4 = top.rearrange("p (h c) a -> p h c a", c=2)
                s4v = s_sb.rearrange("p (h c) n -> p h c n", c=2)
                for h in range(8):
                    ch = cand
                    tt("dve", cand.rearrange("p (a b) -> p a b", a=16), top4[:, h, 0, :].unsqueeze(2).to_broadcast([128, 16, 16]),
                       top4[:, h, 1, :].unsqueeze(1).to_broadcast([128, 16, 16]), ALU.add, [b_top, b_ct], [b_cand])
                    P.op("dve", (lambda h, ch: lambda e: e.max(out=ctop[:, h, 0:8], in_=ch))(h, ch), [b_cand], [b_ct])
                    P.op("dve", (lambda h, ch: lambda e: e.match_replace(out=tmpm, in_to_replace=ctop[:, h, 0:8], in_values=ch, imm_value=-1e30))(h, ch), [b_cand, b_ct], [b_tm])
                    P.op("dve", (lambda h: lambda e: e.max(out=ctop[:, h, 8:16], in_=tmpm))(h), [b_tm], [b_ct])
                    P.op("dve", (lambda h: lambda e: e.match_replace(out=tmpm2, in_to_replace=ctop[:, h, 8:16], in_values=tmpm, imm_value=-1e30))(h), [b_tm, b_ct], [b_tm2])
                    P.op("dve", (lambda h: lambda e: e.max(out=ctop[:, h, 16:24], in_=tmpm2))(h), [b_tm2], [b_ct])
                    yield
                tt("dve", ex16, ctop[:, :, 0:16], ctop[:, :, 0:1].to_broadcast([128, 8, 16]), ALU.subtract, [b_ct], [b_mi])
                act(ex16, ex16, AF.Exp, [b_mi], [b_mi])
                P.op("dve", lambda e: e.reduce_sum(out=misc[:, 8:16], in_=ex16, axis=AX.X), [b_mi], [b_mi])
                recip(misc[:, 8:16], misc[:, 8:16], [b_mi], [b_mi])
                m8 = misc[:, 0:8].unsqueeze(2)
                tt("dve", m8, ctop[:, :, 15:16], ctop[:, :, 16:17], ALU.add, [b_ct, b_mi], [b_mi])
                stt(m8, m8, 0.5, ctop[:, :, 0:1], ALU.mult, ALU.subtract, [b_mi, b_ct], [b_mi])
                act(misc[:, 0:8], misc[:, 0:8], AF.Exp, [b_mi], [b_mi])
                tt("dve", theta[ti], misc[:, 0:8], misc[:, 8:16], ALU.mult, [b_mi], [bk])
                cp("pool", s1k[ti], s4v[:, :, 0, :], [b_s], [bk])
                cp("pool", top1k[ti], top4[:, :, 0, :], [b_top], [bk])
                tt("dve", a2k[ti], s4v[:, :, 1, :], top4[:, :, 1, 0:1].to_broadcast([128, 8, 128]), ALU.subtract, [b_s, b_top], [bk])
                act(a2k[ti], a2k[ti], AF.Exp, [bk], [bk])
                tt("dve", a1t[ti], top4[:, :, 0, :], top4[:, :, 0, 0:1].to_broadcast([128, 8, 16]), ALU.subtract, [b_top], [bk])
                act(a1t[ti], a1t[ti], AF.Exp, [bk], [bk])
                tt("dve", a1t[ti], a1t[ti], misc[:, 8:16].unsqueeze(2).to_broadcast([128, 8, 16]), ALU.mult, [bk, b_mi], [bk])
                yield

        for _ in phaseA(blocks[0][0], blocks[0][1], nbfs[0], b_nbfs[0]):
            pass
        for bi, (t0, Tn) in enumerate(blocks):
            jj = 0 if t0 < S else 1
            nbf = nbfs[bi % 2]
            b_nbf = b_nbfs[bi % 2]
            P.barrier()
            ar.off = base
            pmt = ar.f32(2048).rearrange("p (h a e) -> p h a e", h=8, a=16)
            b_pm = Buf()
            csl = [(ar.bf(2048).rearrange("p (h a e) -> p h a e", h=8, a=16), Buf()) for _ in range(2)]
            CT = ar.bf(128 * 128).rearrange("p (e t) -> p e t", e=128)
            b_CT = Buf()
            osl = ar.bf(4096).rearrange("p (h a e) -> p h a e", h=8, a=16)
            b_osl = Buf()
            OTs = [(ar.bf(4096).rearrange("p (e t) -> p e t", e=32), Buf()) for _ in range(2)]
            WTs = [(ar.bf(4096).rearrange("p (e t) -> p e t", e=32), Buf()) for _ in range(2)]
            PSb = [PS[i][:, :].bitcast(BF16) for i in range(8)]
            evc = 0
            for ti in range(2):
                bk = b_keep[ti]
                for es in range(8):
                    tt("pool", pmt, a1t[ti].unsqueeze(3).to_broadcast([128, 8, 16, 16]),
                       a2k[ti][:, :, es * 16:(es + 1) * 16].unsqueeze(2).to_broadcast([128, 8, 16, 16]), ALU.mult, [bk], [b_pm])
                    cs, b_cs = csl[es % 2]
                    for h in range(8):
                        pmh = pmt[:, h].rearrange("p a e -> p (a e)")
                        stt(cs[:, h].rearrange("p a e -> p (a e)"), pmh, theta[ti][:, h:h + 1], pmh, ALU.is_ge, ALU.mult, [b_pm, bk], [b_cs])
                    csf = cs.rearrange("p h a e -> p (h a) e")
                    for half in range(2):
                        pb = 2 + (es * 2 + half) % 2
                        for e in range(8):
                            tr(PSb[pb][:, e * 128:(e + 1) * 128], csf[:, :, half * 8 + e], identb, [b_cs, b_cst], [PB[pb]])
                        e0 = es * 16 + half * 8
                        cp("act" if half else "dve", CT[:, e0:e0 + 8, :], PSb[pb][:, :].rearrange("p (e t) -> p e t", e=8), [PB[pb]], [b_CT])
                for r in range(4):
                    tt("dve", osl, s1k[ti][:, :, r * 32:(r + 1) * 32].unsqueeze(2).to_broadcast([128, 8, 16, 32]),
                       top1k[ti].unsqueeze(3).to_broadcast([128, 8, 16, 32]), ALU.is_equal, [bk], [b_osl])
                    osf = osl.rearrange("p h a e -> p (h a) e")
                    ot, b_ot = OTs[r % 2]
                    for q in range(4):
                        pb = 4 + q % 2
                        for e in range(8):
                            tr(PSb[pb][:, e * 128:(e + 1) * 128], osf[:, :, q * 8 + e], identb, [b_osl, b_cst], [PB[pb]])
                        cp("act" if q % 2 else "dve", ot[:, q * 8:(q + 1) * 8, :], PSb[pb][:, :].rearrange("p (e t) -> p e t", e=8), [PB[pb]], [b_ot])
                    wt, b_wt = WTs[r % 2]
                    for tg in range(8):
                        pb = 6 + tg % 2
                        for tk in range(16):
                            t_ = tg * 16 + tk
                            mm(PS[pb][:, tk * 32:(tk + 1) * 32], CT[:, :, t_], ot[:, :, t_], True, True, [b_CT, b_ot], [PB[pb]])
                        cp("dve" if evc % 2 else "act", wt[:, :, tg * 16:(tg + 1) * 16], PS[pb][:, :].rearrange("p (t e) -> p e t", t=16), [PB[pb]], [b_wt])
                        evc += 1
                    P.dma(wT_s[ti][:, r * 32:(r + 1) * 32, :], wt, reads=[b_wt], writes=[b_wT[ti]])
            P.barrier()
            genA = phaseA(blocks[bi + 1][0], blocks[bi + 1][1], nbfs[(bi + 1) % 2], b_nbfs[(bi + 1) % 2]) if bi + 1 < len(blocks) else None
            for pb in range(4, 8):
                mm(PS[pb][:, :], zer[:, 0:128], zer[:, 0:512], True, False, [b_zer], [PB[pb]])
            pend = None
            for g in range(32):
                u_, b_u = ub[g % 2]
                v_, b_vv = vbf[g % 2]
                w_, b_w_ = Wt[g % 2]
                P.dma(u_, uTb_s[g], reads=[b_ub], writes=[b_u])
                P.dma(v_, vb_s[:, g * 4:(g + 1) * 4, :], reads=[b_vb], writes=[b_vv])
                for ti in range(2):
                    P.dma(w_[:, :, ti * 128:(ti + 1) * 128], wT_s[ti][:, g * 4:(g + 1) * 4, :], reads=[b_wT[ti]], writes=[b_w_])
                for sub in range(2):
                    it = g * 2 + sub
                    ph = it % 2
                    for i in range(2):
                        chn = sub * 2 + i
                        for k in range(8):
                            mm(PS[ph][:, i * 256:(i + 1) * 256], u_[:, k, chn * 128:(chn + 1) * 128], nbf[:, k, :], k == 0, k == 7, [b_u, b_nbf[k]], [PB[ph]])
                    ge, b_ge = gel[it % 2]
                    gt, b_gt = GT[it % 2]
                    act(ge, PS[ph][:, :], AF.Gelu, [PB[ph]], [b_ge])
                    tt("dve", gt.rearrange("p e t -> p (e t)"), ge, w_[:, sub * 2:(sub + 1) * 2, :].rearrange("p e t -> p (e t)"), ALU.mult, [b_ge, b_w_], [b_gt])
                    if pend is not None:
                        pend()

                    def mk(v_=v_, gt=gt, sub=sub, b_vv=b_vv, b_gt=b_gt):
                        def f():
                            for dk in range(8):
                                pbo = 4 + dk // 2
                                for i in range(2):
                                    mm(PS[pbo][:, (dk % 2) * 256:(dk % 2 + 1) * 256], v_[:, sub * 2 + i, dk * 128:(dk + 1) * 128], gt[:, i, :], False, False, [b_vv, b_gt], [PB[pbo]])
                        return f
                    pend = mk()
                if genA is not None:
                    for _ in range(3):
                        next(genA, None)
            pend()
            if genA is not None:
                for _ in genA:
                    pass
            for dk in range(8):
                pbo = 4 + dk // 2
                xb = b_x[dk][t0 // 512]
                stt(xT[:, dk, t0:t0 + 256], PS[pbo][:, (dk % 2) * 256:(dk % 2 + 1) * 256], modv[:, 5, dk, jj:jj + 1], xT[:, dk, t0:t0 + 256], ALU.mult, ALU.add, [PB[pbo], xb, b_sm], [xb])

    for l in range(depth):
        layer(l)

    P.barrier()
    fin = []
    for k in range(8):
        fin.append(P.dma(yT_d[:, k, :], xT[:, k, 0:S], reads=b_x[k]))
    for name, (src, shape) in dbg_out.items():
        pass
    P.emit(list(P.dmas[-8:]))
    es.close()
    return nc

import ml_dtypes
_CONST = {}


def _consts():
    if _CONST:
        return _CONST
    f64 = np.float64
    c = np.zeros((6, 128, 128), f64)
    c[0] = np.eye(128)
    c[1] = 1.0
    c[2, :64, :64] = 1.0
    c[2, 64:, 64:] = 1.0
    for base in range(0, 128, 32):
        for d in range(16):
            c[3, base + d + 16, base + d] = -1.0
            c[3, base + d, base + d + 16] = 1.0
    ci = np.arange(64)
    ang = 2 * np.pi * np.outer(ci, ci) / 64.0
    for b in range(2):
        c[4, b * 64:(b + 1) * 64, b * 64:(b + 1) * 64] = np.cos(ang)
        c[5, b * 64:(b + 1) * 64, b * 64:(b + 1) * 64] = np.sin(ang)
    _CONST["cst"] = np.ascontiguousarray(c.transpose(1, 0, 2)).astype(np.float32)
    t = np.arange(S)
    row = (t // 64).astype(f64)
    col = (t % 64).astype(f64)
    inv = 10000.0 ** (-np.arange(0, 32, 2, dtype=f64) / 32.0)
    d = np.arange(128) % 64
    pos = np.where((d // 32)[:, None] == 0, row[None, :], col[None, :])
    a = pos * inv[d % 16][:, None]
    _CONST["rope"] = np.stack([np.cos(a), np.sin(a)]).astype(np.float32)
    for L in (S, C):
        p = np.arange(L, dtype=f64)
        tt_ = p / max(L - 1, 1)
        w = 2.0 * np.pi * p / L
        fr = np.linspace(1e-4, 15, 16)
        feats = np.concatenate([tt_[:, None], np.cos(w[:, None] * fr), -np.sin(w[:, None] * fr)], axis=-1)
        _CONST["feats%d" % L] = np.ascontiguousarray(feats.T).astype(np.float32)
        deltas = np.abs(np.linspace(math.log(1e-2) / 1.5, math.log(1e-2) / 0.3, 256))
        _CONST["dec%d" % L] = np.exp(-tt_[:, None] * deltas[None, :]).astype(np.float32)
        nj = L // 128
        s_ = np.arange(L)
        kk = np.outer(s_, 2 * s_ + 1) % (4 * L)
        angF = np.pi * kk / (2.0 * L)
        TcF = np.cos(angF)
        TsF = -np.sin(angF)
        tF = np.stack([TcF, TsF]).reshape(2, nj, 128, nj, 128).transpose(0, 3, 2, 1, 4)
        _CONST["tF%d" % L] = np.ascontiguousarray(tF).astype(np.float32).astype(ml_dtypes.bfloat16)
        TB = min(512, L)
        G = min(4, nj)
        nTB = L // TB
        nG = nj // G
        TcI = TcF.T / L
        TsI = TsF.T / L
        def rl(M):
            return M.reshape(nG, G, 128, nTB, TB).transpose(3, 0, 2, 1, 4)
        _CONST["tI%d" % L] = np.ascontiguousarray(np.stack([rl(TcI), rl(TsI)])).astype(np.float32).astype(ml_dtypes.bfloat16)
        k2 = np.outer(s_, s_) % L
        ang2 = 2 * np.pi * k2 / L
        sc_ = 1.0 / math.sqrt(64.0 * L)
        _CONST["tN%d" % L] = np.ascontiguousarray(np.stack([rl(np.cos(ang2) * sc_), rl(-np.sin(ang2) * sc_)])).astype(np.float32).astype(ml_dtypes.bfloat16)
    return _CONST


def _prep(inp):
    f = lambda a: np.ascontiguousarray(np.asarray(a, dtype=np.float32))
    w = {}
    w["w_ada"] = f(inp["w_ada"])
    w["bada"] = f(np.asarray(inp["b_ada"]).reshape(2, 6, 8, 128).transpose(0, 3, 1, 2))
    w["gmf"] = f(np.stack([np.asarray(inp["g_mix"]).reshape(2, 8, 128), np.asarray(inp["g_ffn"]).reshape(2, 8, 128)], axis=1).transpose(0, 3, 1, 2))
    win = np.asarray(inp["w_in"])
    w["w_in"] = f(win)
    w["w_gate"] = f(win[:, :, 2560:].reshape(2, 8, 128, 3, 8, 128).transpose(0, 4, 2, 1, 3, 5))
    hcw = np.concatenate([np.asarray(inp["hy_conv_w"]), np.asarray(inp["hy_conv_b"])[:, None, :]], axis=1)
    w["hcw"] = f(hcw.reshape(2, 4, 6, 128).transpose(0, 3, 2, 1))
    w["hy_w1"] = f(inp["hy_w1"])
    w["hy_fb"] = f(np.stack([inp["hy_b1"], inp["hy_freq"], inp["hy_b2"]], axis=-1))
    w["hy_w2"] = f(inp["hy_w2"])
    w["hy_w3"] = f(inp["hy_w3"])
    w["hy_bias"] = f(np.asarray(inp["hy_bias"]).reshape(2, 2, 2, 128).transpose(0, 3, 1, 2))
    w["gqk"] = f(np.stack([np.asarray(inp["g_q"]).reshape(2, 128), np.asarray(inp["g_k"]).reshape(2, 128)], axis=-1))
    w["lam"] = f(np.asarray(inp["lam"]).reshape(2, 1, 256))
    w["gsub"] = f(np.asarray(inp["g_sub"]).reshape(2, 128, 1))
    w["w_br"] = f(np.concatenate([inp["w_hy"], inp["w_fn"], inp["w_at"]], axis=1))
    w["w_out"] = f(inp["w_out"])
    w["peer_wq"] = f(inp["peer_wq"])
    w["keysT"] = f(np.asarray(inp["peer_keys"]).reshape(2, 16, 128, 128).transpose(0, 3, 1, 2))
    w["uT"] = f(np.asarray(inp["peer_u"]).transpose(0, 2, 1))
    w["peer_v"] = f(inp["peer_v"])
    w.update(_consts())
    return w


def _core_inputs(inp, b):
    X = np.concatenate([np.asarray(inp["x"][b]), np.asarray(inp["ctx"][b])], axis=0)
    xT = np.ascontiguousarray(X.T.reshape(8, 128, T).transpose(1, 0, 2)).astype(np.float32)
    cc = np.stack([np.asarray(inp["c"][b]), np.asarray(inp["c_ctx"])], axis=-1)
    cc = np.ascontiguousarray(cc.reshape(8, 128, 2).transpose(1, 0, 2)).astype(np.float32)
    return {"xT": xT, "cc": cc}


_NC = {}


def kernel(**inp):
    w = _prep(inp)
    if "nc" not in _NC:
        _NC["nc"] = build()
    nc = _NC["nc"]
    in_maps = []
    for b in range(8):
        m = dict(w)
        m.update(_core_inputs(inp, b))
        in_maps.append(m)
    res = run_bass_kernel_spmd(nc, in_maps, core_ids=list(range(8)))
    out = np.empty((8, S, D), np.float32)
    for b in range(8):
        yT = np.asarray(res.results[b]["yT"])
        out[b] = yT.transpose(2, 1, 0).reshape(S, D)
    return out
```

```python
import numpy as np, math
from contextlib import ExitStack
import concourse.bass as bass
import concourse.mybir as mybir
from concourse.bass_utils import run_bass_kernel_spmd

F32 = mybir.dt.float32
BF16 = mybir.dt.bfloat16
ALU = mybir.AluOpType
AF = mybir.ActivationFunctionType
AX = mybir.AxisListType

NSLOT = 40
ENGS = ("pe", "act", "dve", "pool", "sp")


class Buf:
    __slots__ = ("w", "rs", "rd")

    def __init__(self):
        self.w = None
        self.rs = {}
        self.rd = []


class Op:
    __slots__ = ("eng", "fn", "deps", "sig", "cnt", "slot", "dma")

    def __init__(self, eng, fn, dma=False):
        self.eng = eng
        self.fn = fn
        self.deps = ()
        self.sig = False
        self.cnt = 0
        self.slot = -1
        self.dma = dma


class Prog:
    def __init__(self, nc):
        self.nc = nc
        self.streams = {e: [] for e in ENGS}
        self.dmas = []
        self.live_dmas = []
        self.last_real = {e: None for e in ENGS}

    def op(self, eng, fn, reads=(), writes=(), dma=False):
        o = Op(eng, fn, dma)
        deps = set()
        for b in reads:
            if b.w is not None:
                deps.add(b.w)
        for b in writes:
            if b.w is not None:
                deps.add(b.w)
            deps.update(b.rs.values())
            deps.update(b.rd)
        if eng == "pe" and not dma:
            deps = {d for d in deps if d.dma or d.eng != "pe"}
        o.deps = deps
        for b in writes:
            b.w = o
            b.rs = {}
            b.rd = []
        for b in reads:
            if dma:
                b.rd.append(o)
            else:
                b.rs[eng] = o
        self.streams[eng].append(o)
        if not dma:
            self.last_real[eng] = o
        if dma:
            self.dmas.append(o)
            self.live_dmas.append(o)
        return o

    def dma(self, out, in_, reads=(), writes=(), eng="sp"):
        return self.op(eng, lambda e: e.dma_start(out=out, in_=in_), reads, writes, dma=True)

    def barrier(self):
        last = dict(self.last_real)
        live = list(self.live_dmas)
        self.live_dmas = []
        for e in ENGS:
            o = Op(e, None)
            o.deps = {last[x] for x in ENGS if x != e and last[x] is not None}
            o.deps.update(live)
            self.streams[e].append(o)

    def emit(self, final_dmas):
        nc = self.nc
        for e in ENGS:
            for o in self.streams[e]:
                for d in o.deps:
                    d.sig = True
        with ExitStack() as es:
            sems = {e: es.enter_context(nc.semaphore("s_" + e)) for e in ENGS}
            dsem = [es.enter_context(nc.semaphore("d%d" % i)) for i in range(NSLOT)]
            for e in ENGS:
                c = 0
                for o in self.streams[e]:
                    if o.dma:
                        continue
                    if o.sig:
                        c += 1
                        o.cnt = c
            slot_cnt = [0] * NSLOT
            slot_prev = [None] * NSLOT
            for i, o in enumerate(self.dmas):
                s = i % NSLOT
                o.slot = s
                slot_cnt[s] += 16
                o.cnt = slot_cnt[s]
                if slot_prev[s] is not None:
                    o.deps = set(o.deps)
                    o.deps.add(slot_prev[s])
                slot_prev[s] = o
            block = es.enter_context(nc.Block())

            def run(ename, eng):
                waited = {}
                for o in self.streams[ename]:
                    for d in o.deps:
                        sem = dsem[d.slot] if d.dma else sems[d.eng]
                        if waited.get(sem.name, 0) >= d.cnt:
                            continue
                        eng.wait_ge(sem, d.cnt)
                        waited[sem.name] = d.cnt
                    if o.fn is None:
                        continue
                    ins = o.fn(eng)
                    if o.dma:
                        ins.then_inc(dsem[o.slot], 16)
                    elif o.sig:
                        ins.then_inc(sems[ename], 1)
                if ename == "sp":
                    for d in final_dmas:
                        eng.wait_ge(dsem[d.slot], d.cnt)

            @block.sync
            def _(e):
                run("sp", e)

            @block.tensor
            def _(e):
                run("pe", e)

            @block.scalar
            def _(e):
                run("act", e)

            @block.vector
            def _(e):
                run("dve", e)

            @block.gpsimd
            def _(e):
                run("pool", e)

D = 1024
S = 2048
C = 256
T = S + C
EPS = 1e-6
PI = math.pi
NE = 16384


def tblocks(lo, hi, step=512):
    return [(t, min(step, hi - t)) for t in range(lo, hi, step)]


def build(depth=2, stages=None, dbg=()):
    nc = bass.Bass("TRN2", target_bir_lowering=False)
    P = Prog(nc)

    def din(name, shape, dt=F32):
        return nc.dram_tensor(name, list(shape), dt, kind="ExternalInput").ap()

    def dscr(name, shape, dt=F32):
        if name in dbg:
            return nc.dram_tensor(name, list(shape), dt, kind="ExternalOutput").ap()
        return nc.dram_tensor(name, list(shape), dt).ap()

    xT_d = din("xT", [128, 8, T])
    cc_d = din("cc", [128, 8, 2])
    w_ada_d = din("w_ada", [2, D, 6 * D])
    bada_d = din("bada", [2, 128, 6, 8])
    gmf_d = din("gmf", [2, 128, 2, 8])
    w_in_d = din("w_in", [2, D, 5632])
    w_gate_d = din("w_gate", [2, 8, 128, 8, 3, 128])
    hcw_d = din("hcw", [2, 128, 6, 4])
    hy_w1_d = din("hy_w1", [2, 33, 64])
    hy_fb_d = din("hy_fb", [2, 64, 3])
    hy_w2_d = din("hy_w2", [2, 64, 64])
    hy_w3_d = din("hy_w3", [2, 64, 1024])
    hy_bias_d = din("hy_bias", [2, 128, 2, 2])
    gqk_d = din("gqk", [2, 128, 2])
    lam_d = din("lam", [2, 1, 256])
    gsub_d = din("gsub", [2, 128, 1])
    w_br_d = din("w_br", [2, D, D])
    w_out_d = din("w_out", [2, D, D])
    wq_d = din("peer_wq", [2, D, 2048])
    keysT_d = din("keysT", [2, 128, 16, 128])
    uT_d = din("uT", [2, D, NE])
    v_d_in = din("peer_v", [2, NE, D])
    cst_d = din("cst", [128, 6, 128])
    rope_d = din("rope", [2, 128, S])
    feats_d = {L: din("feats%d" % L, [33, L]) for L in (S, C)}
    dec_d = {L: din("dec%d" % L, [L, 256]) for L in (S, C)}
    tF_d = {L: din("tF%d" % L, [2, L // 128, 128, L // 128, 128], BF16) for L in (S, C)}
    RL = {}
    for L in (S, C):
        TB = min(512, L)
        G = min(4, L // 128)
        RL[L] = (TB, G, L // TB, (L // 128) // G)
    tI_d = {L: din("tI%d" % L, [2, RL[L][2], RL[L][3], 128, RL[L][1], RL[L][0]], BF16) for L in (S, C)}
    tN_d = {L: din("tN%d" % L, [2, RL[L][2], RL[L][3], 128, RL[L][1], RL[L][0]], BF16) for L in (S, C)}
    yT_d = nc.dram_tensor("yT", [128, 8, S], F32, kind="ExternalOutput").ap()
    hy_s = dscr("hy_s", [6, 128, T])
    fn_s = dscr("fn_s", [2, 128, T])
    q_s = dscr("q_s", [4, 128, T], BF16)
    k_s = dscr("k_s", [4, 128, T], BF16)
    v_s = dscr("v_s", [128, 18, 512], BF16)
    kf_s = {L: dscr("kf_s%d" % L, [2, L // 128, 128, 512]) for L in (S, C)}
    hyo_s = dscr("hyo_s", [2, 128, T], BF16)
    fno_s = dscr("fno_s", [2, 128, T], BF16)
    ato_s = dscr("ato_s", [4, 128, T], BF16)
    nT_s = dscr("nT_s", [128, 8, T], BF16)
    uTb_s = dscr("uTb_s", [32, 128, 8, 512], BF16)
    vb_s = dscr("vb_s", [128, 128, D], BF16)
    wT_s = dscr("wT_s", [2, 128, 128, 128], BF16)
    dbg_out = {}

    es = ExitStack()
    xT = es.enter_context(nc.sbuf_tensor("xT_sb", [128, 8, T], F32))
    cst = es.enter_context(nc.sbuf_tensor("cst_sb", [128, 6, 128], F32))
    cstb = es.enter_context(nc.sbuf_tensor("cstb_sb", [128, 2, 128], BF16))
    sm = es.enter_context(nc.sbuf_tensor("small_sb", [128, 512], F32))
    ARENA = 33280
    AR = es.enter_context(nc.sbuf_tensor("arena", [128, ARENA], F32))
    PS = [es.enter_context(nc.psum_tensor("ps%d" % i, [128, 512], F32)) for i in range(8)]
    PB = [Buf() for _ in range(8)]
    ident32 = cst[:, 0, :]
    ones32 = cst[:, 1, :]
    bd64 = cst[:, 2, :]
    rot = cst[:, 3, :]
    bdc = cst[:, 4, :]
    bds = cst[:, 5, :]
    identb = cstb[:, 0, :]
    onesb = cstb[:, 1, :]
    b_cst = Buf()
    b_x = [[Buf() for _ in range(5)] for _ in range(8)]
    b_sm = Buf()
    sc = sm[:, 0:16].rearrange("p (k j) -> p k j", k=8)
    modv = sm[:, 16:112].rearrange("p (g m j) -> p g m j", g=6, m=8)
    A_mix = sm[:, 112:128].rearrange("p (k j) -> p k j", k=8)
    A_ffn = sm[:, 128:144].rearrange("p (k j) -> p k j", k=8)
    gmf = sm[:, 144:160].rearrange("p (a k) -> p a k", a=2)
    tmp16 = sm[:, 160:176].rearrange("p (k j) -> p k j", k=8)
    gqk = sm[:, 176:178]
    gsub_s = sm[:, 178:179]
    neg_lam = sm[:, 179:180]
    lamw = sm[:, 180:182]
    hcw = sm[:, 184:208].rearrange("p (c t) -> p c t", c=6)
    hbias = sm[:, 208:212].rearrange("p (o c) -> p o c", o=2)
    fb = sm[:, 212:217]
    bada = sm[:, 224:272].rearrange("p (g m) -> p g m", g=6)
    lamt = sm[:, 272:400]
    epsc = sm[:, 183:184]

    class Arena:
        def __init__(self):
            self.off = 0

        def reset(self):
            P.barrier()
            self.off = 0

        def f32(self, n):
            a = AR[:, self.off:self.off + n]
            self.off += n
            assert self.off <= ARENA, self.off
            return a

        def bf(self, n):
            w = (n + 1) // 2
            return self.f32(w).bitcast(BF16)[:, 0:n]

    ar = Arena()

    def mm(out, lhsT, rhs, start, stop, rd, wr):
        P.op("pe", lambda e: e.matmul(out, lhsT=lhsT, rhs=rhs, start=start, stop=stop), rd, wr)

    def tr(out, in_, idn, rd, wr):
        P.op("pe", lambda e: e.transpose(out=out, in_=in_, identity=idn), rd, wr)

    def act(out, in_, func, rd, wr, bias=0.0, scale=1.0):
        P.op("act", lambda e: e.activation(out=out, in_=in_, func=func, bias=bias, scale=scale), rd, wr)

    def tt(eng, out, in0, in1, op, rd, wr):
        P.op(eng, lambda e: e.tensor_tensor(out=out, in0=in0, in1=in1, op=op), rd, wr)

    def ts(eng, out, in0, s1, s2, op0, op1, rd, wr):
        if s2 is None:
            P.op(eng, lambda e: e.tensor_scalar(out=out, in0=in0, scalar1=s1, scalar2=None, op0=op0), rd, wr)
        else:
            P.op(eng, lambda e: e.tensor_scalar(out=out, in0=in0, scalar1=s1, scalar2=s2, op0=op0, op1=op1), rd, wr)

    def stt(out, in0, scalar, in1, op0, op1, rd, wr):
        P.op("dve", lambda e: e.scalar_tensor_tensor(out=out, in0=in0, scalar=scalar, in1=in1, op0=op0, op1=op1), rd, wr)

    def cp(eng, out, in_, rd, wr):
        if eng == "act":
            act(out, in_, AF.Copy, rd, wr)
        else:
            P.op(eng, lambda e: e.tensor_copy(out=out, in_=in_), rd, wr)

    def recip(out, in_, rd, wr):
        P.op("dve", lambda e: e.reciprocal(out=out, in_=in_), rd, wr)

    def rsqrt_mean(out, in_, n, rd, wr):
        act(out, in_, AF.Sqrt, list(rd) + [b_sm], wr, bias=epsc, scale=1.0 / n)
        recip(out, out, wr, wr)

    P.dma(cst[:, :, :], cst_d, writes=[b_cst])
    for k in range(8):
        P.dma(xT[:, k, :], xT_d[:, k, :], writes=b_x[k])
    P.dma(sc, cc_d, writes=[b_sm])
    cp("dve", cstb[:, 0, :], ident32, [b_cst], [b_cst])
    cp("dve", cstb[:, 1, :], ones32, [b_cst], [b_cst])
    act(sc, sc, AF.Silu, [b_sm], [b_sm])
    P.op("dve", lambda e: e.memset(epsc, EPS), [], [b_sm])
    ar.reset()

    def want(name):
        return stages is None or name in stages

    def layer(l):
        last = l == depth - 1
        lam_init = 0.8 - 0.6 * math.exp(-0.3 * l)
        streams = [(0, S)] if last else [(0, S), (S, C)]
        tb_all = tblocks(0, T)
        tb_mix = tblocks(0, S) if last else tb_all

        ar.reset()
        P.dma(bada, bada_d[l], writes=[b_sm])
        P.dma(gmf, gmf_d[l], writes=[b_sm])
        P.dma(gqk, gqk_d[l], writes=[b_sm])
        P.dma(gsub_s, gsub_d[l], writes=[b_sm])
        P.dma(hcw, hcw_d[l], writes=[b_sm])
        P.dma(hbias, hy_bias_d[l], writes=[b_sm])
        P.dma(fb[0:64, 0:3], hy_fb_d[l], writes=[b_sm])
        P.dma(lamt, lam_d[l][:, 0:128].partition_broadcast(128), writes=[b_sm])
        lam2 = ar.f32(128)
        b_l2 = Buf()
        P.dma(lam2, lam_d[l][:, 128:256].partition_broadcast(128), writes=[b_l2])
        wts = [(ar.f32(8 * 1024), Buf()) for _ in range(2)]
        for g in range(6):
            wt, bw = wts[g % 2]
            wt3 = wt.rearrange("p (k m) -> p k m", k=8)
            P.dma(wt3, w_ada_d[l][:, g * 1024:(g + 1) * 1024].rearrange("(k p) m -> p k m", p=128), writes=[bw])
            for m in range(8):
                for k in range(8):
                    mm(PS[0][:, m * 2:m * 2 + 2], wt3[:, k, m * 128:(m + 1) * 128], sc[:, k, :], k == 0, k == 7, [bw, b_sm], [PB[0]])
            tt("dve", modv[:, g], PS[0][:, 0:16].rearrange("p (m j) -> p m j", m=8),
               bada[:, g, :].unsqueeze(2).to_broadcast([128, 8, 2]), ALU.add, [PB[0], b_sm], [b_sm])
        for (Aap, gi, si) in ((A_mix, 0, 1), (A_ffn, 1, 4)):
            ts("dve", tmp16, modv[:, si], 1.0, None, ALU.add, None, [b_sm], [b_sm])
            tt("dve", Aap, tmp16, gmf[:, gi, :].unsqueeze(2).to_broadcast([128, 8, 2]), ALU.mult, [b_sm], [b_sm])
        tt("dve", lamt[:, 0:64], lamt[:, 0:64], lamt[:, 64:128], ALU.mult, [b_sm], [b_sm])
        tt("dve", lam2[:, 0:64], lam2[:, 0:64], lam2[:, 64:128], ALU.mult, [b_l2], [b_l2])
        P.op("dve", lambda e: e.reduce_sum(out=lamw[:, 0:1], in_=lamt[:, 0:64], axis=AX.X), [b_sm], [b_sm])
        P.op("dve", lambda e: e.reduce_sum(out=lamw[:, 1:2], in_=lam2[:, 0:64], axis=AX.X), [b_l2, b_sm], [b_sm])
        act(lamw, lamw, AF.Exp, [b_sm], [b_sm])
        tt("dve", neg_lam, lamw[:, 1:2], lamw[:, 0:1], ALU.subtract, [b_sm], [b_sm])
        ts("dve", neg_lam, neg_lam, -lam_init, None, ALU.add, None, [b_sm], [b_sm])
        ts("dve", gsub_s, gsub_s, 1.0 - lam_init, None, ALU.mult, None, [b_sm], [b_sm])
        tt("dve", fb[0:64, 3:4], fb[0:64, 0:1], fb[0:64, 1:2], ALU.mult, [b_sm], [b_sm])
        tt("dve", fb[0:64, 4:5], fb[0:64, 2:3], fb[0:64, 1:2], ALU.mult, [b_sm], [b_sm])

        def mod_tmps(w, n=(3, 2, 3)):
            return dict(sq=[(ar.f32(w), Buf()) for _ in range(n[0])], rs=[(ar.f32(w), Buf()) for _ in range(n[1])],
                        tm=[(ar.f32(w), Buf()) for _ in range(n[2])], c=[0, 0, 0])

        def modulate(Aap, gB, blocks, emit_cb, mt, pbk=7):
            for (t0, Tn) in blocks:
                j = 0 if t0 < S else 1
                xb = t0 // 512
                for k in range(8):
                    sq, b_sq = mt["sq"][mt["c"][0] % len(mt["sq"])]
                    mt["c"][0] += 1
                    act(sq[:, :Tn], xT[:, k, t0:t0 + Tn], AF.Square, [b_x[k][xb]], [b_sq])
                    mm(PS[pbk][:, :Tn], ones32, sq[:, :Tn], k == 0, k == 7, [b_sq, b_cst], [PB[pbk]])
                r, b_r = mt["rs"][mt["c"][1] % len(mt["rs"])]
                mt["c"][1] += 1
                rsqrt_mean(r[:, :Tn], PS[pbk][:, :Tn], D, [PB[pbk]], [b_r])
                for k in range(8):
                    tm, b_t = mt["tm"][mt["c"][2] % len(mt["tm"])]
                    mt["c"][2] += 1
                    tt("dve", tm[:, :Tn], xT[:, k, t0:t0 + Tn], r[:, :Tn], ALU.mult, [b_x[k][xb], b_r], [b_t])
                    emit_cb(k, t0, Tn, tm, b_t, Aap[:, k, j:j + 1], modv[:, gB, k, j:j + 1])

        def hyena_filters(L):
            ar.reset()
            nj = L // 128
            w1 = ar.f32(64)
            w2 = ar.f32(64)
            w3 = ar.f32(1024)
            b_w = Buf()
            P.dma(w1[0:33, :], hy_w1_d[l], writes=[b_w])
            P.dma(w2[0:64, :], hy_w2_d[l], writes=[b_w])
            P.dma(w3[0:64, :], hy_w3_d[l], writes=[b_w])
            h2T = ar.f32(L)
            b_h2 = Buf()
            off_mlp = ar.off
            ft = [(ar.f32(512), Buf()) for _ in range(2)]
            aa = [(ar.f32(512), Buf()) for _ in range(2)]
            h1 = [(ar.f32(512), Buf()) for _ in range(2)]
            aa2 = [(ar.f32(512), Buf()) for _ in range(2)]
            sx = [(ar.f32(512), Buf()) for _ in range(3)]

            def sin_act(out, a, Tn, b_in, b_out):
                (s4, b_s4), (c4, b_c4), (q, b_q) = sx
                act(s4[0:64, :Tn], a, AF.Sin, [b_in], [b_s4], scale=0.25)
                act(c4[0:64, :Tn], a, AF.Abs, [b_in], [b_c4])
                act(c4[0:64, :Tn], c4[0:64, :Tn], AF.Sin, [b_c4, b_sm], [b_c4], bias=halfpi[0:64, :], scale=-0.25)
                tt("dve", q[0:64, :Tn], s4[0:64, :Tn], s4[0:64, :Tn], ALU.mult, [b_s4], [b_q])
                ts("dve", q[0:64, :Tn], q[0:64, :Tn], -2.0, 1.0, ALU.mult, ALU.add, [b_q], [b_q])
                tt("dve", c4[0:64, :Tn], s4[0:64, :Tn], c4[0:64, :Tn], ALU.mult, [b_s4, b_c4], [b_c4])
                stt(out, c4[0:64, :Tn], 4.0, q[0:64, :Tn], ALU.mult, ALU.mult, [b_c4, b_q], [b_out])

            for bi, (t0, Tn) in enumerate(tblocks(0, L)):
                f, b_f = ft[bi % 2]
                P.dma(f[0:33, :Tn], feats_d[L][:, t0:t0 + Tn], writes=[b_f])
                mm(PS[0][0:64, :Tn], w1[0:33, :], f[0:33, :Tn], True, True, [b_w, b_f], [PB[0]])
                a, b_a = aa[bi % 2]
                ts("dve", a[0:64, :Tn], PS[0][0:64, :Tn], fb[0:64, 1:2], fb[0:64, 3:4], ALU.mult, ALU.add, [PB[0], b_sm], [b_a])
                hh, b_h = h1[bi % 2]
                sin_act(hh[0:64, :Tn], a[0:64, :Tn], Tn, b_a, b_h)
                mm(PS[1][0:64, :Tn], w2[0:64, :], hh[0:64, :Tn], True, True, [b_w, b_h], [PB[1]])
                a2_, b_a2 = aa2[bi % 2]
                ts("dve", a2_[0:64, :Tn], PS[1][0:64, :Tn], fb[0:64, 1:2], fb[0:64, 4:5], ALU.mult, ALU.add, [PB[1], b_sm], [b_a2])
                sin_act(h2T[0:64, t0:t0 + Tn], a2_[0:64, :Tn], Tn, b_a2, b_h2)
            dec = ar.f32(nj * 256).rearrange("p (j c) -> p j c", j=nj)
            off_dec_end = ar.off
            b_dec = Buf()
            P.dma(dec, dec_d[L].rearrange("(j p) c -> p j c", p=128), writes=[b_dec])
            hd = ar.f32(nj * 1024).rearrange("p (j c) -> p j c", j=nj)
            b_hd = [Buf() for _ in range(nj)]
            off_hd = ar.off
            for pc in range(nj):
                for half in range(2):
                    pb = half
                    mm(PS[pb][:, :], h2T[0:64, pc * 128:(pc + 1) * 128], w3[0:64, half * 512:(half + 1) * 512], True, True, [b_h2, b_w], [PB[pb]])
                    tt("dve", hd[:, pc, half * 512:(half + 1) * 512].rearrange("p (o c) -> p o c", o=2),
                       PS[pb][:, :].rearrange("p (o c) -> p o c", o=2),
                       dec[:, pc, :].unsqueeze(1).to_broadcast([128, 2, 256]), ALU.mult, [PB[pb], b_dec], [b_hd[pc]])
            P.op("dve", lambda e: e.memset(hd[0:1, 0, 512:1024], 0.0), [], [b_hd[0]])
            ab = [(ar.f32(1024), Buf()) for _ in range(2)]
            for pc in range(nj):
                a, b_a = ab[pc % 2]
                act(a, hd[:, pc, :], AF.Abs, [b_hd[pc]], [b_a])
                for half in range(2):
                    mm(PS[2 + half][:, :], ones32, a[:, half * 512:(half + 1) * 512], pc == 0, pc == nj - 1, [b_a, b_cst], [PB[2 + half]])
            rn = ar.f32(512)
            b_rn = Buf()
            cp("dve", rn, PS[2][:, :], [PB[2]], [b_rn])
            tt("dve", rn, rn, PS[3][:, :], ALU.add, [b_rn, PB[3]], [b_rn])
            recip(rn, rn, [b_rn], [b_rn])
            tmpe = [(ar.f32(512), Buf()) for _ in range(2)]
            P.barrier()
            ar.off = 0
            hdb = ar.bf(nj * 1024).rearrange("p (j c) -> p j c", j=nj)
            b_hdb = [Buf() for _ in range(nj)]
            tabs = [(ar.bf(2 * nj * 128).rearrange("p (c j f) -> p c j f", c=2, j=nj), Buf()) for _ in range(2)]
            assert ar.off <= off_dec_end
            ar.off = off_hd
            sts = [(ar.f32(1024).rearrange("p (c n) -> p c n", c=2), Buf()) for _ in range(1)]
            for pc in range(nj):
                te, b_te = tmpe[pc % 2]
                hf = hd[:, pc, 0:512]
                hb = hd[:, pc, 512:1024]
                tt("dve", te, hf, hb, ALU.add, [b_hd[pc]], [b_te])
                tt("pool", hb, hf, hb, ALU.subtract, [b_hd[pc]], [b_hd[pc]])
                tt("dve", hdb[:, pc, 0:512], te, rn, ALU.mult, [b_te, b_rn], [b_hdb[pc]])
                tt("pool", hdb[:, pc, 512:1024], hb, rn, ALU.mult, [b_hd[pc], b_rn], [b_hdb[pc]])
            for fc in range(nj):
                tb_, b_tb = tabs[fc % 2]
                for c in range(2):
                    P.dma(tb_[:, c], tF_d[L][c, fc], writes=[b_tb])
                for c in range(2):
                    for jc in range(nj):
                        mm(PS[4 + c][:, :], tb_[:, c, jc, :], hdb[:, jc, c * 512:(c + 1) * 512], jc == 0, jc == nj - 1, [b_tb, b_hdb[jc]], [PB[4 + c]])
                st, b_st = sts[0]
                cp("act", st[:, 0, :], PS[4][:, :], [PB[4]], [b_st])
                cp("dve", st[:, 1, :], PS[5][:, :], [PB[5]], [b_st])
                for c in range(2):
                    P.dma(kf_s[L][c, fc], st[:, c, :], reads=[b_st], writes=[b_kf[L]])

        b_kf = {S: Buf(), C: Buf()}
        halfpi = sm[:, 182:183]
        P.op("dve", lambda e: e.memset(halfpi, PI / 2), [], [b_sm])
        if want("hyf"):
            for (t0, L) in streams:
                hyena_filters(L)

        ar.reset()
        nT = ar.bf(8 * T).rearrange("p (k t) -> p k t", k=8)
        b_n = [[Buf() for _ in range(5)] for _ in range(8)]
        b_nTs = Buf()

        def emit_mix(k, t0, Tn, tm, b_t, Asc, Bsc):
            act(nT[:, k, t0:t0 + Tn], tm[:, :Tn], AF.Identity, [b_t, b_sm], [b_n[k][t0 // 512]], bias=Bsc, scale=Asc)

        mark = ar.off
        if want("proj") or want("merge"):
            modulate(A_mix, 0, tb_all, emit_mix, mod_tmps(512))
            for k in range(8):
                P.dma(nT_s[:, k, :], nT[:, k, :], reads=b_n[k], writes=[b_nTs])
        ar.off = mark
        P.barrier()

        b_hy = Buf()
        b_fn = Buf()
        b_q = Buf()
        b_k = Buf()
        b_v = Buf()
        if want("proj"):
            w32 = [(ar.f32(8 * 512).rearrange("p (k m) -> p k m", k=8), Buf()) for _ in range(1)]
            wbf = [(ar.bf(8 * 512).rearrange("p (k m) -> p k m", k=8), Buf()) for _ in range(2)]
            stg = [(ar.f32(512), Buf()) for _ in range(2)]
            sqt = [(ar.f32(512), Buf()) for _ in range(2)]
            rt = [(ar.f32(512), Buf()) for _ in range(2)]
            xnt = [(ar.f32(512), Buf()) for _ in range(2)]
            t1t = [(ar.f32(512), Buf()) for _ in range(2)]
            t2t = [(ar.f32(512), Buf()) for _ in range(2)]
            obt = [(ar.bf(512), Buf()) for _ in range(2)]
            rope_sb = ar.f32(2 * S).rearrange("p (c t) -> p c t", c=2)
            b_rope = Buf()
            for c in range(2):
                P.dma(rope_sb[:, c, :], rope_d[c], writes=[b_rope])
            cnt = [0]

            def qk_cb(ps, pb, which, h, t0, Tn):
                i = cnt[0] % 2
                cnt[0] += 1
                sq_, b_sq_ = sqt[i]
                act(sq_[:, :Tn], ps[:, :Tn], AF.Square, [pb], [b_sq_])
                mm(PS[6][:, :Tn], bd64, sq_[:, :Tn], True, True, [b_sq_, b_cst], [PB[6]])
                r_, b_r_ = rt[i]
                rsqrt_mean(r_[:, :Tn], PS[6][:, :Tn], 64, [PB[6]], [b_r_])
                xn, b_xn = xnt[i]
                stt(xn[:, :Tn], ps[:, :Tn], gqk[:, which:which + 1], r_[:, :Tn], ALU.mult, ALU.mult, [pb, b_r_, b_sm], [b_xn])
                ob, b_ob = obt[i]
                if t0 < S:
                    mm(PS[5][:, :Tn], rot, xn[:, :Tn], True, True, [b_xn, b_cst], [PB[5]])
                    t1, b_t1 = t1t[i]
                    t2, b_t2 = t2t[i]
                    tt("pool", t1[:, :Tn], xn[:, :Tn], rope_sb[:, 0, t0:t0 + Tn], ALU.mult, [b_xn, b_rope], [b_t1])
                    tt("dve", t2[:, :Tn], PS[5][:, :Tn], rope_sb[:, 1, t0:t0 + Tn], ALU.mult, [PB[5], b_rope], [b_t2])
                    tt("pool", ob[:, :Tn], t1[:, :Tn], t2[:, :Tn], ALU.add, [b_t1, b_t2], [b_ob])
                else:
                    cp("pool", ob[:, :Tn], xn[:, :Tn], [b_xn], [b_ob])
                dst = (q_s if which == 0 else k_s)[h][:, t0:t0 + Tn]
                P.dma(dst, ob[:, :Tn], reads=[b_ob], writes=[b_q if which == 0 else b_k])

            pcnt = [0]
            qk_pend = [None]
            for g in range(5):
                if g == 4 and qk_pend[0] is not None:
                    qk_cb(*qk_pend[0])
                    qk_pend[0] = None
                w3_, b_w3 = w32[0]
                P.dma(w3_, w_in_d[l][:, g * 512:(g + 1) * 512].rearrange("(k p) m -> p k m", p=128), writes=[b_w3])
                wb_, b_wb = wbf[g % 2]
                cp("act", wb_[:, 0:4, :], w3_[:, 0:4, :], [b_w3], [b_wb])
                cp("pool", wb_[:, 4:8, :], w3_[:, 4:8, :], [b_w3], [b_wb])
                if g < 4:
                    for mi in range(4):
                        for (t0, Tn) in tb_all:
                            if g == 2 and t0 >= S and last:
                                continue
                            pi = pcnt[0] % 4
                            pcnt[0] += 1
                            for k in range(8):
                                mm(PS[pi][:, :Tn], wb_[:, k, mi * 128:(mi + 1) * 128], nT[:, k, t0:t0 + Tn], k == 0, k == 7,
                                   [b_wb, b_n[k][t0 // 512]], [PB[pi]])
                            if g < 2:
                                st, b_st = stg[pcnt[0] % 2]
                                cp("act" if pcnt[0] % 2 else "dve", st[:, :Tn], PS[pi][:, :Tn], [PB[pi]], [b_st])
                                ch = g * 4 + mi
                                if ch < 6:
                                    P.dma(hy_s[ch][:, t0:t0 + Tn], st[:, :Tn], reads=[b_st], writes=[b_hy])
                                else:
                                    P.dma(fn_s[ch - 6][:, t0:t0 + Tn], st[:, :Tn], reads=[b_st], writes=[b_fn])
                            else:
                                if qk_pend[0] is not None:
                                    qk_cb(*qk_pend[0])
                                qk_pend[0] = (PS[pi], PB[pi], g - 2, mi, t0, Tn)
                else:
                    for i in range(18):
                        pi = pcnt[0] % 4
                        pcnt[0] += 1
                        for k in range(8):
                            mm(PS[pi][:, :], nT[:, k, i * 128:(i + 1) * 128], wb_[:, k, :], k == 0, k == 7, [b_wb, b_n[k][i // 4]], [PB[pi]])
                        ob, b_ob = obt[i % 2]
                        cp("act" if i % 2 else "dve", ob, PS[pi][:, :], [PB[pi]], [b_ob])
                        P.dma(v_s[:, i, :], ob, reads=[b_ob], writes=[b_v])

        b_ato = Buf()
        if want("attn"):
            ar.reset()
            kT2 = [ar.bf(4 * T).rearrange("p (h t) -> p h t", h=4) for _ in range(2)]
            qT = ar.bf(4 * T).rearrange("p (h t) -> p h t", h=4)
            vv = ar.bf(18 * 512).rearrange("p (j c) -> p j c", j=18)
            b_kT = Buf()
            b_qT = Buf()
            b_vv = Buf()
            for h in range(4):
                for c in range(2):
                    P.dma(kT2[c][:, h, :], k_s[h], reads=[b_k], writes=[b_kT])
                P.dma(qT[:, h, :], q_s[h], reads=[b_q], writes=[b_qT])
            P.op("pool", lambda e: e.memset(kT2[0][64:128, :, :], 0.0), [], [b_kT])
            P.op("pool", lambda e: e.memset(kT2[1][0:64, :, :], 0.0), [], [b_kT])
            P.dma(vv, v_s, reads=[b_v], writes=[b_vv])
            Et = [(ar.bf(512), Buf()) for _ in range(4)]
            r0 = ar.f32(512)
            t0_ = ar.f32(512)
            t1_ = ar.f32(512)
            sq_ = ar.f32(512)
            rr_ = ar.f32(512)
            b_r0, b_t0, b_t1, b_sq2, b_rr = [Buf() for _ in range(5)]
            aob = [(ar.bf(512), Buf()) for _ in range(2)]
            ec = 0
            oc = 0
            qblocks = tblocks(0, S) if last else tb_all
            for h in range(4):
                for (q0, Tn) in qblocks:
                    keys = list(range(18)) if q0 < S else [16, 17]
                    seq = [(c, idx, j) for c in range(2) for idx, j in enumerate(keys)]

                    SB3 = (0, 1, 7)

                    def s_mm(n_):
                        c, idx, j = seq[n_]
                        sp = SB3[(ec + n_) % 3]
                        mm(PS[sp][:, :Tn], kT2[c][:, h, j * 128:(j + 1) * 128], qT[:, h, q0:q0 + Tn],
                           True, True, [b_kT, b_qT], [PB[sp]])

                    s_mm(0)
                    if len(seq) > 1:
                        s_mm(1)
                    for n_, (c, idx, j) in enumerate(seq):
                        if n_ + 2 < len(seq):
                            s_mm(n_ + 2)
                        sp = SB3[(ec + n_) % 3]
                        E, b_E = Et[(ec + n_) % 4]
                        act(E[:, :Tn], PS[sp][:, :Tn], AF.Exp, [PB[sp]], [b_E], scale=0.125)
                        mm(PS[2 + 2 * c][:, :Tn], vv[:, j, h * 128:(h + 1) * 128], E[:, :Tn], idx == 0, idx == len(keys) - 1, [b_vv, b_E], [PB[2 + 2 * c]])
                        mm(PS[3 + 2 * c][:, :Tn], onesb, E[:, :Tn], idx == 0, idx == len(keys) - 1, [b_cst, b_E], [PB[3 + 2 * c]])
                    ec += len(seq)
                    recip(r0[:, :Tn], PS[3][:, :Tn], [PB[3]], [b_r0])
                    tt("dve", t0_[:, :Tn], PS[2][:, :Tn], r0[:, :Tn], ALU.mult, [PB[2], b_r0], [b_t0])
                    recip(r0[:, :Tn], PS[5][:, :Tn], [PB[5]], [b_r0])
                    tt("dve", t1_[:, :Tn], PS[4][:, :Tn], r0[:, :Tn], ALU.mult, [PB[4], b_r0], [b_t1])
                    stt(t0_[:, :Tn], t1_[:, :Tn], neg_lam, t0_[:, :Tn], ALU.mult, ALU.add, [b_t1, b_t0, b_sm], [b_t0])
                    act(sq_[:, :Tn], t0_[:, :Tn], AF.Square, [b_t0], [b_sq2])
                    mm(PS[6][:, :Tn], ones32, sq_[:, :Tn], True, True, [b_sq2, b_cst], [PB[6]])
                    rsqrt_mean(rr_[:, :Tn], PS[6][:, :Tn], 128, [PB[6]], [b_rr])
                    ao, b_ao = aob[oc % 2]
                    oc += 1
                    stt(ao[:, :Tn], t0_[:, :Tn], gsub_s, rr_[:, :Tn], ALU.mult, ALU.mult, [b_t0, b_rr, b_sm], [b_ao])
                    P.dma(ato_s[h][:, q0:q0 + Tn], ao[:, :Tn], reads=[b_ao], writes=[b_ato])

        b_hyo = Buf()

        def hyena_main(t0, L):
            TB, G, nTB, nG = RL[L]
            nj = L // 128
            for cc in range(2):
                ar.reset()
                raw = ar.f32(L)
                b_raw = Buf()
                us = [ar.f32(L) for _ in range(3)]
                b_us = [Buf() for _ in range(3)]
                for jx in range(3):
                    ch = jx * 2 + cc
                    P.dma(raw, hy_s[ch][:, t0:t0 + L], reads=[b_hy], writes=[b_raw])
                    u = us[jx]
                    ts("dve", u, raw, hcw[:, ch, 1:2], hcw[:, ch, 3:4], ALU.mult, ALU.add, [b_raw, b_sm], [b_us[jx]])
                    stt(u[:, 1:L], raw[:, 0:L - 1], hcw[:, ch, 0:1], u[:, 1:L], ALU.mult, ALU.add, [b_raw, b_us[jx], b_sm], [b_us[jx]])
                    stt(u[:, 0:L - 1], raw[:, 1:L], hcw[:, ch, 2:3], u[:, 0:L - 1], ALU.mult, ALU.add, [b_raw, b_us[jx], b_sm], [b_us[jx]])
                zbuf = raw
                b_zb = b_raw
                ztm = ar.bf(L).rearrange("p (j c) -> p j c", j=nj)
                b_ztm = Buf()
                Z = ar.f32(2 * L).rearrange("p (c j n) -> p c j n", c=2, j=nj)
                b_Z = Buf()
                Kf = ar.f32(2 * L).rearrange("p (c j n) -> p c j n", c=2, j=nj)
                b_Kf = Buf()
                X = ar.bf(2 * L).rearrange("p (c j n) -> p c j n", c=2, j=nj)
                b_X = Buf()
                tmpx = ar.f32(L).rearrange("p (j n) -> p j n", j=nj)
                b_tx = Buf()
                tmpy = ar.f32(L).rearrange("p (j n) -> p j n", j=nj)
                b_ty = Buf()
                tabs = [(ar.bf(2048), Buf()) for _ in range(4)]
                tcnt = 0
                z, b_z = us[0], b_us[0]
                for o in range(2):
                    for scn in range(nj):
                        tr(PS[0][:, (scn % 4) * 128:(scn % 4 + 1) * 128], z[:, scn * 128:(scn + 1) * 128], ident32, [b_z, b_cst], [PB[0]])
                        if scn % 4 == 3 or scn == nj - 1:
                            n4 = scn % 4 + 1
                            cp("act", ztm[:, scn - n4 + 1:scn + 1, :], PS[0][:, 0:n4 * 128].rearrange("p (j c) -> p j c", j=n4), [PB[0]], [b_ztm])
                    for c in range(2):
                        P.dma(Kf[:, c], kf_s[L][c][:, :, o * 256 + cc * 128:o * 256 + cc * 128 + 128].rearrange("j p n -> p j n"), reads=[b_kf[L]], writes=[b_Kf])
                    for fc in range(nj):
                        tbc, b_tbc = tabs[tcnt % 4]
                        tbs, b_tbs = tabs[(tcnt + 1) % 4]
                        tcnt += 2
                        tbc3 = tbc[:, 0:nj * 128].rearrange("p (j f) -> p j f", j=nj)
                        tbs3 = tbs[:, 0:nj * 128].rearrange("p (j f) -> p j f", j=nj)
                        P.dma(tbc3, tF_d[L][0, fc], writes=[b_tbc])
                        P.dma(tbs3, tF_d[L][1, fc], writes=[b_tbs])
                        pz = 1 + fc % 2
                        for sc_ in range(nj):
                            mm(PS[pz][:, 0:128], tbc3[:, sc_, :], ztm[:, sc_, :], sc_ == 0, sc_ == nj - 1, [b_tbc, b_ztm], [PB[pz]])
                        for sc_ in range(nj):
                            mm(PS[pz][:, 128:256], tbs3[:, sc_, :], ztm[:, sc_, :], sc_ == 0, sc_ == nj - 1, [b_tbs, b_ztm], [PB[pz]])
                        cp("act", Z[:, :, fc, :], PS[pz][:, 0:256].rearrange("p (c n) -> p c n", c=2), [PB[pz]], [b_Z])
                    Zr, Zi, Kr, Ki = Z[:, 0], Z[:, 1], Kf[:, 0], Kf[:, 1]
                    tt("dve", tmpx, Zr, Kr, ALU.mult, [b_Z, b_Kf], [b_tx])
                    tt("pool", tmpy, Zi, Ki, ALU.mult, [b_Z, b_Kf], [b_ty])
                    tt("dve", X[:, 0], tmpx, tmpy, ALU.subtract, [b_tx, b_ty], [b_X])
                    tt("pool", tmpy, Zr, Ki, ALU.mult, [b_Z, b_Kf, b_X], [b_ty])
                    tt("dve", tmpx, Zi, Kr, ALU.mult, [b_Z, b_Kf, b_X], [b_tx])
                    tt("dve", X[:, 1], tmpx, tmpy, ALU.add, [b_tx, b_ty], [b_X])
                    gate, b_g = us[1 + o], b_us[1 + o]
                    znew, b_zn = (zbuf, b_zb) if o == 0 else (us[0], b_us[0])
                    for tb in range(nTB):
                        py = 3 + tb % 2
                        for g in range(nG):
                            tbc, b_tbc = tabs[tcnt % 4]
                            tbs, b_tbs = tabs[(tcnt + 1) % 4]
                            tcnt += 2
                            tc3 = tbc[:, 0:G * TB].rearrange("p (g t) -> p g t", g=G)
                            ts3 = tbs[:, 0:G * TB].rearrange("p (g t) -> p g t", g=G)
                            P.dma(tc3, tI_d[L][0, tb, g], writes=[b_tbc])
                            P.dma(ts3, tI_d[L][1, tb, g], writes=[b_tbs])
                            for fi in range(G):
                                fc = g * G + fi
                                mm(PS[py][:, :TB], X[:, 0, fc, :], tc3[:, fi, :], fc == 0, False, [b_X, b_tbc], [PB[py]])
                                mm(PS[py][:, :TB], X[:, 1, fc, :], ts3[:, fi, :], False, fc == nj - 1, [b_X, b_tbs], [PB[py]])
                        sl = slice(tb * TB, (tb + 1) * TB)
                        stt(tmpx.rearrange("p j n -> p (j n)")[:, sl], z[:, sl], hbias[:, o, cc:cc + 1], PS[py][:, :TB], ALU.mult, ALU.add, [b_z, PB[py], b_sm, b_X], [b_tx])
                        tt("dve", znew[:, sl], tmpx.rearrange("p j n -> p (j n)")[:, sl], gate[:, sl], ALU.mult, [b_tx, b_g], [b_zn])
                    z, b_z = znew, b_zn
                ob = ar.bf(L)
                b_ob = Buf()
                cp("act", ob, z, [b_z], [b_ob])
                P.dma(hyo_s[cc][:, t0:t0 + L], ob, reads=[b_ob], writes=[b_hyo])

        if want("hyena"):
            for (t0, L) in streams:
                hyena_main(t0, L)

        b_fno = Buf()

        def fnet(t0, L):
            TB, G, nTB, nG = RL[L]
            nj = L // 128
            ar.reset()
            fz = [(ar.f32(L), Buf()) for _ in range(2)]
            zc = ar.bf(nj * 256).rearrange("p (j c) -> p j c", j=nj)
            zs = ar.bf(nj * 256).rearrange("p (j c) -> p j c", j=nj)
            b_zc = Buf()
            for cc in range(2):
                f, b_f = fz[cc]
                P.dma(f, fn_s[cc][:, t0:t0 + L], reads=[b_fn], writes=[b_f])
                for scn in range(nj):
                    pa = scn % 2
                    mm(PS[pa][:, 0:128], f[:, scn * 128:(scn + 1) * 128], bdc, True, True, [b_f, b_cst], [PB[pa]])
                    mm(PS[pa][:, 128:256], f[:, scn * 128:(scn + 1) * 128], bds, True, True, [b_f, b_cst], [PB[pa]])
                    cp("act", zc[:, scn, cc * 128:(cc + 1) * 128], PS[pa][:, 0:128], [PB[pa]], [b_zc])
                    cp("dve", zs[:, scn, cc * 128:(cc + 1) * 128], PS[pa][:, 128:256], [PB[pa]], [b_zc])
            tabs = [(ar.bf(2048), Buf()) for _ in range(4)]
            obs = [(ar.bf(512), Buf()) for _ in range(2)]
            tcnt = 0
            oc = 0
            for tb in range(nTB):
                for g in range(nG):
                    tbc, b_tbc = tabs[tcnt % 4]
                    tbs, b_tbs = tabs[(tcnt + 1) % 4]
                    tcnt += 2
                    tc3 = tbc[:, 0:G * TB].rearrange("p (g t) -> p g t", g=G)
                    ts3 = tbs[:, 0:G * TB].rearrange("p (g t) -> p g t", g=G)
                    P.dma(tc3, tN_d[L][0, tb, g], writes=[b_tbc])
                    P.dma(ts3, tN_d[L][1, tb, g], writes=[b_tbs])
                    for cc in range(2):
                        py = 2 + cc + 2 * (tb % 2)
                        for fi in range(G):
                            sc_ = g * G + fi
                            mm(PS[py][:, :TB], zc[:, sc_, cc * 128:(cc + 1) * 128], tc3[:, fi, :], sc_ == 0, False, [b_zc, b_tbc], [PB[py]])
                            mm(PS[py][:, :TB], zs[:, sc_, cc * 128:(cc + 1) * 128], ts3[:, fi, :], False, sc_ == nj - 1, [b_zc, b_tbs], [PB[py]])
                for cc in range(2):
                    py = 2 + cc + 2 * (tb % 2)
                    ob, b_ob = obs[oc % 2]
                    oc += 1
                    cp("act" if cc else "dve", ob[:, :TB], PS[py][:, :TB], [PB[py]], [b_ob])
                    P.dma(fno_s[cc][:, t0 + tb * TB:t0 + (tb + 1) * TB], ob[:, :TB], reads=[b_ob], writes=[b_fno])

        if want("fnet"):
            for (t0, L) in streams:
                fnet(t0, L)

        if want("merge"):
            ar.reset()
            wbr = ar.bf(8 * D).rearrange("p (k m) -> p k m", k=8)
            wo = ar.bf(8 * D).rearrange("p (k m) -> p k m", k=8)
            b_wbr = Buf()
            b_wo = Buf()
            w32 = ar.f32(8 * 512).rearrange("p (k m) -> p k m", k=8)
            b_w32 = Buf()
            for (src, dst, bd) in ((w_br_d, wbr, b_wbr), (w_out_d, wo, b_wo)):
                for half in range(2):
                    P.dma(w32, src[l][:, half * 512:(half + 1) * 512].rearrange("(k p) m -> p k m", p=128), writes=[b_w32])
                    cp("act", dst[:, 0:4, half * 512:(half + 1) * 512], w32[:, 0:4, :], [b_w32], [bd])
                    cp("dve", dst[:, 4:8, half * 512:(half + 1) * 512], w32[:, 4:8, :], [b_w32], [bd])
            ar.off -= 8 * 512
            P.barrier()
            nb = [(ar.bf(8 * 512).rearrange("p (k t) -> p k t", k=8), Buf()) for _ in range(2)]
            sb_ = [(ar.bf(8 * 512).rearrange("p (k t) -> p k t", k=8), Buf()) for _ in range(2)]
            g32 = [(ar.f32(8 * 384).rearrange("p (k j c) -> p k j c", k=8, j=3), Buf()) for _ in range(2)]
            gbf = [(ar.bf(8 * 384).rearrange("p (k j c) -> p k j c", k=8, j=3), Buf()) for _ in range(2)]
            sig = [(ar.f32(512), Buf()) for _ in range(3)]
            yacc = [(ar.f32(512), Buf()) for _ in range(2)]
            ytmp = [(ar.f32(512), Buf()) for _ in range(2)]
            yT = ar.bf(8 * 512).rearrange("p (k t) -> p k t", k=8)
            b_yT = [Buf() for _ in range(8)]
            KR = ((0, 2), (2, 4), (4, 8))
            gc = 0
            for bi, (t0, Tn) in enumerate(tb_mix):
                nbk, b_nb = nb[bi % 2]
                sbk, b_sb = sb_[bi % 2]
                P.dma(nbk[:, :, :Tn], nT_s[:, :, t0:t0 + Tn], reads=[b_nTs], writes=[b_nb])
                for cc in range(2):
                    P.dma(sbk[:, cc, :Tn], hyo_s[cc][:, t0:t0 + Tn], reads=[b_hyo], writes=[b_sb])
                    P.dma(sbk[:, 2 + cc, :Tn], fno_s[cc][:, t0:t0 + Tn], reads=[b_fno], writes=[b_sb])
                for h in range(4):
                    P.dma(sbk[:, 4 + h, :Tn], ato_s[h][:, t0:t0 + Tn], reads=[b_ato], writes=[b_sb])
                for m in range(8):
                    gw, b_gw = g32[gc % 2]
                    gb, b_gb = gbf[gc % 2]
                    gc += 1
                    P.dma(gw, w_gate_d[l, m], writes=[b_gw])
                    cp("act", gb, gw, [b_gw], [b_gb])
                    ya, b_ya = yacc[m % 2]
                    yt, b_yt = ytmp[m % 2]
                    for j in range(3):
                        pg = j
                        pbr = 3 + j
                        for k in range(8):
                            mm(PS[pg][:, :Tn], gb[:, k, j, :], nbk[:, k, :Tn], k == 0, k == 7, [b_gb, b_nb], [PB[pg]])
                        k0, k1 = KR[j]
                        for k in range(k0, k1):
                            mm(PS[pbr][:, :Tn], wbr[:, k, m * 128:(m + 1) * 128], sbk[:, k, :Tn], k == k0, k == k1 - 1, [b_wbr, b_sb], [PB[pbr]])
                        sg, b_sg = sig[(m * 3 + j) % 3]
                        act(sg[:, :Tn], PS[pg][:, :Tn], AF.Sigmoid, [PB[pg]], [b_sg])
                        if j == 0:
                            tt("dve", ya[:, :Tn], sg[:, :Tn], PS[pbr][:, :Tn], ALU.mult, [b_sg, PB[pbr]], [b_ya])
                        else:
                            tt("dve", yt[:, :Tn], sg[:, :Tn], PS[pbr][:, :Tn], ALU.mult, [b_sg, PB[pbr]], [b_yt])
                            if j == 1:
                                tt("dve", ya[:, :Tn], ya[:, :Tn], yt[:, :Tn], ALU.add, [b_ya, b_yt], [b_ya])
                            else:
                                tt("dve", yT[:, m, :Tn], ya[:, :Tn], yt[:, :Tn], ALU.add, [b_ya, b_yt], [b_yT[m]])
                jj = 0 if t0 < S else 1
                for mo in range(8):
                    po = 6 + mo % 2
                    for k in range(8):
                        mm(PS[po][:, :Tn], wo[:, k, mo * 128:(mo + 1) * 128], yT[:, k, :Tn], k == 0, k == 7, [b_wo, b_yT[k]], [PB[po]])
                    xb = b_x[mo][t0 // 512]
                    stt(xT[:, mo, t0:t0 + Tn], PS[po][:, :Tn], modv[:, 2, mo, jj:jj + 1], xT[:, mo, t0:t0 + Tn], ALU.mult, ALU.add, [PB[po], xb, b_sm], [xb])

        if want("peer"):
            peer(l, last, modulate, mod_tmps, A_ffn)

    def peer(l, last, modulate, mod_tmps, A_ffn):
        ar.reset()
        b_ub = Buf()
        b_vb = Buf()
        ld = [(ar.f32(4096), Buf()) for _ in range(3)]
        cv = [(ar.bf(4096), Buf()) for _ in range(3)]
        jobs = []
        for k in range(8):
            for eb in range(4):
                jobs.append(("u", k, eb))
        for e1g in range(32):
            jobs.append(("v", e1g, 0))

        def j_load(ci):
            kind, i0_, i1_ = jobs[ci]
            a, b_a = ld[ci % 3]
            if kind == "u":
                P.dma(a, uT_d[l][i0_ * 128:(i0_ + 1) * 128, i1_ * 4096:(i1_ + 1) * 4096], writes=[b_a])
            else:
                P.dma(a.rearrange("p (e d) -> p e d", e=4), v_d_in[l][i0_ * 512:(i0_ + 1) * 512, :].rearrange("(e p) d -> p e d", p=128), writes=[b_a])

        j_load(0)
        j_load(1)
        for ci in range(len(jobs)):
            if ci + 2 < len(jobs):
                j_load(ci + 2)
            kind, i0_, i1_ = jobs[ci]
            a, b_a = ld[ci % 3]
            o, b_o = cv[ci % 3]
            cp(("act", "dve", "pool")[ci % 3], o, a, [b_a], [b_o])
            if kind == "u":
                P.dma(uTb_s[i1_ * 8:(i1_ + 1) * 8, :, i0_, :].rearrange("g p e -> p g e"), o.rearrange("p (g e) -> p g e", g=8), reads=[b_o], writes=[b_ub])
            else:
                P.dma(vb_s[:, i0_ * 4:(i0_ + 1) * 4, :], o.rearrange("p (e d) -> p e d", e=4), reads=[b_o], writes=[b_vb])
        ar.reset()
        keysT = ar.f32(2048).rearrange("p (h n) -> p h n", h=16)
        b_ky = Buf()
        P.dma(keysT, keysT_d[l], writes=[b_ky])
        nbfs = [ar.bf(8 * 256).rearrange("p (k t) -> p k t", k=8) for _ in range(2)]
        b_nbfs = [[Buf() for _ in range(8)] for _ in range(2)]
        s1k = [ar.f32(1024).rearrange("p (h n) -> p h n", h=8) for _ in range(2)]
        a2k = [ar.f32(1024).rearrange("p (h n) -> p h n", h=8) for _ in range(2)]
        a1t = [ar.f32(128).rearrange("p (h a) -> p h a", h=8) for _ in range(2)]
        top1k = [ar.f32(128).rearrange("p (h a) -> p h a", h=8) for _ in range(2)]
        theta = [ar.f32(8) for _ in range(2)]
        b_keep = [Buf() for _ in range(2)]
        zer = ar.bf(512)
        b_zer = Buf()
        P.op("pool", lambda e: e.memset(zer, 0.0), [], [b_zer])
        base = ar.off
        mt = mod_tmps(256, (2, 1, 2))
        n32 = ar.f32(8 * 256).rearrange("p (k t) -> p k t", k=8)
        b_n32 = [Buf() for _ in range(8)]
        wq = [(ar.f32(8 * 128).rearrange("p (k m) -> p k m", k=8), Buf()) for _ in range(2)]
        qT = ar.f32(16 * 256).rearrange("p (h t) -> p h t", h=16)
        b_qT = Buf()
        s_sb = ar.f32(2048).rearrange("p (h n) -> p h n", h=16)
        b_s = Buf()
        tmpm = ar.f32(256)
        b_tm = Buf()
        tmpm2 = ar.f32(256)
        b_tm2 = Buf()
        top = ar.f32(256).rearrange("p (h a) -> p h a", h=16)
        b_top = Buf()
        cand = ar.f32(256)
        b_cand = Buf()
        ctop = ar.f32(192).rearrange("p (h a) -> p h a", h=8)
        b_ct = Buf()
        misc = ar.f32(16)
        b_mi = Buf()
        ex16 = ar.f32(128).rearrange("p (h a) -> p h a", h=8)
        ub = [(ar.bf(8 * 512).rearrange("p (k e) -> p k e", k=8), Buf()) for _ in range(2)]
        vbf = [(ar.bf(4 * 1024).rearrange("p (e d) -> p e d", e=4), Buf()) for _ in range(2)]
        Wt = [(ar.bf(4 * 256).rearrange("p (e t) -> p e t", e=4), Buf()) for _ in range(2)]
        gel = [(ar.f32(512), Buf()) for _ in range(2)]
        GT = [(ar.bf(512).rearrange("p (e t) -> p e t", e=2), Buf()) for _ in range(2)]
        blocks = tblocks(0, S if last else T, 256)
        b_wT = [Buf(), Buf()]

        def phaseA(t0, Tn, nbf, b_nbf):
            jj = 0 if t0 < S else 1
            def emit_ffn(k, t0_, Tn_, tm, b_t, Asc, Bsc):
                act(n32[:, k, :], tm[:, :Tn_], AF.Identity, [b_t, b_sm], [b_n32[k]], bias=Bsc, scale=Asc)
                cp("pool", nbf[:, k, :], n32[:, k, :], [b_n32[k]], [b_nbf[k]])

            modulate(A_ffn, 3, [(t0, Tn)], emit_ffn, mt, 2)
            yield
            for hp in range(16):
                w, b_w = wq[hp % 2]
                P.dma(w, wq_d[l][:, hp * 128:(hp + 1) * 128].rearrange("(k p) m -> p k m", p=128), writes=[b_w])
                pq = 2 + hp % 2
                for k in range(8):
                    mm(PS[pq][:, :Tn], w[:, k, :], n32[:, k, :], k == 0, k == 7, [b_w, b_n32[k]], [PB[pq]])
                cp("act" if hp % 2 else "dve", qT[:, hp, :], PS[pq][:, :Tn], [PB[pq]], [b_qT])
                yield
            for ti in range(2):
                tsl = slice(ti * 128, (ti + 1) * 128)
                bk = b_keep[ti]
                for hf in range(2):
                    for h8 in range(8):
                        hp = hf * 8 + h8
                        pbk = 2 + h8 // 4
                        mm(PS[pbk][:, (h8 % 4) * 128:(h8 % 4 + 1) * 128], qT[:, hp, tsl], keysT[:, hp, :], True, True, [b_qT, b_ky], [PB[pbk]])
                    for q2 in range(2):
                        cp("act" if q2 else "dve", s_sb[:, hf * 8 + q2 * 4:hf * 8 + (q2 + 1) * 4, :], PS[2 + q2][:, :].rearrange("p (h n) -> p h n", h=4), [PB[2 + q2]], [b_s])
                    yield
                for hp in range(16):
                    P.op("dve", (lambda hp: lambda e: e.max(out=top[:, hp, 0:8], in_=s_sb[:, hp, :]))(hp), [b_s], [b_top])
                    P.op("dve", (lambda hp: lambda e: e.match_replace(out=tmpm[:, 0:128], in_to_replace=top[:, hp, 0:8], in_values=s_sb[:, hp, :], imm_value=-1e30))(hp), [b_s, b_top], [b_tm])
                    P.op("dve", (lambda hp: lambda e: e.max(out=top[:, hp, 8:16], in_=tmpm[:, 0:128]))(hp), [b_tm], [b_top])
                    if hp % 2:
                        yield
                top4 = top.rearrange("p (h c) a -> p h c a", c=2)
                s4v = s_sb.rearrange("p (h c) n -> p h c n", c=2)
                for h in range(8):
                    ch = cand
                    tt("dve", cand.rearrange("p (a b) -> p a b", a=16), top4[:, h, 0, :].unsqueeze(2).to_broadcast([128, 16, 16]),
                       top4[:, h, 1, :].unsqueeze(1).to_broadcast([128, 16, 16]), ALU.add, [b_top, b_ct], [b_cand])
                    P.op("dve", (lambda h, ch: lambda e: e.max(out=ctop[:, h, 0:8], in_=ch))(h, ch), [b_cand], [b_ct])
                    P.op("dve", (lambda h, ch: lambda e: e.match_replace(out=tmpm, in_to_replace=ctop[:, h, 0:8], in_values=ch, imm_value=-1e30))(h, ch), [b_cand, b_ct], [b_tm])
                    P.op("dve", (lambda h: lambda e: e.max(out=ctop[:, h, 8:16], in_=tmpm))(h), [b_tm], [b_ct])
                    P.op("dve", (lambda h: lambda e: e.match_replace(out=tmpm2, in_to_replace=ctop[:, h, 8:16], in_values=tmpm, imm_value=-1e30))(h), [b_tm, b_ct], [b_tm2])
                    P.op("dve", (lambda h: lambda e: e.max(out=ctop[:, h, 16:24], in_=tmpm2))(h), [b_tm2], [b_ct])
                    yield
                tt("dve", ex16, ctop[:, :, 0:16], ctop[:, :, 0:1].to_broadcast([128, 8, 16]), ALU.subtract, [b_ct], [b_mi])
                act(ex16, ex16, AF.Exp, [b_mi], [b_mi])
                P.op("dve", lambda e: e.reduce_sum(out=misc[:, 8:16], in_=ex16, axis=AX.X), [b_mi], [b_mi])
                recip(misc[:, 8:16], misc[:, 8:16], [b_mi], [b_mi])
                m8 = misc[:, 0:8].unsqueeze(2)
                tt("dve", m8, ctop[:, :, 15:16], ctop[:, :, 16:17], ALU.add, [b_ct, b_mi], [b_mi])
                stt(m8, m8, 0.5, ctop[:, :, 0:1], ALU.mult, ALU.subtract, [b_mi, b_ct], [b_mi])
                act(misc[:, 0:8], misc[:, 0:8], AF.Exp, [b_mi], [b_mi])
                tt("dve", theta[ti], misc[:, 0:8], misc[:, 8:16], ALU.mult, [b_mi], [bk])
                cp("pool", s1k[ti], s4v[:, :, 0, :], [b_s], [bk])
                cp("pool", top1k[ti], top4[:, :, 0, :], [b_top], [bk])
                tt("dve", a2k[ti], s4v[:, :, 1, :], top4[:, :, 1, 0:1].to_broadcast([128, 8, 128]), ALU.subtract, [b_s, b_top], [bk])
                act(a2k[ti], a2k[ti], AF.Exp, [bk], [bk])
                tt("dve", a1t[ti], top4[:, :, 0, :], top4[:, :, 0, 0:1].to_broadcast([128, 8, 16]), ALU.subtract, [b_top], [bk])
                act(a1t[ti], a1t[ti], AF.Exp, [bk], [bk])
                tt("dve", a1t[ti], a1t[ti], misc[:, 8:16].unsqueeze(2).to_broadcast([128, 8, 16]), ALU.mult, [bk, b_mi], [bk])
                yield

        for _ in phaseA(blocks[0][0], blocks[0][1], nbfs[0], b_nbfs[0]):
            pass
        for bi, (t0, Tn) in enumerate(blocks):
            jj = 0 if t0 < S else 1
            nbf = nbfs[bi % 2]
            b_nbf = b_nbfs[bi % 2]
            P.barrier()
            ar.off = base
            pmt = ar.f32(2048).rearrange("p (h a e) -> p h a e", h=8, a=16)
            b_pm = Buf()
            csl = [(ar.bf(2048).rearrange("p (h a e) -> p h a e", h=8, a=16), Buf()) for _ in range(2)]
            CT = ar.bf(128 * 128).rearrange("p (e t) -> p e t", e=128)
            b_CT = Buf()
            osl = ar.bf(4096).rearrange("p (h a e) -> p h a e", h=8, a=16)
            b_osl = Buf()
            OTs = [(ar.bf(4096).rearrange("p (e t) -> p e t", e=32), Buf()) for _ in range(2)]
            WTs = [(ar.bf(4096).rearrange("p (e t) -> p e t", e=32), Buf()) for _ in range(2)]
            PSb = [PS[i][:, :].bitcast(BF16) for i in range(8)]
            evc = 0
            for ti in range(2):
                bk = b_keep[ti]
                for es in range(8):
                    tt("pool", pmt, a1t[ti].unsqueeze(3).to_broadcast([128, 8, 16, 16]),
                       a2k[ti][:, :, es * 16:(es + 1) * 16].unsqueeze(2).to_broadcast([128, 8, 16, 16]), ALU.mult, [bk], [b_pm])
                    cs, b_cs = csl[es % 2]
                    for h in range(8):
                        pmh = pmt[:, h].rearrange("p a e -> p (a e)")
                        stt(cs[:, h].rearrange("p a e -> p (a e)"), pmh, theta[ti][:, h:h + 1], pmh, ALU.is_ge, ALU.mult, [b_pm, bk], [b_cs])
                    csf = cs.rearrange("p h a e -> p (h a) e")
                    for half in range(2):
                        pb = 2 + (es * 2 + half) % 2
                        for e in range(8):
                            tr(PSb[pb][:, e * 128:(e + 1) * 128], csf[:, :, half * 8 + e], identb, [b_cs, b_cst], [PB[pb]])
                        e0 = es * 16 + half * 8
                        cp("act" if half else "dve", CT[:, e0:e0 + 8, :], PSb[pb][:, :].rearrange("p (e t) -> p e t", e=8), [PB[pb]], [b_CT])
                for r in range(4):
                    tt("dve", osl, s1k[ti][:, :, r * 32:(r + 1) * 32].unsqueeze(2).to_broadcast([128, 8, 16, 32]),
                       top1k[ti].unsqueeze(3).to_broadcast([128, 8, 16, 32]), ALU.is_equal, [bk], [b_osl])
                    osf = osl.rearrange("p h a e -> p (h a) e")
                    ot, b_ot = OTs[r % 2]
                    for q in range(4):
                        pb = 4 + q % 2
                        for e in range(8):
                            tr(PSb[pb][:, e * 128:(e + 1) * 128], osf[:, :, q * 8 + e], identb, [b_osl, b_cst], [PB[pb]])
                        cp("act" if q % 2 else "dve", ot[:, q * 8:(q + 1) * 8, :], PSb[pb][:, :].rearrange("p (e t) -> p e t", e=8), [PB[pb]], [b_ot])
                    wt, b_wt = WTs[r % 2]
                    for tg in range(8):
                        pb = 6 + tg % 2
                        for tk in range(16):
                            t_ = tg * 16 + tk
                            mm(PS[pb][:, tk * 32:(tk + 1) * 32], CT[:, :, t_], ot[:, :, t_], True, True, [b_CT, b_ot], [PB[pb]])
                        cp("dve" if evc % 2 else "act", wt[:, :, tg * 16:(tg + 1) * 16], PS[pb][:, :].rearrange("p (t e) -> p e t", t=16), [PB[pb]], [b_wt])
                        evc += 1
                    P.dma(wT_s[ti][:, r * 32:(r + 1) * 32, :], wt, reads=[b_wt], writes=[b_wT[ti]])
            P.barrier()
            genA = phaseA(blocks[bi + 1][0], blocks[bi + 1][1], nbfs[(bi + 1) % 2], b_nbfs[(bi + 1) % 2]) if bi + 1 < len(blocks) else None
            for pb in range(4, 8):
                mm(PS[pb][:, :], zer[:, 0:128], zer[:, 0:512], True, False, [b_zer], [PB[pb]])
            pend = None
            for g in range(32):
                u_, b_u = ub[g % 2]
                v_, b_vv = vbf[g % 2]
                w_, b_w_ = Wt[g % 2]
                P.dma(u_, uTb_s[g], reads=[b_ub], writes=[b_u])
                P.dma(v_, vb_s[:, g * 4:(g + 1) * 4, :], reads=[b_vb], writes=[b_vv])
                for ti in range(2):
                    P.dma(w_[:, :, ti * 128:(ti + 1) * 128], wT_s[ti][:, g * 4:(g + 1) * 4, :], reads=[b_wT[ti]], writes=[b_w_])
                for sub in range(2):
                    it = g * 2 + sub
                    ph = it % 2
                    for i in range(2):
                        chn = sub * 2 + i
                        for k in range(8):
                            mm(PS[ph][:, i * 256:(i + 1) * 256], u_[:, k, chn * 128:(chn + 1) * 128], nbf[:, k, :], k == 0, k == 7, [b_u, b_nbf[k]], [PB[ph]])
                    ge, b_ge = gel[it % 2]
                    gt, b_gt = GT[it % 2]
                    act(ge, PS[ph][:, :], AF.Gelu, [PB[ph]], [b_ge])
                    tt("dve", gt.rearrange("p e t -> p (e t)"), ge, w_[:, sub * 2:(sub + 1) * 2, :].rearrange("p e t -> p (e t)"), ALU.mult, [b_ge, b_w_], [b_gt])
                    if pend is not None:
                        pend()

                    def mk(v_=v_, gt=gt, sub=sub, b_vv=b_vv, b_gt=b_gt):
                        def f():
                            for dk in range(8):
                                pbo = 4 + dk // 2
                                for i in range(2):
                                    mm(PS[pbo][:, (dk % 2) * 256:(dk % 2 + 1) * 256], v_[:, sub * 2 + i, dk * 128:(dk + 1) * 128], gt[:, i, :], False, False, [b_vv, b_gt], [PB[pbo]])
                        return f
                    pend = mk()
                if genA is not None:
                    for _ in range(3):
                        next(genA, None)
            pend()
            if genA is not None:
                for _ in genA:
                    pass
            for dk in range(8):
                pbo = 4 + dk // 2
                xb = b_x[dk][t0 // 512]
                stt(xT[:, dk, t0:t0 + 256], PS[pbo][:, (dk % 2) * 256:(dk % 2 + 1) * 256], modv[:, 5, dk, jj:jj + 1], xT[:, dk, t0:t0 + 256], ALU.mult, ALU.add, [PB[pbo], xb, b_sm], [xb])

    for l in range(depth):
        layer(l)

    P.barrier()
    fin = []
    for k in range(8):
        fin.append(P.dma(yT_d[:, k, :], xT[:, k, 0:S], reads=b_x[k]))
    for name, (src, shape) in dbg_out.items():
        pass
    P.emit(list(P.dmas[-8:]))
    es.close()
    return nc

import ml_dtypes
_CONST = {}


def _consts():
    if _CONST:
        return _CONST
    f64 = np.float64
    c = np.zeros((6, 128, 128), f64)
    c[0] = np.eye(128)
    c[1] = 1.0
    c[2, :64, :64] = 1.0
    c[2, 64:, 64:] = 1.0
    for base in range(0, 128, 32):
        for d in range(16):
            c[3, base + d + 16, base + d] = -1.0
            c[3, base + d, base + d + 16] = 1.0
    ci = np.arange(64)
    ang = 2 * np.pi * np.outer(ci, ci) / 64.0
    for b in range(2):
        c[4, b * 64:(b + 1) * 64, b * 64:(b + 1) * 64] = np.cos(ang)
        c[5, b * 64:(b + 1) * 64, b * 64:(b + 1) * 64] = np.sin(ang)
    _CONST["cst"] = np.ascontiguousarray(c.transpose(1, 0, 2)).astype(np.float32)
    t = np.arange(S)
    row = (t // 64).astype(f64)
    col = (t % 64).astype(f64)
    inv = 10000.0 ** (-np.arange(0, 32, 2, dtype=f64) / 32.0)
    d = np.arange(128) % 64
    pos = np.where((d // 32)[:, None] == 0, row[None, :], col[None, :])
    a = pos * inv[d % 16][:, None]
    _CONST["rope"] = np.stack([np.cos(a), np.sin(a)]).astype(np.float32)
    for L in (S, C):
        p = np.arange(L, dtype=f64)
        tt_ = p / max(L - 1, 1)
        w = 2.0 * np.pi * p / L
        fr = np.linspace(1e-4, 15, 16)
        feats = np.concatenate([tt_[:, None], np.cos(w[:, None] * fr), -np.sin(w[:, None] * fr)], axis=-1)
        _CONST["feats%d" % L] = np.ascontiguousarray(feats.T).astype(np.float32)
        deltas = np.abs(np.linspace(math.log(1e-2) / 1.5, math.log(1e-2) / 0.3, 256))
        _CONST["dec%d" % L] = np.exp(-tt_[:, None] * deltas[None, :]).astype(np.float32)
        nj = L // 128
        s_ = np.arange(L)
        kk = np.outer(s_, 2 * s_ + 1) % (4 * L)
        angF = np.pi * kk / (2.0 * L)
        TcF = np.cos(angF)
        TsF = -np.sin(angF)
        tF = np.stack([TcF, TsF]).reshape(2, nj, 128, nj, 128).transpose(0, 3, 2, 1, 4)
        _CONST["tF%d" % L] = np.ascontiguousarray(tF).astype(np.float32).astype(ml_dtypes.bfloat16)
        TB = min(512, L)
        G = min(4, nj)
        nTB = L // TB
        nG = nj // G
        TcI = TcF.T / L
        TsI = TsF.T / L
        def rl(M):
            return M.reshape(nG, G, 128, nTB, TB).transpose(3, 0, 2, 1, 4)
        _CONST["tI%d" % L] = np.ascontiguousarray(np.stack([rl(TcI), rl(TsI)])).astype(np.float32).astype(ml_dtypes.bfloat16)
        k2 = np.outer(s_, s_) % L
        ang2 = 2 * np.pi * k2 / L
        sc_ = 1.0 / math.sqrt(64.0 * L)
        _CONST["tN%d" % L] = np.ascontiguousarray(np.stack([rl(np.cos(ang2) * sc_), rl(-np.sin(ang2) * sc_)])).astype(np.float32).astype(ml_dtypes.bfloat16)
    return _CONST


def _prep(inp):
    f = lambda a: np.ascontiguousarray(np.asarray(a, dtype=np.float32))
    w = {}
    w["w_ada"] = f(inp["w_ada"])
    w["bada"] = f(np.asarray(inp["b_ada"]).reshape(2, 6, 8, 128).transpose(0, 3, 1, 2))
    w["gmf"] = f(np.stack([np.asarray(inp["g_mix"]).reshape(2, 8, 128), np.asarray(inp["g_ffn"]).reshape(2, 8, 128)], axis=1).transpose(0, 3, 1, 2))
    win = np.asarray(inp["w_in"])
    w["w_in"] = f(win)
    w["w_gate"] = f(win[:, :, 2560:].reshape(2, 8, 128, 3, 8, 128).transpose(0, 4, 2, 1, 3, 5))
    hcw = np.concatenate([np.asarray(inp["hy_conv_w"]), np.asarray(inp["hy_conv_b"])[:, None, :]], axis=1)
    w["hcw"] = f(hcw.reshape(2, 4, 6, 128).transpose(0, 3, 2, 1))
    w["hy_w1"] = f(inp["hy_w1"])
    w["hy_fb"] = f(np.stack([inp["hy_b1"], inp["hy_freq"], inp["hy_b2"]], axis=-1))
    w["hy_w2"] = f(inp["hy_w2"])
    w["hy_w3"] = f(inp["hy_w3"])
    w["hy_bias"] = f(np.asarray(inp["hy_bias"]).reshape(2, 2, 2, 128).transpose(0, 3, 1, 2))
    w["gqk"] = f(np.stack([np.asarray(inp["g_q"]).reshape(2, 128), np.asarray(inp["g_k"]).reshape(2, 128)], axis=-1))
    w["lam"] = f(np.asarray(inp["lam"]).reshape(2, 1, 256))
    w["gsub"] = f(np.asarray(inp["g_sub"]).reshape(2, 128, 1))
    w["w_br"] = f(np.concatenate([inp["w_hy"], inp["w_fn"], inp["w_at"]], axis=1))
    w["w_out"] = f(inp["w_out"])
    w["peer_wq"] = f(inp["peer_wq"])
    w["keysT"] = f(np.asarray(inp["peer_keys"]).reshape(2, 16, 128, 128).transpose(0, 3, 1, 2))
    w["uT"] = f(np.asarray(inp["peer_u"]).transpose(0, 2, 1))
    w["peer_v"] = f(inp["peer_v"])
    w.update(_consts())
    return w


def _core_inputs(inp, b):
    X = np.concatenate([np.asarray(inp["x"][b]), np.asarray(inp["ctx"][b])], axis=0)
    xT = np.ascontiguousarray(X.T.reshape(8, 128, T).transpose(1, 0, 2)).astype(np.float32)
    cc = np.stack([np.asarray(inp["c"][b]), np.asarray(inp["c_ctx"])], axis=-1)
    cc = np.ascontiguousarray(cc.reshape(8, 128, 2).transpose(1, 0, 2)).astype(np.float32)
    return {"xT": xT, "cc": cc}


_NC = {}


def kernel(**inp):
    w = _prep(inp)
    if "nc" not in _NC:
        _NC["nc"] = build()
    nc = _NC["nc"]
    in_maps = []
    for b in range(8):
        m = dict(w)
        m.update(_core_inputs(inp, b))
        in_maps.append(m)
    res = run_bass_kernel_spmd(nc, in_maps, core_ids=list(range(8)))
    out = np.empty((8, S, D), np.float32)
    for b in range(8):
        yT = np.asarray(res.results[b]["yT"])
        out[b] = yT.transpose(2, 1, 0).reshape(S, D)
    return out
```
